# Optimizing a Trainium2 kernel written in Bass

```python
import math
import jax, jax.numpy as jnp
from jax import lax
import numpy as np

D_MODEL = 1024
BATCH = 8
SEQ = 4096
DEPTH = 4

DH = 64
QBLK = 128
NSA_HEADS = 8
NSA_GROUPS = 2
NSA_R = NSA_HEADS // NSA_GROUPS
L_CMP = 32
D_CMP = 16
CMP_HIDDEN = 128
L_SLC = 64
N_SEL = 8
WINDOW = 512
MLA_HEADS = 4
MLA_NOPE = 64
MLA_ROPE = 32
MLA_V = 64
MLA_Q_LORA = 384
MLA_KV_LORA = 128
ROPE_THETA = 10000.0
FOX_HEADS = 4
XA_HEADS = 4
MEM_LEN = 256
D_FF = 4 * D_MODEL
T5_BUCKETS = 32
T5_MAX_DIST = 128
NSA_W = NSA_HEADS * DH
NSA_KV = NSA_GROUPS * DH
MLA_W = MLA_HEADS * MLA_V
FOX_W = FOX_HEADS * DH
XA_W = XA_HEADS * DH
IN_SIZES = (NSA_W, NSA_KV, NSA_KV, NSA_KV, NSA_KV, NSA_KV, NSA_KV, 3 * NSA_HEADS,
            MLA_Q_LORA, MLA_KV_LORA, MLA_ROPE,
            FOX_W, FOX_W, FOX_W, FOX_HEADS)
N_IN = sum(IN_SIZES)
DN_ALPHA = (2 * DEPTH) ** 0.25
DN_BETA = (8 * DEPTH) ** -0.25
LN_EPS = 1e-5
RMS_EPS = 1e-6
NEG = -1e30
FORCE = 1e4

kernel_name = 'hybrid_nsa_mla_fox_block'


def layer_norm(x, g, b):
    x32 = x.astype(jnp.float32)
    mu = jnp.mean(x32, axis=-1, keepdims=True)
    var = jnp.mean(jnp.square(x32 - mu), axis=-1, keepdims=True)
    return ((x32 - mu) * lax.rsqrt(var + LN_EPS) * g + b).astype(x.dtype)


def rms_norm(x, g):
    x32 = x.astype(jnp.float32)
    return (x32 * lax.rsqrt(jnp.mean(jnp.square(x32), axis=-1, keepdims=True) + RMS_EPS) * g).astype(x.dtype)


def rope(x, pos):
    half = x.shape[-1] // 2
    inv = ROPE_THETA ** (-jnp.arange(half, dtype=jnp.float32) / half)
    ang = pos.astype(jnp.float32)[:, None] * inv[None, :]
    cos = jnp.cos(ang)[None, :, None, :].astype(x.dtype)
    sin = jnp.sin(ang)[None, :, None, :].astype(x.dtype)
    x1, x2 = x[..., :half], x[..., half:]
    return jnp.concatenate([x1 * cos - x2 * sin, x1 * sin + x2 * cos], axis=-1)


def t5_bucket(dist):
    n = jnp.maximum(dist, 0)
    max_exact = T5_BUCKETS // 2
    nf = jnp.maximum(n, 1).astype(jnp.float32)
    large = max_exact + (jnp.log(nf / max_exact) / math.log(T5_MAX_DIST / max_exact)
                         * (T5_BUCKETS - max_exact)).astype(jnp.int32)
    large = jnp.minimum(large, T5_BUCKETS - 1)
    return jnp.where(n < max_exact, n, large)


def causal_block_attention(q, k, v, cum_log_f=None):
    B, S, H, dk = q.shape
    dv = v.shape[-1]
    nqb = S // QBLK
    scale = dk ** -0.5
    qb = q.reshape(B, nqb, QBLK, H, dk).transpose(1, 0, 2, 3, 4)
    key_pos = jnp.arange(S)
    decay_k = None if cum_log_f is None else cum_log_f.transpose(0, 2, 1)

    def one_block(args):
        i, q_i = args
        q_pos = i * QBLK + jnp.arange(QBLK)
        s = jnp.einsum('bqhd,bkhd->bhqk', q_i, k).astype(jnp.float32) * scale
        if decay_k is not None:
            decay_q = lax.dynamic_slice_in_dim(decay_k, i * QBLK, QBLK, axis=2)
            s = s + decay_q[..., None] - decay_k[:, :, None, :]
        s = jnp.where(key_pos[None, :] <= q_pos[:, None], s, NEG)
        p = jax.nn.softmax(s, axis=-1).astype(v.dtype)
        return jnp.einsum('bhqk,bkhd->bqhd', p, v)

    out = lax.map(one_block, (jnp.arange(nqb), qb))
    return out.transpose(1, 0, 2, 3, 4).reshape(B, S, H * dv)


def nsa_attention(q, k_cmp, v_cmp, k_slc, v_slc, k_win, v_win, gate_logits,
                  cmp_pe, cmp_w1, cmp_w2, t5_table):
    B, S, _ = q.shape
    G, R = NSA_GROUPS, NSA_R
    nqb = S // QBLK
    n_cmp = (S - L_CMP) // D_CMP + 1
    n_slc = S // L_SLC
    n_sel = min(N_SEL, n_slc)
    scale = DH ** -0.5
    k_cmp, v_cmp, k_slc, v_slc, k_win, v_win = [a.reshape(B, S, G, DH) for a in
                                                  (k_cmp, v_cmp, k_slc, v_slc, k_win, v_win)]

    tok_idx = jnp.arange(n_cmp)[:, None] * D_CMP + jnp.arange(L_CMP)[None, :]

    def compress(kv, pe, w1, w2):
        blocks = kv[:, tok_idx] + pe[None, None, :, None, :]
        flat = blocks.transpose(0, 1, 3, 2, 4).reshape(B, n_cmp, G, L_CMP * DH)
        return jax.nn.gelu(flat @ w1) @ w2

    kc = compress(k_cmp, cmp_pe[0], cmp_w1[0], cmp_w2[0])
    vc = compress(v_cmp, cmp_pe[1], cmp_w1[1], cmp_w2[1])
    cmp_end = jnp.arange(n_cmp) * D_CMP + (L_CMP - 1)

    c_lo = jnp.arange(n_cmp)[:, None] * D_CMP
    s_lo = jnp.arange(n_slc)[None, :] * L_SLC
    overlap = (jnp.maximum(jnp.minimum(c_lo + L_CMP, s_lo + L_SLC) - jnp.maximum(c_lo, s_lo), 0)
               .astype(jnp.float32) / D_CMP)

    ks_blk = k_slc.reshape(B, n_slc, L_SLC, G, DH).transpose(0, 3, 1, 2, 4)
    vs_blk = v_slc.reshape(B, n_slc, L_SLC, G, DH).transpose(0, 3, 1, 2, 4)
    kw_pad = jnp.pad(k_win, ((0, 0), (WINDOW, 0), (0, 0), (0, 0)))
    vw_pad = jnp.pad(v_win, ((0, 0), (WINDOW, 0), (0, 0), (0, 0)))

    qb = q.reshape(B, nqb, QBLK, G, R, DH).transpose(1, 0, 3, 4, 2, 5)
    gb = jax.nn.sigmoid(gate_logits).reshape(B, nqb, QBLK, G, R, 3).transpose(1, 0, 3, 4, 2, 5)
    bias_tab = t5_table.reshape(T5_BUCKETS, G, R).transpose(1, 2, 0)
    b_idx = jnp.arange(B)[:, None, None, None]
    g_idx = jnp.arange(G)[None, :, None, None]
    g5 = jnp.arange(G)[None, :, None, None, None]
    r5 = jnp.arange(R)[None, None, :, None, None]
    blk_ids = jnp.arange(n_slc)

    def one_block(args):
        i, q_i, g_i = args
        t = i * QBLK + jnp.arange(QBLK)
        dist_c = t[:, None] - cmp_end[None, :]
        valid_c = dist_c >= 0
        s_c = (jnp.einsum('bgrqd,bcgd->bgrqc', q_i, kc).astype(jnp.float32) * scale
               + bias_tab[:, :, t5_bucket(dist_c)])
        p_c = jax.nn.softmax(jnp.where(valid_c, s_c, NEG), axis=-1) * valid_c
        o_c = jnp.einsum('bgrqc,bcgd->bgrqd', p_c.astype(vc.dtype), vc)
        imp = jnp.einsum('bgrqc,cn->bgqn', p_c, overlap)
        cur = t // L_SLC
        forced = ((blk_ids[None, :] == 0) | (blk_ids[None, :] == cur[:, None])
                  | (blk_ids[None, :] == cur[:, None] - 1))
        causal_blk = blk_ids[None, :] * L_SLC <= t[:, None]
        imp = jnp.where(causal_blk, imp + FORCE * forced, NEG)
        _, sel = lax.top_k(imp, n_sel)
        k_sel = ks_blk[b_idx, g_idx, sel].reshape(B, G, QBLK, n_sel * L_SLC, DH)
        v_sel = vs_blk[b_idx, g_idx, sel].reshape(B, G, QBLK, n_sel * L_SLC, DH)
        pos_sel = (sel[..., None] * L_SLC + jnp.arange(L_SLC)).reshape(B, G, QBLK, n_sel * L_SLC)
        dist_s = t[:, None] - pos_sel
        s_s = (jnp.einsum('bgrqd,bgqkd->bgrqk', q_i, k_sel).astype(jnp.float32) * scale
               + bias_tab[g5, r5, t5_bucket(dist_s)[:, :, None]])
        p_s = jax.nn.softmax(jnp.where((dist_s >= 0)[:, :, None], s_s, NEG), axis=-1)
        o_s = jnp.einsum('bgrqk,bgqkd->bgrqd', p_s.astype(v_sel.dtype), v_sel)
        pos_w = i * QBLK - WINDOW + jnp.arange(WINDOW + QBLK)
        dist_w = t[:, None] - pos_w[None, :]
        valid_w = (dist_w >= 0) & (dist_w < WINDOW) & (pos_w[None, :] >= 0)
        k_w = lax.dynamic_slice_in_dim(kw_pad, i * QBLK, WINDOW + QBLK, axis=1)
        v_w = lax.dynamic_slice_in_dim(vw_pad, i * QBLK, WINDOW + QBLK, axis=1)
        s_w = (jnp.einsum('bgrqd,bkgd->bgrqk', q_i, k_w).astype(jnp.float32) * scale
               + bias_tab[:, :, t5_bucket(dist_w)])
        p_w = jax.nn.softmax(jnp.where(valid_w, s_w, NEG), axis=-1)
        o_w = jnp.einsum('bgrqk,bkgd->bgrqd', p_w.astype(v_w.dtype), v_w)
        return g_i[..., 0:1] * o_c + g_i[..., 1:2] * o_s + g_i[..., 2:3] * o_w

    out = lax.map(one_block, (jnp.arange(nqb), qb, gb))
    return out.transpose(1, 0, 4, 2, 3, 5).reshape(B, S, NSA_W)


def mla_attention(c_q, c_kv, k_rope, q_norm, w_uq, kv_norm, w_ukv):
    B, S, _ = c_q.shape
    pos = jnp.arange(S)
    q = (rms_norm(c_q, q_norm) @ w_uq).reshape(B, S, MLA_HEADS, MLA_NOPE + MLA_ROPE)
    q = jnp.concatenate([q[..., :MLA_NOPE], rope(q[..., MLA_NOPE:], pos)], axis=-1)
    kv = (rms_norm(c_kv, kv_norm) @ w_ukv).reshape(B, S, MLA_HEADS, MLA_NOPE + MLA_V)
    k_r = jnp.broadcast_to(rope(k_rope[:, :, None, :], pos), (B, S, MLA_HEADS, MLA_ROPE))
    k = jnp.concatenate([kv[..., :MLA_NOPE], k_r], axis=-1)
    return causal_block_attention(q, k, kv[..., MLA_NOPE:])


def fox_attention(q, k, v, f_logit, b_f):
    B, S, _ = q.shape
    shp = (B, S, FOX_HEADS, DH)
    cum = jnp.cumsum(jax.nn.log_sigmoid((f_logit + b_f).astype(jnp.float32)), axis=1)
    return causal_block_attention(q.reshape(shp), k.reshape(shp), v.reshape(shp), cum_log_f=cum)


def hybrid_mixer(h, w_in, cmp_pe, cmp_w1, cmp_w2, t5_table, q_norm, w_uq, kv_norm, w_ukv,
                 b_f, w_gate, w_br_nsa, w_br_mla, w_br_fox, w_out):
    B, S, _ = h.shape
    split_points = np.cumsum(IN_SIZES)[:-1].tolist()
    (nq, nkc, nvc, nks, nvs, nkw, nvw, ngate,
     cq, ckv, kr, fq, fk, fv, ff) = jnp.split(h @ w_in, split_points, axis=-1)
    o_nsa = nsa_attention(nq, nkc, nvc, nks, nvs, nkw, nvw, ngate, cmp_pe, cmp_w1, cmp_w2, t5_table)
    o_mla = mla_attention(cq, ckv, kr, q_norm, w_uq, kv_norm, w_ukv)
    o_fox = fox_attention(fq, fk, fv, ff, b_f)
    gates = jax.nn.sigmoid(h @ w_gate).reshape(B, S, 3, D_MODEL)
    merged = (gates[:, :, 0] * (o_nsa @ w_br_nsa) + gates[:, :, 1] * (o_mla @ w_br_mla)
              + gates[:, :, 2] * (o_fox @ w_br_fox))
    return merged @ w_out


def memory_cross_attention(x, mem, w_q, w_kv, w_o):
    B, S, _ = x.shape
    M = mem.shape[1]
    q = (x @ w_q).reshape(B, S, XA_HEADS, DH)
    kv = (mem @ w_kv).reshape(B, M, 2, XA_HEADS, DH)
    k, v = kv[:, :, 0], kv[:, :, 1]
    s = jnp.einsum('bqhd,bmhd->bhqm', q, k).astype(jnp.float32) * (DH ** -0.5)
    p = jax.nn.softmax(s, axis=-1).astype(v.dtype)
    o = jnp.einsum('bhqm,bmhd->bqhd', p, v).reshape(B, S, XA_W)
    return o @ w_o


def setup_inputs(seed: int = 0) -> dict:
    key = jax.random.key(seed)
    ks = jax.random.split(key, 32)
    L = DEPTH

    def dense(k, shape, fan_in, scale=1.0):
        return jax.random.normal(k, shape, jnp.float32) * (scale * fan_in ** -0.5)

    def normal(k, shape, scale=1.0):
        return jax.random.normal(k, shape, jnp.float32) * scale

    return {
        'x': normal(ks[0], (BATCH, SEQ, D_MODEL)),
        'mem': normal(ks[1], (BATCH, MEM_LEN, D_MODEL)),
        'w_in': dense(ks[2], (L, D_MODEL, N_IN), D_MODEL),
        'cmp_pe': normal(ks[3], (L, 2, L_CMP, DH), 0.1),
        'cmp_w1': dense(ks[4], (L, 2, L_CMP * DH, CMP_HIDDEN), L_CMP * DH),
        'cmp_w2': dense(ks[5], (L, 2, CMP_HIDDEN, DH), CMP_HIDDEN),
        't5_table': normal(ks[6], (T5_BUCKETS, NSA_HEADS), 0.5),
        'mla_q_norm': 1.0 + normal(ks[7], (L, MLA_Q_LORA), 0.02),
        'mla_w_uq': dense(ks[8], (L, MLA_Q_LORA, MLA_HEADS * (MLA_NOPE + MLA_ROPE)), MLA_Q_LORA),
        'mla_kv_norm': 1.0 + normal(ks[9], (L, MLA_KV_LORA), 0.02),
        'mla_w_ukv': dense(ks[10], (L, MLA_KV_LORA, MLA_HEADS * (MLA_NOPE + MLA_V)), MLA_KV_LORA),
        'fox_b_f': jax.random.uniform(ks[11], (L, FOX_HEADS), jnp.float32, 1.0, 6.0),
        'w_gate': dense(ks[12], (L, D_MODEL, 3 * D_MODEL), D_MODEL),
        'w_br_nsa': dense(ks[13], (L, NSA_W, D_MODEL), NSA_W),
        'w_br_mla': dense(ks[14], (L, MLA_W, D_MODEL), MLA_W),
        'w_br_fox': dense(ks[15], (L, FOX_W, D_MODEL), FOX_W),
        'w_mix_out': dense(ks[16], (L, D_MODEL, D_MODEL), D_MODEL, DN_BETA),
        'xa_w_q': dense(ks[17], (L, D_MODEL, XA_W), D_MODEL),
        'xa_w_kv': dense(ks[18], (L, D_MODEL, 2 * XA_W), D_MODEL),
        'xa_w_o': dense(ks[19], (L, XA_W, D_MODEL), XA_W, DN_BETA),
        'mlp_w_up': dense(ks[20], (L, D_MODEL, D_FF), D_MODEL),
        'mlp_w_down': dense(ks[21], (L, D_FF, D_MODEL), D_FF, DN_BETA),
        'ln_g': 1.0 + normal(ks[22], (L, 3, D_MODEL), 0.02),
        'ln_b': normal(ks[23], (L, 3, D_MODEL), 0.02),
    }


def reference(x, mem, w_in, cmp_pe, cmp_w1, cmp_w2, t5_table, mla_q_norm, mla_w_uq, mla_kv_norm,
              mla_w_ukv, fox_b_f, w_gate, w_br_nsa, w_br_mla, w_br_fox, w_mix_out, xa_w_q, xa_w_kv,
              xa_w_o, mlp_w_up, mlp_w_down, ln_g, ln_b):
    for l in range(DEPTH):
        y = hybrid_mixer(x, w_in[l], cmp_pe[l], cmp_w1[l], cmp_w2[l], t5_table, mla_q_norm[l],
                         mla_w_uq[l], mla_kv_norm[l], mla_w_ukv[l], fox_b_f[l], w_gate[l],
                         w_br_nsa[l], w_br_mla[l], w_br_fox[l], w_mix_out[l])
        x = layer_norm(DN_ALPHA * x + y, ln_g[l, 0], ln_b[l, 0])
        y = memory_cross_attention(x, mem, xa_w_q[l], xa_w_kv[l], xa_w_o[l])
        x = layer_norm(DN_ALPHA * x + y, ln_g[l, 1], ln_b[l, 1])
        y = jnp.square(jax.nn.relu(x @ mlp_w_up[l])) @ mlp_w_down[l]
        x = layer_norm(DN_ALPHA * x + y, ln_g[l, 2], ln_b[l, 2])
    return x
```

```python
import math
from contextlib import ExitStack
import numpy as np
import ml_dtypes
import concourse.bass as bass
import concourse.mybir as mybir
from concourse.bass_utils import run_bass_kernel_spmd

F32 = mybir.dt.float32
BF16 = mybir.dt.bfloat16
AF = mybir.ActivationFunctionType
ALU = mybir.AluOpType
AX = mybir.AxisListType

D = 1024
DEPTH_FULL = 4
SEQ_FULL = 4096
MEM = 256
N_IN = 2620
DN_ALPHA = (2 * DEPTH_FULL) ** 0.25
BIG = 30000.0


class Tok:
    __slots__ = ("sem", "val", "know")

    def __init__(s, sem, val, know):
        s.sem = sem
        s.val = val
        s.know = know


class Buf:
    __slots__ = ("w", "r", "name")

    def __init__(s, name=""):
        s.w = None
        s.r = {}
        s.name = name


class TT:
    def __init__(s, h, name=""):
        s.h = h
        s.buf = Buf(name)

    def __getitem__(s, k):
        return s.h[k]


class Eng:
    def __init__(s, name, e, sem, semid):
        s.name = name
        s.e = e
        s.sem = sem
        s.semid = semid
        s.cnt = 0
        s.know = {}


class Lane:
    def __init__(s, sem, semid):
        s.sem = sem
        s.semid = semid
        s.val = 0


def _bufs(xs):
    out = []
    for x in xs:
        if x is None:
            continue
        out.append(x.buf if hasattr(x, "buf") else x)
    return out


class Sched:
    NL = 8

    def __init__(s, nc, es):
        s.nc = nc
        s.sems = []

        def mk(n):
            sem = es.enter_context(nc.semaphore(n))
            s.sems.append(sem)
            return sem, len(s.sems) - 1

        s.pe = Eng("pe", nc.tensor, *mk("s_pe"))
        s.dve = Eng("dve", nc.vector, *mk("s_dve"))
        s.act = Eng("act", nc.scalar, *mk("s_act"))
        s.pool = Eng("pool", nc.gpsimd, *mk("s_pool"))
        s.sp = Eng("sp", nc.sync, *mk("s_sp"))
        s.engs = [s.pe, s.dve, s.act, s.pool, s.sp]
        s.lanes = {}
        s.rr = {}
        for q in ("sp", "pool"):
            s.lanes[q] = [Lane(*mk("l_%s_%d" % (q, i))) for i in range(s.NL)]
            s.rr[q] = 0
        s.q = {"sp": s.sp, "pool": s.pool}
        s.ninst = 0

    def _wait(s, E, tok):
        if tok is None:
            return
        if E.know.get(tok.sem, 0) >= tok.val:
            return
        if tok.sem == E.semid and E is s.pe:
            return
        E.e.wait_ge(s.sems[tok.sem], tok.val)
        k = dict(E.know)
        for a, b in tok.know.items():
            if k.get(a, 0) < b:
                k[a] = b
        if k.get(tok.sem, 0) < tok.val:
            k[tok.sem] = tok.val
        E.know = k

    def _deps(s, E, r, w):
        for b in r:
            s._wait(E, b.w)
        for b in w:
            s._wait(E, b.w)
            for t in list(b.r.values()):
                s._wait(E, t)

    def op(s, E, fn, r=(), w=()):
        r = _bufs(r)
        w = _bufs(w)
        s._deps(E, r, w)
        inst = fn(E.e)
        E.cnt += 1
        inst.then_inc(E.sem, 1)
        tok = Tok(E.semid, E.cnt, E.know)
        for b in r:
            b.r[E.semid] = tok
        for b in w:
            b.w = tok
            b.r = {}
        s.ninst += 1
        return tok

    def PE(s, fn, r=(), w=()):
        return s.op(s.pe, fn, r, w)

    def DVE(s, fn, r=(), w=()):
        return s.op(s.dve, fn, r, w)

    def ACT(s, fn, r=(), w=()):
        return s.op(s.act, fn, r, w)

    def POOL(s, fn, r=(), w=()):
        return s.op(s.pool, fn, r, w)

    def dma(s, out, in_, r=(), w=(), q="sp", **kw):
        Q = s.q[q]
        r = _bufs(r)
        w = _bufs(w)
        s._deps(Q, r, w)
        lanes = s.lanes[q]
        i = s.rr[q]
        s.rr[q] = (i + 1) % len(lanes)
        lane = lanes[i]
        if lane.val > 0:
            s._wait(Q, Tok(lane.semid, lane.val, {}))
        inst = Q.e.dma_start(out=out, in_=in_, **kw)
        lane.val += 16
        inst.then_inc(lane.sem, 16)
        tok = Tok(lane.semid, lane.val, Q.know)
        for b in r:
            b.r[lane.semid] = tok
        for b in w:
            b.w = tok
            b.r = {}
        s.ninst += 1
        return tok

    def barrier(s):
        toks = [Tok(E.semid, E.cnt, {}) for E in s.engs if E.cnt > 0]
        for q in s.lanes:
            for l in s.lanes[q]:
                if l.val > 0:
                    toks.append(Tok(l.semid, l.val, {}))
        for E in s.engs:
            for t in toks:
                s._wait(E, t)


class Ring:
    def __init__(s, items):
        s.items = items
        s.i = 0

    def next(s):
        x = s.items[s.i]
        s.i = (s.i + 1) % len(s.items)
        return x


class DT:
    def __init__(s, ap, name):
        s.ap = ap
        s.name = name
        s.bufs = {}
        s.buf = Buf(name)

    def b(s, key):
        if key not in s.bufs:
            s.bufs[key] = Buf("%s_%s" % (s.name, key))
        return s.bufs[key]


def t5_bucket_np(dist):
    n = np.maximum(dist, 0)
    nf = np.maximum(n, 1).astype(np.float32)
    large = 16 + (np.log(nf / np.float32(16)) / np.float32(math.log(128 / 16)) * np.float32(16)).astype(np.int32)
    large = np.minimum(large, 31)
    return np.where(n < 16, n, large)


def host_consts(S):
    NT = S // 128
    NCP = S // 16
    c = {}
    c["ident_bf"] = np.eye(128, dtype=np.float32).astype(ml_dtypes.bfloat16)
    k = np.arange(128)[:, None]
    q = np.arange(128)[None, :]
    tri = (q >= k).astype(np.float32)
    c["tri_bf"] = tri.astype(ml_dtypes.bfloat16)
    c["tri4_bf"] = np.tile(tri[:, None, :], (1, 4, 1)).astype(ml_dtypes.bfloat16)
    atri = (k > q).astype(np.float32)
    c["atri4_bf"] = np.tile(atri[:, None, :], (1, 4, 1)).astype(ml_dtypes.bfloat16)
    m = np.ones((2, 128, 8, 128), np.float32)
    m[0] = np.tile(tri[:, None, :], (1, 8, 1))
    c["nearmask"] = m
    half = 16
    inv = (10000.0 ** (-np.arange(half, dtype=np.float32) / half)).astype(np.float32)
    ang = np.arange(S, dtype=np.float32)[None, :] * inv[:, None]
    cos = np.cos(ang).astype(np.float32)
    sin = np.sin(ang).astype(np.float32)
    cs = np.zeros((96, S), np.float32)
    ss = np.zeros((96, S), np.float32)
    for base in (0, 64):
        cs[base:base + 16] = cos
        cs[base + 16:base + 32] = cos
        ss[base:base + 16] = -sin
        ss[base + 16:base + 32] = sin
    c["rope_cs"] = cs
    c["rope_ss"] = ss
    n_slc = S // 64
    c_lo = np.arange(NCP)[:, None] * 16
    s_lo = np.arange(64)[None, :] * 64
    ov = np.maximum(np.minimum(c_lo + 32, s_lo + 64) - np.maximum(c_lo, s_lo), 0).astype(np.float32) / 16.0
    ov[:, n_slc:] = 0.0
    c["overlap_bf"] = ov.astype(ml_dtypes.bfloat16)
    t = np.arange(S)[:, None]
    blk = np.arange(64)[None, :]
    cur = t // 64
    forced = (blk == 0) | (blk == cur) | (blk == cur - 1)
    causal = (blk * 64 <= t) & (blk < n_slc)
    A = np.where(causal, np.where(forced, 1e30, 0.0), -1e30).astype(np.float32)
    c["aforced"] = A
    ex = np.zeros((64, S), np.float32)
    ex[np.arange(S) // 64, np.arange(S)] = BIG
    c["expand_bf"] = ex.astype(ml_dtypes.bfloat16)
    c["ones_bf"] = np.ones((128, 512), np.float32).astype(ml_dtypes.bfloat16)
    OFF = 8 * (NT - 1)
    NROW = ((NCP + OFF + 127) // 128) * 128
    cc = np.arange(NROW)[:, None] - OFF
    ql = np.arange(128)[None, :]
    dist = ql - 16 * cc - 31
    c["cmp_dist"] = dist
    vc = (dist >= 0).astype(np.float32)
    c["cmp_valid"] = np.tile(vc[:, None, :], (1, 8, 1)).astype(np.float32)
    return c


def t5_gather(t5_table, consts):
    k = np.arange(128)[:, None]
    q = np.arange(128)[None, :]
    b0 = t5_bucket_np(q - k)
    b1 = t5_bucket_np(128 + q - k)
    near = np.stack([t5_table[b0], t5_table[b1]], 0)
    near = np.ascontiguousarray(near.transpose(0, 1, 3, 2))
    bc = t5_bucket_np(consts["cmp_dist"])
    cmpb = np.ascontiguousarray(t5_table[bc].transpose(0, 2, 1))
    c31 = np.ascontiguousarray(np.broadcast_to(t5_table[31][None, :, None], (128, 8, 128)))
    return near.astype(np.float32), cmpb.astype(np.float32), c31.astype(np.float32)


class Prog:
    def __init__(s, S, depth, debug=False):
        s.S = S
        s.depth = depth
        s.debug = debug
        s.NT = S // 128
        s.NB = S // 512
        s.NCP = S // 16
        s.NCMP = (S - 32) // 16 + 1
        s.CR = min(128, s.NCP)
        s.CT = (s.NCP + 127) // 128
        s.OFF = 8 * (s.NT - 1)
        s.nc = bass.Bass("TRN2", target_bir_lowering=False)
        s.es = ExitStack()
        s.sc = Sched(s.nc, s.es)
        s.dbg_outs = []
        s.inputs = {}

    def din(s, name, shape, dt=F32):
        ap = s.nc.dram_tensor(name, list(shape), dt, kind="ExternalInput").ap()
        s.inputs[name] = ap
        return DT(ap, name)

    def dscr(s, name, shape, dt, out=False):
        kind = "ExternalOutput" if (out or (s.debug and name in s.debug)) else "Internal"
        if kind == "ExternalOutput":
            s.dbg_outs.append(name)
        ap = s.nc.dram_tensor(name, list(shape), dt, kind=kind).ap()
        return DT(ap, name)

    def sb(s, es, name, shape, dt):
        s.uid = getattr(s, "uid", 0) + 1
        name = "%s_u%d" % (name, s.uid)
        return TT(es.enter_context(s.nc.sbuf_tensor(name, list(shape), dt)), name)

    def ps(s, es, name, shape, dt=F32):
        s.uid = getattr(s, "uid", 0) + 1
        name = "%s_u%d" % (name, s.uid)
        return TT(es.enter_context(s.nc.psum_tensor(name, list(shape), dt)), name)

    def load_w(s, dst, src2d, P, A, N, stage_ring, dcol=0):
        sc = s.sc
        src3 = src2d.rearrange("(a p) n -> p a n", p=P)
        maxc = max(1, 2048 // A)
        c0 = 0
        k = 0
        while c0 < N:
            ncol = min(maxc, N - c0)
            st = stage_ring.next()
            sv = st[0:P, 0:A * ncol].rearrange("p (a n) -> p a n", a=A)
            sc.dma(sv, src3[:, :, c0:c0 + ncol], w=[st])
            k += 1
            if k % 4 == 0:
                sc.DVE(lambda e: e.tensor_copy(out=dst[0:P, 0:A, dcol + c0:dcol + c0 + ncol], in_=sv), r=[st], w=[dst])
            else:
                sc.POOL(lambda e: e.tensor_copy(out=dst[0:P, 0:A, dcol + c0:dcol + c0 + ncol], in_=sv), r=[st], w=[dst])
            c0 += ncol

    def stages(s, es, n=2):
        return Ring([s.sb(es, "wstage%d" % i, [128, 2048], F32) for i in range(n)])

    def build(s):
        S, NT, NB = s.S, s.NT, s.NB
        L = s.depth
        s.x_in = s.din("x", [S, D])
        s.mem_in = s.din("mem", [MEM, D])
        s.w_in = s.din("w_in", [L, D, N_IN])
        s.cmp_pe = s.din("cmp_pe", [L, 2, 32, 64])
        s.cmp_w1 = s.din("cmp_w1", [L, 2, 2048, 128])
        s.cmp_w2 = s.din("cmp_w2", [L, 2, 128, 64])
        s.q_norm = s.din("mla_q_norm", [L, 384])
        s.w_uq = s.din("mla_w_uq", [L, 384, 384])
        s.kv_norm = s.din("mla_kv_norm", [L, 128])
        s.w_ukv = s.din("mla_w_ukv", [L, 128, 512])
        s.b_f = s.din("fox_b_f", [L, 4])
        s.w_gate = s.din("w_gate", [L, D, 3 * D])
        s.w_br_nsa = s.din("w_br_nsa", [L, 512, D])
        s.w_br_mla = s.din("w_br_mla", [L, 256, D])
        s.w_br_fox = s.din("w_br_fox", [L, 256, D])
        s.w_mix_out = s.din("w_mix_out", [L, D, D])
        s.xa_w_q = s.din("xa_w_q", [L, D, 256])
        s.xa_w_kv = s.din("xa_w_kv", [L, D, 512])
        s.xa_w_o = s.din("xa_w_o", [L, 256, D])
        s.w_up = s.din("mlp_w_up", [L, D, 4 * D])
        s.w_down = s.din("mlp_w_down", [L, 4 * D, D])
        s.ln_g = s.din("ln_g", [L, 3, D])
        s.ln_b = s.din("ln_b", [L, 3, D])
        NROW = ((s.NCP + s.OFF + 127) // 128) * 128
        s.NROW = NROW
        s.c_ident = s.din("ident_bf", [128, 128], BF16)
        s.c_tri = s.din("tri_bf", [128, 128], BF16)
        s.c_tri4 = s.din("tri4_bf", [128, 4, 128], BF16)
        s.c_atri4 = s.din("atri4_bf", [128, 4, 128], BF16)
        s.c_nearmask = s.din("nearmask", [2, 128, 8, 128])
        s.c_cs = s.din("rope_cs", [96, S])
        s.c_ss = s.din("rope_ss", [96, S])
        s.c_overlap = s.din("overlap_bf", [s.NCP, 64], BF16)
        s.c_aforced = s.din("aforced", [S, 64])
        s.c_expand = s.din("expand_bf", [64, S], BF16)
        s.c_ones = s.din("ones_bf", [128, 512], BF16)
        s.c_cmpvalid = s.din("cmp_valid", [NROW, 8, 128])
        s.c_near = s.din("t5_near", [2, 128, 8, 128])
        s.c_cmpb = s.din("t5_cmpb", [NROW, 8, 128])
        s.c_c31 = s.din("t5_c31", [128, 8, 128])
        s.y_out = s.dscr("y", [S, D], F32, out=True)
        s.xa = s.dscr("x_a", [S, D], F32)
        s.xb = s.dscr("x_b", [S, D], F32)
        s.xT = s.dscr("xT", [8, 128, S], BF16)
        s.memT = s.dscr("memT", [8, 128, MEM], BF16)
        s.em = s.dscr("em", [2, 128, 8, 128], BF16)
        s.emc = s.dscr("emc", [NROW, 8, 128], BF16)
        s.nqT = s.dscr("nqT", [8, 64, S], BF16)
        s.nkcT = s.dscr("nkcT", [2, 64, S], BF16)
        s.nvcT = s.dscr("nvcT", [2, 64, S], BF16)
        s.nksT = s.dscr("nksT", [2, 64, S], BF16)
        s.nkwT = s.dscr("nkwT", [2, 64, S], BF16)
        s.nvs = s.dscr("nvs", [S, 2, 72], BF16)
        s.nvw = s.dscr("nvw", [S, 2, 72], BF16)
        s.gate = s.dscr("gate", [S, 24], F32)
        s.cnT = s.dscr("cnT", [4, 128, S], BF16)
        s.krT = s.dscr("krT", [2, 32, S], F32)
        s.mqT = s.dscr("mqT", [4, 96, S], BF16)
        s.mkT = s.dscr("mkT", [4, 96, S], BF16)
        s.mv = s.dscr("mv", [S, 4, 72], BF16)
        s.fqT = s.dscr("fqT", [4, 70, S], BF16)
        s.fkT = s.dscr("fkT", [4, 70, S], BF16)
        s.fv = s.dscr("fv", [S, 4, 72], BF16)
        s.ffT = s.dscr("ffT", [4, S], F32)
        s.o_tm = s.dscr("o_tm", [S, D], BF16)
        s.hT = s.dscr("hT", [32, 128, S], BF16)

        import os
        stop = os.environ.get("K_STOP", "")
        seq = []
        seq.append(("init", lambda: s.phase_init()))
        state = {"cur": s.x_in, "nxt": s.xa}

        def adv():
            state["cur"], state["nxt"] = state["nxt"], (s.xb if state["nxt"] is s.xa else s.xa)
        for l in range(L):
            last = (l == L - 1)
            seq.append(("p1", lambda l=l: s.phase_p1(l)))
            seq.append(("mla_prep", lambda l=l: s.phase_mla_prep(l)))
            seq.append(("fox_prep", lambda l=l: s.phase_fox_prep(l)))
            seq.append(("nsa", lambda l=l: s.phase_nsa(l)))
            seq.append(("mla", lambda l=l: s.phase_causal(l, "mla")))
            seq.append(("fox", lambda l=l: s.phase_causal(l, "fox")))
            seq.append(("combine", lambda l=l: (s.phase_combine(l, state["cur"], state["nxt"]), adv())))
            seq.append(("cross", lambda l=l: (s.phase_cross(l, state["cur"], state["nxt"]), adv())))
            seq.append(("mlp_up", lambda l=l: s.phase_mlp_up(l)))
            seq.append(("mlp_down", lambda l=l, last=last: (s.phase_mlp_down(l, state["cur"], s.y_out if last else state["nxt"], last), adv())))
        skip = set(os.environ.get("K_SKIP", "").split(","))
        for (nm, fn) in seq:
            if nm not in skip:
                fn()
            if stop and nm == stop:
                break
        s.sc.barrier()
        return s.nc

    def transpose_to_xT(s, xbf, TPr, xTt, tile_idx, ident, dst):
        sc = s.sc
        tp = TPr.next()
        for kc in range(8):
            sc.PE(lambda e: e.transpose(out=tp[:, kc, :], in_=xbf[:, kc * 128:(kc + 1) * 128], identity=ident[:, :]),
                  r=[xbf, ident], w=[tp])
        xt = xTt.next()
        sc.ACT(lambda e: e.copy(out=xt[:, :, :], in_=tp[:, :, :]), r=[tp], w=[xt])
        sc.dma(dst.ap.rearrange("k p s -> p k s")[:, :, tile_idx * 128:(tile_idx + 1) * 128], xt[:, :, :], r=[xt], w=[dst.b(("t", tile_idx))], q="pool")

    def ln_setup(s, es, l, which):
        sc = s.sc
        g = s.sb(es, "ln_gam", [128, D], F32)
        b = s.sb(es, "ln_bet", [128, D], F32)
        sc.dma(g[:, :], s.ln_g.ap[l, which, :].partition_broadcast(128), w=[g])
        sc.dma(b[:, :], s.ln_b.ap[l, which, :].partition_broadcast(128), w=[b])
        r = dict(g=g, b=b)
        r["xin"] = Ring([s.sb(es, "ln_xin%d" % i, [128, D], F32) for i in range(2)])
        r["z"] = Ring([s.sb(es, "ln_z%d" % i, [128, D], F32) for i in range(1)])
        r["xo"] = Ring([s.sb(es, "ln_xo%d" % i, [128, D], F32) for i in range(2)])
        r["xbf"] = Ring([s.sb(es, "ln_xbf%d" % i, [128, D], BF16) for i in range(1)])
        r["st"] = Ring([s.sb(es, "ln_st%d" % i, [128, 24], F32) for i in range(2)])
        r["xTt"] = Ring([s.sb(es, "ln_xTt%d" % i, [128, 8, 128], BF16) for i in range(2)])
        return r

    def layer_norm_tile(s, R, Y, tile_idx, xsrc, xdst, TPr, ident, final=False):
        sc = s.sc
        rows = slice(tile_idx * 128, (tile_idx + 1) * 128)
        xin = R["xin"].next()
        sc.dma(xin[:, :], xsrc.ap[rows, :], r=[xsrc.b(("t", tile_idx))], w=[xin])
        z = R["z"].next()
        sc.DVE(lambda e: e.scalar_tensor_tensor(out=z[:, :], in0=xin[:, :], scalar=float(DN_ALPHA), in1=Y[:, :], op0=ALU.mult, op1=ALU.add),
               r=[xin, Y], w=[z])
        st = R["st"].next()
        for c in range(2):
            sc.DVE(lambda e: e.bn_stats(out=st[:, c * 6:(c + 1) * 6], in_=z[:, c * 512:(c + 1) * 512]), r=[z], w=[st])
        sc.DVE(lambda e: e.bn_aggr(out=st[:, 12:14], in_=st[:, 0:12]), r=[st], w=[st])
        sc.DVE(lambda e: e.tensor_scalar(out=st[:, 14:15], in0=st[:, 13:14], scalar1=1e-5, scalar2=None, op0=ALU.add), r=[st], w=[st])
        sc.ACT(lambda e: e.activation(out=st[:, 16:17], in_=st[:, 14:15], func=AF.Sqrt), r=[st], w=[st])
        sc.DVE(lambda e: e.reciprocal(out=st[:, 18:19], in_=st[:, 16:17]), r=[st], w=[st])
        xo = R["xo"].next()
        sc.DVE(lambda e: e.tensor_scalar(out=xo[:, :], in0=z[:, :], scalar1=st[:, 12:13], scalar2=st[:, 18:19], op0=ALU.subtract, op1=ALU.mult),
               r=[z, st], w=[xo])
        sc.POOL(lambda e: e.tensor_tensor(out=xo[:, :], in0=xo[:, :], in1=R["g"][:, :], op=ALU.mult), r=[xo, R["g"]], w=[xo])
        sc.POOL(lambda e: e.tensor_tensor(out=xo[:, :], in0=xo[:, :], in1=R["b"][:, :], op=ALU.add), r=[xo, R["b"]], w=[xo])
        sc.dma(xdst.ap[rows, :], xo[:, :], r=[xo], w=[xdst.b(("t", tile_idx))], q="pool")
        if not final:
            xbf = R["xbf"].next()
            sc.POOL(lambda e: e.tensor_copy(out=xbf[:, :], in_=xo[:, :]), r=[xo], w=[xbf])
            s.transpose_to_xT(xbf, TPr, R["xTt"], tile_idx, ident, s.xT)

    def phase_init(s):
        sc = s.sc
        S, NT = s.S, s.NT
        with ExitStack() as es:
            ident = s.sb(es, "ident", [128, 128], BF16)
            sc.dma(ident[:, :], s.c_ident.ap[:, :], w=[ident])
            c31 = s.sb(es, "c31", [128, 1024], F32)
            sc.dma(c31[:, :], s.c_c31.ap.rearrange("p h q -> p (h q)"), w=[c31])
            tb = Ring([s.sb(es, "t5b%d" % i, [128, 1024], F32) for i in range(2)])
            tm = Ring([s.sb(es, "t5m%d" % i, [128, 1024], F32) for i in range(2)])
            to = Ring([s.sb(es, "t5o%d" % i, [128, 1024], BF16) for i in range(2)])
            jobs = [(s.c_near.ap[i].rearrange("p h q -> p (h q)"), s.c_nearmask.ap[i].rearrange("p h q -> p (h q)"),
                     s.em.ap[i].rearrange("p h q -> p (h q)")) for i in range(2)]
            for rt in range(s.NROW // 128):
                rs = slice(rt * 128, (rt + 1) * 128)
                jobs.append((s.c_cmpb.ap[rs].rearrange("p h q -> p (h q)"), s.c_cmpvalid.ap[rs].rearrange("p h q -> p (h q)"),
                             s.emc.ap[rs].rearrange("p h q -> p (h q)")))
            for (bsrc, msrc, dst) in jobs:
                b = tb.next()
                m = tm.next()
                o = to.next()
                sc.dma(b[:, :], bsrc, w=[b])
                sc.dma(m[:, :], msrc, w=[m])
                sc.DVE(lambda e: e.tensor_tensor(out=b[:, :], in0=b[:, :], in1=c31[:, :], op=ALU.subtract), r=[b, c31], w=[b])
                sc.ACT(lambda e: e.activation(out=b[:, :], in_=b[:, :], func=AF.Exp), r=[b], w=[b])
                sc.DVE(lambda e: e.tensor_tensor(out=o[:, :], in0=b[:, :], in1=m[:, :], op=ALU.mult), r=[b, m], w=[o])
                sc.dma(dst, o[:, :], r=[o], w=[s.em.buf], q="pool")
            mt = s.sb(es, "mem_t", [128, 2, D], F32)
            sc.dma(mt[:, :, :], s.mem_in.ap.rearrange("(t p) d -> p t d", p=128), w=[mt])
            mb = s.sb(es, "mem_b", [128, 2, D], BF16)
            sc.DVE(lambda e: e.tensor_copy(out=mb[:, :, :], in_=mt[:, :, :]), r=[mt], w=[mb])
            tp = s.ps(es, "init_tp", [128, 8, 128], BF16)
            mT = s.sb(es, "memT_s", [128, 8, MEM], BF16)
            for t in range(2):
                for kc in range(8):
                    sc.PE(lambda e: e.transpose(out=tp[:, kc, :], in_=mb[:, t, kc * 128:(kc + 1) * 128], identity=ident[:, :]), r=[mb, ident], w=[tp])
                sc.ACT(lambda e: e.copy(out=mT[:, :, t * 128:(t + 1) * 128], in_=tp[:, :, :]), r=[tp], w=[mT])
            sc.dma(s.memT.ap.rearrange("k p m -> p k m"), mT[:, :, :], r=[mT], w=[s.memT.buf], q="pool")
            xr = Ring([s.sb(es, "ix%d" % i, [128, D], F32) for i in range(2)])
            xbr = Ring([s.sb(es, "ixb%d" % i, [128, D], BF16) for i in range(2)])
            TPr = Ring([tp, s.ps(es, "init_tp2", [128, 8, 128], BF16)])
            xTt = Ring([s.sb(es, "ixT%d" % i, [128, 8, 128], BF16) for i in range(2)])
            for t in range(NT):
                xt = xr.next()
                sc.dma(xt[:, :], s.x_in.ap[t * 128:(t + 1) * 128, :], w=[xt])
                xb = xbr.next()
                sc.DVE(lambda e: e.tensor_copy(out=xb[:, :], in_=xt[:, :]), r=[xt], w=[xb])
                s.transpose_to_xT(xb, TPr, xTt, t, ident, s.xT)
        sc.barrier()

    def phase_p1(s, l):
        sc = s.sc
        S, NT, NB = s.S, s.NT, s.NB
        WN = N_IN + 64
        with ExitStack() as es:
            ident = s.sb(es, "ident", [128, 128], BF16)
            sc.dma(ident[:, :], s.c_ident.ap[:, :], w=[ident])
            W = s.sb(es, "p1_w", [128, 8, WN], BF16)
            stg = s.stages(es)
            wsrc = s.w_in.ap[l]
            s.load_w(W, wsrc, 128, 8, N_IN, stg)
            sc.POOL(lambda e: e.tensor_copy(out=W[:, :, N_IN:N_IN + 32], in_=W[:, :, 1816:1848]), r=[W], w=[W])
            sc.POOL(lambda e: e.tensor_copy(out=W[:, :, N_IN + 32:N_IN + 48], in_=W[:, :, 1832:1848]), r=[W], w=[W])
            sc.POOL(lambda e: e.tensor_copy(out=W[:, :, N_IN + 48:N_IN + 64], in_=W[:, :, 1816:1832]), r=[W], w=[W])
            xT = s.sb(es, "p1_xT", [128, 8, S], BF16)
            xTb = [Buf("xTb%d" % b) for b in range(NB)]
            for b in range(NB):
                sc.dma(xT[:, :, b * 512:(b + 1) * 512], s.xT.ap.rearrange("k p s -> p k s")[:, :, b * 512:(b + 1) * 512], w=[xTb[b]])
            PB = Ring([s.ps(es, "p1_pb%d" % i, [128, 512], F32) for i in range(6)])
            TP = Ring([s.ps(es, "p1_tp%d" % i, [128, 8, 128], BF16) for i in range(1)])
            fo = Ring([s.sb(es, "p1_fo%d" % i, [128, 512], BF16) for i in range(4)])
            fo32 = Ring([s.sb(es, "p1_fo32_%d" % i, [128, 512], F32) for i in range(2)])
            vt_s = Ring([s.sb(es, "p1_vts%d" % i, [128, 4, 2, 72], BF16) for i in range(2)])
            vt_w = Ring([s.sb(es, "p1_vtw%d" % i, [128, 4, 2, 72], BF16) for i in range(2)])
            vt_f = Ring([s.sb(es, "p1_vtf%d" % i, [128, 4, 4, 72], BF16) for i in range(2)])
            for rg in (vt_s, vt_w, vt_f):
                for t_ in rg.items:
                    sc.POOL(lambda e: e.memset(t_[:, :, :, :], 1.0), w=[t_])
            gt = Ring([s.sb(es, "p1_gt%d" % i, [128, 4, 24], F32) for i in range(2)])
            cn = Ring([s.sb(es, "p1_cn%d" % i, [128, 512], BF16) for i in range(2)])
            junk = s.sb(es, "p1_junk", [128, 512], BF16)
            ss = Ring([s.sb(es, "p1_ss%d" % i, [128, 4], F32) for i in range(2)])
            cT = Ring([s.sb(es, "p1_cT%d" % i, [128, 4, 512], BF16) for i in range(2)])
            evac_i = [0]

            def evac(dst_ap, src_ap, rb, wb):
                evac_i[0] += 1
                if evac_i[0] % 2 == 0:
                    sc.ACT(lambda e: e.copy(out=dst_ap, in_=src_ap), r=rb, w=wb)
                else:
                    sc.DVE(lambda e: e.tensor_copy(out=dst_ap, in_=src_ap), r=rb, w=wb)

            fm = []
            for h in range(8):
                fm.append((64 * h, 64, "bf", [(s.nqT.ap[h], 0, 64)]))
            fm.append((512, 128, "bf", [(s.nkcT.ap[0], 0, 64), (s.nkcT.ap[1], 64, 64)]))
            fm.append((640, 128, "bf", [(s.nvcT.ap[0], 0, 64), (s.nvcT.ap[1], 64, 64)]))
            fm.append((768, 128, "bf", [(s.nksT.ap[0], 0, 64), (s.nksT.ap[1], 64, 64)]))
            fm.append((1024, 128, "bf", [(s.nkwT.ap[0], 0, 64), (s.nkwT.ap[1], 64, 64)]))
            fm.append((N_IN, 32, "f32", [(s.krT.ap[0], 0, 32)]))
            fm.append((N_IN + 32, 32, "f32", [(s.krT.ap[1], 0, 32)]))
            for t in range(2):
                fm.append((1848 + 128 * t, 128, "bf", [(s.fqT.ap[2 * t, 0:64, :], 0, 64), (s.fqT.ap[2 * t + 1, 0:64, :], 64, 64)]))
                fm.append((2104 + 128 * t, 128, "bf", [(s.fkT.ap[2 * t, 0:64, :], 0, 64), (s.fkT.ap[2 * t + 1, 0:64, :], 64, 64)]))
            fm.append((2616, 4, "f32", [(s.ffT.ap, 0, 4)]))
            for b in range(NB):
                bs = slice(b * 512, (b + 1) * 512)
                import os
                P1M = int(os.environ.get('K_P1', '31'))
                for (c0, M, kind, dsts) in (fm if P1M & 1 else []):
                    pb = PB.next()
                    for kc in range(8):
                        sc.PE(lambda e: e.matmul(pb[0:M, :], lhsT=W[:, kc, c0:c0 + M], rhs=xT[:, kc, bs], start=(kc == 0), stop=(kc == 7)),
                              r=[W, xTb[b]], w=[pb])
                    o = fo.next() if kind == "bf" else fo32.next()
                    evac(o[0:M, :], pb[0:M, :], [pb], [o])
                    for (dap, p0, pn) in dsts:
                        sc.dma(dap[:, bs], o[p0:p0 + pn, :], r=[o], w=[Buf()], q="pool")
                vs_t = vt_s.next()
                vw_t = vt_w.next()
                vf_t = vt_f.next()
                g_t = gt.next()
                cT_t = cT.next()
                for tt in (range(4) if P1M & 2 else []):
                    t = b * 4 + tt
                    ts_ = slice(t * 128, (t + 1) * 128)
                    pa = PB.next()
                    for kc in range(8):
                        sc.PE(lambda e: e.matmul(pa[:, :], lhsT=xT[:, kc, ts_], rhs=W[:, kc, 1304:1816], start=(kc == 0), stop=(kc == 7)),
                              r=[W, xTb[b]], w=[pa])
                    s_ = ss.next()
                    sc.ACT(lambda e: e.activation(out=junk[:, 0:384], in_=pa[:, 0:384], func=AF.Square, scale=float(384 ** -0.5), accum_out=s_[:, 0:1]),
                           r=[pa], w=[junk, s_])
                    sc.ACT(lambda e: e.activation(out=junk[:, 384:512], in_=pa[:, 384:512], func=AF.Square, scale=float(128 ** -0.5), accum_out=s_[:, 1:2]),
                           r=[pa], w=[junk, s_])
                    sc.DVE(lambda e: e.tensor_scalar(out=s_[:, 0:2], in0=s_[:, 0:2], scalar1=1e-6, scalar2=None, op0=ALU.add), r=[s_], w=[s_])
                    sc.ACT(lambda e: e.activation(out=s_[:, 2:4], in_=s_[:, 0:2], func=AF.Sqrt), r=[s_], w=[s_])
                    sc.DVE(lambda e: e.reciprocal(out=s_[:, 0:2], in_=s_[:, 2:4]), r=[s_], w=[s_])
                    cn_t = cn.next()
                    sc.DVE(lambda e: e.tensor_scalar(out=cn_t[:, 0:384], in0=pa[:, 0:384], scalar1=s_[:, 0:1], scalar2=None, op0=ALU.mult), r=[pa, s_], w=[cn_t])
                    sc.DVE(lambda e: e.tensor_scalar(out=cn_t[:, 384:512], in0=pa[:, 384:512], scalar1=s_[:, 1:2], scalar2=None, op0=ALU.mult), r=[pa, s_], w=[cn_t])
                    tp = TP.next()
                    for j in range(4):
                        sc.PE(lambda e: e.transpose(out=tp[:, j, :], in_=cn_t[:, j * 128:(j + 1) * 128], identity=ident[:, :]), r=[cn_t, ident], w=[tp])
                    sc.ACT(lambda e: e.copy(out=cT_t[:, :, tt * 128:(tt + 1) * 128], in_=tp[:, 0:4, :]), r=[tp], w=[cT_t])
                    if P1M & 4:
                        pb1 = PB.next()
                        for (cc0, o0, n) in ((896, 0, 128), (1152, 128, 128), (2360, 256, 256)):
                            for kc in range(8):
                                sc.PE(lambda e: e.matmul(pb1[:, o0:o0 + n], lhsT=xT[:, kc, ts_], rhs=W[:, kc, cc0:cc0 + n], start=(kc == 0), stop=(kc == 7)),
                                      r=[W, xTb[b]], w=[pb1])
                        sc.ACT(lambda e: e.copy(out=vs_t[:, tt, :, 0:64], in_=pb1[:, 0:128].rearrange("p (g d) -> p g d", g=2)), r=[pb1], w=[vs_t])
                        sc.ACT(lambda e: e.copy(out=vw_t[:, tt, :, 0:64], in_=pb1[:, 128:256].rearrange("p (g d) -> p g d", g=2)), r=[pb1], w=[vw_t])
                        sc.ACT(lambda e: e.copy(out=vf_t[:, tt, :, 0:64], in_=pb1[:, 256:512].rearrange("p (g d) -> p g d", g=4)), r=[pb1], w=[vf_t])
                    if P1M & 8:
                        pb2 = PB.next()
                        for kc in range(8):
                            sc.PE(lambda e: e.matmul(pb2[:, 0:24], lhsT=xT[:, kc, ts_], rhs=W[:, kc, 1280:1304], start=(kc == 0), stop=(kc == 7)),
                                  r=[W, xTb[b]], w=[pb2])
                        sc.ACT(lambda e: e.activation(out=g_t[:, tt, :], in_=pb2[:, 0:24], func=(AF.Identity if os.environ.get("K_X") == "3" else AF.Sigmoid)), r=[pb2], w=[g_t])
                rows = slice(b * 512, (b + 1) * 512)
                if not (P1M & 16):
                    continue
                sc.dma(s.nvs.ap[rows].rearrange("(t p) g e -> p t g e", p=128), vs_t[:, :, :, :], r=[vs_t], w=[Buf()], q="pool")
                sc.dma(s.nvw.ap[rows].rearrange("(t p) g e -> p t g e", p=128), vw_t[:, :, :, :], r=[vw_t], w=[Buf()], q="pool")
                sc.dma(s.fv.ap[rows].rearrange("(t p) g e -> p t g e", p=128), vf_t[:, :, :, :], r=[vf_t], w=[Buf()], q="pool")
                sc.dma(s.gate.ap[rows].rearrange("(t p) c -> p t c", p=128), g_t[:, :, :], r=[g_t], w=[Buf()], q="pool")
                sc.dma(s.cnT.ap.rearrange("j p s -> p j s")[:, :, rows], cT_t[:, :, :], r=[cT_t], w=[Buf()], q="pool")
        sc.barrier()

    def phase_mla_prep(s, l):
        sc = s.sc
        S, NT, NB = s.S, s.NT, s.NB
        with ExitStack() as es:
            cnT = s.sb(es, "mp_cnT", [128, 4, S], BF16)
            cb = [Buf() for _ in range(NB)]
            for b in range(NB):
                sc.dma(cnT[:, :, b * 512:(b + 1) * 512], s.cnT.ap.rearrange("j p s -> p j s")[:, :, b * 512:(b + 1) * 512], w=[cb[b]])
            CS = s.sb(es, "mp_cs", [96, S], F32)
            SS = s.sb(es, "mp_ss", [96, S], F32)
            sc.dma(CS[:, :], s.c_cs.ap[:, :], w=[CS])
            sc.dma(SS[:, :], s.c_ss.ap[:, :], w=[SS])
            kr0 = s.sb(es, "mp_kr0", [32, S], F32)
            kr1 = s.sb(es, "mp_kr1", [32, S], F32)
            sc.dma(kr0[:, :], s.krT.ap[0], w=[kr0])
            sc.dma(kr1[:, :], s.krT.ap[1], w=[kr1])
            gq = s.sb(es, "mp_gq", [128, 4], F32)
            with s.nc.allow_non_contiguous_dma(reason="tiny gain vectors"):
                sc.dma(gq[:, 0:3], s.q_norm.ap[l].rearrange("(a p) -> p a", p=128), w=[gq])
                sc.dma(gq[:, 3:4], s.kv_norm.ap[l].rearrange("(a p) -> p a", p=128), w=[gq])
            stq = s.sb(es, "mp_stq", [128, 3, 384], F32)
            stk = s.sb(es, "mp_stk", [128, 512], F32)
            sc.dma(stq[:, :, :], s.w_uq.ap[l].rearrange("(a p) n -> p a n", p=128), w=[stq])
            sc.dma(stk[:, :], s.w_ukv.ap[l], w=[stk])
            Wq = s.sb(es, "mp_wq", [128, 3, 384], BF16)
            Wqp = s.sb(es, "mp_wqp", [128, 3, 4, 96], BF16)
            Wkv = s.sb(es, "mp_wkv", [128, 512], BF16)
            for a in range(3):
                sc.DVE(lambda e: e.tensor_scalar(out=Wq[:, a, :], in0=stq[:, a, :], scalar1=gq[:, a:a + 1], scalar2=None, op0=ALU.mult), r=[stq, gq], w=[Wq])
            sc.DVE(lambda e: e.tensor_scalar(out=Wkv[:, :], in0=stk[:, :], scalar1=gq[:, 3:4], scalar2=None, op0=ALU.mult), r=[stk, gq], w=[Wkv])
            sc.POOL(lambda e: e.memset(Wqp[:, :, :, :], 0.0), w=[Wqp])
            for h in range(4):
                sc.POOL(lambda e: e.tensor_copy(out=Wqp[:, :, h, 64:80], in_=Wq[:, :, 96 * h + 80:96 * h + 96]), r=[Wq], w=[Wqp])
                sc.POOL(lambda e: e.tensor_copy(out=Wqp[:, :, h, 80:96], in_=Wq[:, :, 96 * h + 64:96 * h + 80]), r=[Wq], w=[Wqp])
            PB = Ring([s.ps(es, "mp_pb%d" % i, [128, 512], F32) for i in range(6)])
            qo = Ring([s.sb(es, "mp_qo%d" % i, [96, 512], BF16) for i in range(3)])
            ko = Ring([s.sb(es, "mp_ko%d" % i, [64, 512], BF16) for i in range(3)])
            t1 = Ring([s.sb(es, "mp_t1_%d" % i, [96, 512], F32) for i in range(2)])
            t2 = Ring([s.sb(es, "mp_t2_%d" % i, [96, 512], F32) for i in range(2)])
            kro = Ring([s.sb(es, "mp_kro%d" % i, [32, 512], BF16) for i in range(2)])
            vt = Ring([s.sb(es, "mp_vt%d" % i, [128, 4, 4, 72], BF16) for i in range(2)])
            for t_ in vt.items:
                sc.POOL(lambda e: e.memset(t_[:, :, :, :], 1.0), w=[t_])
            for b in range(NB):
                bs = slice(b * 512, (b + 1) * 512)
                for h in range(4):
                    p1 = PB.next()
                    for a in range(3):
                        sc.PE(lambda e: e.matmul(p1[0:96, :], lhsT=Wq[:, a, 96 * h:96 * h + 96], rhs=cnT[:, a, bs], start=(a == 0), stop=(a == 2)), r=[Wq, cb[b]], w=[p1])
                    p2 = PB.next()
                    for a in range(3):
                        sc.PE(lambda e: e.matmul(p2[0:96, :], lhsT=Wqp[:, a, h, :], rhs=cnT[:, a, bs], start=(a == 0), stop=(a == 2)), r=[Wqp, cb[b]], w=[p2])
                    q_ = qo.next()
                    sc.ACT(lambda e: e.copy(out=q_[0:64, :], in_=p1[0:64, :]), r=[p1], w=[q_])
                    a1 = t1.next()
                    a2 = t2.next()
                    sc.DVE(lambda e: e.tensor_tensor(out=a1[64:96, :], in0=p1[64:96, :], in1=CS[64:96, bs], op=ALU.mult), r=[p1, CS], w=[a1])
                    sc.DVE(lambda e: e.tensor_tensor(out=a2[64:96, :], in0=p2[64:96, :], in1=SS[64:96, bs], op=ALU.mult), r=[p2, SS], w=[a2])
                    sc.POOL(lambda e: e.tensor_tensor(out=q_[64:96, :], in0=a1[64:96, :], in1=a2[64:96, :], op=ALU.add), r=[a1, a2], w=[q_])
                    sc.dma(s.mqT.ap[h, :, bs], q_[:, :], r=[q_], w=[Buf()], q="pool")
                    p3 = PB.next()
                    sc.PE(lambda e: e.matmul(p3[0:64, :], lhsT=Wkv[:, 128 * h:128 * h + 64], rhs=cnT[:, 3, bs], start=True, stop=True), r=[Wkv, cb[b]], w=[p3])
                    k_ = ko.next()
                    sc.ACT(lambda e: e.copy(out=k_[:, :], in_=p3[0:64, :]), r=[p3], w=[k_])
                    sc.dma(s.mkT.ap[h, 0:64, bs], k_[:, :], r=[k_], w=[Buf()], q="pool")
                a1 = t1.next()
                a2 = t2.next()
                kr_ = kro.next()
                sc.DVE(lambda e: e.tensor_tensor(out=a1[0:32, :], in0=kr0[:, bs], in1=CS[0:32, bs], op=ALU.mult), r=[kr0, CS], w=[a1])
                sc.DVE(lambda e: e.tensor_tensor(out=a2[0:32, :], in0=kr1[:, bs], in1=SS[0:32, bs], op=ALU.mult), r=[kr1, SS], w=[a2])
                sc.POOL(lambda e: e.tensor_tensor(out=kr_[:, :], in0=a1[0:32, :], in1=a2[0:32, :], op=ALU.add), r=[a1, a2], w=[kr_])
                for h in range(4):
                    sc.dma(s.mkT.ap[h, 64:96, bs], kr_[:, :], r=[kr_], w=[Buf()], q="pool")
                v_ = vt.next()
                for tt in range(4):
                    t = b * 4 + tt
                    p4 = PB.next()
                    for h in range(4):
                        sc.PE(lambda e: e.matmul(p4[:, 64 * h:64 * h + 64], lhsT=cnT[:, 3, t * 128:(t + 1) * 128], rhs=Wkv[:, 128 * h + 64:128 * h + 128], start=True, stop=True),
                              r=[Wkv, cb[b]], w=[p4])
                    sc.ACT(lambda e: e.copy(out=v_[:, tt, :, 0:64], in_=p4[:, 0:256].rearrange("p (g d) -> p g d", g=4)), r=[p4], w=[v_])
                sc.dma(s.mv.ap[bs].rearrange("(t p) g e -> p t g e", p=128), v_[:, :, :, :], r=[v_], w=[Buf()], q="pool")
        sc.barrier()

    def phase_fox_prep(s, l):
        sc = s.sc
        S = s.S
        with ExitStack() as es:
            ff = s.sb(es, "fp_ff", [4, S], F32)
            sc.dma(ff[:, :], s.ffT.ap[:, :], w=[ff])
            bf = s.sb(es, "fp_bf", [4, 2], F32)
            with s.nc.allow_non_contiguous_dma(reason="tiny"):
                sc.dma(bf[:, 0:1], s.b_f.ap[l].rearrange("(p a) -> p a", a=1), w=[bf])
            sc.DVE(lambda e: e.tensor_scalar(out=bf[:, 1:2], in0=bf[:, 0:1], scalar1=-1.0, scalar2=None, op0=ALU.mult), r=[bf], w=[bf])
            ex = s.sb(es, "fp_ex", [4, S], F32)
            sc.ACT(lambda e: e.activation(out=ex[:, :], in_=ff[:, :], func=AF.Exp, bias=bf[:, 1:2], scale=-1.0), r=[ff, bf], w=[ex])
            one = s.sb(es, "fp_one", [4, 1], F32)
            sc.DVE(lambda e: e.memset(one[:, :], 1.0), w=[one])
            sc.ACT(lambda e: e.activation(out=ex[:, :], in_=ex[:, :], func=AF.Ln, bias=one[:, 0:1], scale=1.0), r=[ex, one], w=[ex])
            ones = s.sb(es, "fp_ones", [4, S], F32)
            sc.POOL(lambda e: e.memset(ones[:, :], 1.0), w=[ones])
            sc.DVE(lambda e: e.tensor_scalar(out=ex[:, :], in0=ex[:, :], scalar1=-8.0, scalar2=None, op0=ALU.mult), r=[ex], w=[ex])
            cum = s.sb(es, "fp_cum", [4, S], F32)
            sc.DVE(lambda e: e.tensor_tensor_scan(out=cum[:, :], data0=ones[:, :], data1=ex[:, :], initial=0.0, op0=ALU.mult, op1=ALU.add), r=[ones, ex], w=[cum])
            pcs = [s.sb(es, "fp_pc%d" % i, [4, S], BF16) for i in range(3)]
            ngs = [s.sb(es, "fp_ng%d" % i, [4, S], BF16) for i in range(3)]
            rem = s.sb(es, "fp_rem", [4, S], F32)
            src = cum
            for i in range(3):
                sc.DVE(lambda e: e.tensor_copy(out=pcs[i][:, :], in_=src[:, :]), r=[src], w=[pcs[i]])
                sc.DVE(lambda e: e.tensor_scalar(out=ngs[i][:, :], in0=pcs[i][:, :], scalar1=-1.0, scalar2=None, op0=ALU.mult), r=[pcs[i]], w=[ngs[i]])
                if i < 2:
                    sc.DVE(lambda e: e.tensor_tensor(out=rem[:, :], in0=src[:, :], in1=pcs[i][:, :], op=ALU.subtract), r=[src, pcs[i]], w=[rem])
                    src = rem
            onb = s.sb(es, "fp_onb", [4, S], BF16)
            sc.POOL(lambda e: e.memset(onb[:, :], 1.0), w=[onb])
            for i in range(3):
                sc.dma(s.fqT.ap[:, 64 + i, :], pcs[i][:, :], r=[pcs[i]], w=[Buf()], q="pool")
                sc.dma(s.fqT.ap[:, 67 + i, :], onb[:, :], r=[onb], w=[Buf()], q="pool")
                sc.dma(s.fkT.ap[:, 64 + i, :], onb[:, :], r=[onb], w=[Buf()], q="pool")
                sc.dma(s.fkT.ap[:, 67 + i, :], ngs[i][:, :], r=[ngs[i]], w=[Buf()], q="pool")
        sc.barrier()

    def run_steps(s, steps, sT_ring, pT_ring):
        sc = s.sc
        n = len(steps)
        state = [None] * n

        def emit_score(i):
            st = steps[i]
            sT = sT_ring.next()
            pT = pT_ring.next()
            c0, nco, kk = st["c0"], st["ncol"], st["kk"]
            sc.PE(lambda e: e.matmul(sT[0:kk, c0:nco], lhsT=st["kT"], rhs=st["q_fn"](c0, nco), start=True, stop=True),
                  r=st["rb_score"], w=[sT])
            sc.ACT(lambda e: e.activation(out=pT[0:kk, c0:nco], in_=sT[0:kk, c0:nco], func=AF.Exp, scale=float(st["scale"])), r=[sT], w=[pT])
            mi = 0
            for (m0, m1, map_, mb, clamp) in st["masks"]:
                mi += 1
                if clamp:
                    sc.DVE(lambda e: e.scalar_tensor_tensor(out=pT[0:kk, m0:m1], in0=pT[0:kk, m0:m1], scalar=1e30, in1=map_, op0=ALU.min, op1=ALU.mult),
                           r=[pT] + mb, w=[pT])
                elif st.get("mask_pool", False) and mi % 2 == 0:
                    sc.POOL(lambda e: e.tensor_tensor(out=pT[0:kk, m0:m1], in0=pT[0:kk, m0:m1], in1=map_, op=ALU.mult), r=[pT] + mb, w=[pT])
                else:
                    sc.DVE(lambda e: e.tensor_tensor(out=pT[0:kk, m0:m1], in0=pT[0:kk, m0:m1], in1=map_, op=ALU.mult), r=[pT] + mb, w=[pT])
            state[i] = pT

        def emit_pv(i):
            st = steps[i]
            pT = state[i]
            kk = st["kk"]
            first = st["first"]
            for sub in st["subs"]:
                out_ap, accb = st["acc_fn"](sub)
                stt = bool(first and (sub in st.get("start_subs", (st["subs"][0],))))
                sc.PE(lambda e: e.matmul(out_ap, lhsT=pT[0:kk, sub * 128:(sub + 1) * 128], rhs=st["V"], start=stt, stop=True, skip_group_check=True),
                      r=[pT] + st["rb_v"], w=[accb])
            if st["fin"] is not None:
                st["fin"]()

        for i in range(n + 1):
            if i < n:
                emit_score(i)
            if i >= 1:
                emit_pv(i - 1)

    def phase_nsa(s, l):
        sc = s.sc
        S, NT = s.S, s.NT
        CR, CT, NCP, NCMP = s.CR, s.CT, s.NCP, s.NCMP
        with ExitStack() as es:
            ident = s.sb(es, "ident", [128, 128], BF16)
            sc.dma(ident[:, :], s.c_ident.ap[:, :], w=[ident])
            QA = [s.sb(es, "ns_qa%d" % g, [128, NT, 4, 128], BF16) for g in range(2)]
            qa_lo = [Buf() for g in range(2)]
            qa_hi = [[Buf() for i in range(NT)] for g in range(2)]
            ksA = [s.sb(es, "ns_ks%d" % g, [128, S], BF16) for g in range(2)]
            kwT = [s.sb(es, "ns_kw%d" % g, [64, S], BF16) for g in range(2)]
            kcT = [s.sb(es, "ns_kc%d" % g, [64, CT * CR], BF16) for g in range(2)]
            vs = [s.sb(es, "ns_vs%d" % g, [128, NT, 72], BF16) for g in range(2)]
            vw = [s.sb(es, "ns_vw%d" % g, [128, NT, 72], BF16) for g in range(2)]
            vcA = [s.sb(es, "ns_vc%d" % g, [128, CT, 136], BF16) for g in range(2)]
            G = s.sb(es, "ns_gate", [128, NT, 24], F32)
            AFc = s.sb(es, "ns_af", [128, NT, 64], F32)
            EM = s.sb(es, "ns_em", [128, 2, 8, 128], BF16)
            EMW = s.sb(es, "ns_emw", [128, 4, 128], BF16)
            for g in range(2):
                for hh in range(4):
                    sc.dma(QA[g][0:64, :, hh, :], s.nqT.ap[4 * g + hh].rearrange("d (t q) -> d t q", q=128), w=[qa_lo[g]])
                sc.dma(ksA[g][0:64, :], s.nksT.ap[g], w=[ksA[g]])
                sc.dma(ksA[g][64:128, :], s.c_expand.ap[:, :], w=[ksA[g]])
                sc.dma(kwT[g][:, :], s.nkwT.ap[g], w=[kwT[g]])
                sc.dma(vs[g][:, :, :], s.nvs.ap[:, g, :].rearrange("(t p) e -> p t e", p=128), w=[vs[g]])
                sc.dma(vw[g][:, :, :], s.nvw.ap[:, g, :].rearrange("(t p) e -> p t e", p=128), w=[vw[g]])
            sc.dma(G[:, :, :], s.gate.ap.rearrange("(t p) c -> p t c", p=128), w=[G])
            sc.dma(AFc[:, :, :], s.c_aforced.ap.rearrange("(t p) c -> p t c", p=128), w=[AFc])
            for i in range(2):
                sc.dma(EM[:, i, :, :], s.em.ap[i], w=[EM])
            sc.dma(EMW[:, :, :], s.c_atri4.ap[:, :, :], w=[EMW])
            sT_ring = Ring([s.ps(es, "ns_sT%d" % i, [128, 512], F32) for i in range(2)])
            accC = [s.ps(es, "ns_accC%d" % i, [128, 2, 256], F32) for i in range(2)]
            accR = Ring([s.ps(es, "ns_accR%d" % i, [128, 4, 128], F32) for i in range(3)])
            TPb = s.ps(es, "ns_tp", [128, 1024], BF16)
            with ExitStack() as es2:
                W1 = Ring([s.sb(es2, "nc_w1_%d" % i, [64, 32, 128], BF16) for i in range(2)])
                stg = Ring([s.sb(es2, "nc_stg%d" % i, [64, 8, 128], F32) for i in range(2)])
                src = Ring([s.sb(es2, "nc_src%d" % i, [64, S], BF16) for i in range(2)])
                peT = Ring([s.sb(es2, "nc_peT%d" % i, [64, 32], F32) for i in range(2)])
                peTb = Ring([s.sb(es2, "nc_peTb%d" % i, [64, 32], BF16) for i in range(2)])
                W2s = Ring([s.sb(es2, "nc_w2s%d" % i, [128, 64], F32) for i in range(2)])
                W2 = Ring([s.sb(es2, "nc_w2_%d" % i, [128, 64], BF16) for i in range(2)])
                hb = Ring([s.sb(es2, "nc_hb%d" % i, [128, 2], F32) for i in range(2)])
                u = Ring([s.sb(es2, "nc_u%d" % i, [128, 256], F32) for i in range(2)])
                u2 = Ring([s.sb(es2, "nc_u2%d" % i, [128, 256], F32) for i in range(2)])
                hT = Ring([s.sb(es2, "nc_hT%d" % i, [128, 256], BF16) for i in range(2)])
                for g in range(2):
                    sc.POOL(lambda e: e.memset(vcA[g][:, :, :], 0.0), w=[vcA[g]])
                    sc.POOL(lambda e: e.memset(kcT[g][:, :], 0.0), w=[kcT[g]])
                for kv in range(2):
                    w1 = W1.next()
                    w1src = s.cmp_w1.ap[l, kv].rearrange("(a p) n -> p a n", p=64)
                    for hf in range(4):
                        st_ = stg.next()
                        sc.dma(st_[:, :, :], w1src[:, hf * 8:(hf + 1) * 8, :], w=[st_])
                        sc.POOL(lambda e: e.tensor_copy(out=w1[:, hf * 8:(hf + 1) * 8, :], in_=st_[:, :, :]), r=[st_], w=[w1])
                    pT_ = peT.next()
                    with s.nc.allow_non_contiguous_dma(reason="tiny pe transpose"):
                        sc.dma(pT_[:, :], s.cmp_pe.ap[l, kv].rearrange("l d -> d l"), w=[pT_])
                    pTb_ = peTb.next()
                    sc.DVE(lambda e: e.tensor_copy(out=pTb_[:, :], in_=pT_[:, :]), r=[pT_], w=[pTb_])
                    w2s = W2s.next()
                    sc.dma(w2s[:, :], s.cmp_w2.ap[l, kv], w=[w2s])
                    w2 = W2.next()
                    sc.DVE(lambda e: e.tensor_copy(out=w2[:, :], in_=w2s[:, :]), r=[w2s], w=[w2])
                    pbias = sT_ring.next()
                    for ll in range(32):
                        sc.PE(lambda e: e.matmul(pbias[:, 0:1], lhsT=w1[:, ll, :], rhs=pTb_[:, ll:ll + 1], start=(ll == 0), stop=(ll == 31)), r=[w1, pTb_], w=[pbias])
                    hb_ = hb.next()
                    sc.DVE(lambda e: e.tensor_copy(out=hb_[:, 0:1], in_=pbias[:, 0:1]), r=[pbias], w=[hb_])
                    for g in range(2):
                        sr = src.next()
                        sc.dma(sr[:, :], (s.nkcT if kv == 0 else s.nvcT).ap[g], w=[sr])
                        ph = sT_ring.next()
                        for ll in range(32):
                            sc.PE(lambda e: e.matmul(ph[:, 0:NCMP], lhsT=w1[:, ll, :], rhs=sr[:, ll:ll + 16 * (NCMP - 1) + 1:16], start=(ll == 0), stop=(ll == 31)),
                                  r=[w1, sr], w=[ph])
                        u_ = u.next()
                        u2_ = u2.next()
                        h_ = hT.next()
                        n_ = NCMP
                        sc.ACT(lambda e: e.activation(out=u_[:, 0:n_], in_=ph[:, 0:n_], func=AF.Identity, bias=hb_[:, 0:1], scale=1.0), r=[ph, hb_], w=[u_])
                        sc.DVE(lambda e: e.tensor_tensor(out=u2_[:, 0:n_], in0=u_[:, 0:n_], in1=u_[:, 0:n_], op=ALU.mult), r=[u_], w=[u2_])
                        sc.DVE(lambda e: e.tensor_scalar(out=u2_[:, 0:n_], in0=u2_[:, 0:n_], scalar1=0.044715, scalar2=1.0, op0=ALU.mult, op1=ALU.add), r=[u2_], w=[u2_])
                        sc.DVE(lambda e: e.tensor_tensor(out=u2_[:, 0:n_], in0=u2_[:, 0:n_], in1=u_[:, 0:n_], op=ALU.mult), r=[u2_, u_], w=[u2_])
                        sc.ACT(lambda e: e.activation(out=u2_[:, 0:n_], in_=u2_[:, 0:n_], func=AF.Tanh, scale=0.7978845608028654), r=[u2_], w=[u2_])
                        sc.DVE(lambda e: e.tensor_scalar(out=u2_[:, 0:n_], in0=u2_[:, 0:n_], scalar1=1.0, scalar2=0.5, op0=ALU.add, op1=ALU.mult), r=[u2_], w=[u2_])
                        sc.DVE(lambda e: e.tensor_tensor(out=h_[:, 0:n_], in0=u2_[:, 0:n_], in1=u_[:, 0:n_], op=ALU.mult), r=[u2_, u_], w=[h_])
                        po = sT_ring.next()
                        if kv == 0:
                            sc.PE(lambda e: e.matmul(po[0:64, 0:n_], lhsT=w2[:, :], rhs=h_[:, 0:n_], start=True, stop=True), r=[w2, h_], w=[po])
                            sc.DVE(lambda e: e.tensor_copy(out=kcT[g][:, 0:n_], in_=po[0:64, 0:n_]), r=[po], w=[kcT[g]])
                        else:
                            for ct in range(CT):
                                rows = min(CR, n_ - ct * CR)
                                sc.PE(lambda e: e.matmul(po[0:rows, ct * 64:(ct + 1) * 64], lhsT=h_[:, ct * CR:ct * CR + rows], rhs=w2[:, :], start=True, stop=True), r=[w2, h_], w=[po])
                                sc.DVE(lambda e: e.tensor_copy(out=vcA[g][0:rows, ct, 0:64], in_=po[0:rows, ct * 64:(ct + 1) * 64]), r=[po], w=[vcA[g]])
                for g in range(2):
                    sc.POOL(lambda e: e.memset(vcA[g][:, :, 64:66], 1.0), w=[vcA[g]])
                    sc.dma(vcA[g][0:CR, :, 65:129], s.c_overlap.ap.rearrange("(t p) n -> p t n", p=CR), w=[vcA[g]])
            with ExitStack() as es3:
                pT_ring = Ring([s.sb(es3, "ns_pT%d" % i, [128, 512], BF16) for i in range(4)])
                emc_ring = Ring([s.sb(es3, "ns_emc%d" % i, [128, 4, 128], BF16) for i in range(4)])
                Oacc = Ring([s.sb(es3, "ns_O%d" % i, [128, 256], F32) for i in range(3)])
                Obf = Ring([s.sb(es3, "ns_Ob%d" % i, [128, 256], BF16) for i in range(3)])
                sm = Ring([s.sb(es3, "ns_sm%d" % i, [128, 32], F32) for i in range(4)])
                imp = Ring([s.sb(es3, "ns_imp%d" % i, [128, 64], F32) for i in range(2)])
                NS = Ring([s.sb(es3, "ns_NS%d" % i, [128, 128], BF16) for i in range(2)])
                for t_ in NS.items:
                    sc.POOL(lambda e: e.memset(t_[:, :], 0.0), w=[t_])
                steps = []
                for i in range(NT):
                    for g in range(2):
                        O = Oacc.next()
                        Ob = Obf.next()

                        def qfn_lo(i=i, g=g):
                            return lambda c0, c1: QA[g][0:64, i, :, :].rearrange("p h q -> p (h q)")[:, c0:c1]

                        def qfn_full(i=i, g=g):
                            return lambda c0, c1: QA[g][:, i, :, :].rearrange("p h q -> p (h q)")[:, c0:c1]

                        cmax = min(8 * i + 6, NCMP - 1)
                        nct = cmax // CR + 1
                        for ct in range(nct):
                            emc_t = emc_ring.next()
                            r0 = ct * CR - 8 * i + s.OFF
                            sc_dma_args = (emc_t, r0, g)

                            def pre(emc_t=emc_t, r0=r0, g=g):
                                sc.dma(emc_t[0:CR, :, :], s.emc.ap[r0:r0 + CR, 4 * g:4 * g + 4, :], w=[emc_t])
                            fin = None
                            if ct == nct - 1:
                                def fin(i=i, g=g, O=O):
                                    s.nsa_fin_cmp(i, g, O, accC, G, AFc, sm, imp, NS, TPb, ident, QA, qa_hi)
                            steps.append(dict(pre=pre, kT=kcT[g][:, ct * CR:(ct + 1) * CR], q_fn=qfn_lo(), kk=CR, ncol=512, c0=0, scale=0.125,
                                              masks=[(0, 512, emc_t[0:CR, :, :].rearrange("p h q -> p (h q)"), [emc_t], False)],
                                              V=vcA[g][0:CR, ct, 0:129], subs=[0, 1, 2, 3],
                                              acc_fn=(lambda sub: (accC[sub // 2][:, sub % 2, 0:129], accC[sub // 2])),
                                              first=(ct == 0), start_subs=(0, 2), fin=fin, rb_score=[kcT[g], qa_lo[g]], rb_v=[vcA[g]], mask_pool=False))
                        accW = accR.next()
                        js = list(range(max(0, i - 4), i + 1))
                        for j in js:
                            masks = []
                            if j == i:
                                masks.append((0, 512, EM[:, 0, 4 * g:4 * g + 4, :].rearrange("p h q -> p (h q)"), [EM], False))
                            elif j == i - 1:
                                masks.append((0, 512, EM[:, 1, 4 * g:4 * g + 4, :].rearrange("p h q -> p (h q)"), [EM], False))
                            elif j == i - 4:
                                masks.append((0, 512, EMW[:, :, :].rearrange("p h q -> p (h q)"), [EMW], False))
                            fin = None
                            if j == i:
                                def fin(i=i, g=g, O=O, accW=accW):
                                    s.nsa_fin_branch(i, g, O, None, accW, G, sm, 2)
                            steps.append(dict(pre=None, kT=kwT[g][:, j * 128:(j + 1) * 128], q_fn=qfn_lo(), kk=128, ncol=512, c0=0, scale=0.125, masks=masks,
                                              V=vw[g][:, j, 0:65], subs=[0, 1, 2, 3], acc_fn=(lambda sub, accW=accW: (accW[:, sub, 0:65], accW)),
                                              first=(j == js[0]), fin=fin, rb_score=[kwT[g], qa_lo[g]], rb_v=[vw[g]], mask_pool=True))
                        accS = accR.next()
                        for j in range(0, i + 1):
                            masks = []
                            if j == i:
                                masks.append((0, 512, EM[:, 0, 4 * g:4 * g + 4, :].rearrange("p h q -> p (h q)"), [EM], False))
                            elif j == i - 1:
                                masks.append((0, 512, EM[:, 1, 4 * g:4 * g + 4, :].rearrange("p h q -> p (h q)"), [EM], False))
                            fin = None
                            if j == i:
                                def fin(i=i, g=g, O=O, Ob=Ob, accS=accS):
                                    s.nsa_fin_branch(i, g, O, Ob, accS, G, sm, 1)
                            steps.append(dict(pre=None, kT=ksA[g][:, j * 128:(j + 1) * 128], q_fn=qfn_full(), kk=128, ncol=512, c0=0, scale=0.125, masks=masks,
                                              V=vs[g][:, j, 0:65], subs=[0, 1, 2, 3], acc_fn=(lambda sub, accS=accS: (accS[:, sub, 0:65], accS)),
                                              first=(j == 0), fin=fin, rb_score=[ksA[g], qa_lo[g], qa_hi[g][i]], rb_v=[vs[g]], mask_pool=True))
                for st in steps:
                    if st["pre"] is not None:
                        pass
                s.run_steps_pre(steps, sT_ring, pT_ring)
        sc.barrier()

    def run_steps_pre(s, steps, sT_ring, pT_ring):
        n = len(steps)
        for i in range(min(2, n)):
            if steps[i].get("pre"):
                steps[i]["pre"]()
        orig = [st.get("fin") for st in steps]
        for i, st in enumerate(steps):
            nxt = steps[i + 2].get("pre") if i + 2 < n else None
            f0 = orig[i]
            if nxt is not None:
                def mk(f0=f0, nxt=nxt):
                    def f():
                        nxt()
                        if f0 is not None:
                            f0()
                    return f
                st["fin"] = mk()
        s.run_steps(steps, sT_ring, pT_ring)

    def nsa_fin_cmp(s, i, g, O, accC, G, AFc, sm, imp, NS, TPb, ident, QA, qa_hi):
        sc = s.sc
        m = sm.next()
        for bk in range(2):
            sc.DVE(lambda e: e.tensor_scalar(out=m[:, 2 * bk:2 * bk + 2], in0=accC[bk][:, :, 64:65], scalar1=1e-30, scalar2=None, op0=ALU.max),
                   r=[accC[bk]], w=[m])
        sc.DVE(lambda e: e.reciprocal(out=m[:, 4:8], in_=m[:, 0:4]), r=[m], w=[m])
        sc.DVE(lambda e: e.tensor_tensor(out=m[:, 8:12], in0=m[:, 4:8], in1=G[:, i, 12 * g + 0:12 * g + 12:3], op=ALU.mult), r=[m, G], w=[m])
        im = imp.next()
        for hh in range(4):
            U = accC[hh // 2][:, hh % 2, 65:129]
            if hh == 0:
                sc.DVE(lambda e: e.tensor_scalar(out=im[:, :], in0=U, scalar1=m[:, 4:5], scalar2=None, op0=ALU.mult), r=[accC[0], m], w=[im])
            else:
                sc.DVE(lambda e: e.scalar_tensor_tensor(out=im[:, :], in0=U, scalar=m[:, 4 + hh:5 + hh], in1=im[:, :], op0=ALU.mult, op1=ALU.add),
                       r=[accC[hh // 2], m, im], w=[im])
        sc.DVE(lambda e: e.tensor_tensor(out=im[:, :], in0=im[:, :], in1=AFc[:, i, :], op=ALU.add), r=[im, AFc], w=[im])
        sc.DVE(lambda e: e.max(out=m[:, 16:24], in_=im[:, :]), r=[im], w=[m])
        ns = NS.next()
        sc.DVE(lambda e: e.tensor_scalar(out=ns[:, 64:128], in0=im[:, :], scalar1=m[:, 23:24], scalar2=1.0, op0=ALU.is_ge, op1=ALU.subtract), r=[im, m], w=[ns])
        sc.PE(lambda e: e.transpose(out=TPb[:, 0:128], in_=ns[:, :], identity=ident[:, :]), r=[ns, ident], w=[TPb])
        sc.DVE(lambda e: e.tensor_copy(out=QA[g][64:128, i, 0, :], in_=TPb[64:128, 0:128]), r=[TPb], w=[qa_hi[g][i]])
        for hh in range(1, 4):
            sc.POOL(lambda e: e.tensor_copy(out=QA[g][64:128, i, hh, :], in_=QA[g][64:128, i, 0, :]), r=[qa_hi[g][i]], w=[qa_hi[g][i]])
        for hh in range(4):
            num = accC[hh // 2][:, hh % 2, 0:64]
            sc.DVE(lambda e: e.tensor_scalar(out=O[:, hh * 64:(hh + 1) * 64], in0=num, scalar1=m[:, 8 + hh:9 + hh], scalar2=None, op0=ALU.mult), r=[accC[hh // 2], m], w=[O])

    def nsa_fin_branch(s, i, g, O, Ob, acc, G, sm, br):
        sc = s.sc
        m = sm.next()
        sc.DVE(lambda e: e.tensor_scalar(out=m[:, 0:4], in0=acc[:, :, 64:65], scalar1=1e-30, scalar2=None, op0=ALU.max), r=[acc], w=[m])
        sc.DVE(lambda e: e.reciprocal(out=m[:, 4:8], in_=m[:, 0:4]), r=[m], w=[m])
        sc.DVE(lambda e: e.tensor_tensor(out=m[:, 8:12], in0=m[:, 4:8], in1=G[:, i, 12 * g + br:12 * g + 12:3], op=ALU.mult), r=[m, G], w=[m])
        dst = O if Ob is None else Ob
        for hh in range(4):
            sc.DVE(lambda e: e.scalar_tensor_tensor(out=dst[:, hh * 64:(hh + 1) * 64], in0=acc[:, hh, 0:64], scalar=m[:, 8 + hh:9 + hh], in1=O[:, hh * 64:(hh + 1) * 64],
                                                    op0=ALU.mult, op1=ALU.add), r=[acc, m, O], w=[dst])
        if Ob is not None:
            sc.dma(s.o_tm.ap[i * 128:(i + 1) * 128, 256 * g:256 * g + 256], Ob[:, :], r=[Ob], w=[Buf()], q="pool")

    def phase_causal(s, l, kind):
        sc = s.sc
        S, NT, NB = s.S, s.NT, s.NB
        if kind == "mla":
            K, qd, kd, vd, scale, ocol, clamp = 96, s.mqT, s.mkT, s.mv, 96 ** -0.5, 512, False
        else:
            K, qd, kd, vd, scale, ocol, clamp = 70, s.fqT, s.fkT, s.fv, 0.125, 768, True
        with ExitStack() as es:
            tri = s.sb(es, "ca_tri", [128, 128], BF16)
            sc.dma(tri[:, :], s.c_tri.ap[:, :], w=[tri])
            qT = [s.sb(es, "ca_q%d" % h, [K, S], BF16) for h in range(4)]
            kT = [s.sb(es, "ca_k%d" % h, [K, S], BF16) for h in range(4)]
            V = [s.sb(es, "ca_v%d" % h, [128, NT, 72], BF16) for h in range(4)]
            for h in range(4):
                sc.dma(qT[h][:, :], qd.ap[h], w=[qT[h]])
                sc.dma(kT[h][:, :], kd.ap[h], w=[kT[h]])
                sc.dma(V[h][:, :, :], vd.ap[:, h, :].rearrange("(t p) e -> p t e", p=128), w=[V[h]])
            sT_ring = Ring([s.ps(es, "ca_sT%d" % i, [128, 512], F32) for i in range(3)])
            accR = Ring([s.ps(es, "ca_acc%d" % i, [128, 4, 128], F32) for i in range(3)])
            pT_ring = Ring([s.sb(es, "ca_pT%d" % i, [128, 512], BF16) for i in range(4)])
            osb = Ring([s.sb(es, "ca_o%d" % i, [128, 4, 256], BF16) for i in range(2)])
            sm = Ring([s.sb(es, "ca_sm%d" % i, [128, 8], F32) for i in range(3)])
            steps = []
            for qb in range(NB):
                o_ = osb.next()
                for h in range(4):
                    acc = accR.next()
                    nk = 4 * qb + 4
                    for j in range(nk):
                        sp = j - 4 * qb
                        c0 = max(0, sp) * 128
                        masks = []
                        if sp >= 0:
                            masks.append((c0, c0 + 128, tri[:, :], [tri], clamp))
                        subs = list(range(max(0, sp), 4))
                        fin = None
                        if j == nk - 1:
                            def fin(qb=qb, h=h, acc=acc, o_=o_):
                                m = sm.next()
                                sc.DVE(lambda e: e.tensor_scalar(out=m[:, 0:4], in0=acc[:, :, 64:65], scalar1=1e-30, scalar2=None, op0=ALU.max), r=[acc], w=[m])
                                sc.DVE(lambda e: e.reciprocal(out=m[:, 4:8], in_=m[:, 0:4]), r=[m], w=[m])
                                for sub in range(4):
                                    sc.DVE(lambda e: e.tensor_scalar(out=o_[:, sub, 64 * h:64 * h + 64], in0=acc[:, sub, 0:64], scalar1=m[:, 4 + sub:5 + sub], scalar2=None, op0=ALU.mult),
                                           r=[acc, m], w=[o_])
                                if h == 3:
                                    sc.dma(s.o_tm.ap[qb * 512:(qb + 1) * 512, ocol:ocol + 256].rearrange("(s p) c -> p s c", p=128), o_[:, :, :], r=[o_], w=[Buf()], q="pool")
                        steps.append(dict(kT=kT[h][:, j * 128:(j + 1) * 128], q_fn=(lambda c0, c1, h=h, qb=qb: qT[h][:, qb * 512 + c0:qb * 512 + c1]),
                                          kk=128, ncol=512, c0=c0, scale=scale, masks=masks, V=V[h][:, j, 0:65], subs=subs,
                                          acc_fn=(lambda sub, acc=acc: (acc[:, sub, 0:65], acc)), first=(j == 0), fin=fin,
                                          rb_score=[kT[h], qT[h]], rb_v=[V[h]], mask_pool=False))
            s.run_steps(steps, sT_ring, pT_ring)
        sc.barrier()

    def phase_combine(s, l, xsrc, xdst):
        sc = s.sc
        S, NT, NB = s.S, s.NT, s.NB
        with ExitStack() as es:
            ident = s.sb(es, "ident", [128, 128], BF16)
            sc.dma(ident[:, :], s.c_ident.ap[:, :], w=[ident])
            stg = s.stages(es)
            Wg = s.sb(es, "cb_wg", [128, 8, 3 * D], BF16)
            Wb = s.sb(es, "cb_wb", [128, 8, D], BF16)
            Wo = s.sb(es, "cb_wo", [128, 8, D], BF16)
            s.load_w(Wg, s.w_gate.ap[l], 128, 8, 3 * D, stg)
            s.load_w(TTv(Wb, 0, 4), s.w_br_nsa.ap[l], 128, 4, D, stg)
            s.load_w(TTv(Wb, 4, 2), s.w_br_mla.ap[l], 128, 2, D, stg)
            s.load_w(TTv(Wb, 6, 2), s.w_br_fox.ap[l], 128, 2, D, stg)
            s.load_w(Wo, s.w_mix_out.ap[l], 128, 8, D, stg)
            R = s.ln_setup(es, l, 0)
            PG = Ring([s.ps(es, "cb_pg%d" % i, [128, 512], F32) for i in range(2)])
            PP = Ring([s.ps(es, "cb_pp%d" % i, [128, 512], F32) for i in range(2)])
            PY = Ring([s.ps(es, "cb_py%d" % i, [128, 1024], F32) for i in range(1)])
            TPr = Ring([s.ps(es, "cb_tp%d" % i, [128, 8, 128], BF16) for i in range(2)])
            otm = Ring([s.sb(es, "cb_otm%d" % i, [128, D], BF16) for i in range(2)])
            oT = Ring([s.sb(es, "cb_oT%d" % i, [128, 8, 512], BF16) for i in range(2)])
            xTb = Ring([s.sb(es, "cb_xT%d" % i, [128, 8, 512], BF16) for i in range(2)])
            mT = Ring([s.sb(es, "cb_mT%d" % i, [128, 8, 512], BF16) for i in range(1)])
            sg = Ring([s.sb(es, "cb_sg%d" % i, [128, 512], F32) for i in range(2)])
            tA = Ring([s.sb(es, "cb_tA%d" % i, [128, 512], F32) for i in range(2)])
            tB = Ring([s.sb(es, "cb_tB%d" % i, [128, 512], F32) for i in range(2)])
            for b in range(NB):
                bs = slice(b * 512, (b + 1) * 512)
                oT_ = oT.next()
                for tt in range(4):
                    t = b * 4 + tt
                    ot = otm.next()
                    sc.dma(ot[:, :], s.o_tm.ap[t * 128:(t + 1) * 128, :], w=[ot])
                    tp = TPr.next()
                    for kc in range(8):
                        sc.PE(lambda e: e.transpose(out=tp[:, kc, :], in_=ot[:, kc * 128:(kc + 1) * 128], identity=ident[:, :]), r=[ot, ident], w=[tp])
                    sc.ACT(lambda e: e.copy(out=oT_[:, :, tt * 128:(tt + 1) * 128], in_=tp[:, :, :]), r=[tp], w=[oT_])
                x_ = xTb.next()
                sc.dma(x_[:, :, :], s.xT.ap.rearrange("k p s -> p k s")[:, :, bs], w=[x_])
                m_ = mT.next()
                brk = [(0, 4), (4, 6), (6, 8)]
                for n in range(8):
                    ns_ = slice(n * 128, (n + 1) * 128)
                    tA_ = tA.next()
                    for br in range(3):
                        pg = PG.next()
                        for kc in range(8):
                            sc.PE(lambda e: e.matmul(pg[:, :], lhsT=Wg[:, kc, br * D + n * 128:br * D + (n + 1) * 128], rhs=x_[:, kc, :], start=(kc == 0), stop=(kc == 7)), r=[Wg, x_], w=[pg])
                        pp = PP.next()
                        k0, k1 = brk[br]
                        for kc in range(k0, k1):
                            sc.PE(lambda e: e.matmul(pp[:, :], lhsT=Wb[:, kc, ns_], rhs=oT_[:, kc, :], start=(kc == k0), stop=(kc == k1 - 1)), r=[Wb, oT_], w=[pp])
                        sg_ = sg.next()
                        sc.ACT(lambda e: e.activation(out=sg_[:, :], in_=pg[:, :], func=AF.Sigmoid), r=[pg], w=[sg_])
                        if br == 0:
                            sc.DVE(lambda e: e.tensor_tensor(out=tA_[:, :], in0=pp[:, :], in1=sg_[:, :], op=ALU.mult), r=[pp, sg_], w=[tA_])
                        else:
                            tB_ = tB.next()
                            sc.DVE(lambda e: e.tensor_tensor(out=tB_[:, :], in0=pp[:, :], in1=sg_[:, :], op=ALU.mult), r=[pp, sg_], w=[tB_])
                            if br == 1:
                                sc.POOL(lambda e: e.tensor_tensor(out=tA_[:, :], in0=tA_[:, :], in1=tB_[:, :], op=ALU.add), r=[tA_, tB_], w=[tA_])
                            else:
                                sc.POOL(lambda e: e.tensor_tensor(out=m_[:, n, :], in0=tA_[:, :], in1=tB_[:, :], op=ALU.add), r=[tA_, tB_], w=[m_])
                for tt in range(4):
                    t = b * 4 + tt
                    py = PY.next()
                    for hf in range(2):
                        for kc in range(8):
                            sc.PE(lambda e: e.matmul(py[:, hf * 512:(hf + 1) * 512], lhsT=m_[:, kc, tt * 128:(tt + 1) * 128], rhs=Wo[:, kc, hf * 512:(hf + 1) * 512], start=(kc == 0), stop=(kc == 7)),
                                  r=[Wo, m_], w=[py])
                    s.layer_norm_tile(R, py, t, xsrc, xdst, TPr, ident)
        sc.barrier()

    def phase_cross(s, l, xsrc, xdst):
        sc = s.sc
        S, NT, NB = s.S, s.NT, s.NB
        with ExitStack() as es:
            ident = s.sb(es, "ident", [128, 128], BF16)
            sc.dma(ident[:, :], s.c_ident.ap[:, :], w=[ident])
            stg = s.stages(es)
            Wq = s.sb(es, "xc_wq", [128, 8, 256], BF16)
            Wkv = s.sb(es, "xc_wkv", [128, 8, 512], BF16)
            Wo = s.sb(es, "xc_wo", [128, 2, D], BF16)
            s.load_w(Wq, s.xa_w_q.ap[l], 128, 8, 256, stg)
            s.load_w(Wkv, s.xa_w_kv.ap[l], 128, 8, 512, stg)
            s.load_w(Wo, s.xa_w_o.ap[l], 128, 2, D, stg)
            memT = s.sb(es, "xc_memT", [128, 8, MEM], BF16)
            sc.dma(memT[:, :, :], s.memT.ap.rearrange("k p m -> p k m"), w=[memT])
            R = s.ln_setup(es, l, 1)
            sT_ring = Ring([s.ps(es, "xc_sT%d" % i, [128, 512], F32) for i in range(2)])
            accR = Ring([s.ps(es, "xc_acc%d" % i, [128, 4, 128], F32) for i in range(2)])
            PQ = Ring([s.ps(es, "xc_pq%d" % i, [128, 512], F32) for i in range(1)])
            PY = Ring([s.ps(es, "xc_py%d" % i, [128, 1024], F32) for i in range(1)])
            TPr = Ring([s.ps(es, "xc_tp%d" % i, [128, 8, 128], BF16) for i in range(1)])
            pT_ring = Ring([s.sb(es, "xc_pT%d" % i, [128, 512], BF16) for i in range(4)])
            kT = s.sb(es, "xc_kT", [64, 4, MEM], BF16)
            V = s.sb(es, "xc_V", [128, 2, 4, 72], BF16)
            sc.POOL(lambda e: e.memset(V[:, :, :, :], 1.0), w=[V])
            for h in range(4):
                pq = PQ.next()
                for kc in range(8):
                    sc.PE(lambda e: e.matmul(pq[0:64, 0:MEM], lhsT=Wkv[:, kc, 64 * h:64 * h + 64], rhs=memT[:, kc, :], start=(kc == 0), stop=(kc == 7)), r=[Wkv, memT], w=[pq])
                sc.ACT(lambda e: e.copy(out=kT[:, h, :], in_=pq[0:64, 0:MEM]), r=[pq], w=[kT])
            for t in range(2):
                pq = PQ.next()
                for kc in range(8):
                    sc.PE(lambda e: e.matmul(pq[:, 0:256], lhsT=memT[:, kc, t * 128:(t + 1) * 128], rhs=Wkv[:, kc, 256:512], start=(kc == 0), stop=(kc == 7)), r=[Wkv, memT], w=[pq])
                sc.ACT(lambda e: e.copy(out=V[:, t, :, 0:64], in_=pq[:, 0:256].rearrange("p (g d) -> p g d", g=4)), r=[pq], w=[V])
            xTb = Ring([s.sb(es, "xc_xT%d" % i, [128, 8, 512], BF16) for i in range(2)])
            qx = Ring([s.sb(es, "xc_qx%d" % i, [64, 4, 512], BF16) for i in range(2)])
            osb = Ring([s.sb(es, "xc_o%d" % i, [128, 4, 256], BF16) for i in range(2)])
            oTt = Ring([s.sb(es, "xc_oT%d" % i, [128, 2, 128], BF16) for i in range(2)])
            sm = Ring([s.sb(es, "xc_sm%d" % i, [128, 8], F32) for i in range(3)])
            for b in range(NB):
                bs = slice(b * 512, (b + 1) * 512)
                x_ = xTb.next()
                sc.dma(x_[:, :, :], s.xT.ap.rearrange("k p s -> p k s")[:, :, bs], w=[x_])
                q_ = qx.next()
                for h in range(4):
                    pq = PQ.next()
                    for kc in range(8):
                        sc.PE(lambda e: e.matmul(pq[0:64, :], lhsT=Wq[:, kc, 64 * h:64 * h + 64], rhs=x_[:, kc, :], start=(kc == 0), stop=(kc == 7)), r=[Wq, x_], w=[pq])
                    sc.DVE(lambda e: e.tensor_copy(out=q_[:, h, :], in_=pq[0:64, :]), r=[pq], w=[q_])
                o_ = osb.next()
                steps = []
                for h in range(4):
                    acc = accR.next()
                    for j in range(2):
                        fin = None
                        if j == 1:
                            def fin(h=h, acc=acc, o_=o_):
                                m = sm.next()
                                sc.DVE(lambda e: e.reciprocal(out=m[:, 4:8], in_=acc[:, :, 64:65]), r=[acc], w=[m])
                                for sub in range(4):
                                    sc.DVE(lambda e: e.tensor_scalar(out=o_[:, sub, 64 * h:64 * h + 64], in0=acc[:, sub, 0:64], scalar1=m[:, 4 + sub:5 + sub], scalar2=None, op0=ALU.mult),
                                           r=[acc, m], w=[o_])
                        steps.append(dict(kT=kT[:, h, j * 128:(j + 1) * 128], q_fn=(lambda c0, c1, h=h, q_=q_: q_[:, h, c0:c1]), kk=128, ncol=512, c0=0, scale=0.125,
                                          masks=[], V=V[:, j, h, 0:65], subs=[0, 1, 2, 3], acc_fn=(lambda sub, acc=acc: (acc[:, sub, 0:65], acc)), first=(j == 0), fin=fin,
                                          rb_score=[kT, q_], rb_v=[V], mask_pool=False))
                s.run_steps(steps, sT_ring, pT_ring)
                for tt in range(4):
                    t = b * 4 + tt
                    tp = TPr.next()
                    for kc in range(2):
                        sc.PE(lambda e: e.transpose(out=tp[:, kc, :], in_=o_[:, tt, kc * 128:(kc + 1) * 128], identity=ident[:, :]), r=[o_, ident], w=[tp])
                    oT_ = oTt.next()
                    sc.ACT(lambda e: e.copy(out=oT_[:, :, :], in_=tp[:, 0:2, :]), r=[tp], w=[oT_])
                    py = PY.next()
                    for hf in range(2):
                        for kc in range(2):
                            sc.PE(lambda e: e.matmul(py[:, hf * 512:(hf + 1) * 512], lhsT=oT_[:, kc, :], rhs=Wo[:, kc, hf * 512:(hf + 1) * 512], start=(kc == 0), stop=(kc == 1)), r=[Wo, oT_], w=[py])
                    s.layer_norm_tile(R, py, t, xsrc, xdst, TPr, ident)
        sc.barrier()

    def phase_mlp_up(s, l):
        sc = s.sc
        S, NT, NB = s.S, s.NT, s.NB
        with ExitStack() as es:
            stg = s.stages(es)
            W = s.sb(es, "mu_w", [128, 8, 4 * D], BF16)
            s.load_w(W, s.w_up.ap[l], 128, 8, 4 * D, stg)
            PB = Ring([s.ps(es, "mu_pb%d" % i, [128, 512], F32) for i in range(4)])
            xTb = Ring([s.sb(es, "mu_xT%d" % i, [128, 8, 512], BF16) for i in range(2)])
            r_ = Ring([s.sb(es, "mu_r%d" % i, [128, 512], F32) for i in range(3)])
            h_ = Ring([s.sb(es, "mu_h%d" % i, [128, 512], BF16) for i in range(3)])
            for b in range(NB):
                bs = slice(b * 512, (b + 1) * 512)
                x_ = xTb.next()
                sc.dma(x_[:, :, :], s.xT.ap.rearrange("k p s -> p k s")[:, :, bs], w=[x_])
                for n in range(32):
                    pb = PB.next()
                    for kc in range(8):
                        sc.PE(lambda e: e.matmul(pb[:, :], lhsT=W[:, kc, n * 128:(n + 1) * 128], rhs=x_[:, kc, :], start=(kc == 0), stop=(kc == 7)), r=[W, x_], w=[pb])
                    rr = r_.next()
                    sc.ACT(lambda e: e.activation(out=rr[:, :], in_=pb[:, :], func=AF.Relu), r=[pb], w=[rr])
                    hh = h_.next()
                    if n % 2 == 0:
                        sc.POOL(lambda e: e.tensor_tensor(out=hh[:, :], in0=rr[:, :], in1=rr[:, :], op=ALU.mult), r=[rr], w=[hh])
                    else:
                        sc.DVE(lambda e: e.tensor_tensor(out=hh[:, :], in0=rr[:, :], in1=rr[:, :], op=ALU.mult), r=[rr], w=[hh])
                    sc.dma(s.hT.ap[n, :, bs], hh[:, :], r=[hh], w=[Buf()], q="pool")
        sc.barrier()

    def phase_mlp_down(s, l, xsrc, xdst, final):
        sc = s.sc
        S, NT, NB = s.S, s.NT, s.NB
        with ExitStack() as es:
            ident = s.sb(es, "ident", [128, 128], BF16)
            sc.dma(ident[:, :], s.c_ident.ap[:, :], w=[ident])
            stg = s.stages(es)
            W = s.sb(es, "md_w", [128, 32, D], BF16)
            s.load_w(W, s.w_down.ap[l], 128, 32, D, stg)
            R = s.ln_setup(es, l, 2)
            PY = Ring([s.ps(es, "md_py%d" % i, [128, 1024], F32) for i in range(2)])
            TPr = Ring([s.ps(es, "md_tp%d" % i, [128, 8, 128], BF16) for i in range(2)])
            hb = Ring([s.sb(es, "md_h%d" % i, [128, 32, 256], BF16) for i in range(2)])
            for b2 in range(NT // 2):
                bs = slice(b2 * 256, (b2 + 1) * 256)
                h_ = hb.next()
                for qq in range(4):
                    sc.dma(h_[:, qq * 8:(qq + 1) * 8, :], s.hT.ap.rearrange("k p s -> p k s")[:, qq * 8:(qq + 1) * 8, bs], w=[h_])
                for tt in range(2):
                    t = b2 * 2 + tt
                    py = PY.next()
                    for hf in range(2):
                        for kc in range(32):
                            sc.PE(lambda e: e.matmul(py[:, hf * 512:(hf + 1) * 512], lhsT=h_[:, kc, tt * 128:(tt + 1) * 128], rhs=W[:, kc, hf * 512:(hf + 1) * 512], start=(kc == 0), stop=(kc == 31)),
                                  r=[W, h_], w=[py])
                    s.layer_norm_tile(R, py, t, xsrc, xdst, TPr, ident, final=final)
        sc.barrier()


class TTv:
    def __init__(s, tt, a0, n):
        s.tt = tt
        s.a0 = a0
        s.buf = tt.buf

    def __getitem__(s, k):
        p, a, c = k
        if isinstance(a, slice):
            a = slice((a.start or 0) + s.a0, (a.stop if a.stop is not None else 0) + s.a0)
        else:
            a = a + s.a0
        return s.tt[p, a, c]


_CACHE = {}


def get_prog(S, depth, debug=None):
    key = (S, depth, tuple(sorted(debug)) if debug else None)
    if key not in _CACHE:
        p = Prog(S, depth, debug=debug)
        p.build()
        _CACHE[key] = p
    return _CACHE[key]


def make_in_maps(inputs, S, depth, ncores):
    consts = host_consts(S)
    near, cmpb, c31 = t5_gather(np.asarray(inputs["t5_table"], np.float32), consts)
    shared = {}
    for k in ("w_in", "cmp_pe", "cmp_w1", "cmp_w2", "mla_q_norm", "mla_w_uq", "mla_kv_norm", "mla_w_ukv", "fox_b_f", "w_gate",
              "w_br_nsa", "w_br_mla", "w_br_fox", "w_mix_out", "xa_w_q", "xa_w_kv", "xa_w_o", "mlp_w_up", "mlp_w_down", "ln_g", "ln_b"):
        shared[k] = np.ascontiguousarray(np.asarray(inputs[k], np.float32)[:depth])
    for k in ("ident_bf", "tri_bf", "tri4_bf", "atri4_bf", "nearmask", "rope_cs", "rope_ss", "overlap_bf", "aforced", "expand_bf", "ones_bf", "cmp_valid"):
        shared[k] = consts[k]
    shared["t5_near"] = near
    shared["t5_cmpb"] = cmpb
    shared["t5_c31"] = c31
    maps = []
    for c in range(ncores):
        m = dict(shared)
        m["x"] = np.ascontiguousarray(np.asarray(inputs["x"][c], np.float32))
        m["mem"] = np.ascontiguousarray(np.asarray(inputs["mem"][c], np.float32))
        maps.append(m)
    return maps


def kernel(**inputs):
    S, depth, ncores = SEQ_FULL, DEPTH_FULL, 8
    p = get_prog(S, depth)
    maps = make_in_maps(inputs, S, depth, ncores)
    res = run_bass_kernel_spmd(p.nc, maps, core_ids=list(range(ncores)))
    out = np.stack([np.asarray(r["y"], np.float32) for r in res.results], 0)
    return out
```

```python
import math
from contextlib import ExitStack
import numpy as np
import ml_dtypes
import concourse.bass as bass
import concourse.mybir as mybir
from concourse.bass_utils import run_bass_kernel_spmd

F32 = mybir.dt.float32
BF16 = mybir.dt.bfloat16
AF = mybir.ActivationFunctionType
ALU = mybir.AluOpType
AX = mybir.AxisListType

D = 1024
DEPTH_FULL = 4
SEQ_FULL = 4096
MEM = 256
N_IN = 2620
DN_ALPHA = (2 * DEPTH_FULL) ** 0.25
BIG = 30000.0


class Tok:
    __slots__ = ("sem", "val", "know")

    def __init__(s, sem, val, know):
        s.sem = sem
        s.val = val
        s.know = know


class Buf:
    __slots__ = ("w", "r", "name")

    def __init__(s, name=""):
        s.w = None
        s.r = {}
        s.name = name


class TT:
    def __init__(s, h, name=""):
        s.h = h
        s.buf = Buf(name)

    def __getitem__(s, k):
        return s.h[k]


class Eng:
    def __init__(s, name, e, sem, semid):
        s.name = name
        s.e = e
        s.sem = sem
        s.semid = semid
        s.cnt = 0
        s.know = {}


class Lane:
    def __init__(s, sem, semid):
        s.sem = sem
        s.semid = semid
        s.val = 0


def _bufs(xs):
    out = []
    for x in xs:
        if x is None:
            continue
        out.append(x.buf if hasattr(x, "buf") else x)
    return out


class Sched:
    NL = 8

    def __init__(s, nc, es):
        s.nc = nc
        s.sems = []

        def mk(n):
            sem = es.enter_context(nc.semaphore(n))
            s.sems.append(sem)
            return sem, len(s.sems) - 1

        s.pe = Eng("pe", nc.tensor, *mk("s_pe"))
        s.dve = Eng("dve", nc.vector, *mk("s_dve"))
        s.act = Eng("act", nc.scalar, *mk("s_act"))
        s.pool = Eng("pool", nc.gpsimd, *mk("s_pool"))
        s.sp = Eng("sp", nc.sync, *mk("s_sp"))
        s.engs = [s.pe, s.dve, s.act, s.pool, s.sp]
        s.lanes = {}
        s.rr = {}
        for q in ("sp", "pool"):
            s.lanes[q] = [Lane(*mk("l_%s_%d" % (q, i))) for i in range(s.NL)]
            s.rr[q] = 0
        s.q = {"sp": s.sp, "pool": s.pool}
        s.ninst = 0

    def _wait(s, E, tok):
        if tok is None:
            return
        if E.know.get(tok.sem, 0) >= tok.val:
            return
        if tok.sem == E.semid and E is s.pe:
            return
        E.e.wait_ge(s.sems[tok.sem], tok.val)
        k = dict(E.know)
        for a, b in tok.know.items():
            if k.get(a, 0) < b:
                k[a] = b
        if k.get(tok.sem, 0) < tok.val:
            k[tok.sem] = tok.val
        E.know = k

    def _deps(s, E, r, w):
        for b in r:
            s._wait(E, b.w)
        for b in w:
            s._wait(E, b.w)
            for t in list(b.r.values()):
                s._wait(E, t)

    def op(s, E, fn, r=(), w=()):
        r = _bufs(r)
        w = _bufs(w)
        s._deps(E, r, w)
        inst = fn(E.e)
        E.cnt += 1
        inst.then_inc(E.sem, 1)
        tok = Tok(E.semid, E.cnt, E.know)
        for b in r:
            b.r[E.semid] = tok
        for b in w:
            b.w = tok
            b.r = {}
        s.ninst += 1
        return tok

    def PE(s, fn, r=(), w=()):
        return s.op(s.pe, fn, r, w)

    def DVE(s, fn, r=(), w=()):
        return s.op(s.dve, fn, r, w)

    def ACT(s, fn, r=(), w=()):
        return s.op(s.act, fn, r, w)

    def POOL(s, fn, r=(), w=()):
        return s.op(s.pool, fn, r, w)

    def dma(s, out, in_, r=(), w=(), q="sp", **kw):
        Q = s.q[q]
        r = _bufs(r)
        w = _bufs(w)
        s._deps(Q, r, w)
        lanes = s.lanes[q]
        i = s.rr[q]
        s.rr[q] = (i + 1) % len(lanes)
        lane = lanes[i]
        if lane.val > 0:
            s._wait(Q, Tok(lane.semid, lane.val, {}))
        inst = Q.e.dma_start(out=out, in_=in_, **kw)
        lane.val += 16
        inst.then_inc(lane.sem, 16)
        tok = Tok(lane.semid, lane.val, Q.know)
        for b in r:
            b.r[lane.semid] = tok
        for b in w:
            b.w = tok
            b.r = {}
        s.ninst += 1
        return tok

    def barrier(s):
        toks = [Tok(E.semid, E.cnt, {}) for E in s.engs if E.cnt > 0]
        for q in s.lanes:
            for l in s.lanes[q]:
                if l.val > 0:
                    toks.append(Tok(l.semid, l.val, {}))
        for E in s.engs:
            for t in toks:
                s._wait(E, t)


class Ring:
    def __init__(s, items):
        s.items = items
        s.i = 0

    def next(s):
        x = s.items[s.i]
        s.i = (s.i + 1) % len(s.items)
        return x


class DT:
    def __init__(s, ap, name):
        s.ap = ap
        s.name = name
        s.bufs = {}
        s.buf = Buf(name)

    def b(s, key):
        if key not in s.bufs:
            s.bufs[key] = Buf("%s_%s" % (s.name, key))
        return s.bufs[key]


def t5_bucket_np(dist):
    n = np.maximum(dist, 0)
    nf = np.maximum(n, 1).astype(np.float32)
    large = 16 + (np.log(nf / np.float32(16)) / np.float32(math.log(128 / 16)) * np.float32(16)).astype(np.int32)
    large = np.minimum(large, 31)
    return np.where(n < 16, n, large)


def host_consts(S):
    NT = S // 128
    NCP = S // 16
    c = {}
    c["ident_bf"] = np.eye(128, dtype=np.float32).astype(ml_dtypes.bfloat16)
    k = np.arange(128)[:, None]
    q = np.arange(128)[None, :]
    tri = (q >= k).astype(np.float32)
    c["tri_bf"] = tri.astype(ml_dtypes.bfloat16)
    c["tri4_bf"] = np.tile(tri[:, None, :], (1, 4, 1)).astype(ml_dtypes.bfloat16)
    atri = (k > q).astype(np.float32)
    c["atri4_bf"] = np.tile(atri[:, None, :], (1, 4, 1)).astype(ml_dtypes.bfloat16)
    m = np.ones((2, 128, 8, 128), np.float32)
    m[0] = np.tile(tri[:, None, :], (1, 8, 1))
    c["nearmask"] = m
    half = 16
    inv = (10000.0 ** (-np.arange(half, dtype=np.float32) / half)).astype(np.float32)
    ang = np.arange(S, dtype=np.float32)[None, :] * inv[:, None]
    cos = np.cos(ang).astype(np.float32)
    sin = np.sin(ang).astype(np.float32)
    cs = np.zeros((96, S), np.float32)
    ss = np.zeros((96, S), np.float32)
    for base in (0, 64):
        cs[base:base + 16] = cos
        cs[base + 16:base + 32] = cos
        ss[base:base + 16] = -sin
        ss[base + 16:base + 32] = sin
    c["rope_cs"] = cs
    c["rope_ss"] = ss
    n_slc = S // 64
    c_lo = np.arange(NCP)[:, None] * 16
    s_lo = np.arange(64)[None, :] * 64
    ov = np.maximum(np.minimum(c_lo + 32, s_lo + 64) - np.maximum(c_lo, s_lo), 0).astype(np.float32) / 16.0
    ov[:, n_slc:] = 0.0
    c["overlap_bf"] = ov.astype(ml_dtypes.bfloat16)
    t = np.arange(S)[:, None]
    blk = np.arange(64)[None, :]
    cur = t // 64
    forced = (blk == 0) | (blk == cur) | (blk == cur - 1)
    causal = (blk * 64 <= t) & (blk < n_slc)
    A = np.where(causal, np.where(forced, 1e30, 0.0), -1e30).astype(np.float32)
    c["aforced"] = A
    ex = np.zeros((64, S), np.float32)
    ex[np.arange(S) // 64, np.arange(S)] = BIG
    c["expand_bf"] = ex.astype(ml_dtypes.bfloat16)
    c["ones_bf"] = np.ones((128, 512), np.float32).astype(ml_dtypes.bfloat16)
    OFF = 8 * (NT - 1)
    NROW = ((NCP + OFF + 127) // 128) * 128
    cc = np.arange(NROW)[:, None] - OFF
    ql = np.arange(128)[None, :]
    dist = ql - 16 * cc - 31
    c["cmp_dist"] = dist
    vc = (dist >= 0).astype(np.float32)
    c["cmp_valid"] = np.tile(vc[:, None, :], (1, 8, 1)).astype(np.float32)
    return c


def t5_gather(t5_table, consts):
    k = np.arange(128)[:, None]
    q = np.arange(128)[None, :]
    b0 = t5_bucket_np(q - k)
    b1 = t5_bucket_np(128 + q - k)
    near = np.stack([t5_table[b0], t5_table[b1]], 0)
    near = np.ascontiguousarray(near.transpose(0, 1, 3, 2))
    bc = t5_bucket_np(consts["cmp_dist"])
    cmpb = np.ascontiguousarray(t5_table[bc].transpose(0, 2, 1))
    c31 = np.ascontiguousarray(np.broadcast_to(t5_table[31][None, :, None], (128, 8, 128)))
    return near.astype(np.float32), cmpb.astype(np.float32), c31.astype(np.float32)


class Prog:
    def __init__(s, S, depth, debug=False):
        s.S = S
        s.depth = depth
        s.debug = debug
        s.NT = S // 128
        s.NB = S // 512
        s.NCP = S // 16
        s.NCMP = (S - 32) // 16 + 1
        s.CR = min(128, s.NCP)
        s.CT = (s.NCP + 127) // 128
        s.OFF = 8 * (s.NT - 1)
        s.nc = bass.Bass("TRN2", target_bir_lowering=False)
        s.es = ExitStack()
        s.sc = Sched(s.nc, s.es)
        s.dbg_outs = []
        s.inputs = {}

    def din(s, name, shape, dt=F32):
        ap = s.nc.dram_tensor(name, list(shape), dt, kind="ExternalInput").ap()
        s.inputs[name] = ap
        return DT(ap, name)

    def dscr(s, name, shape, dt, out=False):
        kind = "ExternalOutput" if (out or (s.debug and name in s.debug)) else "Internal"
        if kind == "ExternalOutput":
            s.dbg_outs.append(name)
        ap = s.nc.dram_tensor(name, list(shape), dt, kind=kind).ap()
        return DT(ap, name)

    def sb(s, es, name, shape, dt):
        s.uid = getattr(s, "uid", 0) + 1
        name = "%s_u%d" % (name, s.uid)
        return TT(es.enter_context(s.nc.sbuf_tensor(name, list(shape), dt)), name)

    def ps(s, es, name, shape, dt=F32):
        s.uid = getattr(s, "uid", 0) + 1
        name = "%s_u%d" % (name, s.uid)
        return TT(es.enter_context(s.nc.psum_tensor(name, list(shape), dt)), name)

    def load_w(s, dst, src2d, P, A, N, stage_ring, dcol=0):
        sc = s.sc
        src3 = src2d.rearrange("(a p) n -> p a n", p=P)
        maxc = max(1, 2048 // A)
        c0 = 0
        k = 0
        while c0 < N:
            ncol = min(maxc, N - c0)
            st = stage_ring.next()
            sv = st[0:P, 0:A * ncol].rearrange("p (a n) -> p a n", a=A)
            sc.dma(sv, src3[:, :, c0:c0 + ncol], w=[st])
            k += 1
            if k % 4 == 0:
                sc.DVE(lambda e: e.tensor_copy(out=dst[0:P, 0:A, dcol + c0:dcol + c0 + ncol], in_=sv), r=[st], w=[dst])
            else:
                sc.POOL(lambda e: e.tensor_copy(out=dst[0:P, 0:A, dcol + c0:dcol + c0 + ncol], in_=sv), r=[st], w=[dst])
            c0 += ncol

    def stages(s, es, n=2):
        return Ring([s.sb(es, "wstage%d" % i, [128, 2048], F32) for i in range(n)])

    def build(s):
        S, NT, NB = s.S, s.NT, s.NB
        L = s.depth
        s.x_in = s.din("x", [S, D])
        s.mem_in = s.din("mem", [MEM, D])
        s.w_in = s.din("w_in", [L, D, N_IN])
        s.cmp_pe = s.din("cmp_pe", [L, 2, 32, 64])
        s.cmp_w1 = s.din("cmp_w1", [L, 2, 2048, 128])
        s.cmp_w2 = s.din("cmp_w2", [L, 2, 128, 64])
        s.q_norm = s.din("mla_q_norm", [L, 384])
        s.w_uq = s.din("mla_w_uq", [L, 384, 384])
        s.kv_norm = s.din("mla_kv_norm", [L, 128])
        s.w_ukv = s.din("mla_w_ukv", [L, 128, 512])
        s.b_f = s.din("fox_b_f", [L, 4])
        s.w_gate = s.din("w_gate", [L, D, 3 * D])
        s.w_br_nsa = s.din("w_br_nsa", [L, 512, D])
        s.w_br_mla = s.din("w_br_mla", [L, 256, D])
        s.w_br_fox = s.din("w_br_fox", [L, 256, D])
        s.w_mix_out = s.din("w_mix_out", [L, D, D])
        s.xa_w_q = s.din("xa_w_q", [L, D, 256])
        s.xa_w_kv = s.din("xa_w_kv", [L, D, 512])
        s.xa_w_o = s.din("xa_w_o", [L, 256, D])
        s.w_up = s.din("mlp_w_up", [L, D, 4 * D])
        s.w_down = s.din("mlp_w_down", [L, 4 * D, D])
        s.ln_g = s.din("ln_g", [L, 3, D])
        s.ln_b = s.din("ln_b", [L, 3, D])
        NROW = ((s.NCP + s.OFF + 127) // 128) * 128
        s.NROW = NROW
        s.c_ident = s.din("ident_bf", [128, 128], BF16)
        s.c_tri = s.din("tri_bf", [128, 128], BF16)
        s.c_tri4 = s.din("tri4_bf", [128, 4, 128], BF16)
        s.c_atri4 = s.din("atri4_bf", [128, 4, 128], BF16)
        s.c_nearmask = s.din("nearmask", [2, 128, 8, 128])
        s.c_cs = s.din("rope_cs", [96, S])
        s.c_ss = s.din("rope_ss", [96, S])
        s.c_overlap = s.din("overlap_bf", [s.NCP, 64], BF16)
        s.c_aforced = s.din("aforced", [S, 64])
        s.c_expand = s.din("expand_bf", [64, S], BF16)
        s.c_ones = s.din("ones_bf", [128, 512], BF16)
        s.c_cmpvalid = s.din("cmp_valid", [NROW, 8, 128])
        s.c_near = s.din("t5_near", [2, 128, 8, 128])
        s.c_cmpb = s.din("t5_cmpb", [NROW, 8, 128])
        s.c_c31 = s.din("t5_c31", [128, 8, 128])
        s.y_out = s.dscr("y", [S, D], F32, out=True)
        s.xa = s.dscr("x_a", [S, D], F32)
        s.xb = s.dscr("x_b", [S, D], F32)
        s.xT = s.dscr("xT", [8, 128, S], BF16)
        s.memT = s.dscr("memT", [8, 128, MEM], BF16)
        s.em = s.dscr("em", [2, 128, 8, 128], BF16)
        s.emc = s.dscr("emc", [NROW, 8, 128], BF16)
        s.nqT = s.dscr("nqT", [8, 64, S], BF16)
        s.nkcT = s.dscr("nkcT", [2, 64, S], BF16)
        s.nvcT = s.dscr("nvcT", [2, 64, S], BF16)
        s.nksT = s.dscr("nksT", [2, 64, S], BF16)
        s.nkwT = s.dscr("nkwT", [2, 64, S], BF16)
        s.nvs = s.dscr("nvs", [S, 2, 72], BF16)
        s.nvw = s.dscr("nvw", [S, 2, 72], BF16)
        s.gate = s.dscr("gate", [S, 24], F32)
        s.cnT = s.dscr("cnT", [4, 128, S], BF16)
        s.krT = s.dscr("krT", [2, 32, S], F32)
        s.mqT = s.dscr("mqT", [4, 96, S], BF16)
        s.mkT = s.dscr("mkT", [4, 96, S], BF16)
        s.mv = s.dscr("mv", [S, 4, 72], BF16)
        s.fqT = s.dscr("fqT", [4, 70, S], BF16)
        s.fkT = s.dscr("fkT", [4, 70, S], BF16)
        s.fv = s.dscr("fv", [S, 4, 72], BF16)
        s.ffT = s.dscr("ffT", [4, S], F32)
        s.o_tm = s.dscr("o_tm", [S, D], BF16)
        s.hT = s.dscr("hT", [32, 128, S], BF16)

        import os
        stop = os.environ.get("K_STOP", "")
        seq = []
        seq.append(("init", lambda: s.phase_init()))
        state = {"cur": s.x_in, "nxt": s.xa}

        def adv():
            state["cur"], state["nxt"] = state["nxt"], (s.xb if state["nxt"] is s.xa else s.xa)
        for l in range(L):
            last = (l == L - 1)
            seq.append(("p1", lambda l=l: s.phase_p1(l)))
            seq.append(("mla_prep", lambda l=l: s.phase_mla_prep(l)))
            seq.append(("fox_prep", lambda l=l: s.phase_fox_prep(l)))
            seq.append(("nsa", lambda l=l: s.phase_nsa(l)))
            seq.append(("mla", lambda l=l: s.phase_causal(l, "mla")))
            seq.append(("fox", lambda l=l: s.phase_causal(l, "fox")))
            seq.append(("combine", lambda l=l: (s.phase_combine(l, state["cur"], state["nxt"]), adv())))
            seq.append(("cross", lambda l=l: (s.phase_cross(l, state["cur"], state["nxt"]), adv())))
            seq.append(("mlp_up", lambda l=l: s.phase_mlp_up(l)))
            seq.append(("mlp_down", lambda l=l, last=last: (s.phase_mlp_down(l, state["cur"], s.y_out if last else state["nxt"], last), adv())))
        skip = set(os.environ.get("K_SKIP", "").split(","))
        for (nm, fn) in seq:
            if nm not in skip:
                fn()
            if stop and nm == stop:
                break
        s.sc.barrier()
        return s.nc

    def transpose_to_xT(s, xbf, TPr, xTt, tile_idx, ident, dst, q="pool"):
        sc = s.sc
        tp = TPr.next()
        for kc in range(8):
            sc.PE(lambda e: e.transpose(out=tp[:, kc, :], in_=xbf[:, kc * 128:(kc + 1) * 128], identity=ident[:, :]),
                  r=[xbf, ident], w=[tp])
        xt = xTt.next()
        sc.ACT(lambda e: e.copy(out=xt[:, :, :], in_=tp[:, :, :]), r=[tp], w=[xt])
        sc.dma(dst.ap.rearrange("k p s -> p k s")[:, :, tile_idx * 128:(tile_idx + 1) * 128], xt[:, :, :], r=[xt], w=[dst.b(("t", tile_idx))], q=q)

    def ln_setup(s, es, l, which):
        sc = s.sc
        g = s.sb(es, "ln_gam", [128, D], F32)
        b = s.sb(es, "ln_bet", [128, D], F32)
        sc.dma(g[:, :], s.ln_g.ap[l, which, :].partition_broadcast(128), w=[g])
        sc.dma(b[:, :], s.ln_b.ap[l, which, :].partition_broadcast(128), w=[b])
        r = dict(g=g, b=b)
        r["xin"] = Ring([s.sb(es, "ln_xin%d" % i, [128, D], F32) for i in range(2)])
        r["z"] = Ring([s.sb(es, "ln_z%d" % i, [128, D], F32) for i in range(2)])
        r["xo"] = Ring([s.sb(es, "ln_xo%d" % i, [128, D], F32) for i in range(2)])
        r["xbf"] = Ring([s.sb(es, "ln_xbf%d" % i, [128, D], BF16) for i in range(1)])
        r["st"] = Ring([s.sb(es, "ln_st%d" % i, [128, 24], F32) for i in range(3)])
        r["xTt"] = Ring([s.sb(es, "ln_xTt%d" % i, [128, 8, 128], BF16) for i in range(2)])
        return r

    def layer_norm_tile(s, R, Y, tile_idx, xsrc, xdst, TPr, ident, final=False):
        st8 = s.ln_A(R, Y, tile_idx, xsrc)
        if R.get("pend") is not None:
            s.ln_B(R, *R["pend"])
        R["pend"] = (st8, tile_idx, xdst, TPr, ident, final)

    def ln_flush(s, R):
        if R.get("pend") is not None:
            s.ln_B(R, *R["pend"])
            R["pend"] = None

    def ln_A(s, R, Y, tile_idx, xsrc):
        sc = s.sc
        rows = slice(tile_idx * 128, (tile_idx + 1) * 128)
        xin = R["xin"].next()
        sc.dma(xin[:, :], xsrc.ap[rows, :], r=[xsrc.b(("t", tile_idx))], w=[xin], q="pool")
        z = R["z"].next()
        sc.DVE(lambda e: e.scalar_tensor_tensor(out=z[:, :], in0=xin[:, :], scalar=float(DN_ALPHA), in1=Y[:, :], op0=ALU.mult, op1=ALU.add),
               r=[xin, Y], w=[z])
        st = R["st"].next()
        for c in range(2):
            sc.DVE(lambda e: e.bn_stats(out=st[:, c * 6:(c + 1) * 6], in_=z[:, c * 512:(c + 1) * 512]), r=[z], w=[st])
        sc.DVE(lambda e: e.bn_aggr(out=st[:, 12:14], in_=st[:, 0:12]), r=[st], w=[st])
        sc.DVE(lambda e: e.tensor_scalar(out=st[:, 14:15], in0=st[:, 13:14], scalar1=1e-5, scalar2=None, op0=ALU.add), r=[st], w=[st])
        sc.ACT(lambda e: e.activation(out=st[:, 16:17], in_=st[:, 14:15], func=AF.Sqrt), r=[st], w=[st])
        return (z, st)

    def ln_B(s, R, zst, tile_idx, xdst, TPr, ident, final):
        sc = s.sc
        z, st = zst
        rows = slice(tile_idx * 128, (tile_idx + 1) * 128)
        sc.DVE(lambda e: e.reciprocal(out=st[:, 18:19], in_=st[:, 16:17]), r=[st], w=[st])
        xo = R["xo"].next()
        sc.DVE(lambda e: e.scalar_tensor_tensor(out=z[:, :], in0=z[:, :], scalar=st[:, 12:13], in1=R["g"][:, :], op0=ALU.subtract, op1=ALU.mult),
               r=[z, st, R["g"]], w=[z])
        sc.DVE(lambda e: e.scalar_tensor_tensor(out=xo[:, :], in0=z[:, :], scalar=st[:, 18:19], in1=R["b"][:, :], op0=ALU.mult, op1=ALU.add),
               r=[z, st, R["b"]], w=[xo])
        sc.dma(xdst.ap[rows, :], xo[:, :], r=[xo], w=[xdst.b(("t", tile_idx))], q="sp")
        if not final:
            xbf = R["xbf"].next()
            sc.ACT(lambda e: e.copy(out=xbf[:, :], in_=xo[:, :]), r=[xo], w=[xbf])
            s.transpose_to_xT(xbf, TPr, R["xTt"], tile_idx, ident, s.xT, q="sp")

    def phase_init(s):
        sc = s.sc
        S, NT = s.S, s.NT
        with ExitStack() as es:
            ident = s.sb(es, "ident", [128, 128], BF16)
            sc.dma(ident[:, :], s.c_ident.ap[:, :], w=[ident])
            c31 = s.sb(es, "c31", [128, 1024], F32)
            sc.dma(c31[:, :], s.c_c31.ap.rearrange("p h q -> p (h q)"), w=[c31])
            tb = Ring([s.sb(es, "t5b%d" % i, [128, 1024], F32) for i in range(2)])
            tm = Ring([s.sb(es, "t5m%d" % i, [128, 1024], F32) for i in range(2)])
            to = Ring([s.sb(es, "t5o%d" % i, [128, 1024], BF16) for i in range(2)])
            jobs = [(s.c_near.ap[i].rearrange("p h q -> p (h q)"), s.c_nearmask.ap[i].rearrange("p h q -> p (h q)"),
                     s.em.ap[i].rearrange("p h q -> p (h q)")) for i in range(2)]
            for rt in range(s.NROW // 128):
                rs = slice(rt * 128, (rt + 1) * 128)
                jobs.append((s.c_cmpb.ap[rs].rearrange("p h q -> p (h q)"), s.c_cmpvalid.ap[rs].rearrange("p h q -> p (h q)"),
                             s.emc.ap[rs].rearrange("p h q -> p (h q)")))
            for (bsrc, msrc, dst) in jobs:
                b = tb.next()
                m = tm.next()
                o = to.next()
                sc.dma(b[:, :], bsrc, w=[b])
                sc.dma(m[:, :], msrc, w=[m])
                sc.DVE(lambda e: e.tensor_tensor(out=b[:, :], in0=b[:, :], in1=c31[:, :], op=ALU.subtract), r=[b, c31], w=[b])
                sc.ACT(lambda e: e.activation(out=b[:, :], in_=b[:, :], func=AF.Exp), r=[b], w=[b])
                sc.DVE(lambda e: e.tensor_tensor(out=o[:, :], in0=b[:, :], in1=m[:, :], op=ALU.mult), r=[b, m], w=[o])
                sc.dma(dst, o[:, :], r=[o], w=[s.em.buf], q="pool")
            mt = s.sb(es, "mem_t", [128, 2, D], F32)
            sc.dma(mt[:, :, :], s.mem_in.ap.rearrange("(t p) d -> p t d", p=128), w=[mt])
            mb = s.sb(es, "mem_b", [128, 2, D], BF16)
            sc.DVE(lambda e: e.tensor_copy(out=mb[:, :, :], in_=mt[:, :, :]), r=[mt], w=[mb])
            tp = s.ps(es, "init_tp", [128, 8, 128], BF16)
            mT = s.sb(es, "memT_s", [128, 8, MEM], BF16)
            for t in range(2):
                for kc in range(8):
                    sc.PE(lambda e: e.transpose(out=tp[:, kc, :], in_=mb[:, t, kc * 128:(kc + 1) * 128], identity=ident[:, :]), r=[mb, ident], w=[tp])
                sc.ACT(lambda e: e.copy(out=mT[:, :, t * 128:(t + 1) * 128], in_=tp[:, :, :]), r=[tp], w=[mT])
            sc.dma(s.memT.ap.rearrange("k p m -> p k m"), mT[:, :, :], r=[mT], w=[s.memT.buf], q="pool")
            xr = Ring([s.sb(es, "ix%d" % i, [128, D], F32) for i in range(2)])
            xbr = Ring([s.sb(es, "ixb%d" % i, [128, D], BF16) for i in range(2)])
            TPr = Ring([tp, s.ps(es, "init_tp2", [128, 8, 128], BF16)])
            xTt = Ring([s.sb(es, "ixT%d" % i, [128, 8, 128], BF16) for i in range(2)])
            for t in range(NT):
                xt = xr.next()
                sc.dma(xt[:, :], s.x_in.ap[t * 128:(t + 1) * 128, :], w=[xt])
                xb = xbr.next()
                sc.DVE(lambda e: e.tensor_copy(out=xb[:, :], in_=xt[:, :]), r=[xt], w=[xb])
                s.transpose_to_xT(xb, TPr, xTt, t, ident, s.xT)
        sc.barrier()

    def phase_p1(s, l):
        sc = s.sc
        S, NT, NB = s.S, s.NT, s.NB
        WN = N_IN + 64
        with ExitStack() as es:
            ident = s.sb(es, "ident", [128, 128], BF16)
            sc.dma(ident[:, :], s.c_ident.ap[:, :], w=[ident])
            W = s.sb(es, "p1_w", [128, 8, WN], BF16)
            stg = s.stages(es)
            wsrc = s.w_in.ap[l]
            s.load_w(W, wsrc, 128, 8, N_IN, stg)
            sc.POOL(lambda e: e.tensor_copy(out=W[:, :, N_IN:N_IN + 32], in_=W[:, :, 1816:1848]), r=[W], w=[W])
            sc.POOL(lambda e: e.tensor_copy(out=W[:, :, N_IN + 32:N_IN + 48], in_=W[:, :, 1832:1848]), r=[W], w=[W])
            sc.POOL(lambda e: e.tensor_copy(out=W[:, :, N_IN + 48:N_IN + 64], in_=W[:, :, 1816:1832]), r=[W], w=[W])
            xT = s.sb(es, "p1_xT", [128, 8, S], BF16)
            xTb = [Buf("xTb%d" % b) for b in range(NB)]
            for b in range(NB):
                sc.dma(xT[:, :, b * 512:(b + 1) * 512], s.xT.ap.rearrange("k p s -> p k s")[:, :, b * 512:(b + 1) * 512], w=[xTb[b]])
            PB = Ring([s.ps(es, "p1_pb%d" % i, [128, 512], F32) for i in range(6)])
            TP = Ring([s.ps(es, "p1_tp%d" % i, [128, 8, 128], BF16) for i in range(1)])
            fo = Ring([s.sb(es, "p1_fo%d" % i, [128, 512], BF16) for i in range(4)])
            fo32 = Ring([s.sb(es, "p1_fo32_%d" % i, [128, 512], F32) for i in range(2)])
            vt_s = Ring([s.sb(es, "p1_vts%d" % i, [128, 4, 2, 72], BF16) for i in range(2)])
            vt_w = Ring([s.sb(es, "p1_vtw%d" % i, [128, 4, 2, 72], BF16) for i in range(2)])
            vt_f = Ring([s.sb(es, "p1_vtf%d" % i, [128, 4, 4, 72], BF16) for i in range(2)])
            for rg in (vt_s, vt_w, vt_f):
                for t_ in rg.items:
                    sc.POOL(lambda e: e.memset(t_[:, :, :, :], 1.0), w=[t_])
            gt = Ring([s.sb(es, "p1_gt%d" % i, [128, 4, 24], F32) for i in range(2)])
            cn = Ring([s.sb(es, "p1_cn%d" % i, [128, 512], BF16) for i in range(2)])
            junk = s.sb(es, "p1_junk", [128, 512], BF16)
            ss = Ring([s.sb(es, "p1_ss%d" % i, [128, 4], F32) for i in range(2)])
            cT = Ring([s.sb(es, "p1_cT%d" % i, [128, 4, 512], BF16) for i in range(2)])
            evac_i = [0]

            def evac(dst_ap, src_ap, rb, wb):
                evac_i[0] += 1
                if evac_i[0] % 2 == 0:
                    sc.ACT(lambda e: e.copy(out=dst_ap, in_=src_ap), r=rb, w=wb)
                else:
                    sc.DVE(lambda e: e.tensor_copy(out=dst_ap, in_=src_ap), r=rb, w=wb)

            fm = []
            for h in range(8):
                fm.append((64 * h, 64, "bf", [(s.nqT.ap[h], 0, 64)]))
            fm.append((512, 128, "bf", [(s.nkcT.ap[0], 0, 64), (s.nkcT.ap[1], 64, 64)]))
            fm.append((640, 128, "bf", [(s.nvcT.ap[0], 0, 64), (s.nvcT.ap[1], 64, 64)]))
            fm.append((768, 128, "bf", [(s.nksT.ap[0], 0, 64), (s.nksT.ap[1], 64, 64)]))
            fm.append((1024, 128, "bf", [(s.nkwT.ap[0], 0, 64), (s.nkwT.ap[1], 64, 64)]))
            fm.append((N_IN, 32, "f32", [(s.krT.ap[0], 0, 32)]))
            fm.append((N_IN + 32, 32, "f32", [(s.krT.ap[1], 0, 32)]))
            for t in range(2):
                fm.append((1848 + 128 * t, 128, "bf", [(s.fqT.ap[2 * t, 0:64, :], 0, 64), (s.fqT.ap[2 * t + 1, 0:64, :], 64, 64)]))
                fm.append((2104 + 128 * t, 128, "bf", [(s.fkT.ap[2 * t, 0:64, :], 0, 64), (s.fkT.ap[2 * t + 1, 0:64, :], 64, 64)]))
            fm.append((2616, 4, "f32", [(s.ffT.ap, 0, 4)]))
            for b in range(NB):
                bs = slice(b * 512, (b + 1) * 512)
                import os
                P1M = int(os.environ.get('K_P1', '31'))
                for (c0, M, kind, dsts) in (fm if P1M & 1 else []):
                    pb = PB.next()
                    for kc in range(8):
                        sc.PE(lambda e: e.matmul(pb[0:M, :], lhsT=W[:, kc, c0:c0 + M], rhs=xT[:, kc, bs], start=(kc == 0), stop=(kc == 7)),
                              r=[W, xTb[b]], w=[pb])
                    o = fo.next() if kind == "bf" else fo32.next()
                    evac(o[0:M, :], pb[0:M, :], [pb], [o])
                    for (dap, p0, pn) in dsts:
                        sc.dma(dap[:, bs], o[p0:p0 + pn, :], r=[o], w=[Buf()], q="sp")
                vs_t = vt_s.next()
                vw_t = vt_w.next()
                vf_t = vt_f.next()
                g_t = gt.next()
                cT_t = cT.next()
                for tt in (range(4) if P1M & 2 else []):
                    t = b * 4 + tt
                    ts_ = slice(t * 128, (t + 1) * 128)
                    pa = PB.next()
                    for kc in range(8):
                        sc.PE(lambda e: e.matmul(pa[:, :], lhsT=xT[:, kc, ts_], rhs=W[:, kc, 1304:1816], start=(kc == 0), stop=(kc == 7)),
                              r=[W, xTb[b]], w=[pa])
                    s_ = ss.next()
                    sc.ACT(lambda e: e.activation(out=junk[:, 0:384], in_=pa[:, 0:384], func=AF.Square, scale=float(384 ** -0.5), accum_out=s_[:, 0:1]),
                           r=[pa], w=[junk, s_])
                    sc.ACT(lambda e: e.activation(out=junk[:, 384:512], in_=pa[:, 384:512], func=AF.Square, scale=float(128 ** -0.5), accum_out=s_[:, 1:2]),
                           r=[pa], w=[junk, s_])
                    sc.DVE(lambda e: e.tensor_scalar(out=s_[:, 0:2], in0=s_[:, 0:2], scalar1=1e-6, scalar2=None, op0=ALU.add), r=[s_], w=[s_])
                    sc.ACT(lambda e: e.activation(out=s_[:, 2:4], in_=s_[:, 0:2], func=AF.Sqrt), r=[s_], w=[s_])
                    sc.DVE(lambda e: e.reciprocal(out=s_[:, 0:2], in_=s_[:, 2:4]), r=[s_], w=[s_])
                    cn_t = cn.next()
                    sc.DVE(lambda e: e.tensor_scalar(out=cn_t[:, 0:384], in0=pa[:, 0:384], scalar1=s_[:, 0:1], scalar2=None, op0=ALU.mult), r=[pa, s_], w=[cn_t])
                    sc.DVE(lambda e: e.tensor_scalar(out=cn_t[:, 384:512], in0=pa[:, 384:512], scalar1=s_[:, 1:2], scalar2=None, op0=ALU.mult), r=[pa, s_], w=[cn_t])
                    tp = TP.next()
                    for j in range(4):
                        sc.PE(lambda e: e.transpose(out=tp[:, j, :], in_=cn_t[:, j * 128:(j + 1) * 128], identity=ident[:, :]), r=[cn_t, ident], w=[tp])
                    sc.ACT(lambda e: e.copy(out=cT_t[:, :, tt * 128:(tt + 1) * 128], in_=tp[:, 0:4, :]), r=[tp], w=[cT_t])
                    if P1M & 4:
                        pb1 = PB.next()
                        for (cc0, o0, n) in ((896, 0, 128), (1152, 128, 128), (2360, 256, 256)):
                            for kc in range(8):
                                sc.PE(lambda e: e.matmul(pb1[:, o0:o0 + n], lhsT=xT[:, kc, ts_], rhs=W[:, kc, cc0:cc0 + n], start=(kc == 0), stop=(kc == 7)),
                                      r=[W, xTb[b]], w=[pb1])
                        sc.ACT(lambda e: e.copy(out=vs_t[:, tt, :, 0:64], in_=pb1[:, 0:128].rearrange("p (g d) -> p g d", g=2)), r=[pb1], w=[vs_t])
                        sc.ACT(lambda e: e.copy(out=vw_t[:, tt, :, 0:64], in_=pb1[:, 128:256].rearrange("p (g d) -> p g d", g=2)), r=[pb1], w=[vw_t])
                        sc.ACT(lambda e: e.copy(out=vf_t[:, tt, :, 0:64], in_=pb1[:, 256:512].rearrange("p (g d) -> p g d", g=4)), r=[pb1], w=[vf_t])
                    if P1M & 8:
                        pb2 = PB.next()
                        for kc in range(8):
                            sc.PE(lambda e: e.matmul(pb2[:, 0:24], lhsT=xT[:, kc, ts_], rhs=W[:, kc, 1280:1304], start=(kc == 0), stop=(kc == 7)),
                                  r=[W, xTb[b]], w=[pb2])
                        sc.ACT(lambda e: e.activation(out=g_t[:, tt, :], in_=pb2[:, 0:24], func=(AF.Identity if os.environ.get("K_X") == "3" else AF.Sigmoid)), r=[pb2], w=[g_t])
                rows = slice(b * 512, (b + 1) * 512)
                if not (P1M & 16):
                    continue
                sc.dma(s.nvs.ap[rows].rearrange("(t p) g e -> p t g e", p=128), vs_t[:, :, :, :], r=[vs_t], w=[Buf()], q="sp")
                sc.dma(s.nvw.ap[rows].rearrange("(t p) g e -> p t g e", p=128), vw_t[:, :, :, :], r=[vw_t], w=[Buf()], q="sp")
                sc.dma(s.fv.ap[rows].rearrange("(t p) g e -> p t g e", p=128), vf_t[:, :, :, :], r=[vf_t], w=[Buf()], q="sp")
                sc.dma(s.gate.ap[rows].rearrange("(t p) c -> p t c", p=128), g_t[:, :, :], r=[g_t], w=[Buf()], q="sp")
                sc.dma(s.cnT.ap.rearrange("j p s -> p j s")[:, :, rows], cT_t[:, :, :], r=[cT_t], w=[Buf()], q="sp")
        sc.barrier()

    def phase_mla_prep(s, l):
        sc = s.sc
        S, NT, NB = s.S, s.NT, s.NB
        with ExitStack() as es:
            cnT = s.sb(es, "mp_cnT", [128, 4, S], BF16)
            cb = [Buf() for _ in range(NB)]
            for b in range(NB):
                sc.dma(cnT[:, :, b * 512:(b + 1) * 512], s.cnT.ap.rearrange("j p s -> p j s")[:, :, b * 512:(b + 1) * 512], w=[cb[b]])
            CS = s.sb(es, "mp_cs", [96, S], F32)
            SS = s.sb(es, "mp_ss", [96, S], F32)
            sc.dma(CS[:, :], s.c_cs.ap[:, :], w=[CS])
            sc.dma(SS[:, :], s.c_ss.ap[:, :], w=[SS])
            kr0 = s.sb(es, "mp_kr0", [32, S], F32)
            kr1 = s.sb(es, "mp_kr1", [32, S], F32)
            sc.dma(kr0[:, :], s.krT.ap[0], w=[kr0])
            sc.dma(kr1[:, :], s.krT.ap[1], w=[kr1])
            gq = s.sb(es, "mp_gq", [128, 4], F32)
            with s.nc.allow_non_contiguous_dma(reason="tiny gain vectors"):
                sc.dma(gq[:, 0:3], s.q_norm.ap[l].rearrange("(a p) -> p a", p=128), w=[gq])
                sc.dma(gq[:, 3:4], s.kv_norm.ap[l].rearrange("(a p) -> p a", p=128), w=[gq])
            stq = s.sb(es, "mp_stq", [128, 3, 384], F32)
            stk = s.sb(es, "mp_stk", [128, 512], F32)
            sc.dma(stq[:, :, :], s.w_uq.ap[l].rearrange("(a p) n -> p a n", p=128), w=[stq])
            sc.dma(stk[:, :], s.w_ukv.ap[l], w=[stk])
            Wq = s.sb(es, "mp_wq", [128, 3, 384], BF16)
            Wqp = s.sb(es, "mp_wqp", [128, 3, 4, 96], BF16)
            Wkv = s.sb(es, "mp_wkv", [128, 512], BF16)
            for a in range(3):
                sc.DVE(lambda e: e.tensor_scalar(out=Wq[:, a, :], in0=stq[:, a, :], scalar1=gq[:, a:a + 1], scalar2=None, op0=ALU.mult), r=[stq, gq], w=[Wq])
            sc.DVE(lambda e: e.tensor_scalar(out=Wkv[:, :], in0=stk[:, :], scalar1=gq[:, 3:4], scalar2=None, op0=ALU.mult), r=[stk, gq], w=[Wkv])
            sc.POOL(lambda e: e.memset(Wqp[:, :, :, :], 0.0), w=[Wqp])
            for h in range(4):
                sc.POOL(lambda e: e.tensor_copy(out=Wqp[:, :, h, 64:80], in_=Wq[:, :, 96 * h + 80:96 * h + 96]), r=[Wq], w=[Wqp])
                sc.POOL(lambda e: e.tensor_copy(out=Wqp[:, :, h, 80:96], in_=Wq[:, :, 96 * h + 64:96 * h + 80]), r=[Wq], w=[Wqp])
            PB = Ring([s.ps(es, "mp_pb%d" % i, [128, 512], F32) for i in range(6)])
            qo = Ring([s.sb(es, "mp_qo%d" % i, [96, 512], BF16) for i in range(3)])
            ko = Ring([s.sb(es, "mp_ko%d" % i, [64, 512], BF16) for i in range(3)])
            t1 = Ring([s.sb(es, "mp_t1_%d" % i, [96, 512], F32) for i in range(2)])
            t2 = Ring([s.sb(es, "mp_t2_%d" % i, [96, 512], F32) for i in range(2)])
            kro = Ring([s.sb(es, "mp_kro%d" % i, [32, 512], BF16) for i in range(2)])
            vt = Ring([s.sb(es, "mp_vt%d" % i, [128, 4, 4, 72], BF16) for i in range(2)])
            for t_ in vt.items:
                sc.POOL(lambda e: e.memset(t_[:, :, :, :], 1.0), w=[t_])
            for b in range(NB):
                bs = slice(b * 512, (b + 1) * 512)
                for h in range(4):
                    p1 = PB.next()
                    for a in range(3):
                        sc.PE(lambda e: e.matmul(p1[0:96, :], lhsT=Wq[:, a, 96 * h:96 * h + 96], rhs=cnT[:, a, bs], start=(a == 0), stop=(a == 2)), r=[Wq, cb[b]], w=[p1])
                    p2 = PB.next()
                    for a in range(3):
                        sc.PE(lambda e: e.matmul(p2[0:96, :], lhsT=Wqp[:, a, h, :], rhs=cnT[:, a, bs], start=(a == 0), stop=(a == 2)), r=[Wqp, cb[b]], w=[p2])
                    q_ = qo.next()
                    sc.ACT(lambda e: e.copy(out=q_[0:64, :], in_=p1[0:64, :]), r=[p1], w=[q_])
                    a1 = t1.next()
                    a2 = t2.next()
                    sc.DVE(lambda e: e.tensor_tensor(out=a1[64:96, :], in0=p1[64:96, :], in1=CS[64:96, bs], op=ALU.mult), r=[p1, CS], w=[a1])
                    sc.DVE(lambda e: e.tensor_tensor(out=a2[64:96, :], in0=p2[64:96, :], in1=SS[64:96, bs], op=ALU.mult), r=[p2, SS], w=[a2])
                    sc.POOL(lambda e: e.tensor_tensor(out=q_[64:96, :], in0=a1[64:96, :], in1=a2[64:96, :], op=ALU.add), r=[a1, a2], w=[q_])
                    sc.dma(s.mqT.ap[h, :, bs], q_[:, :], r=[q_], w=[Buf()], q="sp")
                    p3 = PB.next()
                    sc.PE(lambda e: e.matmul(p3[0:64, :], lhsT=Wkv[:, 128 * h:128 * h + 64], rhs=cnT[:, 3, bs], start=True, stop=True), r=[Wkv, cb[b]], w=[p3])
                    k_ = ko.next()
                    sc.ACT(lambda e: e.copy(out=k_[:, :], in_=p3[0:64, :]), r=[p3], w=[k_])
                    sc.dma(s.mkT.ap[h, 0:64, bs], k_[:, :], r=[k_], w=[Buf()], q="sp")
                a1 = t1.next()
                a2 = t2.next()
                kr_ = kro.next()
                sc.DVE(lambda e: e.tensor_tensor(out=a1[0:32, :], in0=kr0[:, bs], in1=CS[0:32, bs], op=ALU.mult), r=[kr0, CS], w=[a1])
                sc.DVE(lambda e: e.tensor_tensor(out=a2[0:32, :], in0=kr1[:, bs], in1=SS[0:32, bs], op=ALU.mult), r=[kr1, SS], w=[a2])
                sc.POOL(lambda e: e.tensor_tensor(out=kr_[:, :], in0=a1[0:32, :], in1=a2[0:32, :], op=ALU.add), r=[a1, a2], w=[kr_])
                for h in range(4):
                    sc.dma(s.mkT.ap[h, 64:96, bs], kr_[:, :], r=[kr_], w=[Buf()], q="sp")
                v_ = vt.next()
                for tt in range(4):
                    t = b * 4 + tt
                    p4 = PB.next()
                    for h in range(4):
                        sc.PE(lambda e: e.matmul(p4[:, 64 * h:64 * h + 64], lhsT=cnT[:, 3, t * 128:(t + 1) * 128], rhs=Wkv[:, 128 * h + 64:128 * h + 128], start=True, stop=True),
                              r=[Wkv, cb[b]], w=[p4])
                    sc.ACT(lambda e: e.copy(out=v_[:, tt, :, 0:64], in_=p4[:, 0:256].rearrange("p (g d) -> p g d", g=4)), r=[p4], w=[v_])
                sc.dma(s.mv.ap[bs].rearrange("(t p) g e -> p t g e", p=128), v_[:, :, :, :], r=[v_], w=[Buf()], q="sp")
        sc.barrier()

    def phase_fox_prep(s, l):
        sc = s.sc
        S = s.S
        with ExitStack() as es:
            ff = s.sb(es, "fp_ff", [4, S], F32)
            sc.dma(ff[:, :], s.ffT.ap[:, :], w=[ff])
            bf = s.sb(es, "fp_bf", [4, 2], F32)
            with s.nc.allow_non_contiguous_dma(reason="tiny"):
                sc.dma(bf[:, 0:1], s.b_f.ap[l].rearrange("(p a) -> p a", a=1), w=[bf])
            sc.DVE(lambda e: e.tensor_scalar(out=bf[:, 1:2], in0=bf[:, 0:1], scalar1=-1.0, scalar2=None, op0=ALU.mult), r=[bf], w=[bf])
            ex = s.sb(es, "fp_ex", [4, S], F32)
            sc.ACT(lambda e: e.activation(out=ex[:, :], in_=ff[:, :], func=AF.Exp, bias=bf[:, 1:2], scale=-1.0), r=[ff, bf], w=[ex])
            one = s.sb(es, "fp_one", [4, 1], F32)
            sc.DVE(lambda e: e.memset(one[:, :], 1.0), w=[one])
            sc.ACT(lambda e: e.activation(out=ex[:, :], in_=ex[:, :], func=AF.Ln, bias=one[:, 0:1], scale=1.0), r=[ex, one], w=[ex])
            ones = s.sb(es, "fp_ones", [4, S], F32)
            sc.POOL(lambda e: e.memset(ones[:, :], 1.0), w=[ones])
            sc.DVE(lambda e: e.tensor_scalar(out=ex[:, :], in0=ex[:, :], scalar1=-8.0, scalar2=None, op0=ALU.mult), r=[ex], w=[ex])
            cum = s.sb(es, "fp_cum", [4, S], F32)
            sc.DVE(lambda e: e.tensor_tensor_scan(out=cum[:, :], data0=ones[:, :], data1=ex[:, :], initial=0.0, op0=ALU.mult, op1=ALU.add), r=[ones, ex], w=[cum])
            pcs = [s.sb(es, "fp_pc%d" % i, [4, S], BF16) for i in range(3)]
            ngs = [s.sb(es, "fp_ng%d" % i, [4, S], BF16) for i in range(3)]
            rem = s.sb(es, "fp_rem", [4, S], F32)
            src = cum
            for i in range(3):
                sc.DVE(lambda e: e.tensor_copy(out=pcs[i][:, :], in_=src[:, :]), r=[src], w=[pcs[i]])
                sc.DVE(lambda e: e.tensor_scalar(out=ngs[i][:, :], in0=pcs[i][:, :], scalar1=-1.0, scalar2=None, op0=ALU.mult), r=[pcs[i]], w=[ngs[i]])
                if i < 2:
                    sc.DVE(lambda e: e.tensor_tensor(out=rem[:, :], in0=src[:, :], in1=pcs[i][:, :], op=ALU.subtract), r=[src, pcs[i]], w=[rem])
                    src = rem
            onb = s.sb(es, "fp_onb", [4, S], BF16)
            sc.POOL(lambda e: e.memset(onb[:, :], 1.0), w=[onb])
            for i in range(3):
                sc.dma(s.fqT.ap[:, 64 + i, :], pcs[i][:, :], r=[pcs[i]], w=[Buf()], q="sp")
                sc.dma(s.fqT.ap[:, 67 + i, :], onb[:, :], r=[onb], w=[Buf()], q="sp")
                sc.dma(s.fkT.ap[:, 64 + i, :], onb[:, :], r=[onb], w=[Buf()], q="sp")
                sc.dma(s.fkT.ap[:, 67 + i, :], ngs[i][:, :], r=[ngs[i]], w=[Buf()], q="sp")
        sc.barrier()

    def run_steps(s, steps, sT_ring, pT_ring, skew=1):
        sc = s.sc
        n = len(steps)
        state = [None] * n

        def emit_score(i):
            st = steps[i]
            sT = sT_ring.next()
            pT = pT_ring.next()
            c0, nco, kk = st["c0"], st["ncol"], st["kk"]
            sc.PE(lambda e: e.matmul(sT[0:kk, c0:nco], lhsT=st["kT"], rhs=st["q_fn"](c0, nco), start=True, stop=True),
                  r=st["rb_score"], w=[sT])
            sc.ACT(lambda e: e.activation(out=pT[0:kk, c0:nco], in_=sT[0:kk, c0:nco], func=AF.Exp, scale=float(st["scale"])), r=[sT], w=[pT])
            mi = 0
            for (m0, m1, map_, mb, clamp) in st["masks"]:
                mi += 1
                if clamp:
                    sc.DVE(lambda e: e.scalar_tensor_tensor(out=pT[0:kk, m0:m1], in0=pT[0:kk, m0:m1], scalar=1e30, in1=map_, op0=ALU.min, op1=ALU.mult),
                           r=[pT] + mb, w=[pT])
                elif st.get("mask_pool", False) and mi % 2 == 0:
                    sc.POOL(lambda e: e.tensor_tensor(out=pT[0:kk, m0:m1], in0=pT[0:kk, m0:m1], in1=map_, op=ALU.mult), r=[pT] + mb, w=[pT])
                else:
                    sc.DVE(lambda e: e.tensor_tensor(out=pT[0:kk, m0:m1], in0=pT[0:kk, m0:m1], in1=map_, op=ALU.mult), r=[pT] + mb, w=[pT])
            state[i] = pT

        def emit_pv(i):
            st = steps[i]
            pT = state[i]
            kk = st["kk"]
            first = st["first"]
            for sub in st["subs"]:
                out_ap, accb = st["acc_fn"](sub)
                stt = bool(first and (sub in st.get("start_subs", (st["subs"][0],))))
                sc.PE(lambda e: e.matmul(out_ap, lhsT=pT[0:kk, sub * 128:(sub + 1) * 128], rhs=st["V"], start=stt, stop=True, skip_group_check=True),
                      r=[pT] + st["rb_v"], w=[accb])
            if st["fin"] is not None:
                st["fin"]()

        nxt_pv = 0
        pre_ptr = [0]

        def do_pre(upto):
            while pre_ptr[0] < n and pre_ptr[0] <= upto:
                pf = steps[pre_ptr[0]].get("pre")
                if pf is not None:
                    pf()
                pre_ptr[0] += 1
        for i in range(n + skew):
            if i < n:
                do_pre(i + 3)
                dep = steps[i].get("dep_step")
                while dep is not None and nxt_pv <= dep:
                    emit_pv(nxt_pv)
                    nxt_pv += 1
                emit_score(i)
            if i >= skew and nxt_pv <= i - skew:
                emit_pv(nxt_pv)
                nxt_pv += 1
        while nxt_pv < n:
            emit_pv(nxt_pv)
            nxt_pv += 1

    def phase_nsa(s, l):
        sc = s.sc
        S, NT = s.S, s.NT
        CR, CT, NCP, NCMP = s.CR, s.CT, s.NCP, s.NCMP
        with ExitStack() as es:
            ident = s.sb(es, "ident", [128, 128], BF16)
            sc.dma(ident[:, :], s.c_ident.ap[:, :], w=[ident])
            QA = [s.sb(es, "ns_qa%d" % g, [128, NT, 4, 128], BF16) for g in range(2)]
            qa_lo = [Buf() for g in range(2)]
            qa_hi = [[Buf() for i in range(NT)] for g in range(2)]
            ksA = [s.sb(es, "ns_ks%d" % g, [128, S], BF16) for g in range(2)]
            kwT = [s.sb(es, "ns_kw%d" % g, [64, S], BF16) for g in range(2)]
            kcT = [s.sb(es, "ns_kc%d" % g, [64, CT * CR], BF16) for g in range(2)]
            vs = [s.sb(es, "ns_vs%d" % g, [128, NT, 72], BF16) for g in range(2)]
            vw = [s.sb(es, "ns_vw%d" % g, [128, NT, 72], BF16) for g in range(2)]
            vcA = [s.sb(es, "ns_vc%d" % g, [128, CT, 136], BF16) for g in range(2)]
            G = s.sb(es, "ns_gate", [128, NT, 24], F32)
            AFc = s.sb(es, "ns_af", [128, NT, 64], F32)
            EM = s.sb(es, "ns_em", [128, 2, 8, 128], BF16)
            EMW = s.sb(es, "ns_emw", [128, 4, 128], BF16)
            for g in range(2):
                for hh in range(4):
                    sc.dma(QA[g][0:64, :, hh, :], s.nqT.ap[4 * g + hh].rearrange("d (t q) -> d t q", q=128), w=[qa_lo[g]])
                sc.dma(ksA[g][0:64, :], s.nksT.ap[g], w=[ksA[g]])
                sc.dma(ksA[g][64:128, :], s.c_expand.ap[:, :], w=[ksA[g]])
                sc.dma(kwT[g][:, :], s.nkwT.ap[g], w=[kwT[g]])
                sc.dma(vs[g][:, :, :], s.nvs.ap[:, g, :].rearrange("(t p) e -> p t e", p=128), w=[vs[g]])
                sc.dma(vw[g][:, :, :], s.nvw.ap[:, g, :].rearrange("(t p) e -> p t e", p=128), w=[vw[g]])
            sc.dma(G[:, :, :], s.gate.ap.rearrange("(t p) c -> p t c", p=128), w=[G])
            sc.dma(AFc[:, :, :], s.c_aforced.ap.rearrange("(t p) c -> p t c", p=128), w=[AFc])
            for i in range(2):
                sc.dma(EM[:, i, :, :], s.em.ap[i], w=[EM])
            sc.dma(EMW[:, :, :], s.c_atri4.ap[:, :, :], w=[EMW])
            sT_ring = Ring([s.ps(es, "ns_sT%d" % i, [128, 512], F32) for i in range(3)])
            accC = [s.ps(es, "ns_accC%d" % i, [128, 2, 256], F32) for i in range(2)]
            accR = Ring([s.ps(es, "ns_accR%d" % i, [128, 4, 128], F32) for i in range(2)])
            TPb = s.ps(es, "ns_tp", [128, 1024], BF16)
            with ExitStack() as es2:
                W1 = Ring([s.sb(es2, "nc_w1_%d" % i, [64, 32, 128], BF16) for i in range(2)])
                stg = Ring([s.sb(es2, "nc_stg%d" % i, [64, 8, 128], F32) for i in range(2)])
                src = Ring([s.sb(es2, "nc_src%d" % i, [64, S], BF16) for i in range(2)])
                peT = Ring([s.sb(es2, "nc_peT%d" % i, [64, 32], F32) for i in range(2)])
                peTb = Ring([s.sb(es2, "nc_peTb%d" % i, [64, 32], BF16) for i in range(2)])
                W2s = Ring([s.sb(es2, "nc_w2s%d" % i, [128, 64], F32) for i in range(2)])
                W2 = Ring([s.sb(es2, "nc_w2_%d" % i, [128, 64], BF16) for i in range(2)])
                hb = Ring([s.sb(es2, "nc_hb%d" % i, [128, 2], F32) for i in range(2)])
                u = Ring([s.sb(es2, "nc_u%d" % i, [128, 256], F32) for i in range(2)])
                u2 = Ring([s.sb(es2, "nc_u2%d" % i, [128, 256], F32) for i in range(2)])
                hT = Ring([s.sb(es2, "nc_hT%d" % i, [128, 256], BF16) for i in range(2)])
                for g in range(2):
                    sc.POOL(lambda e: e.memset(vcA[g][:, :, :], 0.0), w=[vcA[g]])
                    sc.POOL(lambda e: e.memset(kcT[g][:, :], 0.0), w=[kcT[g]])
                for kv in range(2):
                    w1 = W1.next()
                    w1src = s.cmp_w1.ap[l, kv].rearrange("(a p) n -> p a n", p=64)
                    for hf in range(4):
                        st_ = stg.next()
                        sc.dma(st_[:, :, :], w1src[:, hf * 8:(hf + 1) * 8, :], w=[st_])
                        sc.POOL(lambda e: e.tensor_copy(out=w1[:, hf * 8:(hf + 1) * 8, :], in_=st_[:, :, :]), r=[st_], w=[w1])
                    pT_ = peT.next()
                    with s.nc.allow_non_contiguous_dma(reason="tiny pe transpose"):
                        sc.dma(pT_[:, :], s.cmp_pe.ap[l, kv].rearrange("l d -> d l"), w=[pT_])
                    pTb_ = peTb.next()
                    sc.DVE(lambda e: e.tensor_copy(out=pTb_[:, :], in_=pT_[:, :]), r=[pT_], w=[pTb_])
                    w2s = W2s.next()
                    sc.dma(w2s[:, :], s.cmp_w2.ap[l, kv], w=[w2s])
                    w2 = W2.next()
                    sc.DVE(lambda e: e.tensor_copy(out=w2[:, :], in_=w2s[:, :]), r=[w2s], w=[w2])
                    pbias = sT_ring.next()
                    for ll in range(32):
                        sc.PE(lambda e: e.matmul(pbias[:, 0:1], lhsT=w1[:, ll, :], rhs=pTb_[:, ll:ll + 1], start=(ll == 0), stop=(ll == 31)), r=[w1, pTb_], w=[pbias])
                    hb_ = hb.next()
                    sc.DVE(lambda e: e.tensor_copy(out=hb_[:, 0:1], in_=pbias[:, 0:1]), r=[pbias], w=[hb_])
                    for g in range(2):
                        sr = src.next()
                        sc.dma(sr[:, :], (s.nkcT if kv == 0 else s.nvcT).ap[g], w=[sr])
                        ph = sT_ring.next()
                        for ll in range(32):
                            sc.PE(lambda e: e.matmul(ph[:, 0:NCMP], lhsT=w1[:, ll, :], rhs=sr[:, ll:ll + 16 * (NCMP - 1) + 1:16], start=(ll == 0), stop=(ll == 31)),
                                  r=[w1, sr], w=[ph])
                        u_ = u.next()
                        u2_ = u2.next()
                        h_ = hT.next()
                        n_ = NCMP
                        sc.ACT(lambda e: e.activation(out=u_[:, 0:n_], in_=ph[:, 0:n_], func=AF.Identity, bias=hb_[:, 0:1], scale=1.0), r=[ph, hb_], w=[u_])
                        sc.DVE(lambda e: e.tensor_tensor(out=u2_[:, 0:n_], in0=u_[:, 0:n_], in1=u_[:, 0:n_], op=ALU.mult), r=[u_], w=[u2_])
                        sc.DVE(lambda e: e.tensor_scalar(out=u2_[:, 0:n_], in0=u2_[:, 0:n_], scalar1=0.044715, scalar2=1.0, op0=ALU.mult, op1=ALU.add), r=[u2_], w=[u2_])
                        sc.DVE(lambda e: e.tensor_tensor(out=u2_[:, 0:n_], in0=u2_[:, 0:n_], in1=u_[:, 0:n_], op=ALU.mult), r=[u2_, u_], w=[u2_])
                        sc.ACT(lambda e: e.activation(out=u2_[:, 0:n_], in_=u2_[:, 0:n_], func=AF.Tanh, scale=0.7978845608028654), r=[u2_], w=[u2_])
                        sc.DVE(lambda e: e.tensor_scalar(out=u2_[:, 0:n_], in0=u2_[:, 0:n_], scalar1=1.0, scalar2=0.5, op0=ALU.add, op1=ALU.mult), r=[u2_], w=[u2_])
                        sc.DVE(lambda e: e.tensor_tensor(out=h_[:, 0:n_], in0=u2_[:, 0:n_], in1=u_[:, 0:n_], op=ALU.mult), r=[u2_, u_], w=[h_])
                        po = sT_ring.next()
                        if kv == 0:
                            sc.PE(lambda e: e.matmul(po[0:64, 0:n_], lhsT=w2[:, :], rhs=h_[:, 0:n_], start=True, stop=True), r=[w2, h_], w=[po])
                            sc.DVE(lambda e: e.tensor_copy(out=kcT[g][:, 0:n_], in_=po[0:64, 0:n_]), r=[po], w=[kcT[g]])
                        else:
                            for ct in range(CT):
                                rows = min(CR, n_ - ct * CR)
                                sc.PE(lambda e: e.matmul(po[0:rows, ct * 64:(ct + 1) * 64], lhsT=h_[:, ct * CR:ct * CR + rows], rhs=w2[:, :], start=True, stop=True), r=[w2, h_], w=[po])
                                sc.DVE(lambda e: e.tensor_copy(out=vcA[g][0:rows, ct, 0:64], in_=po[0:rows, ct * 64:(ct + 1) * 64]), r=[po], w=[vcA[g]])
                for g in range(2):
                    sc.POOL(lambda e: e.memset(vcA[g][:, :, 64:66], 1.0), w=[vcA[g]])
                    sc.dma(vcA[g][0:CR, :, 65:129], s.c_overlap.ap.rearrange("(t p) n -> p t n", p=CR), w=[vcA[g]])
            with ExitStack() as es3:
                pT_ring = Ring([s.sb(es3, "ns_pT%d" % i, [128, 512], BF16) for i in range(5)])
                emc_ring = Ring([s.sb(es3, "ns_emc%d" % i, [128, 4, 128], BF16) for i in range(4)])
                Oacc = Ring([s.sb(es3, "ns_O%d" % i, [128, 256], F32) for i in range(3)])
                Obf = Ring([s.sb(es3, "ns_Ob%d" % i, [128, 256], BF16) for i in range(3)])
                sm = Ring([s.sb(es3, "ns_sm%d" % i, [128, 32], F32) for i in range(4)])
                imp = Ring([s.sb(es3, "ns_imp%d" % i, [128, 64], F32) for i in range(2)])
                NS = Ring([s.sb(es3, "ns_NS%d" % i, [128, 128], BF16) for i in range(2)])
                for t_ in NS.items:
                    sc.POOL(lambda e: e.memset(t_[:, :], 0.0), w=[t_])
                steps = []
                for i in range(NT):
                    for g in range(2):
                        O = Oacc.next()
                        Ob = Obf.next()

                        def qfn_lo(i=i, g=g):
                            return lambda c0, c1: QA[g][0:64, i, :, :].rearrange("p h q -> p (h q)")[:, c0:c1]

                        def qfn_full(i=i, g=g):
                            return lambda c0, c1: QA[g][:, i, :, :].rearrange("p h q -> p (h q)")[:, c0:c1]

                        cmax = min(8 * i + 6, NCMP - 1)
                        nct = cmax // CR + 1
                        for ct in range(nct):
                            emc_t = emc_ring.next()
                            r0 = ct * CR - 8 * i + s.OFF
                            sc_dma_args = (emc_t, r0, g)

                            def pre(emc_t=emc_t, r0=r0, g=g):
                                sc.dma(emc_t[0:CR, :, :], s.emc.ap[r0:r0 + CR, 4 * g:4 * g + 4, :], w=[emc_t])
                            fin = None
                            if ct == nct - 1:
                                def fin(i=i, g=g, O=O):
                                    s.nsa_fin_cmp(i, g, O, accC, G, AFc, sm, imp, NS, TPb, ident, QA, qa_hi)
                            steps.append(dict(pre=pre, kT=kcT[g][:, ct * CR:(ct + 1) * CR], q_fn=qfn_lo(), kk=CR, ncol=512, c0=0, scale=0.125,
                                              masks=[(0, 512, emc_t[0:CR, :, :].rearrange("p h q -> p (h q)"), [emc_t], False)],
                                              V=vcA[g][0:CR, ct, 0:129], subs=[0, 1, 2, 3],
                                              acc_fn=(lambda sub: (accC[sub // 2][:, sub % 2, 0:129], accC[sub // 2])),
                                              first=(ct == 0), start_subs=(0, 2), fin=fin, rb_score=[kcT[g], qa_lo[g]], rb_v=[vcA[g]], mask_pool=False))
                        last_cmp_idx = len(steps) - 1
                        accW = accR.next()
                        js = list(range(max(0, i - 4), i + 1))
                        for j in js:
                            masks = []
                            if j == i:
                                masks.append((0, 512, EM[:, 0, 4 * g:4 * g + 4, :].rearrange("p h q -> p (h q)"), [EM], False))
                            elif j == i - 1:
                                masks.append((0, 512, EM[:, 1, 4 * g:4 * g + 4, :].rearrange("p h q -> p (h q)"), [EM], False))
                            elif j == i - 4:
                                masks.append((0, 512, EMW[:, :, :].rearrange("p h q -> p (h q)"), [EMW], False))
                            fin = None
                            if j == i:
                                def fin(i=i, g=g, O=O, accW=accW):
                                    s.nsa_fin_branch(i, g, O, None, accW, G, sm, 2)
                            steps.append(dict(pre=None, kT=kwT[g][:, j * 128:(j + 1) * 128], q_fn=qfn_lo(), kk=128, ncol=512, c0=0, scale=0.125, masks=masks,
                                              V=vw[g][:, j, 0:65], subs=[0, 1, 2, 3], acc_fn=(lambda sub, accW=accW: (accW[:, sub, 0:65], accW)),
                                              first=(j == js[0]), fin=fin, rb_score=[kwT[g], qa_lo[g]], rb_v=[vw[g]], mask_pool=True))
                        accS = accR.next()
                        for j in range(0, i + 1):
                            masks = []
                            if j == i:
                                masks.append((0, 512, EM[:, 0, 4 * g:4 * g + 4, :].rearrange("p h q -> p (h q)"), [EM], False))
                            elif j == i - 1:
                                masks.append((0, 512, EM[:, 1, 4 * g:4 * g + 4, :].rearrange("p h q -> p (h q)"), [EM], False))
                            fin = None
                            if j == i:
                                def fin(i=i, g=g, O=O, Ob=Ob, accS=accS):
                                    s.nsa_fin_branch(i, g, O, Ob, accS, G, sm, 1)
                            steps.append(dict(pre=None, kT=ksA[g][:, j * 128:(j + 1) * 128], q_fn=qfn_full(), kk=128, ncol=512, c0=0, scale=0.125, masks=masks,
                                              V=vs[g][:, j, 0:65], subs=[0, 1, 2, 3], acc_fn=(lambda sub, accS=accS: (accS[:, sub, 0:65], accS)),
                                              first=(j == 0), fin=fin, rb_score=[ksA[g], qa_lo[g], qa_hi[g][i]], rb_v=[vs[g]], mask_pool=True, dep_step=last_cmp_idx))
                for st in steps:
                    if st["pre"] is not None:
                        pass
                s.run_steps_pre(steps, sT_ring, pT_ring, skew=2)
        sc.barrier()

    def run_steps_pre(s, steps, sT_ring, pT_ring, skew=1):
        s.run_steps(steps, sT_ring, pT_ring, skew=skew)

    def nsa_fin_cmp(s, i, g, O, accC, G, AFc, sm, imp, NS, TPb, ident, QA, qa_hi):
        sc = s.sc
        m = sm.next()
        for bk in range(2):
            sc.DVE(lambda e: e.tensor_scalar(out=m[:, 2 * bk:2 * bk + 2], in0=accC[bk][:, :, 64:65], scalar1=1e-30, scalar2=None, op0=ALU.max),
                   r=[accC[bk]], w=[m])
        sc.DVE(lambda e: e.reciprocal(out=m[:, 4:8], in_=m[:, 0:4]), r=[m], w=[m])
        sc.DVE(lambda e: e.tensor_tensor(out=m[:, 8:12], in0=m[:, 4:8], in1=G[:, i, 12 * g + 0:12 * g + 12:3], op=ALU.mult), r=[m, G], w=[m])
        im = imp.next()
        for hh in range(4):
            U = accC[hh // 2][:, hh % 2, 65:129]
            if hh == 0:
                sc.DVE(lambda e: e.tensor_scalar(out=im[:, :], in0=U, scalar1=m[:, 4:5], scalar2=None, op0=ALU.mult), r=[accC[0], m], w=[im])
            else:
                sc.DVE(lambda e: e.scalar_tensor_tensor(out=im[:, :], in0=U, scalar=m[:, 4 + hh:5 + hh], in1=im[:, :], op0=ALU.mult, op1=ALU.add),
                       r=[accC[hh // 2], m, im], w=[im])
        sc.DVE(lambda e: e.tensor_tensor(out=im[:, :], in0=im[:, :], in1=AFc[:, i, :], op=ALU.add), r=[im, AFc], w=[im])
        sc.DVE(lambda e: e.max(out=m[:, 16:24], in_=im[:, :]), r=[im], w=[m])
        ns = NS.next()
        sc.DVE(lambda e: e.tensor_scalar(out=ns[:, 64:128], in0=im[:, :], scalar1=m[:, 23:24], scalar2=1.0, op0=ALU.is_ge, op1=ALU.subtract), r=[im, m], w=[ns])
        sc.PE(lambda e: e.transpose(out=TPb[:, 0:128], in_=ns[:, :], identity=ident[:, :]), r=[ns, ident], w=[TPb])
        sc.DVE(lambda e: e.tensor_copy(out=QA[g][64:128, i, 0, :], in_=TPb[64:128, 0:128]), r=[TPb], w=[qa_hi[g][i]])
        for hh in range(1, 4):
            sc.POOL(lambda e: e.tensor_copy(out=QA[g][64:128, i, hh, :], in_=QA[g][64:128, i, 0, :]), r=[qa_hi[g][i]], w=[qa_hi[g][i]])
        for hh in range(4):
            num = accC[hh // 2][:, hh % 2, 0:64]
            sc.DVE(lambda e: e.tensor_scalar(out=O[:, hh * 64:(hh + 1) * 64], in0=num, scalar1=m[:, 8 + hh:9 + hh], scalar2=None, op0=ALU.mult), r=[accC[hh // 2], m], w=[O])

    def nsa_fin_branch(s, i, g, O, Ob, acc, G, sm, br):
        sc = s.sc
        m = sm.next()
        sc.DVE(lambda e: e.tensor_scalar(out=m[:, 0:4], in0=acc[:, :, 64:65], scalar1=1e-30, scalar2=None, op0=ALU.max), r=[acc], w=[m])
        sc.DVE(lambda e: e.reciprocal(out=m[:, 4:8], in_=m[:, 0:4]), r=[m], w=[m])
        sc.DVE(lambda e: e.tensor_tensor(out=m[:, 8:12], in0=m[:, 4:8], in1=G[:, i, 12 * g + br:12 * g + 12:3], op=ALU.mult), r=[m, G], w=[m])
        dst = O if Ob is None else Ob
        for hh in range(4):
            sc.DVE(lambda e: e.scalar_tensor_tensor(out=dst[:, hh * 64:(hh + 1) * 64], in0=acc[:, hh, 0:64], scalar=m[:, 8 + hh:9 + hh], in1=O[:, hh * 64:(hh + 1) * 64],
                                                    op0=ALU.mult, op1=ALU.add), r=[acc, m, O], w=[dst])
        if Ob is not None:
            sc.dma(s.o_tm.ap[i * 128:(i + 1) * 128, 256 * g:256 * g + 256], Ob[:, :], r=[Ob], w=[Buf()], q="pool")

    def phase_causal(s, l, kind):
        sc = s.sc
        S, NT, NB = s.S, s.NT, s.NB
        if kind == "mla":
            K, qd, kd, vd, scale, ocol, clamp = 96, s.mqT, s.mkT, s.mv, 96 ** -0.5, 512, False
        else:
            K, qd, kd, vd, scale, ocol, clamp = 70, s.fqT, s.fkT, s.fv, 0.125, 768, True
        with ExitStack() as es:
            tri = s.sb(es, "ca_tri", [128, 128], BF16)
            sc.dma(tri[:, :], s.c_tri.ap[:, :], w=[tri])
            qT = [s.sb(es, "ca_q%d" % h, [K, S], BF16) for h in range(4)]
            kT = [s.sb(es, "ca_k%d" % h, [K, S], BF16) for h in range(4)]
            V = [s.sb(es, "ca_v%d" % h, [128, NT, 72], BF16) for h in range(4)]
            for h in range(4):
                sc.dma(qT[h][:, :], qd.ap[h], w=[qT[h]])
                sc.dma(kT[h][:, :], kd.ap[h], w=[kT[h]])
                sc.dma(V[h][:, :, :], vd.ap[:, h, :].rearrange("(t p) e -> p t e", p=128), w=[V[h]])
            sT_ring = Ring([s.ps(es, "ca_sT%d" % i, [128, 512], F32) for i in range(4)])
            accR = Ring([s.ps(es, "ca_acc%d" % i, [128, 4, 128], F32) for i in range(3)])
            pT_ring = Ring([s.sb(es, "ca_pT%d" % i, [128, 512], BF16) for i in range(6)])
            osb = Ring([s.sb(es, "ca_o%d" % i, [128, 4, 256], BF16) for i in range(2)])
            sm = Ring([s.sb(es, "ca_sm%d" % i, [128, 8], F32) for i in range(3)])
            steps = []
            for qb in range(NB):
                o_ = osb.next()
                for h in range(4):
                    acc = accR.next()
                    nk = 4 * qb + 4
                    for j in range(nk):
                        sp = j - 4 * qb
                        c0 = max(0, sp) * 128
                        masks = []
                        if sp >= 0:
                            masks.append((c0, c0 + 128, tri[:, :], [tri], clamp))
                        subs = list(range(max(0, sp), 4))
                        fin = None
                        if j == nk - 1:
                            def fin(qb=qb, h=h, acc=acc, o_=o_):
                                m = sm.next()
                                sc.DVE(lambda e: e.tensor_scalar(out=m[:, 0:4], in0=acc[:, :, 64:65], scalar1=1e-30, scalar2=None, op0=ALU.max), r=[acc], w=[m])
                                sc.DVE(lambda e: e.reciprocal(out=m[:, 4:8], in_=m[:, 0:4]), r=[m], w=[m])
                                for sub in range(4):
                                    sc.DVE(lambda e: e.tensor_scalar(out=o_[:, sub, 64 * h:64 * h + 64], in0=acc[:, sub, 0:64], scalar1=m[:, 4 + sub:5 + sub], scalar2=None, op0=ALU.mult),
                                           r=[acc, m], w=[o_])
                                if h == 3:
                                    sc.dma(s.o_tm.ap[qb * 512:(qb + 1) * 512, ocol:ocol + 256].rearrange("(s p) c -> p s c", p=128), o_[:, :, :], r=[o_], w=[Buf()], q="sp")
                        steps.append(dict(kT=kT[h][:, j * 128:(j + 1) * 128], q_fn=(lambda c0, c1, h=h, qb=qb: qT[h][:, qb * 512 + c0:qb * 512 + c1]),
                                          kk=128, ncol=512, c0=c0, scale=scale, masks=masks, V=V[h][:, j, 0:65], subs=subs,
                                          acc_fn=(lambda sub, acc=acc: (acc[:, sub, 0:65], acc)), first=(j == 0), fin=fin,
                                          rb_score=[kT[h], qT[h]], rb_v=[V[h]], mask_pool=False))
            s.run_steps(steps, sT_ring, pT_ring, skew=3)
        sc.barrier()

    def phase_combine(s, l, xsrc, xdst):
        sc = s.sc
        S, NT, NB = s.S, s.NT, s.NB
        with ExitStack() as es:
            ident = s.sb(es, "ident", [128, 128], BF16)
            sc.dma(ident[:, :], s.c_ident.ap[:, :], w=[ident])
            stg = s.stages(es)
            Wg = s.sb(es, "cb_wg", [128, 8, 3 * D], BF16)
            Wb = s.sb(es, "cb_wb", [128, 8, D], BF16)
            Wo = s.sb(es, "cb_wo", [128, 8, D], BF16)
            s.load_w(Wg, s.w_gate.ap[l], 128, 8, 3 * D, stg)
            s.load_w(TTv(Wb, 0, 4), s.w_br_nsa.ap[l], 128, 4, D, stg)
            s.load_w(TTv(Wb, 4, 2), s.w_br_mla.ap[l], 128, 2, D, stg)
            s.load_w(TTv(Wb, 6, 2), s.w_br_fox.ap[l], 128, 2, D, stg)
            s.load_w(Wo, s.w_mix_out.ap[l], 128, 8, D, stg)
            R = s.ln_setup(es, l, 0)
            PG = Ring([s.ps(es, "cb_pg%d" % i, [128, 512], F32) for i in range(2)])
            PP = Ring([s.ps(es, "cb_pp%d" % i, [128, 512], F32) for i in range(2)])
            PY = Ring([s.ps(es, "cb_py%d" % i, [128, 1024], F32) for i in range(1)])
            TPr = Ring([s.ps(es, "cb_tp%d" % i, [128, 8, 128], BF16) for i in range(2)])
            otm = Ring([s.sb(es, "cb_otm%d" % i, [128, D], BF16) for i in range(2)])
            oT = Ring([s.sb(es, "cb_oT%d" % i, [128, 8, 512], BF16) for i in range(2)])
            xTb = Ring([s.sb(es, "cb_xT%d" % i, [128, 8, 512], BF16) for i in range(2)])
            mT = Ring([s.sb(es, "cb_mT%d" % i, [128, 8, 512], BF16) for i in range(1)])
            sg = Ring([s.sb(es, "cb_sg%d" % i, [128, 512], F32) for i in range(2)])
            tA = Ring([s.sb(es, "cb_tA%d" % i, [128, 512], F32) for i in range(2)])
            tB = Ring([s.sb(es, "cb_tB%d" % i, [128, 512], F32) for i in range(2)])
            for b in range(NB):
                bs = slice(b * 512, (b + 1) * 512)
                oT_ = oT.next()
                for tt in range(4):
                    t = b * 4 + tt
                    ot = otm.next()
                    sc.dma(ot[:, :], s.o_tm.ap[t * 128:(t + 1) * 128, :], w=[ot], q="pool")
                    tp = TPr.next()
                    for kc in range(8):
                        sc.PE(lambda e: e.transpose(out=tp[:, kc, :], in_=ot[:, kc * 128:(kc + 1) * 128], identity=ident[:, :]), r=[ot, ident], w=[tp])
                    sc.ACT(lambda e: e.copy(out=oT_[:, :, tt * 128:(tt + 1) * 128], in_=tp[:, :, :]), r=[tp], w=[oT_])
                x_ = xTb.next()
                sc.dma(x_[:, :, :], s.xT.ap.rearrange("k p s -> p k s")[:, :, bs], w=[x_], q="pool")
                m_ = mT.next()
                brk = [(0, 4), (4, 6), (6, 8)]
                for n in range(8):
                    ns_ = slice(n * 128, (n + 1) * 128)
                    tA_ = tA.next()
                    for br in range(3):
                        pg = PG.next()
                        for kc in range(8):
                            sc.PE(lambda e: e.matmul(pg[:, :], lhsT=Wg[:, kc, br * D + n * 128:br * D + (n + 1) * 128], rhs=x_[:, kc, :], start=(kc == 0), stop=(kc == 7)), r=[Wg, x_], w=[pg])
                        pp = PP.next()
                        k0, k1 = brk[br]
                        for kc in range(k0, k1):
                            sc.PE(lambda e: e.matmul(pp[:, :], lhsT=Wb[:, kc, ns_], rhs=oT_[:, kc, :], start=(kc == k0), stop=(kc == k1 - 1)), r=[Wb, oT_], w=[pp])
                        sg_ = sg.next()
                        sc.ACT(lambda e: e.activation(out=sg_[:, :], in_=pg[:, :], func=AF.Sigmoid), r=[pg], w=[sg_])
                        if br == 0:
                            sc.DVE(lambda e: e.tensor_tensor(out=tA_[:, :], in0=pp[:, :], in1=sg_[:, :], op=ALU.mult), r=[pp, sg_], w=[tA_])
                        else:
                            tB_ = tB.next()
                            sc.DVE(lambda e: e.tensor_tensor(out=tB_[:, :], in0=pp[:, :], in1=sg_[:, :], op=ALU.mult), r=[pp, sg_], w=[tB_])
                            if br == 1:
                                sc.POOL(lambda e: e.tensor_tensor(out=tA_[:, :], in0=tA_[:, :], in1=tB_[:, :], op=ALU.add), r=[tA_, tB_], w=[tA_])
                            else:
                                sc.POOL(lambda e: e.tensor_tensor(out=m_[:, n, :], in0=tA_[:, :], in1=tB_[:, :], op=ALU.add), r=[tA_, tB_], w=[m_])
                for tt in range(4):
                    t = b * 4 + tt
                    py = PY.next()
                    for hf in range(2):
                        for kc in range(8):
                            sc.PE(lambda e: e.matmul(py[:, hf * 512:(hf + 1) * 512], lhsT=m_[:, kc, tt * 128:(tt + 1) * 128], rhs=Wo[:, kc, hf * 512:(hf + 1) * 512], start=(kc == 0), stop=(kc == 7)),
                                  r=[Wo, m_], w=[py])
                    s.layer_norm_tile(R, py, t, xsrc, xdst, TPr, ident)
            s.ln_flush(R)
        sc.barrier()

    def phase_cross(s, l, xsrc, xdst):
        sc = s.sc
        S, NT, NB = s.S, s.NT, s.NB
        with ExitStack() as es:
            ident = s.sb(es, "ident", [128, 128], BF16)
            sc.dma(ident[:, :], s.c_ident.ap[:, :], w=[ident])
            stg = s.stages(es)
            Wq = s.sb(es, "xc_wq", [128, 8, 256], BF16)
            Wkv = s.sb(es, "xc_wkv", [128, 8, 512], BF16)
            Wo = s.sb(es, "xc_wo", [128, 2, D], BF16)
            s.load_w(Wq, s.xa_w_q.ap[l], 128, 8, 256, stg)
            s.load_w(Wkv, s.xa_w_kv.ap[l], 128, 8, 512, stg)
            s.load_w(Wo, s.xa_w_o.ap[l], 128, 2, D, stg)
            memT = s.sb(es, "xc_memT", [128, 8, MEM], BF16)
            sc.dma(memT[:, :, :], s.memT.ap.rearrange("k p m -> p k m"), w=[memT])
            R = s.ln_setup(es, l, 1)
            osb = s.sb(es, "xc_oall", [128, NT, 256], BF16)
            ob = [Buf() for _ in range(NB)]
            qx = s.sb(es, "xc_qx", [64, 4, S], BF16)
            qb_ = [Buf() for _ in range(NB)]
            with ExitStack() as es1:
                sT_ring = Ring([s.ps(es1, "xc_sT%d" % i, [128, 512], F32) for i in range(3)])
                accR = Ring([s.ps(es1, "xc_acc%d" % i, [128, 4, 128], F32) for i in range(3)])
                PQ = Ring([s.ps(es1, "xc_pq%d" % i, [128, 512], F32) for i in range(2)])
                pT_ring = Ring([s.sb(es1, "xc_pT%d" % i, [128, 512], BF16) for i in range(5)])
                kT = s.sb(es1, "xc_kT", [64, 4, MEM], BF16)
                V = s.sb(es1, "xc_V", [128, 2, 4, 72], BF16)
                sc.POOL(lambda e: e.memset(V[:, :, :, :], 1.0), w=[V])
                for h in range(4):
                    pq = PQ.next()
                    for kc in range(8):
                        sc.PE(lambda e: e.matmul(pq[0:64, 0:MEM], lhsT=Wkv[:, kc, 64 * h:64 * h + 64], rhs=memT[:, kc, :], start=(kc == 0), stop=(kc == 7)), r=[Wkv, memT], w=[pq])
                    sc.ACT(lambda e: e.copy(out=kT[:, h, :], in_=pq[0:64, 0:MEM]), r=[pq], w=[kT])
                for t in range(2):
                    pq = PQ.next()
                    for kc in range(8):
                        sc.PE(lambda e: e.matmul(pq[:, 0:256], lhsT=memT[:, kc, t * 128:(t + 1) * 128], rhs=Wkv[:, kc, 256:512], start=(kc == 0), stop=(kc == 7)), r=[Wkv, memT], w=[pq])
                    sc.ACT(lambda e: e.copy(out=V[:, t, :, 0:64], in_=pq[:, 0:256].rearrange("p (g d) -> p g d", g=4)), r=[pq], w=[V])
                xTb = Ring([s.sb(es1, "xc_xT%d" % i, [128, 8, 512], BF16) for i in range(2)])
                sm = Ring([s.sb(es1, "xc_sm%d" % i, [128, 8], F32) for i in range(3)])
                steps = []
                for b in range(NB):
                    bs = slice(b * 512, (b + 1) * 512)
                    x_ = xTb.next()
                    sc.dma(x_[:, :, :], s.xT.ap.rearrange("k p s -> p k s")[:, :, bs], w=[x_])
                    for h in range(4):
                        pq = PQ.next()
                        for kc in range(8):
                            sc.PE(lambda e: e.matmul(pq[0:64, :], lhsT=Wq[:, kc, 64 * h:64 * h + 64], rhs=x_[:, kc, :], start=(kc == 0), stop=(kc == 7)), r=[Wq, x_], w=[pq])
                        if h % 2 == 0:
                            sc.DVE(lambda e: e.tensor_copy(out=qx[:, h, bs], in_=pq[0:64, :]), r=[pq], w=[qb_[b]])
                        else:
                            sc.ACT(lambda e: e.copy(out=qx[:, h, bs], in_=pq[0:64, :]), r=[pq], w=[qb_[b]])
                    for h in range(4):
                        acc = accR.next()
                        for j in range(2):
                            fin = None
                            if j == 1:
                                def fin(h=h, acc=acc, b=b):
                                    m = sm.next()
                                    sc.DVE(lambda e: e.reciprocal(out=m[:, 4:8], in_=acc[:, :, 64:65]), r=[acc], w=[m])
                                    for sub in range(4):
                                        sc.DVE(lambda e: e.tensor_scalar(out=osb[:, 4 * b + sub, 64 * h:64 * h + 64], in0=acc[:, sub, 0:64], scalar1=m[:, 4 + sub:5 + sub], scalar2=None, op0=ALU.mult),
                                               r=[acc, m], w=[ob[b]])
                            steps.append(dict(kT=kT[:, h, j * 128:(j + 1) * 128], q_fn=(lambda c0, c1, h=h, b=b: qx[:, h, b * 512 + c0:b * 512 + c1]), kk=128, ncol=512, c0=0, scale=0.125,
                                              masks=[], V=V[:, j, h, 0:65], subs=[0, 1, 2, 3], acc_fn=(lambda sub, acc=acc: (acc[:, sub, 0:65], acc)), first=(j == 0), fin=fin,
                                              rb_score=[kT, qb_[b]], rb_v=[V], mask_pool=False))
                s.run_steps(steps, sT_ring, pT_ring, skew=2)
            sc.barrier()
            with ExitStack() as es2:
                TPo = Ring([s.ps(es2, "xc_tpo%d" % i, [128, 8, 128], BF16) for i in range(2)])
                PY = Ring([s.ps(es2, "xc_py%d" % i, [128, 1024], F32) for i in range(2)])
                TPr = Ring([s.ps(es2, "xc_tp%d" % i, [128, 8, 128], BF16) for i in range(2)])
                oTt = Ring([s.sb(es2, "xc_oT%d" % i, [128, 2, 128], BF16) for i in range(3)])
                for t in range(NT):
                    tp = TPo.next()
                    for kc in range(2):
                        sc.PE(lambda e: e.transpose(out=tp[:, kc, :], in_=osb[:, t, kc * 128:(kc + 1) * 128], identity=ident[:, :]), r=[ob[t // 4], ident], w=[tp])
                    oT_ = oTt.next()
                    sc.ACT(lambda e: e.copy(out=oT_[:, :, :], in_=tp[:, 0:2, :]), r=[tp], w=[oT_])
                    py = PY.next()
                    for hf in range(2):
                        for kc in range(2):
                            sc.PE(lambda e: e.matmul(py[:, hf * 512:(hf + 1) * 512], lhsT=oT_[:, kc, :], rhs=Wo[:, kc, hf * 512:(hf + 1) * 512], start=(kc == 0), stop=(kc == 1)), r=[Wo, oT_], w=[py])
                    s.layer_norm_tile(R, py, t, xsrc, xdst, TPr, ident)
                s.ln_flush(R)
        sc.barrier()

    def phase_mlp_up(s, l):
        sc = s.sc
        S, NT, NB = s.S, s.NT, s.NB
        with ExitStack() as es:
            stg = s.stages(es)
            W = s.sb(es, "mu_w", [128, 8, 4 * D], BF16)
            s.load_w(W, s.w_up.ap[l], 128, 8, 4 * D, stg)
            PB = Ring([s.ps(es, "mu_pb%d" % i, [128, 512], F32) for i in range(4)])
            xTb = Ring([s.sb(es, "mu_xT%d" % i, [128, 8, 512], BF16) for i in range(2)])
            r_ = Ring([s.sb(es, "mu_r%d" % i, [128, 512], F32) for i in range(3)])
            h_ = Ring([s.sb(es, "mu_h%d" % i, [128, 512], BF16) for i in range(3)])
            for b in range(NB):
                bs = slice(b * 512, (b + 1) * 512)
                x_ = xTb.next()
                sc.dma(x_[:, :, :], s.xT.ap.rearrange("k p s -> p k s")[:, :, bs], w=[x_])
                for n in range(32):
                    pb = PB.next()
                    for kc in range(8):
                        sc.PE(lambda e: e.matmul(pb[:, :], lhsT=W[:, kc, n * 128:(n + 1) * 128], rhs=x_[:, kc, :], start=(kc == 0), stop=(kc == 7)), r=[W, x_], w=[pb])
                    rr = r_.next()
                    sc.ACT(lambda e: e.activation(out=rr[:, :], in_=pb[:, :], func=AF.Relu), r=[pb], w=[rr])
                    hh = h_.next()
                    if n % 2 == 0:
                        sc.POOL(lambda e: e.tensor_tensor(out=hh[:, :], in0=rr[:, :], in1=rr[:, :], op=ALU.mult), r=[rr], w=[hh])
                    else:
                        sc.DVE(lambda e: e.tensor_tensor(out=hh[:, :], in0=rr[:, :], in1=rr[:, :], op=ALU.mult), r=[rr], w=[hh])
                    sc.dma(s.hT.ap[n, :, bs], hh[:, :], r=[hh], w=[Buf()], q="sp")
        sc.barrier()

    def phase_mlp_down(s, l, xsrc, xdst, final):
        sc = s.sc
        S, NT, NB = s.S, s.NT, s.NB
        with ExitStack() as es:
            ident = s.sb(es, "ident", [128, 128], BF16)
            sc.dma(ident[:, :], s.c_ident.ap[:, :], w=[ident])
            stg = s.stages(es)
            W = s.sb(es, "md_w", [128, 32, D], BF16)
            s.load_w(W, s.w_down.ap[l], 128, 32, D, stg)
            R = s.ln_setup(es, l, 2)
            PY = Ring([s.ps(es, "md_py%d" % i, [128, 1024], F32) for i in range(2)])
            TPr = Ring([s.ps(es, "md_tp%d" % i, [128, 8, 128], BF16) for i in range(2)])
            hb = Ring([s.sb(es, "md_h%d" % i, [128, 32, 256], BF16) for i in range(2)])
            for b2 in range(NT // 2):
                bs = slice(b2 * 256, (b2 + 1) * 256)
                h_ = hb.next()
                for qq in range(4):
                    sc.dma(h_[:, qq * 8:(qq + 1) * 8, :], s.hT.ap.rearrange("k p s -> p k s")[:, qq * 8:(qq + 1) * 8, bs], w=[h_], q="pool")
                for tt in range(2):
                    t = b2 * 2 + tt
                    py = PY.next()
                    for hf in range(2):
                        for kc in range(32):
                            sc.PE(lambda e: e.matmul(py[:, hf * 512:(hf + 1) * 512], lhsT=h_[:, kc, tt * 128:(tt + 1) * 128], rhs=W[:, kc, hf * 512:(hf + 1) * 512], start=(kc == 0), stop=(kc == 31)),
                                  r=[W, h_], w=[py])
                    s.layer_norm_tile(R, py, t, xsrc, xdst, TPr, ident, final=final)
            s.ln_flush(R)
        sc.barrier()


class TTv:
    def __init__(s, tt, a0, n):
        s.tt = tt
        s.a0 = a0
        s.buf = tt.buf

    def __getitem__(s, k):
        p, a, c = k
        if isinstance(a, slice):
            a = slice((a.start or 0) + s.a0, (a.stop if a.stop is not None else 0) + s.a0)
        else:
            a = a + s.a0
        return s.tt[p, a, c]


_CACHE = {}


def get_prog(S, depth, debug=None):
    key = (S, depth, tuple(sorted(debug)) if debug else None)
    if key not in _CACHE:
        p = Prog(S, depth, debug=debug)
        p.build()
        _CACHE[key] = p
    return _CACHE[key]


def make_in_maps(inputs, S, depth, ncores):
    consts = host_consts(S)
    near, cmpb, c31 = t5_gather(np.asarray(inputs["t5_table"], np.float32), consts)
    shared = {}
    for k in ("w_in", "cmp_pe", "cmp_w1", "cmp_w2", "mla_q_norm", "mla_w_uq", "mla_kv_norm", "mla_w_ukv", "fox_b_f", "w_gate",
              "w_br_nsa", "w_br_mla", "w_br_fox", "w_mix_out", "xa_w_q", "xa_w_kv", "xa_w_o", "mlp_w_up", "mlp_w_down", "ln_g", "ln_b"):
        shared[k] = np.ascontiguousarray(np.asarray(inputs[k], np.float32)[:depth])
    for k in ("ident_bf", "tri_bf", "tri4_bf", "atri4_bf", "nearmask", "rope_cs", "rope_ss", "overlap_bf", "aforced", "expand_bf", "ones_bf", "cmp_valid"):
        shared[k] = consts[k]
    shared["t5_near"] = near
    shared["t5_cmpb"] = cmpb
    shared["t5_c31"] = c31
    maps = []
    for c in range(ncores):
        m = dict(shared)
        m["x"] = np.ascontiguousarray(np.asarray(inputs["x"][c], np.float32))
        m["mem"] = np.ascontiguousarray(np.asarray(inputs["mem"][c], np.float32))
        maps.append(m)
    return maps


def kernel(**inputs):
    S, depth, ncores = SEQ_FULL, DEPTH_FULL, 8
    p = get_prog(S, depth)
    maps = make_in_maps(inputs, S, depth, ncores)
    res = run_bass_kernel_spmd(p.nc, maps, core_ids=list(range(ncores)))
    out = np.stack([np.asarray(r["y"], np.float32) for r in res.results], 0)
    return out
```

```python
import math
from contextlib import ExitStack
import numpy as np
import ml_dtypes
import concourse.bass as bass
import concourse.mybir as mybir
from concourse.bass_utils import run_bass_kernel_spmd

F32 = mybir.dt.float32
BF16 = mybir.dt.bfloat16
AF = mybir.ActivationFunctionType
ALU = mybir.AluOpType
AX = mybir.AxisListType

D = 1024
DEPTH_FULL = 4
SEQ_FULL = 4096
MEM = 256
N_IN = 2620
DN_ALPHA = (2 * DEPTH_FULL) ** 0.25
BIG = 30000.0


class Tok:
    __slots__ = ("sem", "val", "know")

    def __init__(s, sem, val, know):
        s.sem = sem
        s.val = val
        s.know = know


class Buf:
    __slots__ = ("w", "r", "name")

    def __init__(s, name=""):
        s.w = None
        s.r = {}
        s.name = name


class TT:
    def __init__(s, h, name=""):
        s.h = h
        s.buf = Buf(name)

    def __getitem__(s, k):
        return s.h[k]


class Eng:
    def __init__(s, name, e, sem, semid):
        s.name = name
        s.e = e
        s.sem = sem
        s.semid = semid
        s.cnt = 0
        s.know = {}


class Lane:
    def __init__(s, sem, semid):
        s.sem = sem
        s.semid = semid
        s.val = 0


def _bufs(xs):
    out = []
    for x in xs:
        if x is None:
            continue
        out.append(x.buf if hasattr(x, "buf") else x)
    return out


class Sched:
    NL = 8

    def __init__(s, nc, es):
        s.nc = nc
        s.sems = []

        def mk(n):
            sem = es.enter_context(nc.semaphore(n))
            s.sems.append(sem)
            return sem, len(s.sems) - 1

        s.pe = Eng("pe", nc.tensor, *mk("s_pe"))
        s.dve = Eng("dve", nc.vector, *mk("s_dve"))
        s.act = Eng("act", nc.scalar, *mk("s_act"))
        s.pool = Eng("pool", nc.gpsimd, *mk("s_pool"))
        s.sp = Eng("sp", nc.sync, *mk("s_sp"))
        s.engs = [s.pe, s.dve, s.act, s.pool, s.sp]
        s.lanes = {}
        s.rr = {}
        for q in ("sp", "pool"):
            s.lanes[q] = [Lane(*mk("l_%s_%d" % (q, i))) for i in range(s.NL)]
            s.rr[q] = 0
        s.q = {"sp": s.sp, "pool": s.pool}
        s.ninst = 0

    def _wait(s, E, tok):
        if tok is None:
            return
        if E.know.get(tok.sem, 0) >= tok.val:
            return
        if tok.sem == E.semid and E is s.pe:
            return
        E.e.wait_ge(s.sems[tok.sem], tok.val)
        k = dict(E.know)
        for a, b in tok.know.items():
            if k.get(a, 0) < b:
                k[a] = b
        if k.get(tok.sem, 0) < tok.val:
            k[tok.sem] = tok.val
        E.know = k

    def _deps(s, E, r, w):
        for b in r:
            s._wait(E, b.w)
        for b in w:
            s._wait(E, b.w)
            for t in list(b.r.values()):
                s._wait(E, t)

    def op(s, E, fn, r=(), w=()):
        r = _bufs(r)
        w = _bufs(w)
        s._deps(E, r, w)
        inst = fn(E.e)
        E.cnt += 1
        inst.then_inc(E.sem, 1)
        tok = Tok(E.semid, E.cnt, E.know)
        for b in r:
            b.r[E.semid] = tok
        for b in w:
            b.w = tok
            b.r = {}
        s.ninst += 1
        return tok

    def PE(s, fn, r=(), w=()):
        return s.op(s.pe, fn, r, w)

    def DVE(s, fn, r=(), w=()):
        return s.op(s.dve, fn, r, w)

    def ACT(s, fn, r=(), w=()):
        return s.op(s.act, fn, r, w)

    def POOL(s, fn, r=(), w=()):
        return s.op(s.pool, fn, r, w)

    def dma(s, out, in_, r=(), w=(), q="sp", **kw):
        Q = s.q[q]
        r = _bufs(r)
        w = _bufs(w)
        s._deps(Q, r, w)
        lanes = s.lanes[q]
        i = s.rr[q]
        s.rr[q] = (i + 1) % len(lanes)
        lane = lanes[i]
        if lane.val > 0:
            s._wait(Q, Tok(lane.semid, lane.val, {}))
        inst = Q.e.dma_start(out=out, in_=in_, **kw)
        lane.val += 16
        inst.then_inc(lane.sem, 16)
        tok = Tok(lane.semid, lane.val, Q.know)
        for b in r:
            b.r[lane.semid] = tok
        for b in w:
            b.w = tok
            b.r = {}
        s.ninst += 1
        return tok

    def barrier(s):
        toks = [Tok(E.semid, E.cnt, {}) for E in s.engs if E.cnt > 0]
        for q in s.lanes:
            for l in s.lanes[q]:
                if l.val > 0:
                    toks.append(Tok(l.semid, l.val, {}))
        for E in s.engs:
            for t in toks:
                s._wait(E, t)


class Ring:
    def __init__(s, items):
        s.items = items
        s.i = 0

    def next(s):
        x = s.items[s.i]
        s.i = (s.i + 1) % len(s.items)
        return x


class DT:
    def __init__(s, ap, name):
        s.ap = ap
        s.name = name
        s.bufs = {}
        s.buf = Buf(name)

    def b(s, key):
        if key not in s.bufs:
            s.bufs[key] = Buf("%s_%s" % (s.name, key))
        return s.bufs[key]


def t5_bucket_np(dist):
    n = np.maximum(dist, 0)
    nf = np.maximum(n, 1).astype(np.float32)
    large = 16 + (np.log(nf / np.float32(16)) / np.float32(math.log(128 / 16)) * np.float32(16)).astype(np.int32)
    large = np.minimum(large, 31)
    return np.where(n < 16, n, large)


def host_consts(S):
    NT = S // 128
    NCP = S // 16
    c = {}
    c["ident_bf"] = np.eye(128, dtype=np.float32).astype(ml_dtypes.bfloat16)
    k = np.arange(128)[:, None]
    q = np.arange(128)[None, :]
    tri = (q >= k).astype(np.float32)
    c["tri_bf"] = tri.astype(ml_dtypes.bfloat16)
    c["tri4_bf"] = np.tile(tri[:, None, :], (1, 4, 1)).astype(ml_dtypes.bfloat16)
    atri = (k > q).astype(np.float32)
    c["atri4_bf"] = np.tile(atri[:, None, :], (1, 4, 1)).astype(ml_dtypes.bfloat16)
    m = np.ones((2, 128, 8, 128), np.float32)
    m[0] = np.tile(tri[:, None, :], (1, 8, 1))
    c["nearmask"] = m
    half = 16
    inv = (10000.0 ** (-np.arange(half, dtype=np.float32) / half)).astype(np.float32)
    ang = np.arange(S, dtype=np.float32)[None, :] * inv[:, None]
    cos = np.cos(ang).astype(np.float32)
    sin = np.sin(ang).astype(np.float32)
    cs = np.zeros((96, S), np.float32)
    ss = np.zeros((96, S), np.float32)
    for base in (0, 64):
        cs[base:base + 16] = cos
        cs[base + 16:base + 32] = cos
        ss[base:base + 16] = -sin
        ss[base + 16:base + 32] = sin
    c["rope_cs"] = cs
    c["rope_ss"] = ss
    n_slc = S // 64
    c_lo = np.arange(NCP)[:, None] * 16
    s_lo = np.arange(64)[None, :] * 64
    ov = np.maximum(np.minimum(c_lo + 32, s_lo + 64) - np.maximum(c_lo, s_lo), 0).astype(np.float32) / 16.0
    ov[:, n_slc:] = 0.0
    c["overlap_bf"] = ov.astype(ml_dtypes.bfloat16)
    t = np.arange(S)[:, None]
    blk = np.arange(64)[None, :]
    cur = t // 64
    forced = (blk == 0) | (blk == cur) | (blk == cur - 1)
    causal = (blk * 64 <= t) & (blk < n_slc)
    A = np.where(causal, np.where(forced, 1e30, 0.0), -1e30).astype(np.float32)
    c["aforced"] = A
    ex = np.zeros((64, S), np.float32)
    ex[np.arange(S) // 64, np.arange(S)] = BIG
    c["expand_bf"] = ex.astype(ml_dtypes.bfloat16)
    c["ones_bf"] = np.ones((128, 512), np.float32).astype(ml_dtypes.bfloat16)
    OFF = 8 * (NT - 1)
    NROW = ((NCP + OFF + 127) // 128) * 128
    cc = np.arange(NROW)[:, None] - OFF
    ql = np.arange(128)[None, :]
    dist = ql - 16 * cc - 31
    c["cmp_dist"] = dist
    vc = (dist >= 0).astype(np.float32)
    c["cmp_valid"] = np.tile(vc[:, None, :], (1, 8, 1)).astype(np.float32)
    return c


def t5_gather(t5_table, consts):
    k = np.arange(128)[:, None]
    q = np.arange(128)[None, :]
    b0 = t5_bucket_np(q - k)
    b1 = t5_bucket_np(128 + q - k)
    near = np.stack([t5_table[b0], t5_table[b1]], 0)
    near = np.ascontiguousarray(near.transpose(0, 1, 3, 2))
    bc = t5_bucket_np(consts["cmp_dist"])
    cmpb = np.ascontiguousarray(t5_table[bc].transpose(0, 2, 1))
    c31 = np.ascontiguousarray(np.broadcast_to(t5_table[31][None, :, None], (128, 8, 128)))
    return near.astype(np.float32), cmpb.astype(np.float32), c31.astype(np.float32)


class Prog:
    def __init__(s, S, depth, debug=False):
        s.S = S
        s.depth = depth
        s.debug = debug
        s.NT = S // 128
        s.NB = S // 512
        s.NCP = S // 16
        s.NCMP = (S - 32) // 16 + 1
        s.CR = min(128, s.NCP)
        s.CT = (s.NCP + 127) // 128
        s.OFF = 8 * (s.NT - 1)
        s.nc = bass.Bass("TRN2", target_bir_lowering=False)
        s.es = ExitStack()
        s.sc = Sched(s.nc, s.es)
        s.dbg_outs = []
        s.inputs = {}

    def din(s, name, shape, dt=F32):
        ap = s.nc.dram_tensor(name, list(shape), dt, kind="ExternalInput").ap()
        s.inputs[name] = ap
        return DT(ap, name)

    def dscr(s, name, shape, dt, out=False):
        kind = "ExternalOutput" if (out or (s.debug and name in s.debug)) else "Internal"
        if kind == "ExternalOutput":
            s.dbg_outs.append(name)
        ap = s.nc.dram_tensor(name, list(shape), dt, kind=kind).ap()
        return DT(ap, name)

    def sb(s, es, name, shape, dt):
        s.uid = getattr(s, "uid", 0) + 1
        name = "%s_u%d" % (name, s.uid)
        return TT(es.enter_context(s.nc.sbuf_tensor(name, list(shape), dt)), name)

    def ps(s, es, name, shape, dt=F32):
        s.uid = getattr(s, "uid", 0) + 1
        name = "%s_u%d" % (name, s.uid)
        return TT(es.enter_context(s.nc.psum_tensor(name, list(shape), dt)), name)

    def load_w(s, dst, src2d, P, A, N, stage_ring, dcol=0, order=None):
        sc = s.sc
        src3 = src2d.rearrange("(a p) n -> p a n", p=P)
        maxc = max(1, 2048 // A)
        chunks = []
        c0 = 0
        while c0 < N:
            ncol = min(maxc, N - c0)
            chunks.append((c0, ncol))
            c0 += ncol
        if order is not None:
            chunks = [chunks[i] for i in order]
        base = dst.tt if hasattr(dst, "tt") else dst
        if not hasattr(base, "cbufs"):
            base.cbufs = []
        a0 = getattr(dst, "a0", 0)
        for (c0, ncol) in chunks:
            st = stage_ring.next()
            sv = st[0:P, 0:A * ncol].rearrange("p (a n) -> p a n", a=A)
            sc.dma(sv, src3[:, :, c0:c0 + ncol], w=[st])
            cb = Buf()
            base.cbufs.append((a0, a0 + A, dcol + c0, dcol + c0 + ncol, cb))
            s.castk = getattr(s, "castk", 0) + 1
            k = s.castk % 3
            dv = dst[0:P, 0:A, dcol + c0:dcol + c0 + ncol]
            if k == 0:
                sc.DVE(lambda e: e.tensor_copy(out=dv, in_=sv), r=[st], w=[cb])
            elif k == 1:
                sc.ACT(lambda e: e.copy(out=dv, in_=sv), r=[st], w=[cb])
            else:
                sc.POOL(lambda e: e.tensor_copy(out=dv, in_=sv), r=[st], w=[cb])

    def wr(s, W, c0, c1, a0=0, a1=10 ** 9):
        cb = getattr(W, "cbufs", None)
        if not cb:
            return [W.buf]
        out = [b for (x0, x1, y0, y1, b) in cb if y0 < c1 and c0 < y1 and x0 < a1 and a0 < x1]
        return out + [W.buf]

    def stages(s, es, n=3):
        return Ring([s.sb(es, "wstage%d" % i, [128, 2048], F32) for i in range(n)])

    def build(s):
        S, NT, NB = s.S, s.NT, s.NB
        L = s.depth
        s.x_in = s.din("x", [S, D])
        s.mem_in = s.din("mem", [MEM, D])
        s.w_in = s.din("w_in", [L, D, N_IN])
        s.cmp_pe = s.din("cmp_pe", [L, 2, 32, 64])
        s.cmp_w1 = s.din("cmp_w1", [L, 2, 2048, 128])
        s.cmp_w2 = s.din("cmp_w2", [L, 2, 128, 64])
        s.q_norm = s.din("mla_q_norm", [L, 384])
        s.w_uq = s.din("mla_w_uq", [L, 384, 384])
        s.kv_norm = s.din("mla_kv_norm", [L, 128])
        s.w_ukv = s.din("mla_w_ukv", [L, 128, 512])
        s.b_f = s.din("fox_b_f", [L, 4])
        s.w_gate = s.din("w_gate", [L, D, 3 * D])
        s.w_br_nsa = s.din("w_br_nsa", [L, 512, D])
        s.w_br_mla = s.din("w_br_mla", [L, 256, D])
        s.w_br_fox = s.din("w_br_fox", [L, 256, D])
        s.w_mix_out = s.din("w_mix_out", [L, D, D])
        s.xa_w_q = s.din("xa_w_q", [L, D, 256])
        s.xa_w_kv = s.din("xa_w_kv", [L, D, 512])
        s.xa_w_o = s.din("xa_w_o", [L, 256, D])
        s.w_up = s.din("mlp_w_up", [L, D, 4 * D])
        s.w_down = s.din("mlp_w_down", [L, 4 * D, D])
        s.ln_g = s.din("ln_g", [L, 3, D])
        s.ln_b = s.din("ln_b", [L, 3, D])
        NROW = ((s.NCP + s.OFF + 127) // 128) * 128
        s.NROW = NROW
        s.c_ident = s.din("ident_bf", [128, 128], BF16)
        s.c_tri = s.din("tri_bf", [128, 128], BF16)
        s.c_tri4 = s.din("tri4_bf", [128, 4, 128], BF16)
        s.c_atri4 = s.din("atri4_bf", [128, 4, 128], BF16)
        s.c_nearmask = s.din("nearmask", [2, 128, 8, 128])
        s.c_cs = s.din("rope_cs", [96, S])
        s.c_ss = s.din("rope_ss", [96, S])
        s.c_overlap = s.din("overlap_bf", [s.NCP, 64], BF16)
        s.c_aforced = s.din("aforced", [S, 64])
        s.c_expand = s.din("expand_bf", [64, S], BF16)
        s.c_ones = s.din("ones_bf", [128, 512], BF16)
        s.c_cmpvalid = s.din("cmp_valid", [NROW, 8, 128])
        s.c_near = s.din("t5_near", [2, 128, 8, 128])
        s.c_cmpb = s.din("t5_cmpb", [NROW, 8, 128])
        s.c_c31 = s.din("t5_c31", [128, 8, 128])
        s.y_out = s.dscr("y", [S, D], F32, out=True)
        s.xa = s.dscr("x_a", [S, D], F32)
        s.xb = s.dscr("x_b", [S, D], F32)
        s.xT = s.dscr("xT", [8, 128, S], BF16)
        s.memT = s.dscr("memT", [8, 128, MEM], BF16)
        s.em = s.dscr("em", [2, 128, 8, 128], BF16)
        s.emc = s.dscr("emc", [NROW, 8, 128], BF16)
        s.nqT = s.dscr("nqT", [8, 64, S], BF16)
        s.nkcT = s.dscr("nkcT", [2, 64, S], BF16)
        s.nvcT = s.dscr("nvcT", [2, 64, S], BF16)
        s.nksT = s.dscr("nksT", [2, 64, S], BF16)
        s.nkwT = s.dscr("nkwT", [2, 64, S], BF16)
        s.nvs = s.dscr("nvs", [S, 2, 72], BF16)
        s.nvw = s.dscr("nvw", [S, 2, 72], BF16)
        s.gate = s.dscr("gate", [S, 24], F32)
        s.cnT = s.dscr("cnT", [4, 128, S], BF16)
        s.krT = s.dscr("krT", [2, 32, S], F32)
        s.mqT = s.dscr("mqT", [4, 96, S], BF16)
        s.mkT = s.dscr("mkT", [4, 96, S], BF16)
        s.mv = s.dscr("mv", [S, 4, 72], BF16)
        s.fqT = s.dscr("fqT", [4, 70, S], BF16)
        s.fkT = s.dscr("fkT", [4, 70, S], BF16)
        s.fv = s.dscr("fv", [S, 4, 72], BF16)
        s.ffT = s.dscr("ffT", [4, S], F32)
        s.o_tm = s.dscr("o_tm", [S, D], BF16)
        s.hT = s.dscr("hT", [32, 128, S], BF16)

        import os
        stop = os.environ.get("K_STOP", "")
        seq = []
        seq.append(("init", lambda: s.phase_init()))
        state = {"cur": s.x_in, "nxt": s.xa}

        def adv():
            state["cur"], state["nxt"] = state["nxt"], (s.xb if state["nxt"] is s.xa else s.xa)
        for l in range(L):
            last = (l == L - 1)
            seq.append(("p1", lambda l=l: s.phase_p1(l)))
            seq.append(("mla_prep", lambda l=l: s.phase_mla_prep(l)))
            seq.append(("fox_prep", lambda l=l: s.phase_fox_prep(l)))
            seq.append(("nsa", lambda l=l: s.phase_nsa(l)))
            seq.append(("mla", lambda l=l: s.phase_causal(l, "mla")))
            seq.append(("fox", lambda l=l: s.phase_causal(l, "fox")))
            seq.append(("combine", lambda l=l: (s.phase_combine(l, state["cur"], state["nxt"]), adv())))
            seq.append(("cross", lambda l=l: (s.phase_cross(l, state["cur"], state["nxt"]), adv())))
            seq.append(("mlp_up", lambda l=l: s.phase_mlp_up(l)))
            seq.append(("mlp_down", lambda l=l, last=last: (s.phase_mlp_down(l, state["cur"], s.y_out if last else state["nxt"], last), adv())))
        skip = set(os.environ.get("K_SKIP", "").split(","))
        for (nm, fn) in seq:
            if nm not in skip:
                fn()
            if stop and nm == stop:
                break
        s.sc.barrier()
        return s.nc

    def transpose_to_xT(s, xbf, TPr, xTt, tile_idx, ident, dst, q="pool"):
        sc = s.sc
        tp = TPr.next()
        for kc in range(8):
            sc.PE(lambda e: e.transpose(out=tp[:, kc, :], in_=xbf[:, kc * 128:(kc + 1) * 128], identity=ident[:, :]),
                  r=[xbf, ident], w=[tp])
        xt = xTt.next()
        sc.ACT(lambda e: e.copy(out=xt[:, :, :], in_=tp[:, :, :]), r=[tp], w=[xt])
        sc.dma(dst.ap.rearrange("k p s -> p k s")[:, :, tile_idx * 128:(tile_idx + 1) * 128], xt[:, :, :], r=[xt], w=[dst.b(("t", tile_idx))], q=q)

    def ln_setup(s, es, l, which, depth=1):
        sc = s.sc
        g = s.sb(es, "ln_gam", [128, D], F32)
        b = s.sb(es, "ln_bet", [128, D], F32)
        sc.dma(g[:, :], s.ln_g.ap[l, which, :].partition_broadcast(128), w=[g])
        sc.dma(b[:, :], s.ln_b.ap[l, which, :].partition_broadcast(128), w=[b])
        r = dict(g=g, b=b, depth=depth)
        r["xin"] = Ring([s.sb(es, "ln_xin%d" % i, [128, D], F32) for i in range(1 + depth)])
        r["z"] = Ring([s.sb(es, "ln_z%d" % i, [128, D], F32) for i in range(1 + depth)])
        r["xo"] = Ring([s.sb(es, "ln_xo%d" % i, [128, D], F32) for i in range(2)])
        r["xbf"] = Ring([s.sb(es, "ln_xbf%d" % i, [128, D], BF16) for i in range(3)])
        r["st"] = Ring([s.sb(es, "ln_st%d" % i, [128, 24], F32) for i in range(2 + depth)])
        r["xTt"] = Ring([s.sb(es, "ln_xTt%d" % i, [128, 8, 128], BF16) for i in range(2)])
        return r

    def layer_norm_tile(s, R, Y, tile_idx, xsrc, xdst, TPr, ident, final=False):
        st8 = s.ln_A(R, Y, tile_idx, xsrc)
        pend = R.setdefault("pend", [])
        pend2 = R.setdefault("pend2", [])
        pend.append((st8, tile_idx, xdst, TPr, ident, final))
        while len(pend2) > 1:
            s.ln_C(R, *pend2.pop(0))
        while len(pend) > R.get("depth", 1):
            c = s.ln_B(R, *pend.pop(0))
            if c is not None:
                pend2.append(c)

    def ln_flush(s, R):
        pend = R.setdefault("pend", [])
        pend2 = R.setdefault("pend2", [])
        while pend:
            while len(pend2) > 1:
                s.ln_C(R, *pend2.pop(0))
            c = s.ln_B(R, *pend.pop(0))
            if c is not None:
                pend2.append(c)
        while pend2:
            s.ln_C(R, *pend2.pop(0))

    def ln_C(s, R, xbf, tile_idx, TPr, ident):
        s.transpose_to_xT(xbf, TPr, R["xTt"], tile_idx, ident, s.xT, q="sp")

    def ln_A(s, R, Y, tile_idx, xsrc):
        sc = s.sc
        rows = slice(tile_idx * 128, (tile_idx + 1) * 128)
        xin = R["xin"].next()
        sc.dma(xin[:, :], xsrc.ap[rows, :], r=[xsrc.b(("t", tile_idx))], w=[xin], q="pool")
        z = R["z"].next()
        sc.DVE(lambda e: e.scalar_tensor_tensor(out=z[:, :], in0=xin[:, :], scalar=float(DN_ALPHA), in1=Y[:, :], op0=ALU.mult, op1=ALU.add),
               r=[xin, Y], w=[z])
        st = R["st"].next()
        for c in range(2):
            sc.DVE(lambda e: e.bn_stats(out=st[:, c * 6:(c + 1) * 6], in_=z[:, c * 512:(c + 1) * 512]), r=[z], w=[st])
        sc.DVE(lambda e: e.bn_aggr(out=st[:, 12:14], in_=st[:, 0:12]), r=[st], w=[st])
        sc.DVE(lambda e: e.tensor_scalar(out=st[:, 14:15], in0=st[:, 13:14], scalar1=1e-5, scalar2=None, op0=ALU.add), r=[st], w=[st])
        sc.ACT(lambda e: e.activation(out=st[:, 16:17], in_=st[:, 14:15], func=AF.Sqrt), r=[st], w=[st])
        return (z, st)

    def ln_B(s, R, zst, tile_idx, xdst, TPr, ident, final):
        sc = s.sc
        z, st = zst
        rows = slice(tile_idx * 128, (tile_idx + 1) * 128)
        sc.DVE(lambda e: e.reciprocal(out=st[:, 18:19], in_=st[:, 16:17]), r=[st], w=[st])
        xo = R["xo"].next()
        sc.DVE(lambda e: e.scalar_tensor_tensor(out=z[:, :], in0=z[:, :], scalar=st[:, 12:13], in1=R["g"][:, :], op0=ALU.subtract, op1=ALU.mult),
               r=[z, st, R["g"]], w=[z])
        sc.DVE(lambda e: e.scalar_tensor_tensor(out=xo[:, :], in0=z[:, :], scalar=st[:, 18:19], in1=R["b"][:, :], op0=ALU.mult, op1=ALU.add),
               r=[z, st, R["b"]], w=[xo])
        sc.dma(xdst.ap[rows, :], xo[:, :], r=[xo], w=[xdst.b(("t", tile_idx))], q="sp")
        if not final:
            xbf = R["xbf"].next()
            sc.ACT(lambda e: e.copy(out=xbf[:, :], in_=xo[:, :]), r=[xo], w=[xbf])
            return (xbf, tile_idx, TPr, ident)
        return None

    def phase_init(s):
        sc = s.sc
        S, NT = s.S, s.NT
        with ExitStack() as es:
            ident = s.sb(es, "ident", [128, 128], BF16)
            sc.dma(ident[:, :], s.c_ident.ap[:, :], w=[ident])
            c31 = s.sb(es, "c31", [128, 1024], F32)
            sc.dma(c31[:, :], s.c_c31.ap.rearrange("p h q -> p (h q)"), w=[c31])
            tb = Ring([s.sb(es, "t5b%d" % i, [128, 1024], F32) for i in range(2)])
            tm = Ring([s.sb(es, "t5m%d" % i, [128, 1024], F32) for i in range(2)])
            to = Ring([s.sb(es, "t5o%d" % i, [128, 1024], BF16) for i in range(2)])
            jobs = [(s.c_near.ap[i].rearrange("p h q -> p (h q)"), s.c_nearmask.ap[i].rearrange("p h q -> p (h q)"),
                     s.em.ap[i].rearrange("p h q -> p (h q)")) for i in range(2)]
            for rt in range(s.NROW // 128):
                rs = slice(rt * 128, (rt + 1) * 128)
                jobs.append((s.c_cmpb.ap[rs].rearrange("p h q -> p (h q)"), s.c_cmpvalid.ap[rs].rearrange("p h q -> p (h q)"),
                             s.emc.ap[rs].rearrange("p h q -> p (h q)")))
            for (bsrc, msrc, dst) in jobs:
                b = tb.next()
                m = tm.next()
                o = to.next()
                sc.dma(b[:, :], bsrc, w=[b])
                sc.dma(m[:, :], msrc, w=[m])
                sc.DVE(lambda e: e.tensor_tensor(out=b[:, :], in0=b[:, :], in1=c31[:, :], op=ALU.subtract), r=[b, c31], w=[b])
                sc.ACT(lambda e: e.activation(out=b[:, :], in_=b[:, :], func=AF.Exp), r=[b], w=[b])
                sc.DVE(lambda e: e.tensor_tensor(out=o[:, :], in0=b[:, :], in1=m[:, :], op=ALU.mult), r=[b, m], w=[o])
                sc.dma(dst, o[:, :], r=[o], w=[s.em.buf], q="pool")
            mt = s.sb(es, "mem_t", [128, 2, D], F32)
            sc.dma(mt[:, :, :], s.mem_in.ap.rearrange("(t p) d -> p t d", p=128), w=[mt])
            mb = s.sb(es, "mem_b", [128, 2, D], BF16)
            sc.DVE(lambda e: e.tensor_copy(out=mb[:, :, :], in_=mt[:, :, :]), r=[mt], w=[mb])
            tp = s.ps(es, "init_tp", [128, 8, 128], BF16)
            mT = s.sb(es, "memT_s", [128, 8, MEM], BF16)
            for t in range(2):
                for kc in range(8):
                    sc.PE(lambda e: e.transpose(out=tp[:, kc, :], in_=mb[:, t, kc * 128:(kc + 1) * 128], identity=ident[:, :]), r=[mb, ident], w=[tp])
                sc.ACT(lambda e: e.copy(out=mT[:, :, t * 128:(t + 1) * 128], in_=tp[:, :, :]), r=[tp], w=[mT])
            sc.dma(s.memT.ap.rearrange("k p m -> p k m"), mT[:, :, :], r=[mT], w=[s.memT.buf], q="pool")
            xr = Ring([s.sb(es, "ix%d" % i, [128, D], F32) for i in range(2)])
            xbr = Ring([s.sb(es, "ixb%d" % i, [128, D], BF16) for i in range(2)])
            TPr = Ring([tp, s.ps(es, "init_tp2", [128, 8, 128], BF16)])
            xTt = Ring([s.sb(es, "ixT%d" % i, [128, 8, 128], BF16) for i in range(2)])
            for t in range(NT):
                xt = xr.next()
                sc.dma(xt[:, :], s.x_in.ap[t * 128:(t + 1) * 128, :], w=[xt])
                xb = xbr.next()
                sc.DVE(lambda e: e.tensor_copy(out=xb[:, :], in_=xt[:, :]), r=[xt], w=[xb])
                s.transpose_to_xT(xb, TPr, xTt, t, ident, s.xT)
        sc.barrier()

    def phase_p1(s, l):
        sc = s.sc
        S, NT, NB = s.S, s.NT, s.NB
        WN = N_IN + 64
        with ExitStack() as es:
            ident = s.sb(es, "ident", [128, 128], BF16)
            sc.dma(ident[:, :], s.c_ident.ap[:, :], w=[ident])
            W = s.sb(es, "p1_w", [128, 8, WN], BF16)
            stg = s.stages(es)
            wsrc = s.w_in.ap[l]
            s.load_w(W, wsrc, 128, 8, N_IN, stg)
            sc.POOL(lambda e: e.tensor_copy(out=W[:, :, N_IN:N_IN + 32], in_=W[:, :, 1816:1848]), r=s.wr(W, 1816, 1848), w=[W])
            sc.POOL(lambda e: e.tensor_copy(out=W[:, :, N_IN + 32:N_IN + 48], in_=W[:, :, 1832:1848]), r=s.wr(W, 1832, 1848), w=[W])
            sc.POOL(lambda e: e.tensor_copy(out=W[:, :, N_IN + 48:N_IN + 64], in_=W[:, :, 1816:1832]), r=s.wr(W, 1816, 1832), w=[W])
            xT = s.sb(es, "p1_xT", [128, 8, S], BF16)
            xTb = [Buf("xTb%d" % b) for b in range(NB)]
            for b in range(NB):
                sc.dma(xT[:, :, b * 512:(b + 1) * 512], s.xT.ap.rearrange("k p s -> p k s")[:, :, b * 512:(b + 1) * 512], w=[xTb[b]])
            PB = Ring([s.ps(es, "p1_pb%d" % i, [128, 512], F32) for i in range(6)])
            TP = Ring([s.ps(es, "p1_tp%d" % i, [128, 8, 128], BF16) for i in range(1)])
            fo = Ring([s.sb(es, "p1_fo%d" % i, [128, 512], BF16) for i in range(4)])
            fo32 = Ring([s.sb(es, "p1_fo32_%d" % i, [128, 512], F32) for i in range(2)])
            vt_s = Ring([s.sb(es, "p1_vts%d" % i, [128, 4, 2, 72], BF16) for i in range(2)])
            vt_w = Ring([s.sb(es, "p1_vtw%d" % i, [128, 4, 2, 72], BF16) for i in range(2)])
            vt_f = Ring([s.sb(es, "p1_vtf%d" % i, [128, 4, 4, 72], BF16) for i in range(2)])
            for rg in (vt_s, vt_w, vt_f):
                for t_ in rg.items:
                    sc.POOL(lambda e: e.memset(t_[:, :, :, :], 1.0), w=[t_])
            gt = Ring([s.sb(es, "p1_gt%d" % i, [128, 4, 24], F32) for i in range(2)])
            cn = Ring([s.sb(es, "p1_cn%d" % i, [128, 512], BF16) for i in range(2)])
            junk = s.sb(es, "p1_junk", [128, 512], BF16)
            ss = Ring([s.sb(es, "p1_ss%d" % i, [128, 4], F32) for i in range(2)])
            cT = Ring([s.sb(es, "p1_cT%d" % i, [128, 4, 512], BF16) for i in range(2)])
            evac_i = [0]

            def evac(dst_ap, src_ap, rb, wb):
                evac_i[0] += 1
                if evac_i[0] % 2 == 0:
                    sc.ACT(lambda e: e.copy(out=dst_ap, in_=src_ap), r=rb, w=wb)
                else:
                    sc.DVE(lambda e: e.tensor_copy(out=dst_ap, in_=src_ap), r=rb, w=wb)

            fm = []
            for h in range(8):
                fm.append((64 * h, 64, "bf", [(s.nqT.ap[h], 0, 64)]))
            fm.append((512, 128, "bf", [(s.nkcT.ap[0], 0, 64), (s.nkcT.ap[1], 64, 64)]))
            fm.append((640, 128, "bf", [(s.nvcT.ap[0], 0, 64), (s.nvcT.ap[1], 64, 64)]))
            fm.append((768, 128, "bf", [(s.nksT.ap[0], 0, 64), (s.nksT.ap[1], 64, 64)]))
            fm.append((1024, 128, "bf", [(s.nkwT.ap[0], 0, 64), (s.nkwT.ap[1], 64, 64)]))
            fm.append((N_IN, 32, "f32", [(s.krT.ap[0], 0, 32)]))
            fm.append((N_IN + 32, 32, "f32", [(s.krT.ap[1], 0, 32)]))
            for t in range(2):
                fm.append((1848 + 128 * t, 128, "bf", [(s.fqT.ap[2 * t, 0:64, :], 0, 64), (s.fqT.ap[2 * t + 1, 0:64, :], 64, 64)]))
                fm.append((2104 + 128 * t, 128, "bf", [(s.fkT.ap[2 * t, 0:64, :], 0, 64), (s.fkT.ap[2 * t + 1, 0:64, :], 64, 64)]))
            fm.append((2616, 4, "f32", [(s.ffT.ap, 0, 4)]))
            for b in range(NB):
                bs = slice(b * 512, (b + 1) * 512)
                import os
                P1M = int(os.environ.get('K_P1', '31'))
                for (c0, M, kind, dsts) in (fm if P1M & 1 else []):
                    pb = PB.next()
                    for kc in range(8):
                        sc.PE(lambda e: e.matmul(pb[0:M, :], lhsT=W[:, kc, c0:c0 + M], rhs=xT[:, kc, bs], start=(kc == 0), stop=(kc == 7)),
                              r=s.wr(W, c0, c0 + M) + [xTb[b]], w=[pb])
                    o = fo.next() if kind == "bf" else fo32.next()
                    evac(o[0:M, :], pb[0:M, :], [pb], [o])
                    for (dap, p0, pn) in dsts:
                        sc.dma(dap[:, bs], o[p0:p0 + pn, :], r=[o], w=[Buf()], q="sp")
                vs_t = vt_s.next()
                vw_t = vt_w.next()
                vf_t = vt_f.next()
                g_t = gt.next()
                cT_t = cT.next()
                for tt in (range(4) if P1M & 2 else []):
                    t = b * 4 + tt
                    ts_ = slice(t * 128, (t + 1) * 128)
                    pa = PB.next()
                    for kc in range(8):
                        sc.PE(lambda e: e.matmul(pa[:, :], lhsT=xT[:, kc, ts_], rhs=W[:, kc, 1304:1816], start=(kc == 0), stop=(kc == 7)),
                              r=s.wr(W, 1304, 1816) + [xTb[b]], w=[pa])
                    s_ = ss.next()
                    sc.ACT(lambda e: e.activation(out=junk[:, 0:384], in_=pa[:, 0:384], func=AF.Square, scale=float(384 ** -0.5), accum_out=s_[:, 0:1]),
                           r=[pa], w=[junk, s_])
                    sc.ACT(lambda e: e.activation(out=junk[:, 384:512], in_=pa[:, 384:512], func=AF.Square, scale=float(128 ** -0.5), accum_out=s_[:, 1:2]),
                           r=[pa], w=[junk, s_])
                    sc.DVE(lambda e: e.tensor_scalar(out=s_[:, 0:2], in0=s_[:, 0:2], scalar1=1e-6, scalar2=None, op0=ALU.add), r=[s_], w=[s_])
                    sc.ACT(lambda e: e.activation(out=s_[:, 2:4], in_=s_[:, 0:2], func=AF.Sqrt), r=[s_], w=[s_])
                    sc.DVE(lambda e: e.reciprocal(out=s_[:, 0:2], in_=s_[:, 2:4]), r=[s_], w=[s_])
                    cn_t = cn.next()
                    sc.DVE(lambda e: e.tensor_scalar(out=cn_t[:, 0:384], in0=pa[:, 0:384], scalar1=s_[:, 0:1], scalar2=None, op0=ALU.mult), r=[pa, s_], w=[cn_t])
                    sc.DVE(lambda e: e.tensor_scalar(out=cn_t[:, 384:512], in0=pa[:, 384:512], scalar1=s_[:, 1:2], scalar2=None, op0=ALU.mult), r=[pa, s_], w=[cn_t])
                    tp = TP.next()
                    for j in range(4):
                        sc.PE(lambda e: e.transpose(out=tp[:, j, :], in_=cn_t[:, j * 128:(j + 1) * 128], identity=ident[:, :]), r=[cn_t, ident], w=[tp])
                    sc.ACT(lambda e: e.copy(out=cT_t[:, :, tt * 128:(tt + 1) * 128], in_=tp[:, 0:4, :]), r=[tp], w=[cT_t])
                    if P1M & 4:
                        pb1 = PB.next()
                        for (cc0, o0, n) in ((896, 0, 128), (1152, 128, 128), (2360, 256, 256)):
                            for kc in range(8):
                                sc.PE(lambda e: e.matmul(pb1[:, o0:o0 + n], lhsT=xT[:, kc, ts_], rhs=W[:, kc, cc0:cc0 + n], start=(kc == 0), stop=(kc == 7)),
                                      r=s.wr(W, cc0, cc0 + n) + [xTb[b]], w=[pb1])
                        sc.ACT(lambda e: e.copy(out=vs_t[:, tt, :, 0:64], in_=pb1[:, 0:128].rearrange("p (g d) -> p g d", g=2)), r=[pb1], w=[vs_t])
                        sc.ACT(lambda e: e.copy(out=vw_t[:, tt, :, 0:64], in_=pb1[:, 128:256].rearrange("p (g d) -> p g d", g=2)), r=[pb1], w=[vw_t])
                        sc.ACT(lambda e: e.copy(out=vf_t[:, tt, :, 0:64], in_=pb1[:, 256:512].rearrange("p (g d) -> p g d", g=4)), r=[pb1], w=[vf_t])
                    if P1M & 8:
                        pb2 = PB.next()
                        for kc in range(8):
                            sc.PE(lambda e: e.matmul(pb2[:, 0:24], lhsT=xT[:, kc, ts_], rhs=W[:, kc, 1280:1304], start=(kc == 0), stop=(kc == 7)),
                                  r=s.wr(W, 1280, 1304) + [xTb[b]], w=[pb2])
                        sc.ACT(lambda e: e.activation(out=g_t[:, tt, :], in_=pb2[:, 0:24], func=(AF.Identity if os.environ.get("K_X") == "3" else AF.Sigmoid)), r=[pb2], w=[g_t])
                rows = slice(b * 512, (b + 1) * 512)
                if not (P1M & 16):
                    continue
                sc.dma(s.nvs.ap[rows].rearrange("(t p) g e -> p t g e", p=128), vs_t[:, :, :, :], r=[vs_t], w=[Buf()], q="sp")
                sc.dma(s.nvw.ap[rows].rearrange("(t p) g e -> p t g e", p=128), vw_t[:, :, :, :], r=[vw_t], w=[Buf()], q="sp")
                sc.dma(s.fv.ap[rows].rearrange("(t p) g e -> p t g e", p=128), vf_t[:, :, :, :], r=[vf_t], w=[Buf()], q="sp")
                sc.dma(s.gate.ap[rows].rearrange("(t p) c -> p t c", p=128), g_t[:, :, :], r=[g_t], w=[Buf()], q="sp")
                sc.dma(s.cnT.ap.rearrange("j p s -> p j s")[:, :, rows], cT_t[:, :, :], r=[cT_t], w=[Buf()], q="sp")
        sc.barrier()

    def phase_mla_prep(s, l):
        sc = s.sc
        S, NT, NB = s.S, s.NT, s.NB
        with ExitStack() as es:
            cnT = s.sb(es, "mp_cnT", [128, 4, S], BF16)
            cb = [Buf() for _ in range(NB)]
            for b in range(NB):
                sc.dma(cnT[:, :, b * 512:(b + 1) * 512], s.cnT.ap.rearrange("j p s -> p j s")[:, :, b * 512:(b + 1) * 512], w=[cb[b]])
            CS = s.sb(es, "mp_cs", [96, S], F32)
            SS = s.sb(es, "mp_ss", [96, S], F32)
            sc.dma(CS[:, :], s.c_cs.ap[:, :], w=[CS])
            sc.dma(SS[:, :], s.c_ss.ap[:, :], w=[SS])
            kr0 = s.sb(es, "mp_kr0", [32, S], F32)
            kr1 = s.sb(es, "mp_kr1", [32, S], F32)
            sc.dma(kr0[:, :], s.krT.ap[0], w=[kr0])
            sc.dma(kr1[:, :], s.krT.ap[1], w=[kr1])
            gq = s.sb(es, "mp_gq", [128, 4], F32)
            with s.nc.allow_non_contiguous_dma(reason="tiny gain vectors"):
                sc.dma(gq[:, 0:3], s.q_norm.ap[l].rearrange("(a p) -> p a", p=128), w=[gq])
                sc.dma(gq[:, 3:4], s.kv_norm.ap[l].rearrange("(a p) -> p a", p=128), w=[gq])
            stq = s.sb(es, "mp_stq", [128, 3, 384], F32)
            stk = s.sb(es, "mp_stk", [128, 512], F32)
            sc.dma(stq[:, :, :], s.w_uq.ap[l].rearrange("(a p) n -> p a n", p=128), w=[stq])
            sc.dma(stk[:, :], s.w_ukv.ap[l], w=[stk])
            Wq = s.sb(es, "mp_wq", [128, 3, 384], BF16)
            Wqp = s.sb(es, "mp_wqp", [128, 3, 4, 96], BF16)
            Wkv = s.sb(es, "mp_wkv", [128, 512], BF16)
            for a in range(3):
                sc.DVE(lambda e: e.tensor_scalar(out=Wq[:, a, :], in0=stq[:, a, :], scalar1=gq[:, a:a + 1], scalar2=None, op0=ALU.mult), r=[stq, gq], w=[Wq])
            sc.DVE(lambda e: e.tensor_scalar(out=Wkv[:, :], in0=stk[:, :], scalar1=gq[:, 3:4], scalar2=None, op0=ALU.mult), r=[stk, gq], w=[Wkv])
            sc.POOL(lambda e: e.memset(Wqp[:, :, :, :], 0.0), w=[Wqp])
            for h in range(4):
                sc.POOL(lambda e: e.tensor_copy(out=Wqp[:, :, h, 64:80], in_=Wq[:, :, 96 * h + 80:96 * h + 96]), r=[Wq], w=[Wqp])
                sc.POOL(lambda e: e.tensor_copy(out=Wqp[:, :, h, 80:96], in_=Wq[:, :, 96 * h + 64:96 * h + 80]), r=[Wq], w=[Wqp])
            PB = Ring([s.ps(es, "mp_pb%d" % i, [128, 512], F32) for i in range(6)])
            qo = Ring([s.sb(es, "mp_qo%d" % i, [96, 512], BF16) for i in range(3)])
            ko = Ring([s.sb(es, "mp_ko%d" % i, [64, 512], BF16) for i in range(3)])
            t1 = Ring([s.sb(es, "mp_t1_%d" % i, [96, 512], F32) for i in range(2)])
            t2 = Ring([s.sb(es, "mp_t2_%d" % i, [96, 512], F32) for i in range(2)])
            kro = Ring([s.sb(es, "mp_kro%d" % i, [32, 512], BF16) for i in range(2)])
            vt = Ring([s.sb(es, "mp_vt%d" % i, [128, 4, 4, 72], BF16) for i in range(2)])
            for t_ in vt.items:
                sc.POOL(lambda e: e.memset(t_[:, :, :, :], 1.0), w=[t_])
            for b in range(NB):
                bs = slice(b * 512, (b + 1) * 512)
                for h in range(4):
                    p1 = PB.next()
                    for a in range(3):
                        sc.PE(lambda e: e.matmul(p1[0:96, :], lhsT=Wq[:, a, 96 * h:96 * h + 96], rhs=cnT[:, a, bs], start=(a == 0), stop=(a == 2)), r=[Wq, cb[b]], w=[p1])
                    p2 = PB.next()
                    for a in range(3):
                        sc.PE(lambda e: e.matmul(p2[0:96, :], lhsT=Wqp[:, a, h, :], rhs=cnT[:, a, bs], start=(a == 0), stop=(a == 2)), r=[Wqp, cb[b]], w=[p2])
                    q_ = qo.next()
                    sc.ACT(lambda e: e.copy(out=q_[0:64, :], in_=p1[0:64, :]), r=[p1], w=[q_])
                    a1 = t1.next()
                    a2 = t2.next()
                    sc.DVE(lambda e: e.tensor_tensor(out=a1[64:96, :], in0=p1[64:96, :], in1=CS[64:96, bs], op=ALU.mult), r=[p1, CS], w=[a1])
                    sc.DVE(lambda e: e.tensor_tensor(out=a2[64:96, :], in0=p2[64:96, :], in1=SS[64:96, bs], op=ALU.mult), r=[p2, SS], w=[a2])
                    sc.POOL(lambda e: e.tensor_tensor(out=q_[64:96, :], in0=a1[64:96, :], in1=a2[64:96, :], op=ALU.add), r=[a1, a2], w=[q_])
                    sc.dma(s.mqT.ap[h, :, bs], q_[:, :], r=[q_], w=[Buf()], q="sp")
                    p3 = PB.next()
                    sc.PE(lambda e: e.matmul(p3[0:64, :], lhsT=Wkv[:, 128 * h:128 * h + 64], rhs=cnT[:, 3, bs], start=True, stop=True), r=[Wkv, cb[b]], w=[p3])
                    k_ = ko.next()
                    sc.ACT(lambda e: e.copy(out=k_[:, :], in_=p3[0:64, :]), r=[p3], w=[k_])
                    sc.dma(s.mkT.ap[h, 0:64, bs], k_[:, :], r=[k_], w=[Buf()], q="sp")
                a1 = t1.next()
                a2 = t2.next()
                kr_ = kro.next()
                sc.DVE(lambda e: e.tensor_tensor(out=a1[0:32, :], in0=kr0[:, bs], in1=CS[0:32, bs], op=ALU.mult), r=[kr0, CS], w=[a1])
                sc.DVE(lambda e: e.tensor_tensor(out=a2[0:32, :], in0=kr1[:, bs], in1=SS[0:32, bs], op=ALU.mult), r=[kr1, SS], w=[a2])
                sc.POOL(lambda e: e.tensor_tensor(out=kr_[:, :], in0=a1[0:32, :], in1=a2[0:32, :], op=ALU.add), r=[a1, a2], w=[kr_])
                for h in range(4):
                    sc.dma(s.mkT.ap[h, 64:96, bs], kr_[:, :], r=[kr_], w=[Buf()], q="sp")
                v_ = vt.next()
                for tt in range(4):
                    t = b * 4 + tt
                    p4 = PB.next()
                    for h in range(4):
                        sc.PE(lambda e: e.matmul(p4[:, 64 * h:64 * h + 64], lhsT=cnT[:, 3, t * 128:(t + 1) * 128], rhs=Wkv[:, 128 * h + 64:128 * h + 128], start=True, stop=True),
                              r=[Wkv, cb[b]], w=[p4])
                    sc.ACT(lambda e: e.copy(out=v_[:, tt, :, 0:64], in_=p4[:, 0:256].rearrange("p (g d) -> p g d", g=4)), r=[p4], w=[v_])
                sc.dma(s.mv.ap[bs].rearrange("(t p) g e -> p t g e", p=128), v_[:, :, :, :], r=[v_], w=[Buf()], q="sp")
        sc.barrier()

    def phase_fox_prep(s, l):
        sc = s.sc
        S = s.S
        with ExitStack() as es:
            ff = s.sb(es, "fp_ff", [4, S], F32)
            sc.dma(ff[:, :], s.ffT.ap[:, :], w=[ff])
            bf = s.sb(es, "fp_bf", [4, 2], F32)
            with s.nc.allow_non_contiguous_dma(reason="tiny"):
                sc.dma(bf[:, 0:1], s.b_f.ap[l].rearrange("(p a) -> p a", a=1), w=[bf])
            sc.DVE(lambda e: e.tensor_scalar(out=bf[:, 1:2], in0=bf[:, 0:1], scalar1=-1.0, scalar2=None, op0=ALU.mult), r=[bf], w=[bf])
            ex = s.sb(es, "fp_ex", [4, S], F32)
            sc.ACT(lambda e: e.activation(out=ex[:, :], in_=ff[:, :], func=AF.Exp, bias=bf[:, 1:2], scale=-1.0), r=[ff, bf], w=[ex])
            one = s.sb(es, "fp_one", [4, 1], F32)
            sc.DVE(lambda e: e.memset(one[:, :], 1.0), w=[one])
            sc.ACT(lambda e: e.activation(out=ex[:, :], in_=ex[:, :], func=AF.Ln, bias=one[:, 0:1], scale=1.0), r=[ex, one], w=[ex])
            ones = s.sb(es, "fp_ones", [4, S], F32)
            sc.POOL(lambda e: e.memset(ones[:, :], 1.0), w=[ones])
            sc.DVE(lambda e: e.tensor_scalar(out=ex[:, :], in0=ex[:, :], scalar1=-8.0, scalar2=None, op0=ALU.mult), r=[ex], w=[ex])
            cum = s.sb(es, "fp_cum", [4, S], F32)
            sc.DVE(lambda e: e.tensor_tensor_scan(out=cum[:, :], data0=ones[:, :], data1=ex[:, :], initial=0.0, op0=ALU.mult, op1=ALU.add), r=[ones, ex], w=[cum])
            pcs = [s.sb(es, "fp_pc%d" % i, [4, S], BF16) for i in range(3)]
            ngs = [s.sb(es, "fp_ng%d" % i, [4, S], BF16) for i in range(3)]
            rem = s.sb(es, "fp_rem", [4, S], F32)
            src = cum
            for i in range(3):
                sc.DVE(lambda e: e.tensor_copy(out=pcs[i][:, :], in_=src[:, :]), r=[src], w=[pcs[i]])
                sc.DVE(lambda e: e.tensor_scalar(out=ngs[i][:, :], in0=pcs[i][:, :], scalar1=-1.0, scalar2=None, op0=ALU.mult), r=[pcs[i]], w=[ngs[i]])
                if i < 2:
                    sc.DVE(lambda e: e.tensor_tensor(out=rem[:, :], in0=src[:, :], in1=pcs[i][:, :], op=ALU.subtract), r=[src, pcs[i]], w=[rem])
                    src = rem
            onb = s.sb(es, "fp_onb", [4, S], BF16)
            sc.POOL(lambda e: e.memset(onb[:, :], 1.0), w=[onb])
            for i in range(3):
                sc.dma(s.fqT.ap[:, 64 + i, :], pcs[i][:, :], r=[pcs[i]], w=[Buf()], q="sp")
                sc.dma(s.fqT.ap[:, 67 + i, :], onb[:, :], r=[onb], w=[Buf()], q="sp")
                sc.dma(s.fkT.ap[:, 64 + i, :], onb[:, :], r=[onb], w=[Buf()], q="sp")
                sc.dma(s.fkT.ap[:, 67 + i, :], ngs[i][:, :], r=[ngs[i]], w=[Buf()], q="sp")
        sc.barrier()

    def run_steps(s, steps, sT_ring, pT_ring, skew=1):
        sc = s.sc
        n = len(steps)
        state = [None] * n

        def emit_score(i):
            st = steps[i]
            sT = sT_ring.next()
            pT = pT_ring.next()
            c0, nco, kk = st["c0"], st["ncol"], st["kk"]
            sc.PE(lambda e: e.matmul(sT[0:kk, c0:nco], lhsT=st["kT"], rhs=st["q_fn"](c0, nco), start=True, stop=True),
                  r=st["rb_score"], w=[sT])
            sc.ACT(lambda e: e.activation(out=pT[0:kk, c0:nco], in_=sT[0:kk, c0:nco], func=AF.Exp, scale=float(st["scale"])), r=[sT], w=[pT])
            mi = 0
            for (m0, m1, map_, mb, clamp) in st["masks"]:
                mi += 1
                if clamp:
                    sc.DVE(lambda e: e.scalar_tensor_tensor(out=pT[0:kk, m0:m1], in0=pT[0:kk, m0:m1], scalar=1e30, in1=map_, op0=ALU.min, op1=ALU.mult),
                           r=[pT] + mb, w=[pT])
                elif st.get("mask_pool", False) and mi % 2 == 0:
                    sc.POOL(lambda e: e.tensor_tensor(out=pT[0:kk, m0:m1], in0=pT[0:kk, m0:m1], in1=map_, op=ALU.mult), r=[pT] + mb, w=[pT])
                else:
                    sc.DVE(lambda e: e.tensor_tensor(out=pT[0:kk, m0:m1], in0=pT[0:kk, m0:m1], in1=map_, op=ALU.mult), r=[pT] + mb, w=[pT])
            state[i] = pT

        def emit_pv(i):
            st = steps[i]
            pT = state[i]
            kk = st["kk"]
            first = st["first"]
            for sub in st["subs"]:
                out_ap, accb = st["acc_fn"](sub)
                stt = bool(first and (sub in st.get("start_subs", (st["subs"][0],))))
                sc.PE(lambda e: e.matmul(out_ap, lhsT=pT[0:kk, sub * 128:(sub + 1) * 128], rhs=st["V"], start=stt, stop=True, skip_group_check=True),
                      r=[pT] + st["rb_v"], w=[accb])
            if st["fin"] is not None:
                st["fin"]()

        nxt_pv = 0
        pre_ptr = [0]

        def do_pre(upto):
            while pre_ptr[0] < n and pre_ptr[0] <= upto:
                pf = steps[pre_ptr[0]].get("pre")
                if pf is not None:
                    pf()
                pre_ptr[0] += 1
        for i in range(n + skew):
            if i < n:
                do_pre(i + 3)
                dep = steps[i].get("dep_step")
                while dep is not None and nxt_pv <= dep:
                    emit_pv(nxt_pv)
                    nxt_pv += 1
                emit_score(i)
            if i >= skew and nxt_pv <= i - skew:
                emit_pv(nxt_pv)
                nxt_pv += 1
        while nxt_pv < n:
            emit_pv(nxt_pv)
            nxt_pv += 1

    def phase_nsa(s, l):
        sc = s.sc
        S, NT = s.S, s.NT
        CR, CT, NCP, NCMP = s.CR, s.CT, s.NCP, s.NCMP
        with ExitStack() as es:
            ident = s.sb(es, "ident", [128, 128], BF16)
            sc.dma(ident[:, :], s.c_ident.ap[:, :], w=[ident])
            QA = [s.sb(es, "ns_qa%d" % g, [128, NT, 4, 128], BF16) for g in range(2)]
            qa_lo = [Buf() for g in range(2)]
            qa_hi = [[Buf() for i in range(NT)] for g in range(2)]
            ksA = [s.sb(es, "ns_ks%d" % g, [128, S], BF16) for g in range(2)]
            kwT = [s.sb(es, "ns_kw%d" % g, [64, S], BF16) for g in range(2)]
            kcT = [s.sb(es, "ns_kc%d" % g, [64, CT * CR], BF16) for g in range(2)]
            vs = [s.sb(es, "ns_vs%d" % g, [128, NT, 72], BF16) for g in range(2)]
            vw = [s.sb(es, "ns_vw%d" % g, [128, NT, 72], BF16) for g in range(2)]
            vcA = [s.sb(es, "ns_vc%d" % g, [128, CT, 136], BF16) for g in range(2)]
            G = s.sb(es, "ns_gate", [128, NT, 24], F32)
            AFc = s.sb(es, "ns_af", [128, NT, 64], F32)
            EM = s.sb(es, "ns_em", [128, 2, 8, 128], BF16)
            EMW = s.sb(es, "ns_emw", [128, 4, 128], BF16)
            for g in range(2):
                for hh in range(4):
                    sc.dma(QA[g][0:64, :, hh, :], s.nqT.ap[4 * g + hh].rearrange("d (t q) -> d t q", q=128), w=[qa_lo[g]])
                sc.dma(ksA[g][0:64, :], s.nksT.ap[g], w=[ksA[g]])
                sc.dma(ksA[g][64:128, :], s.c_expand.ap[:, :], w=[ksA[g]])
                sc.dma(kwT[g][:, :], s.nkwT.ap[g], w=[kwT[g]])
                sc.dma(vs[g][:, :, :], s.nvs.ap[:, g, :].rearrange("(t p) e -> p t e", p=128), w=[vs[g]])
                sc.dma(vw[g][:, :, :], s.nvw.ap[:, g, :].rearrange("(t p) e -> p t e", p=128), w=[vw[g]])
            sc.dma(G[:, :, :], s.gate.ap.rearrange("(t p) c -> p t c", p=128), w=[G])
            sc.dma(AFc[:, :, :], s.c_aforced.ap.rearrange("(t p) c -> p t c", p=128), w=[AFc])
            for i in range(2):
                sc.dma(EM[:, i, :, :], s.em.ap[i], w=[EM])
            sc.dma(EMW[:, :, :], s.c_atri4.ap[:, :, :], w=[EMW])
            sT_ring = Ring([s.ps(es, "ns_sT%d" % i, [128, 512], F32) for i in range(3)])
            accC = [s.ps(es, "ns_accC%d" % i, [128, 2, 256], F32) for i in range(2)]
            accR = Ring([s.ps(es, "ns_accR%d" % i, [128, 4, 128], F32) for i in range(2)])
            TPb = s.ps(es, "ns_tp", [128, 1024], BF16)
            with ExitStack() as es2:
                W1 = Ring([s.sb(es2, "nc_w1_%d" % i, [64, 32, 128], BF16) for i in range(2)])
                stg = Ring([s.sb(es2, "nc_stg%d" % i, [64, 8, 128], F32) for i in range(2)])
                src = Ring([s.sb(es2, "nc_src%d" % i, [64, S], BF16) for i in range(2)])
                peT = Ring([s.sb(es2, "nc_peT%d" % i, [64, 32], F32) for i in range(2)])
                peTb = Ring([s.sb(es2, "nc_peTb%d" % i, [64, 32], BF16) for i in range(2)])
                W2s = Ring([s.sb(es2, "nc_w2s%d" % i, [128, 64], F32) for i in range(2)])
                W2 = Ring([s.sb(es2, "nc_w2_%d" % i, [128, 64], BF16) for i in range(2)])
                hb = Ring([s.sb(es2, "nc_hb%d" % i, [128, 2], F32) for i in range(2)])
                u = Ring([s.sb(es2, "nc_u%d" % i, [128, 256], F32) for i in range(2)])
                u2 = Ring([s.sb(es2, "nc_u2%d" % i, [128, 256], F32) for i in range(2)])
                hT = Ring([s.sb(es2, "nc_hT%d" % i, [128, 256], BF16) for i in range(2)])
                for g in range(2):
                    sc.POOL(lambda e: e.memset(vcA[g][:, :, :], 0.0), w=[vcA[g]])
                    sc.POOL(lambda e: e.memset(kcT[g][:, :], 0.0), w=[kcT[g]])
                for kv in range(2):
                    w1 = W1.next()
                    w1src = s.cmp_w1.ap[l, kv].rearrange("(a p) n -> p a n", p=64)
                    for hf in range(4):
                        st_ = stg.next()
                        sc.dma(st_[:, :, :], w1src[:, hf * 8:(hf + 1) * 8, :], w=[st_])
                        sc.POOL(lambda e: e.tensor_copy(out=w1[:, hf * 8:(hf + 1) * 8, :], in_=st_[:, :, :]), r=[st_], w=[w1])
                    pT_ = peT.next()
                    with s.nc.allow_non_contiguous_dma(reason="tiny pe transpose"):
                        sc.dma(pT_[:, :], s.cmp_pe.ap[l, kv].rearrange("l d -> d l"), w=[pT_])
                    pTb_ = peTb.next()
                    sc.DVE(lambda e: e.tensor_copy(out=pTb_[:, :], in_=pT_[:, :]), r=[pT_], w=[pTb_])
                    w2s = W2s.next()
                    sc.dma(w2s[:, :], s.cmp_w2.ap[l, kv], w=[w2s])
                    w2 = W2.next()
                    sc.DVE(lambda e: e.tensor_copy(out=w2[:, :], in_=w2s[:, :]), r=[w2s], w=[w2])
                    pbias = sT_ring.next()
                    for ll in range(32):
                        sc.PE(lambda e: e.matmul(pbias[:, 0:1], lhsT=w1[:, ll, :], rhs=pTb_[:, ll:ll + 1], start=(ll == 0), stop=(ll == 31)), r=[w1, pTb_], w=[pbias])
                    hb_ = hb.next()
                    sc.DVE(lambda e: e.tensor_copy(out=hb_[:, 0:1], in_=pbias[:, 0:1]), r=[pbias], w=[hb_])
                    for g in range(2):
                        sr = src.next()
                        sc.dma(sr[:, :], (s.nkcT if kv == 0 else s.nvcT).ap[g], w=[sr])
                        ph = sT_ring.next()
                        for ll in range(32):
                            sc.PE(lambda e: e.matmul(ph[:, 0:NCMP], lhsT=w1[:, ll, :], rhs=sr[:, ll:ll + 16 * (NCMP - 1) + 1:16], start=(ll == 0), stop=(ll == 31)),
                                  r=[w1, sr], w=[ph])
                        u_ = u.next()
                        u2_ = u2.next()
                        h_ = hT.next()
                        n_ = NCMP
                        sc.ACT(lambda e: e.activation(out=u_[:, 0:n_], in_=ph[:, 0:n_], func=AF.Identity, bias=hb_[:, 0:1], scale=1.0), r=[ph, hb_], w=[u_])
                        sc.DVE(lambda e: e.tensor_tensor(out=u2_[:, 0:n_], in0=u_[:, 0:n_], in1=u_[:, 0:n_], op=ALU.mult), r=[u_], w=[u2_])
                        sc.DVE(lambda e: e.tensor_scalar(out=u2_[:, 0:n_], in0=u2_[:, 0:n_], scalar1=0.044715, scalar2=1.0, op0=ALU.mult, op1=ALU.add), r=[u2_], w=[u2_])
                        sc.DVE(lambda e: e.tensor_tensor(out=u2_[:, 0:n_], in0=u2_[:, 0:n_], in1=u_[:, 0:n_], op=ALU.mult), r=[u2_, u_], w=[u2_])
                        sc.ACT(lambda e: e.activation(out=u2_[:, 0:n_], in_=u2_[:, 0:n_], func=AF.Tanh, scale=0.7978845608028654), r=[u2_], w=[u2_])
                        sc.DVE(lambda e: e.tensor_scalar(out=u2_[:, 0:n_], in0=u2_[:, 0:n_], scalar1=1.0, scalar2=0.5, op0=ALU.add, op1=ALU.mult), r=[u2_], w=[u2_])
                        sc.DVE(lambda e: e.tensor_tensor(out=h_[:, 0:n_], in0=u2_[:, 0:n_], in1=u_[:, 0:n_], op=ALU.mult), r=[u2_, u_], w=[h_])
                        po = sT_ring.next()
                        if kv == 0:
                            sc.PE(lambda e: e.matmul(po[0:64, 0:n_], lhsT=w2[:, :], rhs=h_[:, 0:n_], start=True, stop=True), r=[w2, h_], w=[po])
                            sc.DVE(lambda e: e.tensor_copy(out=kcT[g][:, 0:n_], in_=po[0:64, 0:n_]), r=[po], w=[kcT[g]])
                        else:
                            for ct in range(CT):
                                rows = min(CR, n_ - ct * CR)
                                sc.PE(lambda e: e.matmul(po[0:rows, ct * 64:(ct + 1) * 64], lhsT=h_[:, ct * CR:ct * CR + rows], rhs=w2[:, :], start=True, stop=True), r=[w2, h_], w=[po])
                                sc.DVE(lambda e: e.tensor_copy(out=vcA[g][0:rows, ct, 0:64], in_=po[0:rows, ct * 64:(ct + 1) * 64]), r=[po], w=[vcA[g]])
                for g in range(2):
                    sc.POOL(lambda e: e.memset(vcA[g][:, :, 64:66], 1.0), w=[vcA[g]])
                    sc.dma(vcA[g][0:CR, :, 65:129], s.c_overlap.ap.rearrange("(t p) n -> p t n", p=CR), w=[vcA[g]])
            with ExitStack() as es3:
                pT_ring = Ring([s.sb(es3, "ns_pT%d" % i, [128, 512], BF16) for i in range(5)])
                emc_ring = Ring([s.sb(es3, "ns_emc%d" % i, [128, 4, 128], BF16) for i in range(4)])
                Oacc = Ring([s.sb(es3, "ns_O%d" % i, [128, 256], F32) for i in range(3)])
                Obf = Ring([s.sb(es3, "ns_Ob%d" % i, [128, 256], BF16) for i in range(3)])
                sm = Ring([s.sb(es3, "ns_sm%d" % i, [128, 32], F32) for i in range(4)])
                imp = Ring([s.sb(es3, "ns_imp%d" % i, [128, 64], F32) for i in range(2)])
                NS = Ring([s.sb(es3, "ns_NS%d" % i, [128, 128], BF16) for i in range(2)])
                for t_ in NS.items:
                    sc.POOL(lambda e: e.memset(t_[:, :], 0.0), w=[t_])
                steps = []
                for i in range(NT):
                    for g in range(2):
                        O = Oacc.next()
                        Ob = Obf.next()

                        def qfn_lo(i=i, g=g):
                            return lambda c0, c1: QA[g][0:64, i, :, :].rearrange("p h q -> p (h q)")[:, c0:c1]

                        def qfn_full(i=i, g=g):
                            return lambda c0, c1: QA[g][:, i, :, :].rearrange("p h q -> p (h q)")[:, c0:c1]

                        cmax = min(8 * i + 6, NCMP - 1)
                        nct = cmax // CR + 1
                        for ct in range(nct):
                            emc_t = emc_ring.next()
                            r0 = ct * CR - 8 * i + s.OFF
                            sc_dma_args = (emc_t, r0, g)

                            def pre(emc_t=emc_t, r0=r0, g=g):
                                sc.dma(emc_t[0:CR, :, :], s.emc.ap[r0:r0 + CR, 4 * g:4 * g + 4, :], w=[emc_t])
                            fin = None
                            if ct == nct - 1:
                                def fin(i=i, g=g, O=O):
                                    s.nsa_fin_cmp(i, g, O, accC, G, AFc, sm, imp, NS, TPb, ident, QA, qa_hi)
                            steps.append(dict(pre=pre, kT=kcT[g][:, ct * CR:(ct + 1) * CR], q_fn=qfn_lo(), kk=CR, ncol=512, c0=0, scale=0.125,
                                              masks=[(0, 512, emc_t[0:CR, :, :].rearrange("p h q -> p (h q)"), [emc_t], False)],
                                              V=vcA[g][0:CR, ct, 0:129], subs=[0, 1, 2, 3],
                                              acc_fn=(lambda sub: (accC[sub // 2][:, sub % 2, 0:129], accC[sub // 2])),
                                              first=(ct == 0), start_subs=(0, 2), fin=fin, rb_score=[kcT[g], qa_lo[g]], rb_v=[vcA[g]], mask_pool=False))
                        last_cmp_idx = len(steps) - 1
                        accW = accR.next()
                        js = list(range(max(0, i - 4), i + 1))
                        for j in js:
                            masks = []
                            if j == i:
                                masks.append((0, 512, EM[:, 0, 4 * g:4 * g + 4, :].rearrange("p h q -> p (h q)"), [EM], False))
                            elif j == i - 1:
                                masks.append((0, 512, EM[:, 1, 4 * g:4 * g + 4, :].rearrange("p h q -> p (h q)"), [EM], False))
                            elif j == i - 4:
                                masks.append((0, 512, EMW[:, :, :].rearrange("p h q -> p (h q)"), [EMW], False))
                            fin = None
                            if j == i:
                                def fin(i=i, g=g, O=O, accW=accW):
                                    s.nsa_fin_branch(i, g, O, None, accW, G, sm, 2)
                            steps.append(dict(pre=None, kT=kwT[g][:, j * 128:(j + 1) * 128], q_fn=qfn_lo(), kk=128, ncol=512, c0=0, scale=0.125, masks=masks,
                                              V=vw[g][:, j, 0:65], subs=[0, 1, 2, 3], acc_fn=(lambda sub, accW=accW: (accW[:, sub, 0:65], accW)),
                                              first=(j == js[0]), fin=fin, rb_score=[kwT[g], qa_lo[g]], rb_v=[vw[g]], mask_pool=True))
                        accS = accR.next()
                        for j in range(0, i + 1):
                            masks = []
                            if j == i:
                                masks.append((0, 512, EM[:, 0, 4 * g:4 * g + 4, :].rearrange("p h q -> p (h q)"), [EM], False))
                            elif j == i - 1:
                                masks.append((0, 512, EM[:, 1, 4 * g:4 * g + 4, :].rearrange("p h q -> p (h q)"), [EM], False))
                            fin = None
                            if j == i:
                                def fin(i=i, g=g, O=O, Ob=Ob, accS=accS):
                                    s.nsa_fin_branch(i, g, O, Ob, accS, G, sm, 1)
                            steps.append(dict(pre=None, kT=ksA[g][:, j * 128:(j + 1) * 128], q_fn=qfn_full(), kk=128, ncol=512, c0=0, scale=0.125, masks=masks,
                                              V=vs[g][:, j, 0:65], subs=[0, 1, 2, 3], acc_fn=(lambda sub, accS=accS: (accS[:, sub, 0:65], accS)),
                                              first=(j == 0), fin=fin, rb_score=[ksA[g], qa_lo[g], qa_hi[g][i]], rb_v=[vs[g]], mask_pool=True, dep_step=last_cmp_idx))
                for st in steps:
                    if st["pre"] is not None:
                        pass
                s.run_steps_pre(steps, sT_ring, pT_ring, skew=2)
        sc.barrier()

    def run_steps_pre(s, steps, sT_ring, pT_ring, skew=1):
        s.run_steps(steps, sT_ring, pT_ring, skew=skew)

    def nsa_fin_cmp(s, i, g, O, accC, G, AFc, sm, imp, NS, TPb, ident, QA, qa_hi):
        sc = s.sc
        m = sm.next()
        for bk in range(2):
            sc.DVE(lambda e: e.tensor_scalar(out=m[:, 2 * bk:2 * bk + 2], in0=accC[bk][:, :, 64:65], scalar1=1e-30, scalar2=None, op0=ALU.max),
                   r=[accC[bk]], w=[m])
        sc.DVE(lambda e: e.reciprocal(out=m[:, 4:8], in_=m[:, 0:4]), r=[m], w=[m])
        sc.DVE(lambda e: e.tensor_tensor(out=m[:, 8:12], in0=m[:, 4:8], in1=G[:, i, 12 * g + 0:12 * g + 12:3], op=ALU.mult), r=[m, G], w=[m])
        im = imp.next()
        for hh in range(4):
            U = accC[hh // 2][:, hh % 2, 65:129]
            if hh == 0:
                sc.DVE(lambda e: e.tensor_scalar(out=im[:, :], in0=U, scalar1=m[:, 4:5], scalar2=None, op0=ALU.mult), r=[accC[0], m], w=[im])
            else:
                sc.DVE(lambda e: e.scalar_tensor_tensor(out=im[:, :], in0=U, scalar=m[:, 4 + hh:5 + hh], in1=im[:, :], op0=ALU.mult, op1=ALU.add),
                       r=[accC[hh // 2], m, im], w=[im])
        sc.DVE(lambda e: e.tensor_tensor(out=im[:, :], in0=im[:, :], in1=AFc[:, i, :], op=ALU.add), r=[im, AFc], w=[im])
        sc.DVE(lambda e: e.max(out=m[:, 16:24], in_=im[:, :]), r=[im], w=[m])
        ns = NS.next()
        sc.DVE(lambda e: e.tensor_scalar(out=ns[:, 64:128], in0=im[:, :], scalar1=m[:, 23:24], scalar2=1.0, op0=ALU.is_ge, op1=ALU.subtract), r=[im, m], w=[ns])
        sc.PE(lambda e: e.transpose(out=TPb[:, 0:128], in_=ns[:, :], identity=ident[:, :]), r=[ns, ident], w=[TPb])
        sc.DVE(lambda e: e.tensor_copy(out=QA[g][64:128, i, 0, :], in_=TPb[64:128, 0:128]), r=[TPb], w=[qa_hi[g][i]])
        for hh in range(1, 4):
            sc.POOL(lambda e: e.tensor_copy(out=QA[g][64:128, i, hh, :], in_=QA[g][64:128, i, 0, :]), r=[qa_hi[g][i]], w=[qa_hi[g][i]])
        for hh in range(4):
            num = accC[hh // 2][:, hh % 2, 0:64]
            sc.DVE(lambda e: e.tensor_scalar(out=O[:, hh * 64:(hh + 1) * 64], in0=num, scalar1=m[:, 8 + hh:9 + hh], scalar2=None, op0=ALU.mult), r=[accC[hh // 2], m], w=[O])

    def nsa_fin_branch(s, i, g, O, Ob, acc, G, sm, br):
        sc = s.sc
        m = sm.next()
        sc.DVE(lambda e: e.tensor_scalar(out=m[:, 0:4], in0=acc[:, :, 64:65], scalar1=1e-30, scalar2=None, op0=ALU.max), r=[acc], w=[m])
        sc.DVE(lambda e: e.reciprocal(out=m[:, 4:8], in_=m[:, 0:4]), r=[m], w=[m])
        sc.DVE(lambda e: e.tensor_tensor(out=m[:, 8:12], in0=m[:, 4:8], in1=G[:, i, 12 * g + br:12 * g + 12:3], op=ALU.mult), r=[m, G], w=[m])
        dst = O if Ob is None else Ob
        for hh in range(4):
            sc.DVE(lambda e: e.scalar_tensor_tensor(out=dst[:, hh * 64:(hh + 1) * 64], in0=acc[:, hh, 0:64], scalar=m[:, 8 + hh:9 + hh], in1=O[:, hh * 64:(hh + 1) * 64],
                                                    op0=ALU.mult, op1=ALU.add), r=[acc, m, O], w=[dst])
        if Ob is not None:
            sc.dma(s.o_tm.ap[i * 128:(i + 1) * 128, 256 * g:256 * g + 256], Ob[:, :], r=[Ob], w=[Buf()], q="pool")

    def phase_causal(s, l, kind):
        sc = s.sc
        S, NT, NB = s.S, s.NT, s.NB
        if kind == "mla":
            K, qd, kd, vd, scale, ocol, clamp = 96, s.mqT, s.mkT, s.mv, 96 ** -0.5, 512, False
        else:
            K, qd, kd, vd, scale, ocol, clamp = 70, s.fqT, s.fkT, s.fv, 0.125, 768, True
        with ExitStack() as es:
            tri = s.sb(es, "ca_tri", [128, 128], BF16)
            sc.dma(tri[:, :], s.c_tri.ap[:, :], w=[tri])
            qT = [s.sb(es, "ca_q%d" % h, [K, S], BF16) for h in range(4)]
            kT = [s.sb(es, "ca_k%d" % h, [K, S], BF16) for h in range(4)]
            V = [s.sb(es, "ca_v%d" % h, [128, NT, 72], BF16) for h in range(4)]
            for h in range(4):
                sc.dma(qT[h][:, :], qd.ap[h], w=[qT[h]])
                sc.dma(kT[h][:, :], kd.ap[h], w=[kT[h]])
                sc.dma(V[h][:, :, :], vd.ap[:, h, :].rearrange("(t p) e -> p t e", p=128), w=[V[h]])
            sT_ring = Ring([s.ps(es, "ca_sT%d" % i, [128, 512], F32) for i in range(4)])
            accR = Ring([s.ps(es, "ca_acc%d" % i, [128, 4, 128], F32) for i in range(3)])
            pT_ring = Ring([s.sb(es, "ca_pT%d" % i, [128, 512], BF16) for i in range(6)])
            osb = Ring([s.sb(es, "ca_o%d" % i, [128, 4, 256], BF16) for i in range(2)])
            sm = Ring([s.sb(es, "ca_sm%d" % i, [128, 8], F32) for i in range(3)])
            steps = []
            for qb in range(NB):
                o_ = osb.next()
                for h in range(4):
                    acc = accR.next()
                    nk = 4 * qb + 4
                    for j in range(nk):
                        sp = j - 4 * qb
                        c0 = max(0, sp) * 128
                        masks = []
                        if sp >= 0:
                            masks.append((c0, c0 + 128, tri[:, :], [tri], clamp))
                        subs = list(range(max(0, sp), 4))
                        fin = None
                        if j == nk - 1:
                            def fin(qb=qb, h=h, acc=acc, o_=o_):
                                m = sm.next()
                                sc.DVE(lambda e: e.tensor_scalar(out=m[:, 0:4], in0=acc[:, :, 64:65], scalar1=1e-30, scalar2=None, op0=ALU.max), r=[acc], w=[m])
                                sc.DVE(lambda e: e.reciprocal(out=m[:, 4:8], in_=m[:, 0:4]), r=[m], w=[m])
                                for sub in range(4):
                                    sc.DVE(lambda e: e.tensor_scalar(out=o_[:, sub, 64 * h:64 * h + 64], in0=acc[:, sub, 0:64], scalar1=m[:, 4 + sub:5 + sub], scalar2=None, op0=ALU.mult),
                                           r=[acc, m], w=[o_])
                                if h == 3:
                                    sc.dma(s.o_tm.ap[qb * 512:(qb + 1) * 512, ocol:ocol + 256].rearrange("(s p) c -> p s c", p=128), o_[:, :, :], r=[o_], w=[Buf()], q="sp")
                        steps.append(dict(kT=kT[h][:, j * 128:(j + 1) * 128], q_fn=(lambda c0, c1, h=h, qb=qb: qT[h][:, qb * 512 + c0:qb * 512 + c1]),
                                          kk=128, ncol=512, c0=c0, scale=scale, masks=masks, V=V[h][:, j, 0:65], subs=subs,
                                          acc_fn=(lambda sub, acc=acc: (acc[:, sub, 0:65], acc)), first=(j == 0), fin=fin,
                                          rb_score=[kT[h], qT[h]], rb_v=[V[h]], mask_pool=False))
            s.run_steps(steps, sT_ring, pT_ring, skew=3)
        sc.barrier()

    def phase_combine(s, l, xsrc, xdst):
        sc = s.sc
        S, NT, NB = s.S, s.NT, s.NB
        with ExitStack() as es:
            ident = s.sb(es, "ident", [128, 128], BF16)
            sc.dma(ident[:, :], s.c_ident.ap[:, :], w=[ident])
            stg = s.stages(es, 2)
            Wg = s.sb(es, "cb_wg", [128, 8, 3 * D], BF16)
            Wb = s.sb(es, "cb_wb", [128, 8, D], BF16)
            Wo = s.sb(es, "cb_wo", [128, 8, D], BF16)
            s.load_w(TTv(Wb, 0, 4), s.w_br_nsa.ap[l], 128, 4, D, stg)
            s.load_w(TTv(Wb, 4, 2), s.w_br_mla.ap[l], 128, 2, D, stg)
            s.load_w(TTv(Wb, 6, 2), s.w_br_fox.ap[l], 128, 2, D, stg)
            s.load_w(Wg, s.w_gate.ap[l], 128, 8, 3 * D, stg, order=[0, 4, 8, 1, 5, 9, 2, 6, 10, 3, 7, 11])
            s.load_w(Wo, s.w_mix_out.ap[l], 128, 8, D, stg)
            R = s.ln_setup(es, l, 0)
            PG = Ring([s.ps(es, "cb_pg%d" % i, [128, 512], F32) for i in range(2)])
            PP = Ring([s.ps(es, "cb_pp%d" % i, [128, 512], F32) for i in range(2)])
            PY = Ring([s.ps(es, "cb_py%d" % i, [128, 1024], F32) for i in range(1)])
            TPr = Ring([s.ps(es, "cb_tp%d" % i, [128, 8, 128], BF16) for i in range(2)])
            otm = Ring([s.sb(es, "cb_otm%d" % i, [128, D], BF16) for i in range(2)])
            oT = Ring([s.sb(es, "cb_oT%d" % i, [128, 8, 512], BF16) for i in range(2)])
            xTb = Ring([s.sb(es, "cb_xT%d" % i, [128, 8, 512], BF16) for i in range(2)])
            mT = Ring([s.sb(es, "cb_mT%d" % i, [128, 8, 512], BF16) for i in range(1)])
            sg = Ring([s.sb(es, "cb_sg%d" % i, [128, 512], F32) for i in range(2)])
            tA = Ring([s.sb(es, "cb_tA%d" % i, [128, 512], F32) for i in range(2)])
            tB = Ring([s.sb(es, "cb_tB%d" % i, [128, 512], F32) for i in range(2)])
            for b in range(NB):
                bs = slice(b * 512, (b + 1) * 512)
                oT_ = oT.next()
                for tt in range(4):
                    t = b * 4 + tt
                    ot = otm.next()
                    sc.dma(ot[:, :], s.o_tm.ap[t * 128:(t + 1) * 128, :], w=[ot], q="pool")
                    tp = TPr.next()
                    for kc in range(8):
                        sc.PE(lambda e: e.transpose(out=tp[:, kc, :], in_=ot[:, kc * 128:(kc + 1) * 128], identity=ident[:, :]), r=[ot, ident], w=[tp])
                    sc.ACT(lambda e: e.copy(out=oT_[:, :, tt * 128:(tt + 1) * 128], in_=tp[:, :, :]), r=[tp], w=[oT_])
                x_ = xTb.next()
                sc.dma(x_[:, :, :], s.xT.ap.rearrange("k p s -> p k s")[:, :, bs], w=[x_], q="pool")
                m_ = mT.next()
                brk = [(0, 4), (4, 6), (6, 8)]
                for n in range(8):
                    ns_ = slice(n * 128, (n + 1) * 128)
                    tA_ = tA.next()
                    for br in range(3):
                        pg = PG.next()
                        for kc in range(8):
                            sc.PE(lambda e: e.matmul(pg[:, :], lhsT=Wg[:, kc, br * D + n * 128:br * D + (n + 1) * 128], rhs=x_[:, kc, :], start=(kc == 0), stop=(kc == 7)), r=s.wr(Wg, br * D + n * 128, br * D + (n + 1) * 128) + [x_], w=[pg])
                        pp = PP.next()
                        k0, k1 = brk[br]
                        for kc in range(k0, k1):
                            sc.PE(lambda e: e.matmul(pp[:, :], lhsT=Wb[:, kc, ns_], rhs=oT_[:, kc, :], start=(kc == k0), stop=(kc == k1 - 1)), r=s.wr(Wb, n * 128, (n + 1) * 128, kc, kc + 1) + [oT_], w=[pp])
                        sg_ = sg.next()
                        sc.ACT(lambda e: e.activation(out=sg_[:, :], in_=pg[:, :], func=AF.Sigmoid), r=[pg], w=[sg_])
                        if br == 0:
                            sc.DVE(lambda e: e.tensor_tensor(out=tA_[:, :], in0=pp[:, :], in1=sg_[:, :], op=ALU.mult), r=[pp, sg_], w=[tA_])
                        else:
                            tB_ = tB.next()
                            sc.DVE(lambda e: e.tensor_tensor(out=tB_[:, :], in0=pp[:, :], in1=sg_[:, :], op=ALU.mult), r=[pp, sg_], w=[tB_])
                            if br == 1:
                                sc.DVE(lambda e: e.tensor_tensor(out=tA_[:, :], in0=tA_[:, :], in1=tB_[:, :], op=ALU.add), r=[tA_, tB_], w=[tA_])
                            else:
                                sc.DVE(lambda e: e.tensor_tensor(out=m_[:, n, :], in0=tA_[:, :], in1=tB_[:, :], op=ALU.add), r=[tA_, tB_], w=[m_])
                for tt in range(4):
                    t = b * 4 + tt
                    py = PY.next()
                    for hf in range(2):
                        for kc in range(8):
                            sc.PE(lambda e: e.matmul(py[:, hf * 512:(hf + 1) * 512], lhsT=m_[:, kc, tt * 128:(tt + 1) * 128], rhs=Wo[:, kc, hf * 512:(hf + 1) * 512], start=(kc == 0), stop=(kc == 7)),
                                  r=s.wr(Wo, hf * 512, (hf + 1) * 512) + [m_], w=[py])
                    s.layer_norm_tile(R, py, t, xsrc, xdst, TPr, ident)
            s.ln_flush(R)
        sc.barrier()

    def phase_cross(s, l, xsrc, xdst):
        sc = s.sc
        S, NT, NB = s.S, s.NT, s.NB
        with ExitStack() as es:
            ident = s.sb(es, "ident", [128, 128], BF16)
            sc.dma(ident[:, :], s.c_ident.ap[:, :], w=[ident])
            stg = s.stages(es)
            Wq = s.sb(es, "xc_wq", [128, 8, 256], BF16)
            Wkv = s.sb(es, "xc_wkv", [128, 8, 512], BF16)
            Wo = s.sb(es, "xc_wo", [128, 2, D], BF16)
            s.load_w(Wq, s.xa_w_q.ap[l], 128, 8, 256, stg)
            s.load_w(Wkv, s.xa_w_kv.ap[l], 128, 8, 512, stg)
            s.load_w(Wo, s.xa_w_o.ap[l], 128, 2, D, stg)
            memT = s.sb(es, "xc_memT", [128, 8, MEM], BF16)
            sc.dma(memT[:, :, :], s.memT.ap.rearrange("k p m -> p k m"), w=[memT])
            R = s.ln_setup(es, l, 1, depth=2)
            osb = s.sb(es, "xc_oall", [128, NT, 256], BF16)
            ob = [Buf() for _ in range(NB)]
            qx = s.sb(es, "xc_qx", [64, 4, S], BF16)
            qb_ = [Buf() for _ in range(NB)]
            with ExitStack() as es1:
                sT_ring = Ring([s.ps(es1, "xc_sT%d" % i, [128, 512], F32) for i in range(3)])
                accR = Ring([s.ps(es1, "xc_acc%d" % i, [128, 4, 128], F32) for i in range(3)])
                PQ = Ring([s.ps(es1, "xc_pq%d" % i, [128, 512], F32) for i in range(2)])
                pT_ring = Ring([s.sb(es1, "xc_pT%d" % i, [128, 512], BF16) for i in range(5)])
                kT = s.sb(es1, "xc_kT", [64, 4, MEM], BF16)
                V = s.sb(es1, "xc_V", [128, 2, 4, 72], BF16)
                sc.POOL(lambda e: e.memset(V[:, :, :, :], 1.0), w=[V])
                for h in range(4):
                    pq = PQ.next()
                    for kc in range(8):
                        sc.PE(lambda e: e.matmul(pq[0:64, 0:MEM], lhsT=Wkv[:, kc, 64 * h:64 * h + 64], rhs=memT[:, kc, :], start=(kc == 0), stop=(kc == 7)), r=[Wkv, memT], w=[pq])
                    sc.ACT(lambda e: e.copy(out=kT[:, h, :], in_=pq[0:64, 0:MEM]), r=[pq], w=[kT])
                for t in range(2):
                    pq = PQ.next()
                    for kc in range(8):
                        sc.PE(lambda e: e.matmul(pq[:, 0:256], lhsT=memT[:, kc, t * 128:(t + 1) * 128], rhs=Wkv[:, kc, 256:512], start=(kc == 0), stop=(kc == 7)), r=[Wkv, memT], w=[pq])
                    sc.ACT(lambda e: e.copy(out=V[:, t, :, 0:64], in_=pq[:, 0:256].rearrange("p (g d) -> p g d", g=4)), r=[pq], w=[V])
                xTb = Ring([s.sb(es1, "xc_xT%d" % i, [128, 8, 512], BF16) for i in range(2)])
                sm = Ring([s.sb(es1, "xc_sm%d" % i, [128, 8], F32) for i in range(3)])
                steps = []
                for b in range(NB):
                    bs = slice(b * 512, (b + 1) * 512)
                    x_ = xTb.next()
                    sc.dma(x_[:, :, :], s.xT.ap.rearrange("k p s -> p k s")[:, :, bs], w=[x_])
                    for h in range(4):
                        pq = PQ.next()
                        for kc in range(8):
                            sc.PE(lambda e: e.matmul(pq[0:64, :], lhsT=Wq[:, kc, 64 * h:64 * h + 64], rhs=x_[:, kc, :], start=(kc == 0), stop=(kc == 7)), r=[Wq, x_], w=[pq])
                        if h % 2 == 0:
                            sc.DVE(lambda e: e.tensor_copy(out=qx[:, h, bs], in_=pq[0:64, :]), r=[pq], w=[qb_[b]])
                        else:
                            sc.ACT(lambda e: e.copy(out=qx[:, h, bs], in_=pq[0:64, :]), r=[pq], w=[qb_[b]])
                    for h in range(4):
                        acc = accR.next()
                        for j in range(2):
                            fin = None
                            if j == 1:
                                def fin(h=h, acc=acc, b=b):
                                    m = sm.next()
                                    sc.DVE(lambda e: e.reciprocal(out=m[:, 4:8], in_=acc[:, :, 64:65]), r=[acc], w=[m])
                                    for sub in range(4):
                                        sc.DVE(lambda e: e.tensor_scalar(out=osb[:, 4 * b + sub, 64 * h:64 * h + 64], in0=acc[:, sub, 0:64], scalar1=m[:, 4 + sub:5 + sub], scalar2=None, op0=ALU.mult),
                                               r=[acc, m], w=[ob[b]])
                            steps.append(dict(kT=kT[:, h, j * 128:(j + 1) * 128], q_fn=(lambda c0, c1, h=h, b=b: qx[:, h, b * 512 + c0:b * 512 + c1]), kk=128, ncol=512, c0=0, scale=0.125,
                                              masks=[], V=V[:, j, h, 0:65], subs=[0, 1, 2, 3], acc_fn=(lambda sub, acc=acc: (acc[:, sub, 0:65], acc)), first=(j == 0), fin=fin,
                                              rb_score=[kT, qb_[b]], rb_v=[V], mask_pool=False))
                s.run_steps(steps, sT_ring, pT_ring, skew=2)
            sc.barrier()
            with ExitStack() as es2:
                TPo = Ring([s.ps(es2, "xc_tpo%d" % i, [128, 8, 128], BF16) for i in range(2)])
                PY = Ring([s.ps(es2, "xc_py%d" % i, [128, 1024], F32) for i in range(2)])
                TPr = Ring([s.ps(es2, "xc_tp%d" % i, [128, 8, 128], BF16) for i in range(2)])
                oTt = Ring([s.sb(es2, "xc_oT%d" % i, [128, 2, 128], BF16) for i in range(3)])
                for t in range(NT):
                    tp = TPo.next()
                    for kc in range(2):
                        sc.PE(lambda e: e.transpose(out=tp[:, kc, :], in_=osb[:, t, kc * 128:(kc + 1) * 128], identity=ident[:, :]), r=[ob[t // 4], ident], w=[tp])
                    oT_ = oTt.next()
                    sc.ACT(lambda e: e.copy(out=oT_[:, :, :], in_=tp[:, 0:2, :]), r=[tp], w=[oT_])
                    py = PY.next()
                    for hf in range(2):
                        for kc in range(2):
                            sc.PE(lambda e: e.matmul(py[:, hf * 512:(hf + 1) * 512], lhsT=oT_[:, kc, :], rhs=Wo[:, kc, hf * 512:(hf + 1) * 512], start=(kc == 0), stop=(kc == 1)), r=[Wo, oT_], w=[py])
                    s.layer_norm_tile(R, py, t, xsrc, xdst, TPr, ident)
                s.ln_flush(R)
        sc.barrier()

    def phase_mlp_up(s, l):
        sc = s.sc
        S, NT, NB = s.S, s.NT, s.NB
        with ExitStack() as es:
            stg = s.stages(es)
            W = s.sb(es, "mu_w", [128, 8, 4 * D], BF16)
            s.load_w(W, s.w_up.ap[l], 128, 8, 4 * D, stg)
            PB = Ring([s.ps(es, "mu_pb%d" % i, [128, 512], F32) for i in range(4)])
            xTb = Ring([s.sb(es, "mu_xT%d" % i, [128, 8, 512], BF16) for i in range(2)])
            r_ = Ring([s.sb(es, "mu_r%d" % i, [128, 512], F32) for i in range(3)])
            h_ = Ring([s.sb(es, "mu_h%d" % i, [128, 512], BF16) for i in range(3)])
            for b in range(NB):
                bs = slice(b * 512, (b + 1) * 512)
                x_ = xTb.next()
                sc.dma(x_[:, :, :], s.xT.ap.rearrange("k p s -> p k s")[:, :, bs], w=[x_])
                for n in range(32):
                    pb = PB.next()
                    for kc in range(8):
                        sc.PE(lambda e: e.matmul(pb[:, :], lhsT=W[:, kc, n * 128:(n + 1) * 128], rhs=x_[:, kc, :], start=(kc == 0), stop=(kc == 7)), r=s.wr(W, n * 128, (n + 1) * 128) + [x_], w=[pb])
                    rr = r_.next()
                    sc.ACT(lambda e: e.activation(out=rr[:, :], in_=pb[:, :], func=AF.Relu), r=[pb], w=[rr])
                    hh = h_.next()
                    if n % 2 == 0:
                        sc.POOL(lambda e: e.tensor_tensor(out=hh[:, :], in0=rr[:, :], in1=rr[:, :], op=ALU.mult), r=[rr], w=[hh])
                    else:
                        sc.DVE(lambda e: e.tensor_tensor(out=hh[:, :], in0=rr[:, :], in1=rr[:, :], op=ALU.mult), r=[rr], w=[hh])
                    sc.dma(s.hT.ap[n, :, bs], hh[:, :], r=[hh], w=[Buf()], q="sp")
        sc.barrier()

    def phase_mlp_down(s, l, xsrc, xdst, final):
        sc = s.sc
        S, NT, NB = s.S, s.NT, s.NB
        with ExitStack() as es:
            ident = s.sb(es, "ident", [128, 128], BF16)
            sc.dma(ident[:, :], s.c_ident.ap[:, :], w=[ident])
            stg = s.stages(es)
            W = s.sb(es, "md_w", [128, 32, D], BF16)
            W.kb = []
            for c in range(16):
                st = stg.next()
                sv = st[:, 0:2048].rearrange("p (a n) -> p a n", a=2)
                sc.dma(sv, s.w_down.ap[l, c * 256:(c + 1) * 256, :].rearrange("(a p) n -> p a n", p=128), w=[st])
                cb = Buf()
                W.kb.append(cb)
                if c % 3 == 0:
                    sc.DVE(lambda e: e.tensor_copy(out=W[:, 2 * c:2 * c + 2, :], in_=sv), r=[st], w=[cb])
                elif c % 3 == 1:
                    sc.ACT(lambda e: e.copy(out=W[:, 2 * c:2 * c + 2, :], in_=sv), r=[st], w=[cb])
                else:
                    sc.POOL(lambda e: e.tensor_copy(out=W[:, 2 * c:2 * c + 2, :], in_=sv), r=[st], w=[cb])
            R = s.ln_setup(es, l, 2, depth=2)
            PY = Ring([s.ps(es, "md_py%d" % i, [128, 1024], F32) for i in range(2)])
            TPr = Ring([s.ps(es, "md_tp%d" % i, [128, 8, 128], BF16) for i in range(2)])
            hb = Ring([s.sb(es, "md_h%d" % i, [128, 32, 256], BF16) for i in range(2)])
            for b2 in range(NT // 2):
                bs = slice(b2 * 256, (b2 + 1) * 256)
                h_ = hb.next()
                for qq in range(4):
                    sc.dma(h_[:, qq * 8:(qq + 1) * 8, :], s.hT.ap.rearrange("k p s -> p k s")[:, qq * 8:(qq + 1) * 8, bs], w=[h_], q="pool")
                for tt in range(2):
                    t = b2 * 2 + tt
                    py = PY.next()
                    for hf in range(2):
                        for kc in range(32):
                            sc.PE(lambda e: e.matmul(py[:, hf * 512:(hf + 1) * 512], lhsT=h_[:, kc, tt * 128:(tt + 1) * 128], rhs=W[:, kc, hf * 512:(hf + 1) * 512], start=(kc == 0), stop=(kc == 31)),
                                  r=[W.kb[kc // 2], h_], w=[py])
                    s.layer_norm_tile(R, py, t, xsrc, xdst, TPr, ident, final=final)
            s.ln_flush(R)
        sc.barrier()


class TTv:
    def __init__(s, tt, a0, n):
        s.tt = tt
        s.a0 = a0
        s.buf = tt.buf

    def __getitem__(s, k):
        p, a, c = k
        if isinstance(a, slice):
            a = slice((a.start or 0) + s.a0, (a.stop if a.stop is not None else 0) + s.a0)
        else:
            a = a + s.a0
        return s.tt[p, a, c]


_CACHE = {}


def get_prog(S, depth, debug=None):
    key = (S, depth, tuple(sorted(debug)) if debug else None)
    if key not in _CACHE:
        p = Prog(S, depth, debug=debug)
        p.build()
        _CACHE[key] = p
    return _CACHE[key]


def make_in_maps(inputs, S, depth, ncores):
    consts = host_consts(S)
    near, cmpb, c31 = t5_gather(np.asarray(inputs["t5_table"], np.float32), consts)
    shared = {}
    for k in ("w_in", "cmp_pe", "cmp_w1", "cmp_w2", "mla_q_norm", "mla_w_uq", "mla_kv_norm", "mla_w_ukv", "fox_b_f", "w_gate",
              "w_br_nsa", "w_br_mla", "w_br_fox", "w_mix_out", "xa_w_q", "xa_w_kv", "xa_w_o", "mlp_w_up", "mlp_w_down", "ln_g", "ln_b"):
        shared[k] = np.ascontiguousarray(np.asarray(inputs[k], np.float32)[:depth])
    for k in ("ident_bf", "tri_bf", "tri4_bf", "atri4_bf", "nearmask", "rope_cs", "rope_ss", "overlap_bf", "aforced", "expand_bf", "ones_bf", "cmp_valid"):
        shared[k] = consts[k]
    shared["t5_near"] = near
    shared["t5_cmpb"] = cmpb
    shared["t5_c31"] = c31
    maps = []
    for c in range(ncores):
        m = dict(shared)
        m["x"] = np.ascontiguousarray(np.asarray(inputs["x"][c], np.float32))
        m["mem"] = np.ascontiguousarray(np.asarray(inputs["mem"][c], np.float32))
        maps.append(m)
    return maps


def kernel(**inputs):
    S, depth, ncores = SEQ_FULL, DEPTH_FULL, 8
    p = get_prog(S, depth)
    maps = make_in_maps(inputs, S, depth, ncores)
    res = run_bass_kernel_spmd(p.nc, maps, core_ids=list(range(ncores)))
    out = np.stack([np.asarray(r["y"], np.float32) for r in res.results], 0)
    return out
```

```python
import math
from contextlib import ExitStack
import numpy as np
import ml_dtypes
import concourse.bass as bass
import concourse.mybir as mybir
from concourse.bass_utils import run_bass_kernel_spmd

F32 = mybir.dt.float32
BF16 = mybir.dt.bfloat16
AF = mybir.ActivationFunctionType
ALU = mybir.AluOpType
AX = mybir.AxisListType

D = 1024
DEPTH_FULL = 4
SEQ_FULL = 4096
MEM = 256
N_IN = 2620
DN_ALPHA = (2 * DEPTH_FULL) ** 0.25
BIG = 30000.0


class Tok:
    __slots__ = ("sem", "val", "know")

    def __init__(s, sem, val, know):
        s.sem = sem
        s.val = val
        s.know = know


class Buf:
    __slots__ = ("w", "r", "name")

    def __init__(s, name=""):
        s.w = None
        s.r = {}
        s.name = name


class TT:
    def __init__(s, h, name=""):
        s.h = h
        s.buf = Buf(name)

    def __getitem__(s, k):
        return s.h[k]


class Eng:
    def __init__(s, name, e, sem, semid):
        s.name = name
        s.e = e
        s.sem = sem
        s.semid = semid
        s.cnt = 0
        s.know = {}


class Lane:
    def __init__(s, sem, semid):
        s.sem = sem
        s.semid = semid
        s.val = 0


def _bufs(xs):
    out = []
    for x in xs:
        if x is None:
            continue
        out.append(x.buf if hasattr(x, "buf") else x)
    return out


class Sched:
    NL = 8

    def __init__(s, nc, es):
        s.nc = nc
        s.sems = []

        def mk(n):
            sem = es.enter_context(nc.semaphore(n))
            s.sems.append(sem)
            return sem, len(s.sems) - 1

        s.pe = Eng("pe", nc.tensor, *mk("s_pe"))
        s.dve = Eng("dve", nc.vector, *mk("s_dve"))
        s.act = Eng("act", nc.scalar, *mk("s_act"))
        s.pool = Eng("pool", nc.gpsimd, *mk("s_pool"))
        s.sp = Eng("sp", nc.sync, *mk("s_sp"))
        s.engs = [s.pe, s.dve, s.act, s.pool, s.sp]
        s.lanes = {}
        s.rr = {}
        for q in ("sp", "pool"):
            s.lanes[q] = [Lane(*mk("l_%s_%d" % (q, i))) for i in range(s.NL)]
            s.rr[q] = 0
        s.q = {"sp": s.sp, "pool": s.pool}
        s.ninst = 0

    def _wait(s, E, tok):
        if tok is None:
            return
        if E.know.get(tok.sem, 0) >= tok.val:
            return
        if tok.sem == E.semid and E is s.pe:
            return
        E.e.wait_ge(s.sems[tok.sem], tok.val)
        k = dict(E.know)
        for a, b in tok.know.items():
            if k.get(a, 0) < b:
                k[a] = b
        if k.get(tok.sem, 0) < tok.val:
            k[tok.sem] = tok.val
        E.know = k

    def _deps(s, E, r, w):
        for b in r:
            s._wait(E, b.w)
        for b in w:
            s._wait(E, b.w)
            for t in list(b.r.values()):
                s._wait(E, t)

    def op(s, E, fn, r=(), w=()):
        r = _bufs(r)
        w = _bufs(w)
        s._deps(E, r, w)
        inst = fn(E.e)
        E.cnt += 1
        inst.then_inc(E.sem, 1)
        tok = Tok(E.semid, E.cnt, E.know)
        for b in r:
            b.r[E.semid] = tok
        for b in w:
            b.w = tok
            b.r = {}
        s.ninst += 1
        return tok

    def PE(s, fn, r=(), w=()):
        return s.op(s.pe, fn, r, w)

    def DVE(s, fn, r=(), w=()):
        return s.op(s.dve, fn, r, w)

    def ACT(s, fn, r=(), w=()):
        return s.op(s.act, fn, r, w)

    def POOL(s, fn, r=(), w=()):
        return s.op(s.pool, fn, r, w)

    def dma(s, out, in_, r=(), w=(), q="sp", **kw):
        Q = s.q[q]
        r = _bufs(r)
        w = _bufs(w)
        s._deps(Q, r, w)
        lanes = s.lanes[q]
        i = s.rr[q]
        s.rr[q] = (i + 1) % len(lanes)
        lane = lanes[i]
        if lane.val > 0:
            s._wait(Q, Tok(lane.semid, lane.val, {}))
        inst = Q.e.dma_start(out=out, in_=in_, **kw)
        lane.val += 16
        inst.then_inc(lane.sem, 16)
        tok = Tok(lane.semid, lane.val, Q.know)
        for b in r:
            b.r[lane.semid] = tok
        for b in w:
            b.w = tok
            b.r = {}
        s.ninst += 1
        return tok

    def barrier(s):
        toks = [Tok(E.semid, E.cnt, {}) for E in s.engs if E.cnt > 0]
        for q in s.lanes:
            for l in s.lanes[q]:
                if l.val > 0:
                    toks.append(Tok(l.semid, l.val, {}))
        for E in s.engs:
            for t in toks:
                s._wait(E, t)


class Ring:
    def __init__(s, items):
        s.items = items
        s.i = 0

    def next(s):
        x = s.items[s.i]
        s.i = (s.i + 1) % len(s.items)
        return x


class DT:
    def __init__(s, ap, name):
        s.ap = ap
        s.name = name
        s.bufs = {}
        s.buf = Buf(name)

    def b(s, key):
        if key not in s.bufs:
            s.bufs[key] = Buf("%s_%s" % (s.name, key))
        return s.bufs[key]


def t5_bucket_np(dist):
    n = np.maximum(dist, 0)
    nf = np.maximum(n, 1).astype(np.float32)
    large = 16 + (np.log(nf / np.float32(16)) / np.float32(math.log(128 / 16)) * np.float32(16)).astype(np.int32)
    large = np.minimum(large, 31)
    return np.where(n < 16, n, large)


def host_consts(S):
    NT = S // 128
    NCP = S // 16
    c = {}
    c["ident_bf"] = np.eye(128, dtype=np.float32).astype(ml_dtypes.bfloat16)
    k = np.arange(128)[:, None]
    q = np.arange(128)[None, :]
    tri = (q >= k).astype(np.float32)
    c["tri_bf"] = tri.astype(ml_dtypes.bfloat16)
    c["tri4_bf"] = np.tile(tri[:, None, :], (1, 4, 1)).astype(ml_dtypes.bfloat16)
    atri = (k > q).astype(np.float32)
    c["atri4_bf"] = np.tile(atri[:, None, :], (1, 4, 1)).astype(ml_dtypes.bfloat16)
    m = np.ones((2, 128, 8, 128), np.float32)
    m[0] = np.tile(tri[:, None, :], (1, 8, 1))
    c["nearmask"] = m
    half = 16
    inv = (10000.0 ** (-np.arange(half, dtype=np.float32) / half)).astype(np.float32)
    ang = np.arange(S, dtype=np.float32)[None, :] * inv[:, None]
    cos = np.cos(ang).astype(np.float32)
    sin = np.sin(ang).astype(np.float32)
    cs = np.zeros((96, S), np.float32)
    ss = np.zeros((96, S), np.float32)
    for base in (0, 64):
        cs[base:base + 16] = cos
        cs[base + 16:base + 32] = cos
        ss[base:base + 16] = -sin
        ss[base + 16:base + 32] = sin
    c["rope_cs"] = cs
    c["rope_ss"] = ss
    n_slc = S // 64
    c_lo = np.arange(NCP)[:, None] * 16
    s_lo = np.arange(64)[None, :] * 64
    ov = np.maximum(np.minimum(c_lo + 32, s_lo + 64) - np.maximum(c_lo, s_lo), 0).astype(np.float32) / 16.0
    ov[:, n_slc:] = 0.0
    c["overlap_bf"] = ov.astype(ml_dtypes.bfloat16)
    t = np.arange(S)[:, None]
    blk = np.arange(64)[None, :]
    cur = t // 64
    forced = (blk == 0) | (blk == cur) | (blk == cur - 1)
    causal = (blk * 64 <= t) & (blk < n_slc)
    A = np.where(causal, np.where(forced, 1e30, 0.0), -1e30).astype(np.float32)
    c["aforced"] = A
    ex = np.zeros((64, S), np.float32)
    ex[np.arange(S) // 64, np.arange(S)] = BIG
    c["expand_bf"] = ex.astype(ml_dtypes.bfloat16)
    c["ones_bf"] = np.ones((128, 512), np.float32).astype(ml_dtypes.bfloat16)
    OFF = 8 * (NT - 1)
    NROW = ((NCP + OFF + 127) // 128) * 128
    cc = np.arange(NROW)[:, None] - OFF
    ql = np.arange(128)[None, :]
    dist = ql - 16 * cc - 31
    c["cmp_dist"] = dist
    vc = (dist >= 0).astype(np.float32)
    c["cmp_valid"] = np.tile(vc[:, None, :], (1, 8, 1)).astype(np.float32)
    return c


def t5_gather(t5_table, consts):
    k = np.arange(128)[:, None]
    q = np.arange(128)[None, :]
    b0 = t5_bucket_np(q - k)
    b1 = t5_bucket_np(128 + q - k)
    near = np.stack([t5_table[b0], t5_table[b1]], 0)
    near = np.ascontiguousarray(near.transpose(0, 1, 3, 2))
    bc = t5_bucket_np(consts["cmp_dist"])
    cmpb = np.ascontiguousarray(t5_table[bc].transpose(0, 2, 1))
    c31 = np.ascontiguousarray(np.broadcast_to(t5_table[31][None, :, None], (128, 8, 128)))
    return near.astype(np.float32), cmpb.astype(np.float32), c31.astype(np.float32)


class Prog:
    def __init__(s, S, depth, debug=False):
        s.S = S
        s.depth = depth
        s.debug = debug
        s.NT = S // 128
        s.NB = S // 512
        s.NCP = S // 16
        s.NCMP = (S - 32) // 16 + 1
        s.CR = min(128, s.NCP)
        s.CT = (s.NCP + 127) // 128
        s.OFF = 8 * (s.NT - 1)
        s.nc = bass.Bass("TRN2", target_bir_lowering=False)
        s.es = ExitStack()
        s.sc = Sched(s.nc, s.es)
        s.dbg_outs = []
        s.inputs = {}

    def din(s, name, shape, dt=F32):
        ap = s.nc.dram_tensor(name, list(shape), dt, kind="ExternalInput").ap()
        s.inputs[name] = ap
        return DT(ap, name)

    def dscr(s, name, shape, dt, out=False):
        kind = "ExternalOutput" if (out or (s.debug and name in s.debug)) else "Internal"
        if kind == "ExternalOutput":
            s.dbg_outs.append(name)
        ap = s.nc.dram_tensor(name, list(shape), dt, kind=kind).ap()
        return DT(ap, name)

    def sb(s, es, name, shape, dt):
        s.uid = getattr(s, "uid", 0) + 1
        name = "%s_u%d" % (name, s.uid)
        return TT(es.enter_context(s.nc.sbuf_tensor(name, list(shape), dt)), name)

    def ps(s, es, name, shape, dt=F32):
        s.uid = getattr(s, "uid", 0) + 1
        name = "%s_u%d" % (name, s.uid)
        return TT(es.enter_context(s.nc.psum_tensor(name, list(shape), dt)), name)

    def load_w(s, dst, src2d, P, A, N, stage_ring, dcol=0, order=None):
        sc = s.sc
        src3 = src2d.rearrange("(a p) n -> p a n", p=P)
        maxc = max(1, 2048 // A)
        chunks = []
        c0 = 0
        while c0 < N:
            ncol = min(maxc, N - c0)
            chunks.append((c0, ncol))
            c0 += ncol
        if order is not None:
            chunks = [chunks[i] for i in order]
        base = dst.tt if hasattr(dst, "tt") else dst
        if not hasattr(base, "cbufs"):
            base.cbufs = []
        a0 = getattr(dst, "a0", 0)
        for (c0, ncol) in chunks:
            st = stage_ring.next()
            sv = st[0:P, 0:A * ncol].rearrange("p (a n) -> p a n", a=A)
            sc.dma(sv, src3[:, :, c0:c0 + ncol], w=[st])
            cb = Buf()
            base.cbufs.append((a0, a0 + A, dcol + c0, dcol + c0 + ncol, cb))
            s.castk = getattr(s, "castk", 0) + 1
            k = s.castk % 3
            dv = dst[0:P, 0:A, dcol + c0:dcol + c0 + ncol]
            if k == 0:
                sc.DVE(lambda e: e.tensor_copy(out=dv, in_=sv), r=[st], w=[cb])
            elif k == 1:
                sc.ACT(lambda e: e.copy(out=dv, in_=sv), r=[st], w=[cb])
            else:
                sc.POOL(lambda e: e.tensor_copy(out=dv, in_=sv), r=[st], w=[cb])

    def wr(s, W, c0, c1, a0=0, a1=10 ** 9):
        cb = getattr(W, "cbufs", None)
        if not cb:
            return [W.buf]
        out = [b for (x0, x1, y0, y1, b) in cb if y0 < c1 and c0 < y1 and x0 < a1 and a0 < x1]
        return out + [W.buf]

    def stages(s, es, n=3):
        return Ring([s.sb(es, "wstage%d" % i, [128, 2048], F32) for i in range(n)])

    def build(s):
        S, NT, NB = s.S, s.NT, s.NB
        L = s.depth
        s.x_in = s.din("x", [S, D])
        s.mem_in = s.din("mem", [MEM, D])
        s.w_in = s.din("w_in", [L, D, N_IN])
        s.cmp_pe = s.din("cmp_pe", [L, 2, 32, 64])
        s.cmp_w1 = s.din("cmp_w1", [L, 2, 2048, 128])
        s.cmp_w2 = s.din("cmp_w2", [L, 2, 128, 64])
        s.q_norm = s.din("mla_q_norm", [L, 384])
        s.w_uq = s.din("mla_w_uq", [L, 384, 384])
        s.kv_norm = s.din("mla_kv_norm", [L, 128])
        s.w_ukv = s.din("mla_w_ukv", [L, 128, 512])
        s.b_f = s.din("fox_b_f", [L, 4])
        s.w_gate = s.din("w_gate", [L, D, 3 * D])
        s.w_br_nsa = s.din("w_br_nsa", [L, 512, D])
        s.w_br_mla = s.din("w_br_mla", [L, 256, D])
        s.w_br_fox = s.din("w_br_fox", [L, 256, D])
        s.w_mix_out = s.din("w_mix_out", [L, D, D])
        s.xa_w_q = s.din("xa_w_q", [L, D, 256])
        s.xa_w_kv = s.din("xa_w_kv", [L, D, 512])
        s.xa_w_o = s.din("xa_w_o", [L, 256, D])
        s.w_up = s.din("mlp_w_up", [L, D, 4 * D])
        s.w_down = s.din("mlp_w_down", [L, 4 * D, D])
        s.ln_g = s.din("ln_g", [L, 3, D])
        s.ln_b = s.din("ln_b", [L, 3, D])
        NROW = ((s.NCP + s.OFF + 127) // 128) * 128
        s.NROW = NROW
        s.c_ident = s.din("ident_bf", [128, 128], BF16)
        s.c_tri = s.din("tri_bf", [128, 128], BF16)
        s.c_tri4 = s.din("tri4_bf", [128, 4, 128], BF16)
        s.c_atri4 = s.din("atri4_bf", [128, 4, 128], BF16)
        s.c_nearmask = s.din("nearmask", [2, 128, 8, 128])
        s.c_cs = s.din("rope_cs", [96, S])
        s.c_ss = s.din("rope_ss", [96, S])
        s.c_overlap = s.din("overlap_bf", [s.NCP, 64], BF16)
        s.c_aforced = s.din("aforced", [S, 64])
        s.c_expand = s.din("expand_bf", [64, S], BF16)
        s.c_ones = s.din("ones_bf", [128, 512], BF16)
        s.c_cmpvalid = s.din("cmp_valid", [NROW, 8, 128])
        s.c_near = s.din("t5_near", [2, 128, 8, 128])
        s.c_cmpb = s.din("t5_cmpb", [NROW, 8, 128])
        s.c_c31 = s.din("t5_c31", [128, 8, 128])
        s.y_out = s.dscr("y", [S, D], F32, out=True)
        s.xa = s.dscr("x_a", [S, D], F32)
        s.xb = s.dscr("x_b", [S, D], F32)
        s.xT = s.dscr("xT", [8, 128, S], BF16)
        s.memT = s.dscr("memT", [8, 128, MEM], BF16)
        s.em = s.dscr("em", [2, 128, 8, 128], BF16)
        s.emc = s.dscr("emc", [NROW, 8, 128], BF16)
        s.nqT = s.dscr("nqT", [8, 64, S], BF16)
        s.nkcT = s.dscr("nkcT", [2, 64, S], BF16)
        s.nvcT = s.dscr("nvcT", [2, 64, S], BF16)
        s.nksT = s.dscr("nksT", [2, 64, S], BF16)
        s.nkwT = s.dscr("nkwT", [2, 64, S], BF16)
        s.nvs = s.dscr("nvs", [S, 2, 72], BF16)
        s.nvw = s.dscr("nvw", [S, 2, 72], BF16)
        s.gate = s.dscr("gate", [S, 24], F32)
        s.cnT = s.dscr("cnT", [4, 128, S], BF16)
        s.krT = s.dscr("krT", [2, 32, S], F32)
        s.mqT = s.dscr("mqT", [4, 96, S], BF16)
        s.mkT = s.dscr("mkT", [4, 96, S], BF16)
        s.mv = s.dscr("mv", [S, 4, 72], BF16)
        s.fqT = s.dscr("fqT", [4, 70, S], BF16)
        s.fkT = s.dscr("fkT", [4, 70, S], BF16)
        s.fv = s.dscr("fv", [S, 4, 72], BF16)
        s.ffT = s.dscr("ffT", [4, S], F32)
        s.o_tm = s.dscr("o_tm", [S, D], BF16)
        s.hT = s.dscr("hT", [32, 128, S], BF16)

        import os
        stop = os.environ.get("K_STOP", "")
        seq = []
        seq.append(("init", lambda: s.phase_init()))
        state = {"cur": s.x_in, "nxt": s.xa}

        def adv():
            state["cur"], state["nxt"] = state["nxt"], (s.xb if state["nxt"] is s.xa else s.xa)
        for l in range(L):
            last = (l == L - 1)
            seq.append(("p1", lambda l=l: s.phase_p1(l)))
            seq.append(("mla_prep", lambda l=l: s.phase_mla_prep(l)))
            seq.append(("fox_prep", lambda l=l: s.phase_fox_prep(l)))
            seq.append(("nsa", lambda l=l: s.phase_nsa(l)))
            seq.append(("mla", lambda l=l: s.phase_causal(l, "mla")))
            seq.append(("fox", lambda l=l: s.phase_causal(l, "fox")))
            seq.append(("combine", lambda l=l: (s.phase_combine(l, state["cur"], state["nxt"]), adv())))
            seq.append(("cross", lambda l=l: (s.phase_cross(l, state["cur"], state["nxt"]), adv())))
            seq.append(("mlp_up", lambda l=l: s.phase_mlp_up(l)))
            seq.append(("mlp_down", lambda l=l, last=last: (s.phase_mlp_down(l, state["cur"], s.y_out if last else state["nxt"], last), adv())))
        skip = set(os.environ.get("K_SKIP", "").split(","))
        for (nm, fn) in seq:
            if nm not in skip:
                fn()
            if stop and nm == stop:
                break
        s.sc.barrier()
        return s.nc

    def transpose_to_xT(s, xbf, TPr, xTt, tile_idx, ident, dst, q="pool"):
        sc = s.sc
        tp = TPr.next()
        for kc in range(8):
            sc.PE(lambda e: e.transpose(out=tp[:, kc, :], in_=xbf[:, kc * 128:(kc + 1) * 128], identity=ident[:, :]),
                  r=[xbf, ident], w=[tp])
        xt = xTt.next()
        sc.ACT(lambda e: e.copy(out=xt[:, :, :], in_=tp[:, :, :]), r=[tp], w=[xt])
        sc.dma(dst.ap.rearrange("k p s -> p k s")[:, :, tile_idx * 128:(tile_idx + 1) * 128], xt[:, :, :], r=[xt], w=[dst.b(("t", tile_idx))], q=q)

    def ln_setup(s, es, l, which, depth=1):
        sc = s.sc
        g = s.sb(es, "ln_gam", [128, D], F32)
        b = s.sb(es, "ln_bet", [128, D], F32)
        sc.dma(g[:, :], s.ln_g.ap[l, which, :].partition_broadcast(128), w=[g])
        sc.dma(b[:, :], s.ln_b.ap[l, which, :].partition_broadcast(128), w=[b])
        r = dict(g=g, b=b, depth=depth)
        r["xin"] = Ring([s.sb(es, "ln_xin%d" % i, [128, D], F32) for i in range(1 + depth)])
        r["z"] = Ring([s.sb(es, "ln_z%d" % i, [128, D], F32) for i in range(1 + depth)])
        r["xo"] = Ring([s.sb(es, "ln_xo%d" % i, [128, D], F32) for i in range(2)])
        r["xbf"] = Ring([s.sb(es, "ln_xbf%d" % i, [128, D], BF16) for i in range(3)])
        r["st"] = Ring([s.sb(es, "ln_st%d" % i, [128, 24], F32) for i in range(2 + depth)])
        r["xTt"] = Ring([s.sb(es, "ln_xTt%d" % i, [128, 8, 128], BF16) for i in range(2)])
        return r

    def layer_norm_tile(s, R, Y, tile_idx, xsrc, xdst, TPr, ident, final=False):
        st8 = s.ln_A(R, Y, tile_idx, xsrc)
        pend = R.setdefault("pend", [])
        pend2 = R.setdefault("pend2", [])
        pend.append((st8, tile_idx, xdst, TPr, ident, final))
        while len(pend2) > 1:
            s.ln_C(R, *pend2.pop(0))
        while len(pend) > R.get("depth", 1):
            c = s.ln_B(R, *pend.pop(0))
            if c is not None:
                pend2.append(c)

    def ln_flush(s, R):
        pend = R.setdefault("pend", [])
        pend2 = R.setdefault("pend2", [])
        while pend:
            while len(pend2) > 1:
                s.ln_C(R, *pend2.pop(0))
            c = s.ln_B(R, *pend.pop(0))
            if c is not None:
                pend2.append(c)
        while pend2:
            s.ln_C(R, *pend2.pop(0))

    def ln_C(s, R, xbf, tile_idx, TPr, ident):
        s.transpose_to_xT(xbf, TPr, R["xTt"], tile_idx, ident, s.xT, q="sp")

    def ln_A(s, R, Y, tile_idx, xsrc):
        sc = s.sc
        rows = slice(tile_idx * 128, (tile_idx + 1) * 128)
        xin = R["xin"].next()
        sc.dma(xin[:, :], xsrc.ap[rows, :], r=[xsrc.b(("t", tile_idx))], w=[xin], q="pool")
        z = R["z"].next()
        sc.DVE(lambda e: e.scalar_tensor_tensor(out=z[:, :], in0=xin[:, :], scalar=float(DN_ALPHA), in1=Y[:, :], op0=ALU.mult, op1=ALU.add),
               r=[xin, Y], w=[z])
        st = R["st"].next()
        for c in range(2):
            sc.DVE(lambda e: e.bn_stats(out=st[:, c * 6:(c + 1) * 6], in_=z[:, c * 512:(c + 1) * 512]), r=[z], w=[st])
        sc.DVE(lambda e: e.bn_aggr(out=st[:, 12:14], in_=st[:, 0:12]), r=[st], w=[st])
        sc.DVE(lambda e: e.tensor_scalar(out=st[:, 14:15], in0=st[:, 13:14], scalar1=1e-5, scalar2=None, op0=ALU.add), r=[st], w=[st])
        sc.ACT(lambda e: e.activation(out=st[:, 16:17], in_=st[:, 14:15], func=AF.Sqrt), r=[st], w=[st])
        return (z, st)

    def ln_B(s, R, zst, tile_idx, xdst, TPr, ident, final):
        sc = s.sc
        z, st = zst
        rows = slice(tile_idx * 128, (tile_idx + 1) * 128)
        sc.DVE(lambda e: e.reciprocal(out=st[:, 18:19], in_=st[:, 16:17]), r=[st], w=[st])
        xo = R["xo"].next()
        sc.DVE(lambda e: e.scalar_tensor_tensor(out=z[:, :], in0=z[:, :], scalar=st[:, 12:13], in1=R["g"][:, :], op0=ALU.subtract, op1=ALU.mult),
               r=[z, st, R["g"]], w=[z])
        sc.DVE(lambda e: e.scalar_tensor_tensor(out=xo[:, :], in0=z[:, :], scalar=st[:, 18:19], in1=R["b"][:, :], op0=ALU.mult, op1=ALU.add),
               r=[z, st, R["b"]], w=[xo])
        sc.dma(xdst.ap[rows, :], xo[:, :], r=[xo], w=[xdst.b(("t", tile_idx))], q="sp")
        if not final:
            xbf = R["xbf"].next()
            sc.ACT(lambda e: e.copy(out=xbf[:, :], in_=xo[:, :]), r=[xo], w=[xbf])
            return (xbf, tile_idx, TPr, ident)
        return None

    def phase_init(s):
        sc = s.sc
        S, NT = s.S, s.NT
        with ExitStack() as es:
            ident = s.sb(es, "ident", [128, 128], BF16)
            sc.dma(ident[:, :], s.c_ident.ap[:, :], w=[ident])
            c31 = s.sb(es, "c31", [128, 1024], F32)
            sc.dma(c31[:, :], s.c_c31.ap.rearrange("p h q -> p (h q)"), w=[c31])
            tb = Ring([s.sb(es, "t5b%d" % i, [128, 1024], F32) for i in range(2)])
            tm = Ring([s.sb(es, "t5m%d" % i, [128, 1024], F32) for i in range(2)])
            to = Ring([s.sb(es, "t5o%d" % i, [128, 1024], BF16) for i in range(2)])
            jobs = [(s.c_near.ap[i].rearrange("p h q -> p (h q)"), s.c_nearmask.ap[i].rearrange("p h q -> p (h q)"),
                     s.em.ap[i].rearrange("p h q -> p (h q)")) for i in range(2)]
            for rt in range(s.NROW // 128):
                rs = slice(rt * 128, (rt + 1) * 128)
                jobs.append((s.c_cmpb.ap[rs].rearrange("p h q -> p (h q)"), s.c_cmpvalid.ap[rs].rearrange("p h q -> p (h q)"),
                             s.emc.ap[rs].rearrange("p h q -> p (h q)")))
            for (bsrc, msrc, dst) in jobs:
                b = tb.next()
                m = tm.next()
                o = to.next()
                sc.dma(b[:, :], bsrc, w=[b])
                sc.dma(m[:, :], msrc, w=[m])
                sc.DVE(lambda e: e.tensor_tensor(out=b[:, :], in0=b[:, :], in1=c31[:, :], op=ALU.subtract), r=[b, c31], w=[b])
                sc.ACT(lambda e: e.activation(out=b[:, :], in_=b[:, :], func=AF.Exp), r=[b], w=[b])
                sc.DVE(lambda e: e.tensor_tensor(out=o[:, :], in0=b[:, :], in1=m[:, :], op=ALU.mult), r=[b, m], w=[o])
                sc.dma(dst, o[:, :], r=[o], w=[s.em.buf], q="pool")
            mt = s.sb(es, "mem_t", [128, 2, D], F32)
            sc.dma(mt[:, :, :], s.mem_in.ap.rearrange("(t p) d -> p t d", p=128), w=[mt])
            mb = s.sb(es, "mem_b", [128, 2, D], BF16)
            sc.DVE(lambda e: e.tensor_copy(out=mb[:, :, :], in_=mt[:, :, :]), r=[mt], w=[mb])
            tp = s.ps(es, "init_tp", [128, 8, 128], BF16)
            mT = s.sb(es, "memT_s", [128, 8, MEM], BF16)
            for t in range(2):
                for kc in range(8):
                    sc.PE(lambda e: e.transpose(out=tp[:, kc, :], in_=mb[:, t, kc * 128:(kc + 1) * 128], identity=ident[:, :]), r=[mb, ident], w=[tp])
                sc.ACT(lambda e: e.copy(out=mT[:, :, t * 128:(t + 1) * 128], in_=tp[:, :, :]), r=[tp], w=[mT])
            sc.dma(s.memT.ap.rearrange("k p m -> p k m"), mT[:, :, :], r=[mT], w=[s.memT.buf], q="pool")
            xr = Ring([s.sb(es, "ix%d" % i, [128, D], F32) for i in range(2)])
            xbr = Ring([s.sb(es, "ixb%d" % i, [128, D], BF16) for i in range(2)])
            TPr = Ring([tp, s.ps(es, "init_tp2", [128, 8, 128], BF16)])
            xTt = Ring([s.sb(es, "ixT%d" % i, [128, 8, 128], BF16) for i in range(2)])
            for t in range(NT):
                xt = xr.next()
                sc.dma(xt[:, :], s.x_in.ap[t * 128:(t + 1) * 128, :], w=[xt])
                xb = xbr.next()
                sc.DVE(lambda e: e.tensor_copy(out=xb[:, :], in_=xt[:, :]), r=[xt], w=[xb])
                s.transpose_to_xT(xb, TPr, xTt, t, ident, s.xT)
        sc.barrier()

    def phase_p1(s, l):
        sc = s.sc
        S, NT, NB = s.S, s.NT, s.NB
        WN = N_IN + 64
        with ExitStack() as es:
            ident = s.sb(es, "ident", [128, 128], BF16)
            sc.dma(ident[:, :], s.c_ident.ap[:, :], w=[ident])
            W = s.sb(es, "p1_w", [128, 8, WN], BF16)
            stg = s.stages(es)
            wsrc = s.w_in.ap[l]
            s.load_w(W, wsrc, 128, 8, N_IN, stg)
            sc.POOL(lambda e: e.tensor_copy(out=W[:, :, N_IN:N_IN + 32], in_=W[:, :, 1816:1848]), r=s.wr(W, 1816, 1848), w=[W])
            sc.POOL(lambda e: e.tensor_copy(out=W[:, :, N_IN + 32:N_IN + 48], in_=W[:, :, 1832:1848]), r=s.wr(W, 1832, 1848), w=[W])
            sc.POOL(lambda e: e.tensor_copy(out=W[:, :, N_IN + 48:N_IN + 64], in_=W[:, :, 1816:1832]), r=s.wr(W, 1816, 1832), w=[W])
            xT = s.sb(es, "p1_xT", [128, 8, S], BF16)
            xTb = [Buf("xTb%d" % b) for b in range(NB)]
            for b in range(NB):
                sc.dma(xT[:, :, b * 512:(b + 1) * 512], s.xT.ap.rearrange("k p s -> p k s")[:, :, b * 512:(b + 1) * 512], w=[xTb[b]])
            PB = Ring([s.ps(es, "p1_pb%d" % i, [128, 512], F32) for i in range(6)])
            TP = Ring([s.ps(es, "p1_tp%d" % i, [128, 8, 128], BF16) for i in range(1)])
            fo = Ring([s.sb(es, "p1_fo%d" % i, [128, 512], BF16) for i in range(4)])
            fo32 = Ring([s.sb(es, "p1_fo32_%d" % i, [128, 512], F32) for i in range(2)])
            vt_s = Ring([s.sb(es, "p1_vts%d" % i, [128, 4, 2, 72], BF16) for i in range(2)])
            vt_w = Ring([s.sb(es, "p1_vtw%d" % i, [128, 4, 2, 72], BF16) for i in range(2)])
            vt_f = Ring([s.sb(es, "p1_vtf%d" % i, [128, 4, 4, 72], BF16) for i in range(2)])
            for rg in (vt_s, vt_w, vt_f):
                for t_ in rg.items:
                    sc.POOL(lambda e: e.memset(t_[:, :, :, :], 1.0), w=[t_])
            gt = Ring([s.sb(es, "p1_gt%d" % i, [128, 4, 24], F32) for i in range(2)])
            cn = Ring([s.sb(es, "p1_cn%d" % i, [128, 512], BF16) for i in range(2)])
            junk = s.sb(es, "p1_junk", [128, 512], BF16)
            ss = Ring([s.sb(es, "p1_ss%d" % i, [128, 4], F32) for i in range(2)])
            cT = Ring([s.sb(es, "p1_cT%d" % i, [128, 4, 512], BF16) for i in range(2)])
            evac_i = [0]

            def evac(dst_ap, src_ap, rb, wb):
                evac_i[0] += 1
                if evac_i[0] % 2 == 0:
                    sc.ACT(lambda e: e.copy(out=dst_ap, in_=src_ap), r=rb, w=wb)
                else:
                    sc.DVE(lambda e: e.tensor_copy(out=dst_ap, in_=src_ap), r=rb, w=wb)

            fm = []
            for h in range(8):
                fm.append((64 * h, 64, "bf", [(s.nqT.ap[h], 0, 64)]))
            fm.append((512, 128, "bf", [(s.nkcT.ap[0], 0, 64), (s.nkcT.ap[1], 64, 64)]))
            fm.append((640, 128, "bf", [(s.nvcT.ap[0], 0, 64), (s.nvcT.ap[1], 64, 64)]))
            fm.append((768, 128, "bf", [(s.nksT.ap[0], 0, 64), (s.nksT.ap[1], 64, 64)]))
            fm.append((1024, 128, "bf", [(s.nkwT.ap[0], 0, 64), (s.nkwT.ap[1], 64, 64)]))
            fm.append((N_IN, 32, "f32", [(s.krT.ap[0], 0, 32)]))
            fm.append((N_IN + 32, 32, "f32", [(s.krT.ap[1], 0, 32)]))
            for t in range(2):
                fm.append((1848 + 128 * t, 128, "bf", [(s.fqT.ap[2 * t, 0:64, :], 0, 64), (s.fqT.ap[2 * t + 1, 0:64, :], 64, 64)]))
                fm.append((2104 + 128 * t, 128, "bf", [(s.fkT.ap[2 * t, 0:64, :], 0, 64), (s.fkT.ap[2 * t + 1, 0:64, :], 64, 64)]))
            fm.append((2616, 4, "f32", [(s.ffT.ap, 0, 4)]))
            for b in range(NB):
                bs = slice(b * 512, (b + 1) * 512)
                import os
                P1M = int(os.environ.get('K_P1', '31'))
                for (c0, M, kind, dsts) in (fm if P1M & 1 else []):
                    pb = PB.next()
                    for kc in range(8):
                        sc.PE(lambda e: e.matmul(pb[0:M, :], lhsT=W[:, kc, c0:c0 + M], rhs=xT[:, kc, bs], start=(kc == 0), stop=(kc == 7)),
                              r=s.wr(W, c0, c0 + M) + [xTb[b]], w=[pb])
                    o = fo.next() if kind == "bf" else fo32.next()
                    evac(o[0:M, :], pb[0:M, :], [pb], [o])
                    for (dap, p0, pn) in dsts:
                        sc.dma(dap[:, bs], o[p0:p0 + pn, :], r=[o], w=[Buf()], q="sp")
                vs_t = vt_s.next()
                vw_t = vt_w.next()
                vf_t = vt_f.next()
                g_t = gt.next()
                cT_t = cT.next()
                for tt in (range(4) if P1M & 2 else []):
                    t = b * 4 + tt
                    ts_ = slice(t * 128, (t + 1) * 128)
                    pa = PB.next()
                    for kc in range(8):
                        sc.PE(lambda e: e.matmul(pa[:, :], lhsT=xT[:, kc, ts_], rhs=W[:, kc, 1304:1816], start=(kc == 0), stop=(kc == 7)),
                              r=s.wr(W, 1304, 1816) + [xTb[b]], w=[pa])
                    s_ = ss.next()
                    sc.ACT(lambda e: e.activation(out=junk[:, 0:384], in_=pa[:, 0:384], func=AF.Square, scale=float(384 ** -0.5), accum_out=s_[:, 0:1]),
                           r=[pa], w=[junk, s_])
                    sc.ACT(lambda e: e.activation(out=junk[:, 384:512], in_=pa[:, 384:512], func=AF.Square, scale=float(128 ** -0.5), accum_out=s_[:, 1:2]),
                           r=[pa], w=[junk, s_])
                    sc.DVE(lambda e: e.tensor_scalar(out=s_[:, 0:2], in0=s_[:, 0:2], scalar1=1e-6, scalar2=None, op0=ALU.add), r=[s_], w=[s_])
                    sc.ACT(lambda e: e.activation(out=s_[:, 2:4], in_=s_[:, 0:2], func=AF.Sqrt), r=[s_], w=[s_])
                    sc.DVE(lambda e: e.reciprocal(out=s_[:, 0:2], in_=s_[:, 2:4]), r=[s_], w=[s_])
                    cn_t = cn.next()
                    sc.DVE(lambda e: e.tensor_scalar(out=cn_t[:, 0:384], in0=pa[:, 0:384], scalar1=s_[:, 0:1], scalar2=None, op0=ALU.mult), r=[pa, s_], w=[cn_t])
                    sc.DVE(lambda e: e.tensor_scalar(out=cn_t[:, 384:512], in0=pa[:, 384:512], scalar1=s_[:, 1:2], scalar2=None, op0=ALU.mult), r=[pa, s_], w=[cn_t])
                    if P1M & 4:
                        pb1 = PB.next()
                        for (cc0, o0, n) in ((896, 0, 128), (1152, 128, 128), (2360, 256, 256)):
                            for kc in range(8):
                                sc.PE(lambda e: e.matmul(pb1[:, o0:o0 + n], lhsT=xT[:, kc, ts_], rhs=W[:, kc, cc0:cc0 + n], start=(kc == 0), stop=(kc == 7)),
                                      r=s.wr(W, cc0, cc0 + n) + [xTb[b]], w=[pb1])
                        sc.ACT(lambda e: e.copy(out=vs_t[:, tt, :, 0:64], in_=pb1[:, 0:128].rearrange("p (g d) -> p g d", g=2)), r=[pb1], w=[vs_t])
                        sc.ACT(lambda e: e.copy(out=vw_t[:, tt, :, 0:64], in_=pb1[:, 128:256].rearrange("p (g d) -> p g d", g=2)), r=[pb1], w=[vw_t])
                        sc.ACT(lambda e: e.copy(out=vf_t[:, tt, :, 0:64], in_=pb1[:, 256:512].rearrange("p (g d) -> p g d", g=4)), r=[pb1], w=[vf_t])
                    if P1M & 8:
                        pb2 = PB.next()
                        for kc in range(8):
                            sc.PE(lambda e: e.matmul(pb2[:, 0:24], lhsT=xT[:, kc, ts_], rhs=W[:, kc, 1280:1304], start=(kc == 0), stop=(kc == 7)),
                                  r=s.wr(W, 1280, 1304) + [xTb[b]], w=[pb2])
                        sc.ACT(lambda e: e.activation(out=g_t[:, tt, :], in_=pb2[:, 0:24], func=(AF.Identity if os.environ.get("K_X") == "3" else AF.Sigmoid)), r=[pb2], w=[g_t])
                    tp = TP.next()
                    for j in range(4):
                        sc.PE(lambda e: e.transpose(out=tp[:, j, :], in_=cn_t[:, j * 128:(j + 1) * 128], identity=ident[:, :]), r=[cn_t, ident], w=[tp])
                    sc.ACT(lambda e: e.copy(out=cT_t[:, :, tt * 128:(tt + 1) * 128], in_=tp[:, 0:4, :]), r=[tp], w=[cT_t])
                rows = slice(b * 512, (b + 1) * 512)
                if not (P1M & 16):
                    continue
                sc.dma(s.nvs.ap[rows].rearrange("(t p) g e -> p t g e", p=128), vs_t[:, :, :, :], r=[vs_t], w=[Buf()], q="sp")
                sc.dma(s.nvw.ap[rows].rearrange("(t p) g e -> p t g e", p=128), vw_t[:, :, :, :], r=[vw_t], w=[Buf()], q="sp")
                sc.dma(s.fv.ap[rows].rearrange("(t p) g e -> p t g e", p=128), vf_t[:, :, :, :], r=[vf_t], w=[Buf()], q="sp")
                sc.dma(s.gate.ap[rows].rearrange("(t p) c -> p t c", p=128), g_t[:, :, :], r=[g_t], w=[Buf()], q="sp")
                sc.dma(s.cnT.ap.rearrange("j p s -> p j s")[:, :, rows], cT_t[:, :, :], r=[cT_t], w=[Buf()], q="sp")
        sc.barrier()

    def phase_mla_prep(s, l):
        sc = s.sc
        S, NT, NB = s.S, s.NT, s.NB
        with ExitStack() as es:
            cnT = s.sb(es, "mp_cnT", [128, 4, S], BF16)
            cb = [Buf() for _ in range(NB)]
            for b in range(NB):
                sc.dma(cnT[:, :, b * 512:(b + 1) * 512], s.cnT.ap.rearrange("j p s -> p j s")[:, :, b * 512:(b + 1) * 512], w=[cb[b]])
            CS = s.sb(es, "mp_cs", [96, S], F32)
            SS = s.sb(es, "mp_ss", [96, S], F32)
            sc.dma(CS[:, :], s.c_cs.ap[:, :], w=[CS])
            sc.dma(SS[:, :], s.c_ss.ap[:, :], w=[SS])
            kr0 = s.sb(es, "mp_kr0", [32, S], F32)
            kr1 = s.sb(es, "mp_kr1", [32, S], F32)
            sc.dma(kr0[:, :], s.krT.ap[0], w=[kr0])
            sc.dma(kr1[:, :], s.krT.ap[1], w=[kr1])
            gq = s.sb(es, "mp_gq", [128, 4], F32)
            with s.nc.allow_non_contiguous_dma(reason="tiny gain vectors"):
                sc.dma(gq[:, 0:3], s.q_norm.ap[l].rearrange("(a p) -> p a", p=128), w=[gq])
                sc.dma(gq[:, 3:4], s.kv_norm.ap[l].rearrange("(a p) -> p a", p=128), w=[gq])
            stq = s.sb(es, "mp_stq", [128, 3, 384], F32)
            stk = s.sb(es, "mp_stk", [128, 512], F32)
            sc.dma(stq[:, :, :], s.w_uq.ap[l].rearrange("(a p) n -> p a n", p=128), w=[stq])
            sc.dma(stk[:, :], s.w_ukv.ap[l], w=[stk])
            Wq = s.sb(es, "mp_wq", [128, 3, 384], BF16)
            Wqp = s.sb(es, "mp_wqp", [128, 3, 4, 96], BF16)
            Wkv = s.sb(es, "mp_wkv", [128, 512], BF16)
            for a in range(3):
                sc.DVE(lambda e: e.tensor_scalar(out=Wq[:, a, :], in0=stq[:, a, :], scalar1=gq[:, a:a + 1], scalar2=None, op0=ALU.mult), r=[stq, gq], w=[Wq])
            sc.DVE(lambda e: e.tensor_scalar(out=Wkv[:, :], in0=stk[:, :], scalar1=gq[:, 3:4], scalar2=None, op0=ALU.mult), r=[stk, gq], w=[Wkv])
            sc.POOL(lambda e: e.memset(Wqp[:, :, :, :], 0.0), w=[Wqp])
            for h in range(4):
                sc.POOL(lambda e: e.tensor_copy(out=Wqp[:, :, h, 64:80], in_=Wq[:, :, 96 * h + 80:96 * h + 96]), r=[Wq], w=[Wqp])
                sc.POOL(lambda e: e.tensor_copy(out=Wqp[:, :, h, 80:96], in_=Wq[:, :, 96 * h + 64:96 * h + 80]), r=[Wq], w=[Wqp])
            PB = Ring([s.ps(es, "mp_pb%d" % i, [128, 512], F32) for i in range(6)])
            qo = Ring([s.sb(es, "mp_qo%d" % i, [96, 512], BF16) for i in range(3)])
            ko = Ring([s.sb(es, "mp_ko%d" % i, [64, 512], BF16) for i in range(3)])
            t1 = Ring([s.sb(es, "mp_t1_%d" % i, [96, 512], F32) for i in range(2)])
            t2 = Ring([s.sb(es, "mp_t2_%d" % i, [96, 512], F32) for i in range(2)])
            kro = Ring([s.sb(es, "mp_kro%d" % i, [32, 512], BF16) for i in range(2)])
            vt = Ring([s.sb(es, "mp_vt%d" % i, [128, 4, 4, 72], BF16) for i in range(2)])
            for t_ in vt.items:
                sc.POOL(lambda e: e.memset(t_[:, :, :, :], 1.0), w=[t_])
            for b in range(NB):
                bs = slice(b * 512, (b + 1) * 512)
                for h in range(4):
                    p1 = PB.next()
                    for a in range(3):
                        sc.PE(lambda e: e.matmul(p1[0:96, :], lhsT=Wq[:, a, 96 * h:96 * h + 96], rhs=cnT[:, a, bs], start=(a == 0), stop=(a == 2)), r=[Wq, cb[b]], w=[p1])
                    p2 = PB.next()
                    for a in range(3):
                        sc.PE(lambda e: e.matmul(p2[0:96, :], lhsT=Wqp[:, a, h, :], rhs=cnT[:, a, bs], start=(a == 0), stop=(a == 2)), r=[Wqp, cb[b]], w=[p2])
                    q_ = qo.next()
                    sc.ACT(lambda e: e.copy(out=q_[0:64, :], in_=p1[0:64, :]), r=[p1], w=[q_])
                    a1 = t1.next()
                    a2 = t2.next()
                    sc.DVE(lambda e: e.tensor_tensor(out=a1[64:96, :], in0=p1[64:96, :], in1=CS[64:96, bs], op=ALU.mult), r=[p1, CS], w=[a1])
                    sc.DVE(lambda e: e.tensor_tensor(out=a2[64:96, :], in0=p2[64:96, :], in1=SS[64:96, bs], op=ALU.mult), r=[p2, SS], w=[a2])
                    sc.POOL(lambda e: e.tensor_tensor(out=q_[64:96, :], in0=a1[64:96, :], in1=a2[64:96, :], op=ALU.add), r=[a1, a2], w=[q_])
                    sc.dma(s.mqT.ap[h, :, bs], q_[:, :], r=[q_], w=[Buf()], q="sp")
                    p3 = PB.next()
                    sc.PE(lambda e: e.matmul(p3[0:64, :], lhsT=Wkv[:, 128 * h:128 * h + 64], rhs=cnT[:, 3, bs], start=True, stop=True), r=[Wkv, cb[b]], w=[p3])
                    k_ = ko.next()
                    sc.ACT(lambda e: e.copy(out=k_[:, :], in_=p3[0:64, :]), r=[p3], w=[k_])
                    sc.dma(s.mkT.ap[h, 0:64, bs], k_[:, :], r=[k_], w=[Buf()], q="sp")
                a1 = t1.next()
                a2 = t2.next()
                kr_ = kro.next()
                sc.DVE(lambda e: e.tensor_tensor(out=a1[0:32, :], in0=kr0[:, bs], in1=CS[0:32, bs], op=ALU.mult), r=[kr0, CS], w=[a1])
                sc.DVE(lambda e: e.tensor_tensor(out=a2[0:32, :], in0=kr1[:, bs], in1=SS[0:32, bs], op=ALU.mult), r=[kr1, SS], w=[a2])
                sc.POOL(lambda e: e.tensor_tensor(out=kr_[:, :], in0=a1[0:32, :], in1=a2[0:32, :], op=ALU.add), r=[a1, a2], w=[kr_])
                for h in range(4):
                    sc.dma(s.mkT.ap[h, 64:96, bs], kr_[:, :], r=[kr_], w=[Buf()], q="sp")
                v_ = vt.next()
                for tt in range(4):
                    t = b * 4 + tt
                    p4 = PB.next()
                    for h in range(4):
                        sc.PE(lambda e: e.matmul(p4[:, 64 * h:64 * h + 64], lhsT=cnT[:, 3, t * 128:(t + 1) * 128], rhs=Wkv[:, 128 * h + 64:128 * h + 128], start=True, stop=True),
                              r=[Wkv, cb[b]], w=[p4])
                    sc.ACT(lambda e: e.copy(out=v_[:, tt, :, 0:64], in_=p4[:, 0:256].rearrange("p (g d) -> p g d", g=4)), r=[p4], w=[v_])
                sc.dma(s.mv.ap[bs].rearrange("(t p) g e -> p t g e", p=128), v_[:, :, :, :], r=[v_], w=[Buf()], q="sp")
        sc.barrier()

    def phase_fox_prep(s, l):
        sc = s.sc
        S = s.S
        with ExitStack() as es:
            ff = s.sb(es, "fp_ff", [4, S], F32)
            sc.dma(ff[:, :], s.ffT.ap[:, :], w=[ff])
            bf = s.sb(es, "fp_bf", [4, 2], F32)
            with s.nc.allow_non_contiguous_dma(reason="tiny"):
                sc.dma(bf[:, 0:1], s.b_f.ap[l].rearrange("(p a) -> p a", a=1), w=[bf])
            sc.DVE(lambda e: e.tensor_scalar(out=bf[:, 1:2], in0=bf[:, 0:1], scalar1=-1.0, scalar2=None, op0=ALU.mult), r=[bf], w=[bf])
            ex = s.sb(es, "fp_ex", [4, S], F32)
            sc.ACT(lambda e: e.activation(out=ex[:, :], in_=ff[:, :], func=AF.Exp, bias=bf[:, 1:2], scale=-1.0), r=[ff, bf], w=[ex])
            one = s.sb(es, "fp_one", [4, 1], F32)
            sc.DVE(lambda e: e.memset(one[:, :], 1.0), w=[one])
            sc.ACT(lambda e: e.activation(out=ex[:, :], in_=ex[:, :], func=AF.Ln, bias=one[:, 0:1], scale=1.0), r=[ex, one], w=[ex])
            ones = s.sb(es, "fp_ones", [4, S], F32)
            sc.POOL(lambda e: e.memset(ones[:, :], 1.0), w=[ones])
            sc.DVE(lambda e: e.tensor_scalar(out=ex[:, :], in0=ex[:, :], scalar1=-8.0, scalar2=None, op0=ALU.mult), r=[ex], w=[ex])
            cum = s.sb(es, "fp_cum", [4, S], F32)
            sc.DVE(lambda e: e.tensor_tensor_scan(out=cum[:, :], data0=ones[:, :], data1=ex[:, :], initial=0.0, op0=ALU.mult, op1=ALU.add), r=[ones, ex], w=[cum])
            pcs = [s.sb(es, "fp_pc%d" % i, [4, S], BF16) for i in range(3)]
            ngs = [s.sb(es, "fp_ng%d" % i, [4, S], BF16) for i in range(3)]
            rem = s.sb(es, "fp_rem", [4, S], F32)
            src = cum
            for i in range(3):
                sc.DVE(lambda e: e.tensor_copy(out=pcs[i][:, :], in_=src[:, :]), r=[src], w=[pcs[i]])
                sc.DVE(lambda e: e.tensor_scalar(out=ngs[i][:, :], in0=pcs[i][:, :], scalar1=-1.0, scalar2=None, op0=ALU.mult), r=[pcs[i]], w=[ngs[i]])
                if i < 2:
                    sc.DVE(lambda e: e.tensor_tensor(out=rem[:, :], in0=src[:, :], in1=pcs[i][:, :], op=ALU.subtract), r=[src, pcs[i]], w=[rem])
                    src = rem
            onb = s.sb(es, "fp_onb", [4, S], BF16)
            sc.POOL(lambda e: e.memset(onb[:, :], 1.0), w=[onb])
            for i in range(3):
                sc.dma(s.fqT.ap[:, 64 + i, :], pcs[i][:, :], r=[pcs[i]], w=[Buf()], q="sp")
                sc.dma(s.fqT.ap[:, 67 + i, :], onb[:, :], r=[onb], w=[Buf()], q="sp")
                sc.dma(s.fkT.ap[:, 64 + i, :], onb[:, :], r=[onb], w=[Buf()], q="sp")
                sc.dma(s.fkT.ap[:, 67 + i, :], ngs[i][:, :], r=[ngs[i]], w=[Buf()], q="sp")
        sc.barrier()

    def run_steps(s, steps, sT_ring, pT_ring, skew=1):
        sc = s.sc
        n = len(steps)
        state = [None] * n

        def emit_score(i):
            st = steps[i]
            sT = sT_ring.next()
            pT = pT_ring.next()
            c0, nco, kk = st["c0"], st["ncol"], st["kk"]
            sc.PE(lambda e: e.matmul(sT[0:kk, c0:nco], lhsT=st["kT"], rhs=st["q_fn"](c0, nco), start=True, stop=True),
                  r=st["rb_score"], w=[sT])
            sc.ACT(lambda e: e.activation(out=pT[0:kk, c0:nco], in_=sT[0:kk, c0:nco], func=AF.Exp, scale=float(st["scale"])), r=[sT], w=[pT])
            mi = 0
            for (m0, m1, map_, mb, clamp) in st["masks"]:
                mi += 1
                if clamp:
                    sc.DVE(lambda e: e.scalar_tensor_tensor(out=pT[0:kk, m0:m1], in0=pT[0:kk, m0:m1], scalar=1e30, in1=map_, op0=ALU.min, op1=ALU.mult),
                           r=[pT] + mb, w=[pT])
                elif st.get("mask_pool", False) and mi % 2 == 0:
                    sc.POOL(lambda e: e.tensor_tensor(out=pT[0:kk, m0:m1], in0=pT[0:kk, m0:m1], in1=map_, op=ALU.mult), r=[pT] + mb, w=[pT])
                else:
                    sc.DVE(lambda e: e.tensor_tensor(out=pT[0:kk, m0:m1], in0=pT[0:kk, m0:m1], in1=map_, op=ALU.mult), r=[pT] + mb, w=[pT])
            state[i] = pT

        def emit_pv(i):
            st = steps[i]
            pT = state[i]
            kk = st["kk"]
            first = st["first"]
            for sub in st["subs"]:
                out_ap, accb = st["acc_fn"](sub)
                stt = bool(first and (sub in st.get("start_subs", (st["subs"][0],))))
                sc.PE(lambda e: e.matmul(out_ap, lhsT=pT[0:kk, sub * 128:(sub + 1) * 128], rhs=st["V"], start=stt, stop=True, skip_group_check=True),
                      r=[pT] + st["rb_v"], w=[accb])
            if st["fin"] is not None:
                st["fin"]()

        nxt_pv = 0
        pre_ptr = [0]

        def do_pre(upto):
            while pre_ptr[0] < n and pre_ptr[0] <= upto:
                pf = steps[pre_ptr[0]].get("pre")
                if pf is not None:
                    pf()
                pre_ptr[0] += 1
        for i in range(n + skew):
            if i < n:
                do_pre(i + 3)
                dep = steps[i].get("dep_step")
                while dep is not None and nxt_pv <= dep:
                    emit_pv(nxt_pv)
                    nxt_pv += 1
                emit_score(i)
            if i >= skew and nxt_pv <= i - skew:
                emit_pv(nxt_pv)
                nxt_pv += 1
        while nxt_pv < n:
            emit_pv(nxt_pv)
            nxt_pv += 1

    def phase_nsa(s, l):
        sc = s.sc
        S, NT = s.S, s.NT
        CR, CT, NCP, NCMP = s.CR, s.CT, s.NCP, s.NCMP
        with ExitStack() as es:
            ident = s.sb(es, "ident", [128, 128], BF16)
            sc.dma(ident[:, :], s.c_ident.ap[:, :], w=[ident])
            QA = [s.sb(es, "ns_qa%d" % g, [128, NT, 4, 128], BF16) for g in range(2)]
            qa_lo = [Buf() for g in range(2)]
            qa_hi = [[Buf() for i in range(NT)] for g in range(2)]
            ksA = [s.sb(es, "ns_ks%d" % g, [128, S], BF16) for g in range(2)]
            kwT = [s.sb(es, "ns_kw%d" % g, [64, S], BF16) for g in range(2)]
            kcT = [s.sb(es, "ns_kc%d" % g, [64, CT * CR], BF16) for g in range(2)]
            vs = [s.sb(es, "ns_vs%d" % g, [128, NT, 72], BF16) for g in range(2)]
            vw = [s.sb(es, "ns_vw%d" % g, [128, NT, 72], BF16) for g in range(2)]
            vcA = [s.sb(es, "ns_vc%d" % g, [128, CT, 136], BF16) for g in range(2)]
            G = s.sb(es, "ns_gate", [128, NT, 24], F32)
            AFc = s.sb(es, "ns_af", [128, NT, 64], F32)
            EM = s.sb(es, "ns_em", [128, 2, 8, 128], BF16)
            EMW = s.sb(es, "ns_emw", [128, 4, 128], BF16)
            for g in range(2):
                for hh in range(4):
                    sc.dma(QA[g][0:64, :, hh, :], s.nqT.ap[4 * g + hh].rearrange("d (t q) -> d t q", q=128), w=[qa_lo[g]])
                sc.dma(ksA[g][0:64, :], s.nksT.ap[g], w=[ksA[g]])
                sc.dma(ksA[g][64:128, :], s.c_expand.ap[:, :], w=[ksA[g]])
                sc.dma(kwT[g][:, :], s.nkwT.ap[g], w=[kwT[g]])
                sc.dma(vs[g][:, :, :], s.nvs.ap[:, g, :].rearrange("(t p) e -> p t e", p=128), w=[vs[g]])
                sc.dma(vw[g][:, :, :], s.nvw.ap[:, g, :].rearrange("(t p) e -> p t e", p=128), w=[vw[g]])
            sc.dma(G[:, :, :], s.gate.ap.rearrange("(t p) c -> p t c", p=128), w=[G])
            sc.dma(AFc[:, :, :], s.c_aforced.ap.rearrange("(t p) c -> p t c", p=128), w=[AFc])
            for i in range(2):
                sc.dma(EM[:, i, :, :], s.em.ap[i], w=[EM])
            sc.dma(EMW[:, :, :], s.c_atri4.ap[:, :, :], w=[EMW])
            sT_ring = Ring([s.ps(es, "ns_sT%d" % i, [128, 512], F32) for i in range(3)])
            accC = [s.ps(es, "ns_accC%d" % i, [128, 2, 256], F32) for i in range(2)]
            accR = Ring([s.ps(es, "ns_accR%d" % i, [128, 4, 128], F32) for i in range(2)])
            TPb = s.ps(es, "ns_tp", [128, 1024], BF16)
            with ExitStack() as es2:
                W1 = Ring([s.sb(es2, "nc_w1_%d" % i, [64, 32, 128], BF16) for i in range(2)])
                stg = Ring([s.sb(es2, "nc_stg%d" % i, [64, 8, 128], F32) for i in range(2)])
                src = Ring([s.sb(es2, "nc_src%d" % i, [64, S], BF16) for i in range(2)])
                peT = Ring([s.sb(es2, "nc_peT%d" % i, [64, 32], F32) for i in range(2)])
                peTb = Ring([s.sb(es2, "nc_peTb%d" % i, [64, 32], BF16) for i in range(2)])
                W2s = Ring([s.sb(es2, "nc_w2s%d" % i, [128, 64], F32) for i in range(2)])
                W2 = Ring([s.sb(es2, "nc_w2_%d" % i, [128, 64], BF16) for i in range(2)])
                hb = Ring([s.sb(es2, "nc_hb%d" % i, [128, 2], F32) for i in range(2)])
                u = Ring([s.sb(es2, "nc_u%d" % i, [128, 256], F32) for i in range(2)])
                u2 = Ring([s.sb(es2, "nc_u2%d" % i, [128, 256], F32) for i in range(2)])
                hT = Ring([s.sb(es2, "nc_hT%d" % i, [128, 256], BF16) for i in range(2)])
                for g in range(2):
                    sc.POOL(lambda e: e.memset(vcA[g][:, :, :], 0.0), w=[vcA[g]])
                    sc.POOL(lambda e: e.memset(kcT[g][:, :], 0.0), w=[kcT[g]])
                for kv in range(2):
                    w1 = W1.next()
                    w1src = s.cmp_w1.ap[l, kv].rearrange("(a p) n -> p a n", p=64)
                    for hf in range(4):
                        st_ = stg.next()
                        sc.dma(st_[:, :, :], w1src[:, hf * 8:(hf + 1) * 8, :], w=[st_])
                        sc.POOL(lambda e: e.tensor_copy(out=w1[:, hf * 8:(hf + 1) * 8, :], in_=st_[:, :, :]), r=[st_], w=[w1])
                    pT_ = peT.next()
                    with s.nc.allow_non_contiguous_dma(reason="tiny pe transpose"):
                        sc.dma(pT_[:, :], s.cmp_pe.ap[l, kv].rearrange("l d -> d l"), w=[pT_])
                    pTb_ = peTb.next()
                    sc.DVE(lambda e: e.tensor_copy(out=pTb_[:, :], in_=pT_[:, :]), r=[pT_], w=[pTb_])
                    w2s = W2s.next()
                    sc.dma(w2s[:, :], s.cmp_w2.ap[l, kv], w=[w2s])
                    w2 = W2.next()
                    sc.DVE(lambda e: e.tensor_copy(out=w2[:, :], in_=w2s[:, :]), r=[w2s], w=[w2])
                    pbias = sT_ring.next()
                    for ll in range(32):
                        sc.PE(lambda e: e.matmul(pbias[:, 0:1], lhsT=w1[:, ll, :], rhs=pTb_[:, ll:ll + 1], start=(ll == 0), stop=(ll == 31)), r=[w1, pTb_], w=[pbias])
                    hb_ = hb.next()
                    sc.DVE(lambda e: e.tensor_copy(out=hb_[:, 0:1], in_=pbias[:, 0:1]), r=[pbias], w=[hb_])
                    for g in range(2):
                        sr = src.next()
                        sc.dma(sr[:, :], (s.nkcT if kv == 0 else s.nvcT).ap[g], w=[sr])
                        ph = sT_ring.next()
                        for ll in range(32):
                            sc.PE(lambda e: e.matmul(ph[:, 0:NCMP], lhsT=w1[:, ll, :], rhs=sr[:, ll:ll + 16 * (NCMP - 1) + 1:16], start=(ll == 0), stop=(ll == 31)),
                                  r=[w1, sr], w=[ph])
                        u_ = u.next()
                        u2_ = u2.next()
                        h_ = hT.next()
                        n_ = NCMP
                        sc.ACT(lambda e: e.activation(out=u_[:, 0:n_], in_=ph[:, 0:n_], func=AF.Identity, bias=hb_[:, 0:1], scale=1.0), r=[ph, hb_], w=[u_])
                        sc.DVE(lambda e: e.tensor_tensor(out=u2_[:, 0:n_], in0=u_[:, 0:n_], in1=u_[:, 0:n_], op=ALU.mult), r=[u_], w=[u2_])
                        sc.DVE(lambda e: e.tensor_scalar(out=u2_[:, 0:n_], in0=u2_[:, 0:n_], scalar1=0.044715, scalar2=1.0, op0=ALU.mult, op1=ALU.add), r=[u2_], w=[u2_])
                        sc.DVE(lambda e: e.tensor_tensor(out=u2_[:, 0:n_], in0=u2_[:, 0:n_], in1=u_[:, 0:n_], op=ALU.mult), r=[u2_, u_], w=[u2_])
                        sc.ACT(lambda e: e.activation(out=u2_[:, 0:n_], in_=u2_[:, 0:n_], func=AF.Tanh, scale=0.7978845608028654), r=[u2_], w=[u2_])
                        sc.DVE(lambda e: e.tensor_scalar(out=u2_[:, 0:n_], in0=u2_[:, 0:n_], scalar1=1.0, scalar2=0.5, op0=ALU.add, op1=ALU.mult), r=[u2_], w=[u2_])
                        sc.DVE(lambda e: e.tensor_tensor(out=h_[:, 0:n_], in0=u2_[:, 0:n_], in1=u_[:, 0:n_], op=ALU.mult), r=[u2_, u_], w=[h_])
                        po = sT_ring.next()
                        if kv == 0:
                            sc.PE(lambda e: e.matmul(po[0:64, 0:n_], lhsT=w2[:, :], rhs=h_[:, 0:n_], start=True, stop=True), r=[w2, h_], w=[po])
                            sc.DVE(lambda e: e.tensor_copy(out=kcT[g][:, 0:n_], in_=po[0:64, 0:n_]), r=[po], w=[kcT[g]])
                        else:
                            for ct in range(CT):
                                rows = min(CR, n_ - ct * CR)
                                sc.PE(lambda e: e.matmul(po[0:rows, ct * 64:(ct + 1) * 64], lhsT=h_[:, ct * CR:ct * CR + rows], rhs=w2[:, :], start=True, stop=True), r=[w2, h_], w=[po])
                                sc.DVE(lambda e: e.tensor_copy(out=vcA[g][0:rows, ct, 0:64], in_=po[0:rows, ct * 64:(ct + 1) * 64]), r=[po], w=[vcA[g]])
                for g in range(2):
                    sc.POOL(lambda e: e.memset(vcA[g][:, :, 64:66], 1.0), w=[vcA[g]])
                    sc.dma(vcA[g][0:CR, :, 65:129], s.c_overlap.ap.rearrange("(t p) n -> p t n", p=CR), w=[vcA[g]])
            with ExitStack() as es3:
                pT_ring = Ring([s.sb(es3, "ns_pT%d" % i, [128, 512], BF16) for i in range(5)])
                emc_ring = Ring([s.sb(es3, "ns_emc%d" % i, [128, 4, 128], BF16) for i in range(4)])
                Oacc = Ring([s.sb(es3, "ns_O%d" % i, [128, 256], F32) for i in range(3)])
                Obf = Ring([s.sb(es3, "ns_Ob%d" % i, [128, 256], BF16) for i in range(3)])
                sm = Ring([s.sb(es3, "ns_sm%d" % i, [128, 32], F32) for i in range(4)])
                imp = Ring([s.sb(es3, "ns_imp%d" % i, [128, 64], F32) for i in range(2)])
                NS = Ring([s.sb(es3, "ns_NS%d" % i, [128, 128], BF16) for i in range(2)])
                for t_ in NS.items:
                    sc.POOL(lambda e: e.memset(t_[:, :], 0.0), w=[t_])
                steps = []
                for i in range(NT):
                    for g in range(2):
                        O = Oacc.next()
                        Ob = Obf.next()

                        def qfn_lo(i=i, g=g):
                            return lambda c0, c1: QA[g][0:64, i, :, :].rearrange("p h q -> p (h q)")[:, c0:c1]

                        def qfn_full(i=i, g=g):
                            return lambda c0, c1: QA[g][:, i, :, :].rearrange("p h q -> p (h q)")[:, c0:c1]

                        cmax = min(8 * i + 6, NCMP - 1)
                        nct = cmax // CR + 1
                        for ct in range(nct):
                            emc_t = emc_ring.next()
                            r0 = ct * CR - 8 * i + s.OFF
                            sc_dma_args = (emc_t, r0, g)

                            def pre(emc_t=emc_t, r0=r0, g=g):
                                sc.dma(emc_t[0:CR, :, :], s.emc.ap[r0:r0 + CR, 4 * g:4 * g + 4, :], w=[emc_t])
                            fin = None
                            if ct == nct - 1:
                                def fin(i=i, g=g, O=O):
                                    s.nsa_fin_cmp(i, g, O, accC, G, AFc, sm, imp, NS, TPb, ident, QA, qa_hi)
                            steps.append(dict(pre=pre, kT=kcT[g][:, ct * CR:(ct + 1) * CR], q_fn=qfn_lo(), kk=CR, ncol=512, c0=0, scale=0.125,
                                              masks=[(0, 512, emc_t[0:CR, :, :].rearrange("p h q -> p (h q)"), [emc_t], False)],
                                              V=vcA[g][0:CR, ct, 0:129], subs=[0, 1, 2, 3],
                                              acc_fn=(lambda sub: (accC[sub // 2][:, sub % 2, 0:129], accC[sub // 2])),
                                              first=(ct == 0), start_subs=(0, 2), fin=fin, rb_score=[kcT[g], qa_lo[g]], rb_v=[vcA[g]], mask_pool=False))
                        last_cmp_idx = len(steps) - 1
                        accW = accR.next()
                        js = list(range(max(0, i - 4), i + 1))
                        for j in js:
                            masks = []
                            if j == i:
                                masks.append((0, 512, EM[:, 0, 4 * g:4 * g + 4, :].rearrange("p h q -> p (h q)"), [EM], False))
                            elif j == i - 1:
                                masks.append((0, 512, EM[:, 1, 4 * g:4 * g + 4, :].rearrange("p h q -> p (h q)"), [EM], False))
                            elif j == i - 4:
                                masks.append((0, 512, EMW[:, :, :].rearrange("p h q -> p (h q)"), [EMW], False))
                            fin = None
                            if j == i:
                                def fin(i=i, g=g, O=O, accW=accW):
                                    s.nsa_fin_branch(i, g, O, None, accW, G, sm, 2)
                            steps.append(dict(pre=None, kT=kwT[g][:, j * 128:(j + 1) * 128], q_fn=qfn_lo(), kk=128, ncol=512, c0=0, scale=0.125, masks=masks,
                                              V=vw[g][:, j, 0:65], subs=[0, 1, 2, 3], acc_fn=(lambda sub, accW=accW: (accW[:, sub, 0:65], accW)),
                                              first=(j == js[0]), fin=fin, rb_score=[kwT[g], qa_lo[g]], rb_v=[vw[g]], mask_pool=True))
                        accS = accR.next()
                        for j in range(0, i + 1):
                            masks = []
                            if j == i:
                                masks.append((0, 512, EM[:, 0, 4 * g:4 * g + 4, :].rearrange("p h q -> p (h q)"), [EM], False))
                            elif j == i - 1:
                                masks.append((0, 512, EM[:, 1, 4 * g:4 * g + 4, :].rearrange("p h q -> p (h q)"), [EM], False))
                            fin = None
                            if j == i:
                                def fin(i=i, g=g, O=O, Ob=Ob, accS=accS):
                                    s.nsa_fin_branch(i, g, O, Ob, accS, G, sm, 1)
                            steps.append(dict(pre=None, kT=ksA[g][:, j * 128:(j + 1) * 128], q_fn=qfn_full(), kk=128, ncol=512, c0=0, scale=0.125, masks=masks,
                                              V=vs[g][:, j, 0:65], subs=[0, 1, 2, 3], acc_fn=(lambda sub, accS=accS: (accS[:, sub, 0:65], accS)),
                                              first=(j == 0), fin=fin, rb_score=[ksA[g], qa_lo[g], qa_hi[g][i]], rb_v=[vs[g]], mask_pool=True, dep_step=last_cmp_idx))
                for st in steps:
                    if st["pre"] is not None:
                        pass
                s.run_steps_pre(steps, sT_ring, pT_ring, skew=2)
        sc.barrier()

    def run_steps_pre(s, steps, sT_ring, pT_ring, skew=1):
        s.run_steps(steps, sT_ring, pT_ring, skew=skew)

    def nsa_fin_cmp(s, i, g, O, accC, G, AFc, sm, imp, NS, TPb, ident, QA, qa_hi):
        sc = s.sc
        m = sm.next()
        for bk in range(2):
            sc.DVE(lambda e: e.tensor_scalar(out=m[:, 2 * bk:2 * bk + 2], in0=accC[bk][:, :, 64:65], scalar1=1e-30, scalar2=None, op0=ALU.max),
                   r=[accC[bk]], w=[m])
        sc.DVE(lambda e: e.reciprocal(out=m[:, 4:8], in_=m[:, 0:4]), r=[m], w=[m])
        sc.DVE(lambda e: e.tensor_tensor(out=m[:, 8:12], in0=m[:, 4:8], in1=G[:, i, 12 * g + 0:12 * g + 12:3], op=ALU.mult), r=[m, G], w=[m])
        im = imp.next()
        for hh in range(4):
            U = accC[hh // 2][:, hh % 2, 65:129]
            if hh == 0:
                sc.DVE(lambda e: e.tensor_scalar(out=im[:, :], in0=U, scalar1=m[:, 4:5], scalar2=None, op0=ALU.mult), r=[accC[0], m], w=[im])
            else:
                sc.DVE(lambda e: e.scalar_tensor_tensor(out=im[:, :], in0=U, scalar=m[:, 4 + hh:5 + hh], in1=im[:, :], op0=ALU.mult, op1=ALU.add),
                       r=[accC[hh // 2], m, im], w=[im])
        sc.DVE(lambda e: e.tensor_tensor(out=im[:, :], in0=im[:, :], in1=AFc[:, i, :], op=ALU.add), r=[im, AFc], w=[im])
        sc.DVE(lambda e: e.max(out=m[:, 16:24], in_=im[:, :]), r=[im], w=[m])
        ns = NS.next()
        sc.DVE(lambda e: e.tensor_scalar(out=ns[:, 64:128], in0=im[:, :], scalar1=m[:, 23:24], scalar2=1.0, op0=ALU.is_ge, op1=ALU.subtract), r=[im, m], w=[ns])
        sc.PE(lambda e: e.transpose(out=TPb[:, 0:128], in_=ns[:, :], identity=ident[:, :]), r=[ns, ident], w=[TPb])
        sc.DVE(lambda e: e.tensor_copy(out=QA[g][64:128, i, 0, :], in_=TPb[64:128, 0:128]), r=[TPb], w=[qa_hi[g][i]])
        for hh in range(1, 4):
            sc.POOL(lambda e: e.tensor_copy(out=QA[g][64:128, i, hh, :], in_=QA[g][64:128, i, 0, :]), r=[qa_hi[g][i]], w=[qa_hi[g][i]])
        for hh in range(4):
            num = accC[hh // 2][:, hh % 2, 0:64]
            sc.DVE(lambda e: e.tensor_scalar(out=O[:, hh * 64:(hh + 1) * 64], in0=num, scalar1=m[:, 8 + hh:9 + hh], scalar2=None, op0=ALU.mult), r=[accC[hh // 2], m], w=[O])

    def nsa_fin_branch(s, i, g, O, Ob, acc, G, sm, br):
        sc = s.sc
        m = sm.next()
        sc.DVE(lambda e: e.tensor_scalar(out=m[:, 0:4], in0=acc[:, :, 64:65], scalar1=1e-30, scalar2=None, op0=ALU.max), r=[acc], w=[m])
        sc.DVE(lambda e: e.reciprocal(out=m[:, 4:8], in_=m[:, 0:4]), r=[m], w=[m])
        sc.DVE(lambda e: e.tensor_tensor(out=m[:, 8:12], in0=m[:, 4:8], in1=G[:, i, 12 * g + br:12 * g + 12:3], op=ALU.mult), r=[m, G], w=[m])
        dst = O if Ob is None else Ob
        for hh in range(4):
            sc.DVE(lambda e: e.scalar_tensor_tensor(out=dst[:, hh * 64:(hh + 1) * 64], in0=acc[:, hh, 0:64], scalar=m[:, 8 + hh:9 + hh], in1=O[:, hh * 64:(hh + 1) * 64],
                                                    op0=ALU.mult, op1=ALU.add), r=[acc, m, O], w=[dst])
        if Ob is not None:
            sc.dma(s.o_tm.ap[i * 128:(i + 1) * 128, 256 * g:256 * g + 256], Ob[:, :], r=[Ob], w=[Buf()], q="pool")

    def phase_causal(s, l, kind):
        sc = s.sc
        S, NT, NB = s.S, s.NT, s.NB
        if kind == "mla":
            K, qd, kd, vd, scale, ocol, clamp = 96, s.mqT, s.mkT, s.mv, 96 ** -0.5, 512, False
        else:
            K, qd, kd, vd, scale, ocol, clamp = 70, s.fqT, s.fkT, s.fv, 0.125, 768, True
        with ExitStack() as es:
            tri = s.sb(es, "ca_tri", [128, 128], BF16)
            sc.dma(tri[:, :], s.c_tri.ap[:, :], w=[tri])
            qT = [s.sb(es, "ca_q%d" % h, [K, S], BF16) for h in range(4)]
            kT = [s.sb(es, "ca_k%d" % h, [K, S], BF16) for h in range(4)]
            V = [s.sb(es, "ca_v%d" % h, [128, NT, 72], BF16) for h in range(4)]
            for h in range(4):
                sc.dma(qT[h][:, :], qd.ap[h], w=[qT[h]])
                sc.dma(kT[h][:, :], kd.ap[h], w=[kT[h]])
                sc.dma(V[h][:, :, :], vd.ap[:, h, :].rearrange("(t p) e -> p t e", p=128), w=[V[h]])
            sT_ring = Ring([s.ps(es, "ca_sT%d" % i, [128, 512], F32) for i in range(4)])
            accR = Ring([s.ps(es, "ca_acc%d" % i, [128, 4, 128], F32) for i in range(3)])
            pT_ring = Ring([s.sb(es, "ca_pT%d" % i, [128, 512], BF16) for i in range(6)])
            osb = Ring([s.sb(es, "ca_o%d" % i, [128, 4, 256], BF16) for i in range(2)])
            sm = Ring([s.sb(es, "ca_sm%d" % i, [128, 8], F32) for i in range(3)])
            steps = []
            for qb in range(NB):
                o_ = osb.next()
                for h in range(4):
                    acc = accR.next()
                    nk = 4 * qb + 4
                    for j in range(nk):
                        sp = j - 4 * qb
                        c0 = max(0, sp) * 128
                        masks = []
                        if sp >= 0:
                            masks.append((c0, c0 + 128, tri[:, :], [tri], clamp))
                        subs = list(range(max(0, sp), 4))
                        fin = None
                        if j == nk - 1:
                            def fin(qb=qb, h=h, acc=acc, o_=o_):
                                m = sm.next()
                                sc.DVE(lambda e: e.tensor_scalar(out=m[:, 0:4], in0=acc[:, :, 64:65], scalar1=1e-30, scalar2=None, op0=ALU.max), r=[acc], w=[m])
                                sc.DVE(lambda e: e.reciprocal(out=m[:, 4:8], in_=m[:, 0:4]), r=[m], w=[m])
                                for sub in range(4):
                                    sc.DVE(lambda e: e.tensor_scalar(out=o_[:, sub, 64 * h:64 * h + 64], in0=acc[:, sub, 0:64], scalar1=m[:, 4 + sub:5 + sub], scalar2=None, op0=ALU.mult),
                                           r=[acc, m], w=[o_])
                                if h == 3:
                                    sc.dma(s.o_tm.ap[qb * 512:(qb + 1) * 512, ocol:ocol + 256].rearrange("(s p) c -> p s c", p=128), o_[:, :, :], r=[o_], w=[Buf()], q="sp")
                        steps.append(dict(kT=kT[h][:, j * 128:(j + 1) * 128], q_fn=(lambda c0, c1, h=h, qb=qb: qT[h][:, qb * 512 + c0:qb * 512 + c1]),
                                          kk=128, ncol=512, c0=c0, scale=scale, masks=masks, V=V[h][:, j, 0:65], subs=subs,
                                          acc_fn=(lambda sub, acc=acc: (acc[:, sub, 0:65], acc)), first=(j == 0), fin=fin,
                                          rb_score=[kT[h], qT[h]], rb_v=[V[h]], mask_pool=False))
            s.run_steps(steps, sT_ring, pT_ring, skew=3)
        sc.barrier()

    def phase_combine(s, l, xsrc, xdst):
        sc = s.sc
        S, NT, NB = s.S, s.NT, s.NB
        with ExitStack() as es:
            ident = s.sb(es, "ident", [128, 128], BF16)
            sc.dma(ident[:, :], s.c_ident.ap[:, :], w=[ident])
            stg = s.stages(es, 2)
            Wg = s.sb(es, "cb_wg", [128, 8, 3 * D], BF16)
            Wb = s.sb(es, "cb_wb", [128, 8, D], BF16)
            Wo = s.sb(es, "cb_wo", [128, 8, D], BF16)
            s.load_w(TTv(Wb, 0, 4), s.w_br_nsa.ap[l], 128, 4, D, stg)
            s.load_w(TTv(Wb, 4, 2), s.w_br_mla.ap[l], 128, 2, D, stg)
            s.load_w(TTv(Wb, 6, 2), s.w_br_fox.ap[l], 128, 2, D, stg)
            s.load_w(Wg, s.w_gate.ap[l], 128, 8, 3 * D, stg, order=[0, 4, 8, 1, 5, 9, 2, 6, 10, 3, 7, 11])
            s.load_w(Wo, s.w_mix_out.ap[l], 128, 8, D, stg)
            R = s.ln_setup(es, l, 0)
            PG = Ring([s.ps(es, "cb_pg%d" % i, [128, 512], F32) for i in range(2)])
            PP = Ring([s.ps(es, "cb_pp%d" % i, [128, 512], F32) for i in range(2)])
            PY = Ring([s.ps(es, "cb_py%d" % i, [128, 1024], F32) for i in range(1)])
            TPr = Ring([s.ps(es, "cb_tp%d" % i, [128, 8, 128], BF16) for i in range(2)])
            otm = Ring([s.sb(es, "cb_otm%d" % i, [128, D], BF16) for i in range(2)])
            oT = Ring([s.sb(es, "cb_oT%d" % i, [128, 8, 512], BF16) for i in range(2)])
            xTb = Ring([s.sb(es, "cb_xT%d" % i, [128, 8, 512], BF16) for i in range(2)])
            mT = Ring([s.sb(es, "cb_mT%d" % i, [128, 8, 512], BF16) for i in range(1)])
            sg = Ring([s.sb(es, "cb_sg%d" % i, [128, 512], F32) for i in range(2)])
            tA = Ring([s.sb(es, "cb_tA%d" % i, [128, 512], F32) for i in range(2)])
            tB = Ring([s.sb(es, "cb_tB%d" % i, [128, 512], F32) for i in range(2)])
            for b in range(NB):
                bs = slice(b * 512, (b + 1) * 512)
                oT_ = oT.next()
                for tt in range(4):
                    t = b * 4 + tt
                    ot = otm.next()
                    sc.dma(ot[:, :], s.o_tm.ap[t * 128:(t + 1) * 128, :], w=[ot], q="pool")
                    tp = TPr.next()
                    for kc in range(8):
                        sc.PE(lambda e: e.transpose(out=tp[:, kc, :], in_=ot[:, kc * 128:(kc + 1) * 128], identity=ident[:, :]), r=[ot, ident], w=[tp])
                    sc.ACT(lambda e: e.copy(out=oT_[:, :, tt * 128:(tt + 1) * 128], in_=tp[:, :, :]), r=[tp], w=[oT_])
                x_ = xTb.next()
                sc.dma(x_[:, :, :], s.xT.ap.rearrange("k p s -> p k s")[:, :, bs], w=[x_], q="pool")
                m_ = mT.next()
                brk = [(0, 4), (4, 6), (6, 8)]
                for n in range(8):
                    ns_ = slice(n * 128, (n + 1) * 128)
                    tA_ = tA.next()
                    for br in range(3):
                        pg = PG.next()
                        for kc in range(8):
                            sc.PE(lambda e: e.matmul(pg[:, :], lhsT=Wg[:, kc, br * D + n * 128:br * D + (n + 1) * 128], rhs=x_[:, kc, :], start=(kc == 0), stop=(kc == 7)), r=s.wr(Wg, br * D + n * 128, br * D + (n + 1) * 128) + [x_], w=[pg])
                        pp = PP.next()
                        k0, k1 = brk[br]
                        for kc in range(k0, k1):
                            sc.PE(lambda e: e.matmul(pp[:, :], lhsT=Wb[:, kc, ns_], rhs=oT_[:, kc, :], start=(kc == k0), stop=(kc == k1 - 1)), r=s.wr(Wb, n * 128, (n + 1) * 128, kc, kc + 1) + [oT_], w=[pp])
                        sg_ = sg.next()
                        sc.ACT(lambda e: e.activation(out=sg_[:, :], in_=pg[:, :], func=AF.Sigmoid), r=[pg], w=[sg_])
                        if br == 0:
                            sc.DVE(lambda e: e.tensor_tensor(out=tA_[:, :], in0=pp[:, :], in1=sg_[:, :], op=ALU.mult), r=[pp, sg_], w=[tA_])
                        else:
                            tB_ = tB.next()
                            sc.DVE(lambda e: e.tensor_tensor(out=tB_[:, :], in0=pp[:, :], in1=sg_[:, :], op=ALU.mult), r=[pp, sg_], w=[tB_])
                            if br == 1:
                                sc.DVE(lambda e: e.tensor_tensor(out=tA_[:, :], in0=tA_[:, :], in1=tB_[:, :], op=ALU.add), r=[tA_, tB_], w=[tA_])
                            else:
                                sc.DVE(lambda e: e.tensor_tensor(out=m_[:, n, :], in0=tA_[:, :], in1=tB_[:, :], op=ALU.add), r=[tA_, tB_], w=[m_])
                for tt in range(4):
                    t = b * 4 + tt
                    py = PY.next()
                    for hf in range(2):
                        for kc in range(8):
                            sc.PE(lambda e: e.matmul(py[:, hf * 512:(hf + 1) * 512], lhsT=m_[:, kc, tt * 128:(tt + 1) * 128], rhs=Wo[:, kc, hf * 512:(hf + 1) * 512], start=(kc == 0), stop=(kc == 7)),
                                  r=s.wr(Wo, hf * 512, (hf + 1) * 512) + [m_], w=[py])
                    s.layer_norm_tile(R, py, t, xsrc, xdst, TPr, ident)
            s.ln_flush(R)
        sc.barrier()

    def phase_cross(s, l, xsrc, xdst):
        sc = s.sc
        S, NT, NB = s.S, s.NT, s.NB
        with ExitStack() as es:
            ident = s.sb(es, "ident", [128, 128], BF16)
            sc.dma(ident[:, :], s.c_ident.ap[:, :], w=[ident])
            stg = s.stages(es)
            Wq = s.sb(es, "xc_wq", [128, 8, 256], BF16)
            Wkv = s.sb(es, "xc_wkv", [128, 8, 512], BF16)
            Wo = s.sb(es, "xc_wo", [128, 2, D], BF16)
            s.load_w(Wq, s.xa_w_q.ap[l], 128, 8, 256, stg)
            s.load_w(Wkv, s.xa_w_kv.ap[l], 128, 8, 512, stg)
            s.load_w(Wo, s.xa_w_o.ap[l], 128, 2, D, stg)
            memT = s.sb(es, "xc_memT", [128, 8, MEM], BF16)
            sc.dma(memT[:, :, :], s.memT.ap.rearrange("k p m -> p k m"), w=[memT])
            R = s.ln_setup(es, l, 1, depth=2)
            osb = s.sb(es, "xc_oall", [128, NT, 256], BF16)
            ob = [Buf() for _ in range(NB)]
            qx = s.sb(es, "xc_qx", [64, 4, S], BF16)
            qb_ = [Buf() for _ in range(NB)]
            with ExitStack() as es1:
                sT_ring = Ring([s.ps(es1, "xc_sT%d" % i, [128, 512], F32) for i in range(3)])
                accR = Ring([s.ps(es1, "xc_acc%d" % i, [128, 4, 128], F32) for i in range(3)])
                PQ = Ring([s.ps(es1, "xc_pq%d" % i, [128, 512], F32) for i in range(2)])
                pT_ring = Ring([s.sb(es1, "xc_pT%d" % i, [128, 512], BF16) for i in range(5)])
                kT = s.sb(es1, "xc_kT", [64, 4, MEM], BF16)
                V = s.sb(es1, "xc_V", [128, 2, 4, 72], BF16)
                sc.POOL(lambda e: e.memset(V[:, :, :, :], 1.0), w=[V])
                for h in range(4):
                    pq = PQ.next()
                    for kc in range(8):
                        sc.PE(lambda e: e.matmul(pq[0:64, 0:MEM], lhsT=Wkv[:, kc, 64 * h:64 * h + 64], rhs=memT[:, kc, :], start=(kc == 0), stop=(kc == 7)), r=[Wkv, memT], w=[pq])
                    sc.ACT(lambda e: e.copy(out=kT[:, h, :], in_=pq[0:64, 0:MEM]), r=[pq], w=[kT])
                for t in range(2):
                    pq = PQ.next()
                    for kc in range(8):
                        sc.PE(lambda e: e.matmul(pq[:, 0:256], lhsT=memT[:, kc, t * 128:(t + 1) * 128], rhs=Wkv[:, kc, 256:512], start=(kc == 0), stop=(kc == 7)), r=[Wkv, memT], w=[pq])
                    sc.ACT(lambda e: e.copy(out=V[:, t, :, 0:64], in_=pq[:, 0:256].rearrange("p (g d) -> p g d", g=4)), r=[pq], w=[V])
                xTb = Ring([s.sb(es1, "xc_xT%d" % i, [128, 8, 512], BF16) for i in range(2)])
                sm = Ring([s.sb(es1, "xc_sm%d" % i, [128, 8], F32) for i in range(3)])
                steps = []
                for b in range(NB):
                    bs = slice(b * 512, (b + 1) * 512)
                    x_ = xTb.next()
                    sc.dma(x_[:, :, :], s.xT.ap.rearrange("k p s -> p k s")[:, :, bs], w=[x_])
                    for h in range(4):
                        pq = PQ.next()
                        for kc in range(8):
                            sc.PE(lambda e: e.matmul(pq[0:64, :], lhsT=Wq[:, kc, 64 * h:64 * h + 64], rhs=x_[:, kc, :], start=(kc == 0), stop=(kc == 7)), r=[Wq, x_], w=[pq])
                        if h % 2 == 0:
                            sc.DVE(lambda e: e.tensor_copy(out=qx[:, h, bs], in_=pq[0:64, :]), r=[pq], w=[qb_[b]])
                        else:
                            sc.ACT(lambda e: e.copy(out=qx[:, h, bs], in_=pq[0:64, :]), r=[pq], w=[qb_[b]])
                    for h in range(4):
                        acc = accR.next()
                        for j in range(2):
                            fin = None
                            if j == 1:
                                def fin(h=h, acc=acc, b=b):
                                    m = sm.next()
                                    sc.DVE(lambda e: e.reciprocal(out=m[:, 4:8], in_=acc[:, :, 64:65]), r=[acc], w=[m])
                                    for sub in range(4):
                                        sc.DVE(lambda e: e.tensor_scalar(out=osb[:, 4 * b + sub, 64 * h:64 * h + 64], in0=acc[:, sub, 0:64], scalar1=m[:, 4 + sub:5 + sub], scalar2=None, op0=ALU.mult),
                                               r=[acc, m], w=[ob[b]])
                            steps.append(dict(kT=kT[:, h, j * 128:(j + 1) * 128], q_fn=(lambda c0, c1, h=h, b=b: qx[:, h, b * 512 + c0:b * 512 + c1]), kk=128, ncol=512, c0=0, scale=0.125,
                                              masks=[], V=V[:, j, h, 0:65], subs=[0, 1, 2, 3], acc_fn=(lambda sub, acc=acc: (acc[:, sub, 0:65], acc)), first=(j == 0), fin=fin,
                                              rb_score=[kT, qb_[b]], rb_v=[V], mask_pool=False))
                s.run_steps(steps, sT_ring, pT_ring, skew=2)
            sc.barrier()
            with ExitStack() as es2:
                TPo = Ring([s.ps(es2, "xc_tpo%d" % i, [128, 8, 128], BF16) for i in range(2)])
                PY = Ring([s.ps(es2, "xc_py%d" % i, [128, 1024], F32) for i in range(2)])
                TPr = Ring([s.ps(es2, "xc_tp%d" % i, [128, 8, 128], BF16) for i in range(2)])
                oTt = Ring([s.sb(es2, "xc_oT%d" % i, [128, 2, 128], BF16) for i in range(3)])
                for t in range(NT):
                    tp = TPo.next()
                    for kc in range(2):
                        sc.PE(lambda e: e.transpose(out=tp[:, kc, :], in_=osb[:, t, kc * 128:(kc + 1) * 128], identity=ident[:, :]), r=[ob[t // 4], ident], w=[tp])
                    oT_ = oTt.next()
                    sc.ACT(lambda e: e.copy(out=oT_[:, :, :], in_=tp[:, 0:2, :]), r=[tp], w=[oT_])
                    py = PY.next()
                    for hf in range(2):
                        for kc in range(2):
                            sc.PE(lambda e: e.matmul(py[:, hf * 512:(hf + 1) * 512], lhsT=oT_[:, kc, :], rhs=Wo[:, kc, hf * 512:(hf + 1) * 512], start=(kc == 0), stop=(kc == 1)), r=[Wo, oT_], w=[py])
                    s.layer_norm_tile(R, py, t, xsrc, xdst, TPr, ident)
                s.ln_flush(R)
        sc.barrier()

    def phase_mlp_up(s, l):
        sc = s.sc
        S, NT, NB = s.S, s.NT, s.NB
        with ExitStack() as es:
            stg = s.stages(es)
            W = s.sb(es, "mu_w", [128, 8, 4 * D], BF16)
            s.load_w(W, s.w_up.ap[l], 128, 8, 4 * D, stg)
            PB = Ring([s.ps(es, "mu_pb%d" % i, [128, 512], F32) for i in range(4)])
            xTb = Ring([s.sb(es, "mu_xT%d" % i, [128, 8, 512], BF16) for i in range(2)])
            r_ = Ring([s.sb(es, "mu_r%d" % i, [128, 512], F32) for i in range(3)])
            h_ = Ring([s.sb(es, "mu_h%d" % i, [128, 512], BF16) for i in range(3)])
            for b in range(NB):
                bs = slice(b * 512, (b + 1) * 512)
                x_ = xTb.next()
                sc.dma(x_[:, :, :], s.xT.ap.rearrange("k p s -> p k s")[:, :, bs], w=[x_])
                for n in range(32):
                    pb = PB.next()
                    for kc in range(8):
                        sc.PE(lambda e: e.matmul(pb[:, :], lhsT=W[:, kc, n * 128:(n + 1) * 128], rhs=x_[:, kc, :], start=(kc == 0), stop=(kc == 7)), r=s.wr(W, n * 128, (n + 1) * 128) + [x_], w=[pb])
                    rr = r_.next()
                    sc.ACT(lambda e: e.activation(out=rr[:, :], in_=pb[:, :], func=AF.Relu), r=[pb], w=[rr])
                    hh = h_.next()
                    if n % 2 == 0:
                        sc.POOL(lambda e: e.tensor_tensor(out=hh[:, :], in0=rr[:, :], in1=rr[:, :], op=ALU.mult), r=[rr], w=[hh])
                    else:
                        sc.DVE(lambda e: e.tensor_tensor(out=hh[:, :], in0=rr[:, :], in1=rr[:, :], op=ALU.mult), r=[rr], w=[hh])
                    sc.dma(s.hT.ap[n, :, bs], hh[:, :], r=[hh], w=[Buf()], q="sp")
        sc.barrier()

    def phase_mlp_down(s, l, xsrc, xdst, final):
        sc = s.sc
        S, NT, NB = s.S, s.NT, s.NB
        with ExitStack() as es:
            ident = s.sb(es, "ident", [128, 128], BF16)
            sc.dma(ident[:, :], s.c_ident.ap[:, :], w=[ident])
            stg = s.stages(es)
            W = s.sb(es, "md_w", [128, 32, D], BF16)
            W.kb = []
            for c in range(16):
                st = stg.next()
                sv = st[:, 0:2048].rearrange("p (a n) -> p a n", a=2)
                sc.dma(sv, s.w_down.ap[l, c * 256:(c + 1) * 256, :].rearrange("(a p) n -> p a n", p=128), w=[st])
                cb = Buf()
                W.kb.append(cb)
                if c % 3 == 0:
                    sc.DVE(lambda e: e.tensor_copy(out=W[:, 2 * c:2 * c + 2, :], in_=sv), r=[st], w=[cb])
                elif c % 3 == 1:
                    sc.ACT(lambda e: e.copy(out=W[:, 2 * c:2 * c + 2, :], in_=sv), r=[st], w=[cb])
                else:
                    sc.POOL(lambda e: e.tensor_copy(out=W[:, 2 * c:2 * c + 2, :], in_=sv), r=[st], w=[cb])
            R = s.ln_setup(es, l, 2, depth=2)
            PY = Ring([s.ps(es, "md_py%d" % i, [128, 1024], F32) for i in range(2)])
            TPr = Ring([s.ps(es, "md_tp%d" % i, [128, 8, 128], BF16) for i in range(2)])
            hb = Ring([s.sb(es, "md_h%d" % i, [128, 32, 256], BF16) for i in range(2)])
            for b2 in range(NT // 2):
                bs = slice(b2 * 256, (b2 + 1) * 256)
                h_ = hb.next()
                for qq in range(4):
                    sc.dma(h_[:, qq * 8:(qq + 1) * 8, :], s.hT.ap.rearrange("k p s -> p k s")[:, qq * 8:(qq + 1) * 8, bs], w=[h_], q="pool")
                for tt in range(2):
                    t = b2 * 2 + tt
                    py = PY.next()
                    for hf in range(2):
                        for kc in range(32):
                            sc.PE(lambda e: e.matmul(py[:, hf * 512:(hf + 1) * 512], lhsT=h_[:, kc, tt * 128:(tt + 1) * 128], rhs=W[:, kc, hf * 512:(hf + 1) * 512], start=(kc == 0), stop=(kc == 31)),
                                  r=[W.kb[kc // 2], h_], w=[py])
                    s.layer_norm_tile(R, py, t, xsrc, xdst, TPr, ident, final=final)
            s.ln_flush(R)
        sc.barrier()


class TTv:
    def __init__(s, tt, a0, n):
        s.tt = tt
        s.a0 = a0
        s.buf = tt.buf

    def __getitem__(s, k):
        p, a, c = k
        if isinstance(a, slice):
            a = slice((a.start or 0) + s.a0, (a.stop if a.stop is not None else 0) + s.a0)
        else:
            a = a + s.a0
        return s.tt[p, a, c]


_CACHE = {}


def get_prog(S, depth, debug=None):
    key = (S, depth, tuple(sorted(debug)) if debug else None)
    if key not in _CACHE:
        p = Prog(S, depth, debug=debug)
        p.build()
        _CACHE[key] = p
    return _CACHE[key]


def make_in_maps(inputs, S, depth, ncores):
    consts = host_consts(S)
    near, cmpb, c31 = t5_gather(np.asarray(inputs["t5_table"], np.float32), consts)
    shared = {}
    for k in ("w_in", "cmp_pe", "cmp_w1", "cmp_w2", "mla_q_norm", "mla_w_uq", "mla_kv_norm", "mla_w_ukv", "fox_b_f", "w_gate",
              "w_br_nsa", "w_br_mla", "w_br_fox", "w_mix_out", "xa_w_q", "xa_w_kv", "xa_w_o", "mlp_w_up", "mlp_w_down", "ln_g", "ln_b"):
        shared[k] = np.ascontiguousarray(np.asarray(inputs[k], np.float32)[:depth])
    for k in ("ident_bf", "tri_bf", "tri4_bf", "atri4_bf", "nearmask", "rope_cs", "rope_ss", "overlap_bf", "aforced", "expand_bf", "ones_bf", "cmp_valid"):
        shared[k] = consts[k]
    shared["t5_near"] = near
    shared["t5_cmpb"] = cmpb
    shared["t5_c31"] = c31
    maps = []
    for c in range(ncores):
        m = dict(shared)
        m["x"] = np.ascontiguousarray(np.asarray(inputs["x"][c], np.float32))
        m["mem"] = np.ascontiguousarray(np.asarray(inputs["mem"][c], np.float32))
        maps.append(m)
    return maps


def kernel(**inputs):
    S, depth, ncores = SEQ_FULL, DEPTH_FULL, 8
    p = get_prog(S, depth)
    maps = make_in_maps(inputs, S, depth, ncores)
    res = run_bass_kernel_spmd(p.nc, maps, core_ids=list(range(ncores)))
    out = np.stack([np.asarray(r["y"], np.float32) for r in res.results], 0)
    return out
```
